# Optimizing a Trainium2 kernel written in Bass

```python
import jax
import jax.numpy as jnp
from jax import lax
import numpy as np

D_MODEL = 1024
BATCH = 8
SEQ = 2048
DEPTH = 4

N_META = 16
CHUNK = 64
PAD_FRONT = (-N_META) % CHUNK
BRANCH_WIDTH = 512
N_BRANCH = 4
D_FF = 4 * D_MODEL
NORM_EPS = 1e-6

RET_HEADS = 4
RET_DK = 64
RET_DV = 128
RET_ROPE_BASE = 10000.0

S5_GROUP = 16
S5_GROUPS = BRANCH_WIDTH // S5_GROUP
S5_STATE = 64

SSD_HEADDIM = 64
SSD_HEADS = BRANCH_WIDTH // SSD_HEADDIM
SSD_GROUPS = 2
SSD_STATE = 128
SSD_CONV = 4
SSD_CONV_DIM = BRANCH_WIDTH + 2 * SSD_GROUPS * SSD_STATE

HG_HEADS = 4
HG_DK = BRANCH_WIDTH // HG_HEADS
HG_DV = BRANCH_WIDTH // HG_HEADS

IN_SIZES = (RET_HEADS * RET_DK, RET_HEADS * RET_DK, RET_HEADS * RET_DV, RET_HEADS * RET_DV,
            BRANCH_WIDTH,
            BRANCH_WIDTH, SSD_CONV_DIM, SSD_HEADS,
            HG_HEADS * HG_DK, HG_HEADS * HG_DK, HG_HEADS * HG_DV, HG_HEADS * HG_DV,
            N_BRANCH * D_MODEL)
IN_DIM = sum(IN_SIZES)

kernel_name = 'hybrid_ret_s5_ssd_hgrn2_trunk'


def rms_norm(x, w):
    xf = x.astype(jnp.float32)
    y = xf * lax.rsqrt(jnp.mean(xf * xf, axis=-1, keepdims=True) + NORM_EPS)
    return (y * w.astype(jnp.float32)).astype(x.dtype)


def to_chunks(t):
    pad = [(0, 0)] * t.ndim
    pad[1] = (PAD_FRONT, 0)
    t = jnp.pad(t, pad)
    b, tt = t.shape[0], t.shape[1]
    t = t.reshape((b, tt // CHUNK, CHUNK) + t.shape[2:])
    return jnp.moveaxis(t, 1, 0)


def from_chunks(y):
    y = jnp.moveaxis(y, 0, 1)
    b, n = y.shape[0], y.shape[1]
    y = y.reshape((b, n * CHUNK) + y.shape[3:])
    return y[:, PAD_FRONT:]


def chunked_scalar_decay_attention(q, k, v, log_a):
    b, _, g, dk = q.shape
    hg, dv = v.shape[3], v.shape[4]
    causal = jnp.tril(jnp.ones((CHUNK, CHUNK), dtype=bool))[None, :, :, None, None]

    def step(state, inp):
        qc, kc, vc, lac = inp
        cum = jnp.cumsum(lac.astype(jnp.float32), axis=1)
        seg = cum[:, :, None] - cum[:, None, :]
        decay = jnp.exp(jnp.where(causal, seg, -jnp.inf)).astype(qc.dtype)
        scores = jnp.einsum('btgd,bsgd->btsg', qc, kc)
        y_in = jnp.einsum('btsgh,bsghe->btghe', scores[..., None] * decay, vc)
        y_x = jnp.einsum('btgd,bghde->btghe', qc, state) * jnp.exp(cum)[..., None]
        last = cum[:, -1]
        w = jnp.exp(last[:, None] - cum).astype(kc.dtype)
        state = state * jnp.exp(last)[..., None, None] + jnp.einsum('bsgd,bsgh,bsghe->bghde', kc, w, vc)
        return state, y_in + y_x

    state0 = jnp.zeros((b, g, hg, dk, dv), jnp.float32)
    _, y = lax.scan(step, state0, (to_chunks(q), to_chunks(k), to_chunks(v), to_chunks(log_a)))
    return from_chunks(y).astype(v.dtype)


def chunked_vector_decay_attention(q, k, v, log_f):
    b, _, h, dk = q.shape
    dv = v.shape[-1]
    causal = jnp.tril(jnp.ones((CHUNK, CHUNK), dtype=bool))[None, :, :, None, None]

    def step(state, inp):
        qc, kc, vc, lfc = inp
        cum = jnp.cumsum(lfc.astype(jnp.float32), axis=1)
        decay = jnp.exp(jnp.where(causal, cum[:, :, None] - cum[:, None, :], -jnp.inf)).astype(qc.dtype)
        scores = jnp.einsum('bthd,bshd,btshd->btsh', qc, kc, decay)
        y_in = jnp.einsum('btsh,bshe->bthe', scores, vc)
        y_x = jnp.einsum('bthd,bhde->bthe', qc * jnp.exp(cum).astype(qc.dtype), state)
        last = cum[:, -1]
        kw = kc * jnp.exp(last[:, None] - cum).astype(kc.dtype)
        state = state * jnp.exp(last)[..., None] + jnp.einsum('bshd,bshe->bhde', kw, vc)
        return state, y_in + y_x

    state0 = jnp.zeros((b, h, dk, dv), jnp.float32)
    _, y = lax.scan(step, state0, (to_chunks(q), to_chunks(k), to_chunks(v), to_chunks(log_f)))
    return from_chunks(y).astype(v.dtype)


def rotary(x):
    t, dk = x.shape[1], x.shape[-1]
    half = dk // 2
    inv_freq = RET_ROPE_BASE ** (-jnp.arange(half, dtype=jnp.float32) / half)
    ang = jnp.arange(t, dtype=jnp.float32)[:, None] * inv_freq[None, :]
    cos = jnp.cos(ang)[None, :, None, :].astype(x.dtype)
    sin = jnp.sin(ang)[None, :, None, :].astype(x.dtype)
    x1, x2 = x[..., :half], x[..., half:]
    return jnp.concatenate([x1 * cos - x2 * sin, x1 * sin + x2 * cos], axis=-1)


def retention_mixer(q, k, v, g, gn_w):
    bsz, t, _ = q.shape
    q = rotary(q.reshape(bsz, t, RET_HEADS, RET_DK))
    k = rotary(k.reshape(bsz, t, RET_HEADS, RET_DK)) * RET_DK ** -0.5
    v = v.reshape(bsz, t, RET_HEADS, 1, RET_DV)
    gamma = 1.0 - jnp.exp2(-5.0 - jnp.arange(RET_HEADS, dtype=jnp.float32))
    log_a = jnp.broadcast_to(jnp.log(gamma)[:, None], (bsz, t, RET_HEADS, 1))
    y = chunked_scalar_decay_attention(q, k, v, log_a).reshape(bsz, t, RET_HEADS, RET_DV)
    yf = y.astype(jnp.float32)
    mu = jnp.mean(yf, axis=-1, keepdims=True)
    var = jnp.mean(jnp.square(yf - mu), axis=-1, keepdims=True)
    yn = ((yf - mu) * lax.rsqrt(var + NORM_EPS)).reshape(bsz, t, BRANCH_WIDTH) * gn_w.astype(jnp.float32)
    return jax.nn.silu(g) * yn.astype(g.dtype)


def s5_mixer(u, lam_re, lam_im, b_re, b_im, c_re, c_im, d_skip, log_step, glu_w, glu_b):
    bsz, t, _ = u.shape
    uf = u.astype(jnp.float32).reshape(bsz, t, S5_GROUPS, S5_GROUP)
    step = jnp.exp(log_step.astype(jnp.float32))[:, None]
    lr, li = lam_re.astype(jnp.float32), lam_im.astype(jnp.float32)
    mag = jnp.exp(lr * step)
    ab_re, ab_im = mag * jnp.cos(li * step), mag * jnp.sin(li * step)
    inv = 1.0 / (lr * lr + li * li)
    co_re = ((ab_re - 1.0) * lr + ab_im * li) * inv
    co_im = (ab_im * lr - (ab_re - 1.0) * li) * inv
    br, bi = b_re.astype(jnp.float32), b_im.astype(jnp.float32)
    bb_re = co_re[..., None] * br - co_im[..., None] * bi
    bb_im = co_re[..., None] * bi + co_im[..., None] * br
    bu_re = jnp.einsum('btgj,gpj->btgp', uf, bb_re)
    bu_im = jnp.einsum('btgj,gpj->btgp', uf, bb_im)
    a_re = jnp.broadcast_to(ab_re, bu_re.shape)
    a_im = jnp.broadcast_to(ab_im, bu_im.shape)

    def combine(e1, e2):
        a1r, a1i, b1r, b1i = e1
        a2r, a2i, b2r, b2i = e2
        return (a2r * a1r - a2i * a1i, a2r * a1i + a2i * a1r,
                a2r * b1r - a2i * b1i + b2r, a2r * b1i + a2i * b1r + b2i)

    _, _, xr, xi = lax.associative_scan(combine, (a_re, a_im, bu_re, bu_im), axis=1)
    y = (jnp.einsum('btgp,gjp->btgj', xr, c_re.astype(jnp.float32))
         - jnp.einsum('btgp,gjp->btgj', xi, c_im.astype(jnp.float32)))
    y = y.reshape(bsz, t, BRANCH_WIDTH) + d_skip.astype(jnp.float32) * u.astype(jnp.float32)
    y = jax.nn.gelu(y).astype(u.dtype)
    a, gt = jnp.split(y @ glu_w + glu_b, 2, axis=-1)
    return a * jax.nn.sigmoid(gt)


def ssd_mixer(z, xbc, dt_raw, conv_w, conv_b, dt_bias, a_log, d_skip, norm_w):
    bsz, t, _ = z.shape
    hg = SSD_HEADS // SSD_GROUPS
    xbc = lax.conv_general_dilated(xbc, conv_w[:, None, :], window_strides=(1,), padding=[(SSD_CONV - 1, 0)],
                                   dimension_numbers=('NWC', 'WIO', 'NWC'),
                                   feature_group_count=SSD_CONV_DIM) + conv_b
    xbc = jax.nn.silu(xbc)
    xs, bm, cm = jnp.split(xbc, [BRANCH_WIDTH, BRANCH_WIDTH + SSD_GROUPS * SSD_STATE], axis=-1)
    xs = xs.reshape(bsz, t, SSD_GROUPS, hg, SSD_HEADDIM)
    bm = bm.reshape(bsz, t, SSD_GROUPS, SSD_STATE)
    cm = cm.reshape(bsz, t, SSD_GROUPS, SSD_STATE)
    dt = jax.nn.softplus((dt_raw + dt_bias).astype(jnp.float32)).reshape(bsz, t, SSD_GROUPS, hg)
    a = -jnp.exp(a_log.astype(jnp.float32)).reshape(SSD_GROUPS, hg)
    y = chunked_scalar_decay_attention(cm, bm, xs * dt[..., None].astype(xs.dtype), dt * a)
    y = y + d_skip.reshape(SSD_GROUPS, hg)[..., None] * xs
    y = y.reshape(bsz, t, BRANCH_WIDTH) * jax.nn.silu(z)
    y = rms_norm(y.reshape(bsz, t, SSD_GROUPS, -1), norm_w.reshape(SSD_GROUPS, -1))
    return y.reshape(bsz, t, BRANCH_WIDTH)


def hgrn2_mixer(q, f_raw, i, g, lb, norm_w):
    bsz, t, _ = q.shape
    q = jax.nn.silu(q).reshape(bsz, t, HG_HEADS, HG_DK)
    f = lb + (1.0 - lb) * jax.nn.sigmoid(f_raw.astype(jnp.float32))
    log_f = jnp.log(f).reshape(bsz, t, HG_HEADS, HG_DK)
    k = (1.0 - f).astype(q.dtype).reshape(bsz, t, HG_HEADS, HG_DK)
    v = i.reshape(bsz, t, HG_HEADS, HG_DV)
    o = chunked_vector_decay_attention(q, k, v, log_f)
    o = rms_norm(o, norm_w.reshape(HG_HEADS, HG_DV)).reshape(bsz, t, BRANCH_WIDTH)
    return o * jax.nn.silu(g)


def setup_inputs(seed: int = 0) -> dict:
    key = jax.random.key(seed)
    ks = jax.random.split(key, 32)
    f32 = jnp.float32

    def nrm(k, shape, scale):
        return jax.random.normal(k, shape, f32) * scale

    def gain(k, shape):
        return 1.0 + 0.05 * jax.random.normal(k, shape, f32)

    n = jnp.arange(S5_STATE, dtype=f32)
    dt0 = jnp.exp(jax.random.uniform(ks[24], (DEPTH, SSD_HEADS), f32, np.log(1e-3), np.log(1e-1)))
    return {
        'x': nrm(ks[0], (BATCH, SEQ, D_MODEL), 1.0),
        'meta_tokens': nrm(ks[1], (N_META, D_MODEL), 1.0),
        'w_in': nrm(ks[2], (DEPTH, D_MODEL, IN_DIM), D_MODEL ** -0.5),
        'w_branch': nrm(ks[3], (DEPTH, N_BRANCH, BRANCH_WIDTH, D_MODEL), BRANCH_WIDTH ** -0.5),
        'w_out': nrm(ks[4], (DEPTH, D_MODEL, D_MODEL), D_MODEL ** -0.5),
        'norm_pre_mix': gain(ks[5], (DEPTH, D_MODEL)),
        'norm_post_mix': gain(ks[6], (DEPTH, D_MODEL)),
        'norm_pre_mlp': gain(ks[7], (DEPTH, D_MODEL)),
        'norm_post_mlp': gain(ks[8], (DEPTH, D_MODEL)),
        'w_up': nrm(ks[9], (DEPTH, D_MODEL, D_FF), D_MODEL ** -0.5),
        'w_down': nrm(ks[10], (DEPTH, D_FF, D_MODEL), D_FF ** -0.5),
        'ret_gn_w': gain(ks[11], (DEPTH, BRANCH_WIDTH)),
        's5_lam_re': -0.5 + 0.01 * jax.random.normal(ks[12], (DEPTH, S5_GROUPS, S5_STATE), f32),
        's5_lam_im': jnp.pi * n + 0.01 * jax.random.normal(ks[13], (DEPTH, S5_GROUPS, S5_STATE), f32),
        's5_b_re': nrm(ks[14], (DEPTH, S5_GROUPS, S5_STATE, S5_GROUP), 1.0),
        's5_b_im': nrm(ks[15], (DEPTH, S5_GROUPS, S5_STATE, S5_GROUP), 1.0),
        's5_c_re': nrm(ks[16], (DEPTH, S5_GROUPS, S5_GROUP, S5_STATE), S5_STATE ** -0.5),
        's5_c_im': nrm(ks[17], (DEPTH, S5_GROUPS, S5_GROUP, S5_STATE), S5_STATE ** -0.5),
        's5_d': nrm(ks[18], (DEPTH, BRANCH_WIDTH), 1.0),
        's5_log_step': jax.random.uniform(ks[19], (DEPTH, S5_GROUPS), f32, np.log(1e-3), np.log(1e-1)),
        's5_glu_w': nrm(ks[20], (DEPTH, BRANCH_WIDTH, 2 * BRANCH_WIDTH), BRANCH_WIDTH ** -0.5),
        's5_glu_b': nrm(ks[21], (DEPTH, 2 * BRANCH_WIDTH), 0.02),
        'ssd_conv_w': nrm(ks[22], (DEPTH, SSD_CONV, SSD_CONV_DIM), SSD_CONV ** -0.5),
        'ssd_conv_b': nrm(ks[23], (DEPTH, SSD_CONV_DIM), 0.02),
        'ssd_dt_bias': dt0 + jnp.log(-jnp.expm1(-dt0)),
        'ssd_a_log': jnp.log(jax.random.uniform(ks[25], (DEPTH, SSD_HEADS), f32, 1.0, 16.0)),
        'ssd_d': gain(ks[26], (DEPTH, SSD_HEADS)),
        'ssd_norm_w': gain(ks[27], (DEPTH, BRANCH_WIDTH)),
        'hgrn_lb': nrm(ks[28], (DEPTH, HG_HEADS * HG_DK), 0.1),
        'hgrn_norm_w': gain(ks[29], (DEPTH, BRANCH_WIDTH)),
    }


def reference(x, meta_tokens, w_in, w_branch, w_out, norm_pre_mix, norm_post_mix, norm_pre_mlp, norm_post_mlp,
              w_up, w_down, ret_gn_w, s5_lam_re, s5_lam_im, s5_b_re, s5_b_im, s5_c_re, s5_c_im, s5_d,
              s5_log_step, s5_glu_w, s5_glu_b, ssd_conv_w, ssd_conv_b, ssd_dt_bias, ssd_a_log, ssd_d,
              ssd_norm_w, hgrn_lb, hgrn_norm_w):
    bsz = x.shape[0]
    meta = jnp.broadcast_to(meta_tokens[None].astype(x.dtype), (bsz, N_META, D_MODEL))
    h = jnp.concatenate([meta, x], axis=1)
    t = h.shape[1]
    lb_all = jnp.cumsum(jax.nn.softmax(hgrn_lb.astype(jnp.float32), axis=0), axis=0)
    lb_all = lb_all - lb_all[0]
    split_at = [int(s) for s in np.cumsum(IN_SIZES)[:-1]]
    for l in range(DEPTH):
        u = rms_norm(h, norm_pre_mix[l])
        proj = u @ w_in[l]
        (rq, rk, rv, rg, s5u, sz, sxbc, sdt, hq, hf, hi, hgate, gates) = jnp.split(proj, split_at, axis=-1)
        y_ret = retention_mixer(rq, rk, rv, rg, ret_gn_w[l])
        y_s5 = s5_mixer(s5u, s5_lam_re[l], s5_lam_im[l], s5_b_re[l], s5_b_im[l], s5_c_re[l], s5_c_im[l],
                        s5_d[l], s5_log_step[l], s5_glu_w[l], s5_glu_b[l])
        y_ssd = ssd_mixer(sz, sxbc, sdt, ssd_conv_w[l], ssd_conv_b[l], ssd_dt_bias[l], ssd_a_log[l],
                          ssd_d[l], ssd_norm_w[l])
        y_hg = hgrn2_mixer(hq, hf, hi, hgate, lb_all[l], hgrn_norm_w[l])
        branches = jnp.stack([y_ret, y_s5, y_ssd, y_hg], axis=2)
        proj_b = jnp.einsum('btnc,ncd->btnd', branches, w_branch[l])
        gate = jax.nn.sigmoid(gates.reshape(bsz, t, N_BRANCH, D_MODEL))
        mixed = jnp.sum(gate * proj_b, axis=2) @ w_out[l]
        h = h + rms_norm(mixed, norm_post_mix[l])
        m = rms_norm(h, norm_pre_mlp[l])
        m = jnp.square(jax.nn.relu(m @ w_up[l])) @ w_down[l]
        h = h + rms_norm(m, norm_post_mlp[l])
    return h[:, N_META:]
```

```python
import math
import os
import numpy as np
from contextlib import ExitStack
import concourse.bass as bass
import concourse.mybir as mybir
from concourse.bass_utils import run_bass_kernel_spmd

F32 = mybir.dt.float32
BF16 = mybir.dt.bfloat16
ALU = mybir.AluOpType
AF = mybir.ActivationFunctionType
AX = mybir.AxisListType

DEPTH = 4
D = 1024
T = 2064
NT = 17
TP = NT * 128
EPS = 1e-6
N_IN = 9736
C_RET, C_S5, C_SSD, C_HG, C_GATE = 0, 1536, 2048, 3592, 5640
GAMMA = [1.0 - 2.0 ** (-5.0 - h) for h in range(4)]


class Buf:
    __slots__ = ("name", "w", "r")

    def __init__(self, name=""):
        self.name = name
        self.w = None
        self.r = []


class TL:
    def __init__(self, t, name):
        self.t = t
        self.b = Buf(name)


class FW:
    N_DMA_SEMS = 16

    def __init__(self, nc, es):
        self.nc = nc
        self.es = es
        self.eng = {"pe": nc.tensor, "dve": nc.vector, "act": nc.scalar, "pool": nc.gpsimd, "sp": nc.sync}
        self.sems = {}
        self.cnt = {}
        for e in ("pe", "dve", "act", "pool"):
            self.sems[e] = es.enter_context(nc.semaphore("s_" + e))
            self.cnt[e] = 0
        self.dma_keys = {}
        self.dma_rr = {}
        for q in ("sp", "pool"):
            ks = []
            for i in range(self.N_DMA_SEMS):
                k = "d_%s_%d" % (q, i)
                self.sems[k] = es.enter_context(nc.semaphore(k))
                self.cnt[k] = 0
                ks.append(k)
            self.dma_keys[q] = ks
            self.dma_rr[q] = 0
        self.known = {e: {} for e in self.eng}
        self.n_inst = 0
        self.n_wait = 0
        self.uid = 0

    def tile(self, stack, name, shape, dt):
        self.uid += 1
        nm = "%s_%d" % (name, self.uid)
        return TL(stack.enter_context(self.nc.sbuf_tensor(nm, list(shape), dt)), nm)

    def _wait(self, e, ev):
        if ev is None:
            return
        k, v = ev
        if e == "pe" and k == "pe":
            return
        kn = self.known[e]
        if kn.get(k, 0) >= v:
            return
        self.eng[e].wait_ge(self.sems[k], v)
        kn[k] = v
        self.n_wait += 1

    def _deps(self, e, reads, writes):
        for b in reads:
            self._wait(e, b.w)
        for b in writes:
            self._wait(e, b.w)
            for ev in b.r:
                self._wait(e, ev)

    def _mark(self, ev, reads, writes):
        for b in reads:
            b.r.append(ev)
        for b in writes:
            b.w = ev
            b.r = []

    def op(self, e, fn, reads=(), writes=()):
        self._deps(e, reads, writes)
        ins = fn(self.eng[e])
        self.cnt[e] += 1
        ins.then_inc(self.sems[e], 1)
        ev = (e, self.cnt[e])
        self._mark(ev, reads, writes)
        self.n_inst += 1
        return ev

    def dma(self, q, out, in_, reads=(), writes=(), **kw):
        self._deps(q, reads, writes)
        ks = self.dma_keys[q]
        k = ks[self.dma_rr[q] % len(ks)]
        self.dma_rr[q] += 1
        if self.cnt[k] > 0:
            self._wait(q, (k, self.cnt[k]))
        ins = self.eng[q].dma_start(out=out, in_=in_, **kw)
        self.cnt[k] += 16
        ins.then_inc(self.sems[k], 16)
        ev = (k, self.cnt[k])
        self._mark(ev, reads, writes)
        self.n_inst += 1
        return ev

    def barrier(self, engines=("pe", "dve", "act", "pool", "sp")):
        for e in engines:
            for k, v in self.cnt.items():
                if v > 0:
                    self._wait(e, (k, v))

    def tt(self, e, out, a, b, op, R, W):
        return self.op(e, lambda g: g.tensor_tensor(out, a, b, op), R, W)

    def ts(self, e, out, a, s1, s2, op0, op1, R, W):
        if s2 is None:
            return self.op(e, lambda g: g.tensor_scalar(out, a, s1, None, op0=op0), R, W)
        return self.op(e, lambda g: g.tensor_scalar(out, a, s1, s2, op0=op0, op1=op1), R, W)

    def stt(self, e, out, in0, sc, in1, op0, op1, R, W):
        e = "dve"
        return self.op(e, lambda g: g.scalar_tensor_tensor(out, in0, sc, in1, op0=op0, op1=op1), R, W)

    def cp(self, e, out, in_, R, W):
        if e == "act":
            return self.op(e, lambda g: g.copy(out, in_), R, W)
        return self.op(e, lambda g: g.tensor_copy(out, in_), R, W)

    def act(self, out, in_, func, R, W, bias=None, scale=None, accum=None):
        kw = {}
        if bias is not None:
            kw["bias"] = bias
        if scale is not None:
            kw["scale"] = scale
        if accum is not None:
            kw["accum_out"] = accum
        return self.op("act", lambda g: g.activation(out, in_, func, **kw), R, W)

    def mm(self, out, lhsT, rhs, start, stop, R, W):
        return self.op("pe", lambda g: g.matmul(out, lhsT, rhs, start=start, stop=stop), R, W)

    def tr(self, out, in_, ident, R, W):
        return self.op("pe", lambda g: g.transpose(out, in_, ident), R, W)

    def memset(self, e, ap, val, W):
        return self.op(e, lambda g: g.memset(ap, val), (), W)


CO = {}


def _pack(items):
    off = 0
    cols = []
    for name, arr in items:
        arr = np.asarray(arr, np.float32).reshape(128, -1)
        CO[name] = (off, arr.shape[1])
        off += arr.shape[1]
        cols.append(arr)
    return np.ascontiguousarray(np.concatenate(cols, axis=1))


def host_consts():
    s = np.arange(128)[:, None]
    t = np.arange(128)[None, :]
    ident = (s == t).astype(np.float32)
    triu = (s <= t).astype(np.float32)
    negm = np.where(s <= t, 0.0, -30000.0).astype(np.float32)
    ones = np.ones((128, 128), np.float32)
    retdt = np.zeros((128, 4, 128), np.float64)
    qdec = np.zeros((128, 4, 64), np.float64)
    kdec = np.zeros((128, 4, 64), np.float64)
    for h in range(4):
        g = GAMMA[h]
        retdt[:, h, :] = np.where(s <= t, 0.125 * g ** np.maximum(t - s, 0), 0.0)
        qdec[:, h, :] = (g ** (np.arange(128) + 1.0))[:, None]
        kdec[:, h, :] = (0.125 * g ** (127.0 - np.arange(128)))[:, None]
    half = 32
    inv_freq = (10000.0 ** (-np.arange(half, dtype=np.float32) / half)).astype(np.float32)
    pos = (np.arange(NT)[None, :] * 128 + np.arange(128)[:, None]).astype(np.float32)
    ang = pos[:, :, None] * inv_freq[None, None, :]
    cos = np.cos(ang).astype(np.float32)
    sin = np.sin(ang).astype(np.float32)
    halfpi = np.full((128, 1), math.pi / 2, np.float32)
    mvals = np.array(list(range(17)) + [0.5], np.float32)
    mtab = np.broadcast_to(mvals[None, :, None], (128, 18, 16))
    return _pack([("mtab", mtab), ("ident", ident), ("triu", triu), ("negm", negm), ("ones", ones), ("retdt", retdt),
                  ("qdec", qdec), ("kdec", kdec), ("cos", cos), ("sin", sin), ("halfpi", halfpi)])


def host_params(inp):
    P = {}
    f = lambda a: np.ascontiguousarray(np.asarray(a, np.float32))
    P["lbT"] = f(np.asarray(inp["hgrn_lb"]).reshape(4, 4, 128).transpose(2, 0, 1))
    P["ssd_cw"] = f(np.asarray(inp["ssd_conv_w"]).reshape(4, 4, 8, 128).transpose(0, 3, 2, 1))
    P["ssd_cb"] = f(np.asarray(inp["ssd_conv_b"]).reshape(4, 8, 128).transpose(0, 2, 1))
    P["ssd_dsk"] = f(np.repeat(np.asarray(inp["ssd_d"]), 64, axis=1))
    def pl_small(a):
        a = np.asarray(a).reshape(4, 16, 2, 64)
        return a.transpose(0, 2, 3, 1).reshape(4, 128, 16)
    ls = np.broadcast_to(np.asarray(inp["s5_log_step"])[:, :, None], (4, 32, 64))
    P["s5_small"] = f(np.concatenate([pl_small(inp["s5_lam_re"]), pl_small(inp["s5_lam_im"]), pl_small(ls)], axis=2))
    def pl_b(b):
        b = np.asarray(b)
        out = np.zeros((4, 2, 64, 16, 8, 16), np.float32)
        for g in range(32):
            out[:, g % 2, :, g // 2, g % 8, :] = b[:, g]
        return out.reshape(4, 128, 16, 128)
    def pl_c(c):
        c = np.asarray(c)
        out = np.zeros((4, 2, 64, 16, 8, 16), np.float32)
        for g in range(32):
            out[:, g % 2, :, g // 2, g % 8, :] = c[:, g].transpose(0, 2, 1)
        return out.reshape(4, 128, 16, 128)
    P["s5_bre"] = pl_b(inp["s5_b_re"])
    P["s5_bim"] = pl_b(inp["s5_b_im"])
    P["s5_cre"] = pl_c(inp["s5_c_re"])
    P["s5_cim"] = pl_c(inp["s5_c_im"])
    P["s5_dT"] = f(np.asarray(inp["s5_d"]).reshape(4, 4, 128).transpose(0, 2, 1))
    P["s5_gbT"] = f(np.asarray(inp["s5_glu_b"]).reshape(4, 8, 128).transpose(0, 2, 1))
    return P


class Ctx:
    pass


def cslice(C, name, a=None, b=None):
    off, n = CO[name]
    if a is None:
        return C.consts.t[:, off:off + n]
    return C.consts.t[:, off + a:off + b]


def load_w(fw, src2d, dst, c0, c1, kcs, R=()):
    v = src2d.rearrange("(kc p) n -> p kc n", p=128)
    step = 2048
    for kc in range(kcs):
        for a in range(c0, c1, step):
            b = min(a + step, c1)
            fw.dma("pool", dst.t[:, kc, a - c0:b - c0], v[:, kc, a:b], reads=R, writes=[dst.b])


def rstd_from_ss(fw, C, ss_ap, out_ap, n, R, W):
    fw.ts("dve", out_ap, ss_ap, 1.0 / n, EPS, ALU.mult, ALU.add, R, W)
    fw.act(out_ap, out_ap, AF.Sqrt, W, W)
    fw.op("dve", lambda g: g.reciprocal(out_ap, out_ap), W, W)


def norm_rows_to_T(fw, C, ph, x_ap, xb, nw, dstT, dst_bufs, i, wk):
    junk, ss, ub = wk
    fw.act(junk.t[:, :], x_ap, AF.Square, [xb], [junk.b, ss.b], accum=ss.t[:, 0:1])
    rstd_from_ss(fw, C, ss.t[:, 0:1], ss.t[:, 1:2], 1024.0, [ss.b], [ss.b])
    fw.stt("dve", ub.t[:, :], x_ap, ss.t[:, 1:2], nw.t[:, :], ALU.mult, ALU.mult, [xb, ss.b, nw.b], [ub.b])
    pb = C.pb[i % 2]
    for kc in range(8):
        fw.tr(pb.t[:, kc * 128:(kc + 1) * 128], ub.t[:, kc * 128:(kc + 1) * 128], C.identb.t[:, :], [ub.b, C.identb.b], [pb.b])
    fw.cp("act" if i % 2 else "dve", dstT.t[:, :, i * 128:(i + 1) * 128],
          pb.t[:, :].rearrange("p (k t) -> p k t", k=8), [pb.b], [dst_bufs[i]])


def phase_norm1(fw, C, l):
    with ExitStack() as ph:
        nw = fw.tile(ph, "nw", [128, D], F32)
        fw.dma("sp", nw.t[:, :], C.norm_pre_mix[l].partition_broadcast(128), writes=[nw.b])
        xts = [fw.tile(ph, "xt", [128, D], F32) for _ in range(2)]
        wk = (fw.tile(ph, "junk", [128, D], BF16), fw.tile(ph, "ss", [128, 2], F32), fw.tile(ph, "ub", [128, D], BF16))
        for i in range(NT):
            xt = xts[i % 2]
            src = C.h0 if l == 0 else C.hbuf
            fw.dma("sp", xt.t[:, :], src[i * 128:(i + 1) * 128, :], reads=([] if l == 0 else [C.hb[i]]), writes=[xt.b])
            norm_rows_to_T(fw, C, ph, xt.t[:, :], xt.b, nw, C.uT, C.uTb, i, wk)
        fw.barrier()


def emit_yT(fw, C, yo, yT, mix, i):
    pb = C.pb[1]
    for cb in range(4):
        fw.tr(pb.t[:, cb * 128:(cb + 1) * 128], yo.t[:, cb * 128:(cb + 1) * 128], C.identb.t[:, :], [yo.b, C.identb.b], [pb.b])
    fw.cp("act", yT.t[:, :, :], pb.t[:, 0:512].rearrange("p (c t) -> p c t", c=4), [pb.b], [yT.b])
    fw.dma("sp", C.YT[mix * 4:(mix + 1) * 4, :, i * 128:(i + 1) * 128].rearrange("c p t -> p c t"), yT.t[:, :, :],
           reads=[yT.b], writes=[C.YTb[mix][i]])


def tok_mm(fw, C, ps, Wt, c0, n, i, ncols=None):
    for kc in range(8):
        fw.mm(ps.t[:, 0:n], C.uT.t[:, kc, i * 128:(i + 1) * 128], Wt.t[:, kc, c0:c0 + n], kc == 0, kc == 7,
              [C.uTb[i], Wt.b], [ps.b])


def phase_ret(fw, C, l):
    with ExitStack() as ph:
        WR = fw.tile(ph, "WR", [128, 8, 1536], BF16)
        load_w(fw, C.w_in[l], WR, C_RET, C_RET + 1536, 8)
        gnw = fw.tile(ph, "gnw", [128, 512], F32)
        fw.dma("sp", gnw.t[:, :], C.ret_gn_w[l].partition_broadcast(128), writes=[gnw.b])
        S = fw.tile(ph, "rS", [64, 4, 128], F32)
        Sb = fw.tile(ph, "rSb", [64, 4, 128], BF16)
        fw.memset("dve", S.t[:, :, :], 0.0, [S.b])
        fw.memset("dve", Sb.t[:, :, :], 0.0, [Sb.b])
        qkr = fw.tile(ph, "qkr", [128, 8, 64], F32)
        t1 = fw.tile(ph, "t1", [128, 8, 32], F32)
        t2 = fw.tile(ph, "t2", [128, 8, 32], F32)
        qb = fw.tile(ph, "qb", [128, 256], BF16)
        qdb = fw.tile(ph, "qdb", [128, 256], BF16)
        kb = fw.tile(ph, "kb", [128, 256], BF16)
        khb = fw.tile(ph, "khb", [128, 256], BF16)
        qkT = fw.tile(ph, "qkT", [64, 12, 128], BF16)
        vb = fw.tile(ph, "vb", [128, 512], BF16)
        sg = fw.tile(ph, "sg", [128, 512], F32)
        PT = fw.tile(ph, "PT", [128, 512], BF16)
        ysq = fw.tile(ph, "ysq", [128, 512], F32)
        st = fw.tile(ph, "st", [128, 16], F32)
        yn = fw.tile(ph, "yn", [128, 512], F32)
        yo = fw.tile(ph, "yo", [128, 512], BF16)
        yT = fw.tile(ph, "yT", [128, 4, 128], BF16)
        ps_qk, ps_v, ps_g, ps_s, ps_y, ps_st = C.ps[0:6]
        cb_ = C.consts.b
        for i in range(NT):
            tok_mm(fw, C, ps_qk, WR, 0, 512, i)
            tok_mm(fw, C, ps_v, WR, 512, 512, i)
            tok_mm(fw, C, ps_g, WR, 1024, 512, i)
            qv = ps_qk.t[:, :].rearrange("p (h d) -> p h d", d=64)
            x1, x2 = qv[:, :, 0:32], qv[:, :, 32:64]
            co, cn = CO["cos"][0], CO["sin"][0]
            cosb = C.consts.t[:, co + i * 32:co + (i + 1) * 32].unsqueeze(1).to_broadcast([128, 8, 32])
            sinb = C.consts.t[:, cn + i * 32:cn + (i + 1) * 32].unsqueeze(1).to_broadcast([128, 8, 32])
            fw.tt("dve", t1.t[:, :, :], x1, cosb, ALU.mult, [ps_qk.b, cb_], [t1.b])
            fw.tt("dve", t2.t[:, :, :], x2, sinb, ALU.mult, [ps_qk.b, cb_], [t2.b])
            fw.tt("pool", qkr.t[:, :, 0:32], t1.t[:, :, :], t2.t[:, :, :], ALU.subtract, [t1.b, t2.b], [qkr.b])
            fw.tt("dve", t1.t[:, :, :], x1, sinb, ALU.mult, [ps_qk.b, cb_], [t1.b])
            fw.tt("dve", t2.t[:, :, :], x2, cosb, ALU.mult, [ps_qk.b, cb_], [t2.b])
            fw.tt("pool", qkr.t[:, :, 32:64], t1.t[:, :, :], t2.t[:, :, :], ALU.add, [t1.b, t2.b], [qkr.b])
            qf = qkr.t[:, 0:4, :].rearrange("p h d -> p (h d)")
            kf = qkr.t[:, 4:8, :].rearrange("p h d -> p (h d)")
            fw.cp("act", qb.t[:, :], qf, [qkr.b], [qb.b])
            fw.tt("pool", qdb.t[:, :], qf, cslice(C, "qdec"), ALU.mult, [qkr.b, cb_], [qdb.b])
            fw.cp("act", kb.t[:, :], kf, [qkr.b], [kb.b])
            fw.tt("pool", khb.t[:, :], kf, cslice(C, "kdec"), ALU.mult, [qkr.b, cb_], [khb.b])
            pb0, pb1 = C.pb
            for h in range(4):
                fw.tr(pb0.t[0:64, h * 128:(h + 1) * 128], qb.t[:, h * 64:(h + 1) * 64], C.identb.t[:, :], [qb.b, C.identb.b], [pb0.b])
                fw.tr(pb0.t[0:64, (4 + h) * 128:(5 + h) * 128], qdb.t[:, h * 64:(h + 1) * 64], C.identb.t[:, :], [qdb.b, C.identb.b], [pb0.b])
                fw.tr(pb1.t[0:64, h * 128:(h + 1) * 128], kb.t[:, h * 64:(h + 1) * 64], C.identb.t[:, :], [kb.b, C.identb.b], [pb1.b])
            fw.cp("dve", qkT.t[:, 0:8, :], pb0.t[0:64, :].rearrange("p (j t) -> p j t", j=8), [pb0.b], [qkT.b])
            fw.cp("act", qkT.t[:, 8:12, :], pb1.t[0:64, 0:512].rearrange("p (j t) -> p j t", j=4), [pb1.b], [qkT.b])
            fw.cp("act", vb.t[:, :], ps_v.t[:, :], [ps_v.b], [vb.b])
            fw.act(sg.t[:, :], ps_g.t[:, :], AF.Silu, [ps_g.b], [sg.b])
            for h in range(4):
                fw.mm(ps_s.t[:, h * 128:(h + 1) * 128], qkT.t[:, 8 + h, :], qkT.t[:, h, :], True, True, [qkT.b], [ps_s.b])
            fw.tt("dve", PT.t[:, :], ps_s.t[:, :], cslice(C, "retdt"), ALU.mult, [ps_s.b, cb_], [PT.b])
            for h in range(4):
                hs = slice(h * 128, (h + 1) * 128)
                fw.mm(ps_y.t[:, hs], PT.t[:, hs], vb.t[:, hs], True, False, [PT.b, vb.b], [ps_y.b])
                fw.mm(ps_y.t[:, hs], qkT.t[:, 4 + h, :], Sb.t[:, h, :], False, True, [qkT.b, Sb.b], [ps_y.b])
            for h in range(4):
                hs = slice(h * 128, (h + 1) * 128)
                fw.mm(ps_st.t[0:64, hs], khb.t[:, h * 64:(h + 1) * 64], vb.t[:, hs], True, True, [khb.b, vb.b], [ps_st.b])
            for h in range(4):
                hs = slice(h * 128, (h + 1) * 128)
                fw.stt("dve", S.t[:, h, :], S.t[:, h, :], float(GAMMA[h] ** 128), ps_st.t[0:64, hs], ALU.mult, ALU.add,
                       [S.b, ps_st.b], [S.b])
            fw.cp("pool", Sb.t[:, :, :], S.t[:, :, :], [S.b], [Sb.b])
            yv = ps_y.t[:, :].rearrange("p (h e) -> p h e", h=4)
            fw.op("dve", lambda g: g.reduce_sum(st.t[:, 0:4], yv, axis=AX.X), [ps_y.b], [st.b])
            fw.act(ysq.t[:, :], ps_y.t[:, :], AF.Square, [ps_y.b], [ysq.b])
            fw.op("dve", lambda g: g.reduce_sum(st.t[:, 4:8], ysq.t[:, :].rearrange("p (h e) -> p h e", h=4), axis=AX.X), [ysq.b], [st.b])
            fw.ts("dve", st.t[:, 8:12], st.t[:, 0:4], 1.0 / 128, None, ALU.mult, None, [st.b], [st.b])
            fw.tt("dve", st.t[:, 0:4], st.t[:, 8:12], st.t[:, 8:12], ALU.mult, [st.b], [st.b])
            fw.stt("dve", st.t[:, 12:16], st.t[:, 4:8], 1.0 / 128, st.t[:, 0:4], ALU.mult, ALU.subtract, [st.b], [st.b])
            fw.ts("dve", st.t[:, 12:16], st.t[:, 12:16], EPS, None, ALU.add, None, [st.b], [st.b])
            fw.act(st.t[:, 12:16], st.t[:, 12:16], AF.Sqrt, [st.b], [st.b])
            fw.op("dve", lambda g: g.reciprocal(st.t[:, 12:16], st.t[:, 12:16]), [st.b], [st.b])
            for h in range(4):
                hs = slice(h * 128, (h + 1) * 128)
                fw.ts("dve", yn.t[:, hs], ps_y.t[:, hs], st.t[:, 8 + h:9 + h], st.t[:, 12 + h:13 + h], ALU.subtract, ALU.mult,
                      [ps_y.b, st.b], [yn.b])
            fw.tt("pool", yn.t[:, :], yn.t[:, :], gnw.t[:, :], ALU.mult, [yn.b, gnw.b], [yn.b])
            fw.tt("pool", yo.t[:, :], yn.t[:, :], sg.t[:, :], ALU.mult, [yn.b, sg.b], [yo.b])
            emit_yT(fw, C, yo, yT, 0, i)
        fw.barrier()


def phase_ssd(fw, C, l):
    with ExitStack() as ph:
        WS = fw.tile(ph, "WS", [128, 8, 1544], BF16)
        load_w(fw, C.w_in[l], WS, C_SSD, C_SSD + 1544, 8)
        cw = fw.tile(ph, "cw", [128, 8, 4], F32)
        cbi = fw.tile(ph, "cbi", [128, 8], F32)
        dtb = fw.tile(ph, "dtb", [128, 8], F32)
        arow = fw.tile(ph, "arow", [128, 8], F32)
        dsk = fw.tile(ph, "dsk", [128, 512], F32)
        nw = fw.tile(ph, "snw", [128, 512], F32)
        fw.dma("sp", cw.t[:, :, :], C.ssd_cw[l], writes=[cw.b])
        fw.dma("sp", cbi.t[:, :], C.ssd_cb[l], writes=[cbi.b])
        fw.dma("sp", dtb.t[:, :], C.ssd_dt_bias[l].partition_broadcast(128), writes=[dtb.b])
        fw.dma("sp", arow.t[:, :], C.ssd_a_log[l].partition_broadcast(128), writes=[arow.b])
        fw.dma("sp", dsk.t[:, :], C.ssd_dsk[l].partition_broadcast(128), writes=[dsk.b])
        fw.dma("sp", nw.t[:, :], C.ssd_norm_w[l].partition_broadcast(128), writes=[nw.b])
        fw.act(arow.t[:, :], arow.t[:, :], AF.Exp, [arow.b], [arow.b])
        fw.ts("dve", arow.t[:, :], arow.t[:, :], -1.0, None, ALU.mult, None, [arow.b], [arow.b])
        S = fw.tile(ph, "sS", [128, 512], F32)
        Sb = fw.tile(ph, "sSb", [128, 512], BF16)
        fw.memset("dve", S.t[:, :], 0.0, [S.b])
        fw.memset("dve", Sb.t[:, :], 0.0, [Sb.b])
        xraw = fw.tile(ph, "xraw", [128, 8, 131], F32)
        fw.memset("pool", xraw.t[:, :, :], 0.0, [xraw.b])
        xc = fw.tile(ph, "xc", [128, 8, 128], F32)
        xcb = fw.tile(ph, "xcb", [128, 8, 128], BF16)
        xs = fw.tile(ph, "xs", [128, 512], F32)
        Btok = fw.tile(ph, "Btok", [128, 2, 128], BF16)
        sz = fw.tile(ph, "sz", [128, 512], F32)
        sm = fw.tile(ph, "sm", [128, 104], F32)
        laB = [fw.tile(ph, "laB", [128, 128], F32) for _ in range(2)]
        decT = fw.tile(ph, "decT", [128, 8, 128], F32)
        PT = fw.tile(ph, "sPT", [128, 8, 128], BF16)
        xdt = fw.tile(ph, "xdt", [128, 512], BF16)
        xdtw = fw.tile(ph, "xdtw", [128, 512], BF16)
        ya = fw.tile(ph, "ya", [128, 512], F32)
        tmp = fw.tile(ph, "stmp", [128, 512], F32)
        yo = fw.tile(ph, "syo", [128, 512], BF16)
        yT = fw.tile(ph, "syT", [128, 4, 128], BF16)
        ps = C.ps
        pb0, pb1 = C.pb
        cb_ = C.consts.b
        XD, AXc, EX, LN, DT, LA, CUM, NCUM, ECUM, WW, EDEC, DW = [slice(8 * j, 8 * j + 8) for j in range(12)]
        triu = cslice(C, "triu")
        ones = cslice(C, "ones")
        for i in range(NT):
            tok = slice(i * 128, (i + 1) * 128)
            for cb in range(8):
                pst = ps[0] if cb < 4 else ps[1]
                for kc in range(8):
                    fw.mm(pst.t[:, (cb % 4) * 128:(cb % 4 + 1) * 128], WS.t[:, kc, 512 + cb * 128:512 + (cb + 1) * 128],
                          C.uT.t[:, kc, tok], kc == 0, kc == 7, [WS.b, C.uTb[i]], [pst.b])
            if i > 0:
                fw.cp("pool", xraw.t[:, :, 0:3], xraw.t[:, :, 128:131], [xraw.b], [xraw.b])
            fw.cp("act", xraw.t[:, 0:4, 3:131], ps[0].t[:, :].rearrange("p (c t) -> p c t", c=4), [ps[0].b], [xraw.b])
            fw.cp("dve", xraw.t[:, 4:8, 3:131], ps[1].t[:, :].rearrange("p (c t) -> p c t", c=4), [ps[1].b], [xraw.b])
            for cb in range(8):
                e = "dve" if cb % 2 == 0 else "pool"
                fw.ts(e, xc.t[:, cb, :], xraw.t[:, cb, 3:131], cw.t[:, cb, 3:4], cbi.t[:, cb:cb + 1], ALU.mult, ALU.add,
                      [xraw.b, cw.b, cbi.b], [xc.b])
                for j in (2, 1, 0):
                    fw.stt(e, xc.t[:, cb, :], xraw.t[:, cb, j:j + 128], cw.t[:, cb, j:j + 1], xc.t[:, cb, :], ALU.mult, ALU.add,
                           [xraw.b, cw.b, xc.b], [xc.b])
            fw.act(xcb.t[:, :, :], xc.t[:, :, :], AF.Silu, [xc.b], [xcb.b])
            for cb in range(4):
                fw.tr(pb0.t[:, cb * 128:(cb + 1) * 128], xcb.t[:, cb, :], C.identb.t[:, :], [xcb.b, C.identb.b], [pb0.b])
            fw.cp("dve", xs.t[:, :], pb0.t[:, 0:512], [pb0.b], [xs.b])
            for g in range(2):
                fw.tr(pb1.t[:, g * 128:(g + 1) * 128], xcb.t[:, 4 + g, :], C.identb.t[:, :], [xcb.b, C.identb.b], [pb1.b])
            fw.cp("act", Btok.t[:, :, :], pb1.t[:, 0:256].rearrange("p (g n) -> p g n", g=2), [pb1.b], [Btok.b])
            tok_mm(fw, C, ps[2], WS, 0, 512, i)
            fw.act(sz.t[:, :], ps[2].t[:, :], AF.Silu, [ps[2].b], [sz.b])
            for kc in range(8):
                fw.mm(ps[3].t[:, 0:8], C.uT.t[:, kc, tok], WS.t[:, kc, 1536:1544], kc == 0, kc == 7, [C.uTb[i], WS.b], [ps[3].b])
            smb = [sm.b]
            fw.tt("dve", sm.t[:, XD], ps[3].t[:, 0:8], dtb.t[:, :], ALU.add, [ps[3].b, dtb.b], smb)
            fw.ts("dve", sm.t[:, AXc], sm.t[:, XD], -1.0, None, ALU.mult, None, smb, smb)
            fw.tt("dve", sm.t[:, AXc], sm.t[:, AXc], sm.t[:, XD], ALU.max, smb, smb)
            fw.act(sm.t[:, EX], sm.t[:, AXc], AF.Exp, smb, smb, scale=-1.0)
            fw.ts("dve", sm.t[:, EX], sm.t[:, EX], 1.0, None, ALU.add, None, smb, smb)
            fw.act(sm.t[:, LN], sm.t[:, EX], AF.Ln, smb, smb)
            fw.stt("dve", sm.t[:, DT], sm.t[:, XD], 0.0, sm.t[:, LN], ALU.max, ALU.add, smb, smb)
            fw.tt("dve", sm.t[:, LA], sm.t[:, DT], arow.t[:, :], ALU.mult, smb + [arow.b], smb)
            fw.mm(ps[3].t[:, 16:24], triu, sm.t[:, LA], True, True, [cb_, sm.b], [ps[3].b])
            fw.mm(ps[3].t[:, 32:40], ones, sm.t[:, LA], True, True, [cb_, sm.b], [ps[3].b])
            fw.cp("dve", sm.t[:, CUM], ps[3].t[:, 16:24], [ps[3].b], smb)
            fw.ts("dve", sm.t[:, NCUM], sm.t[:, CUM], -1.0, None, ALU.mult, None, smb, smb)
            fw.act(sm.t[:, ECUM], sm.t[:, CUM], AF.Exp, smb, smb)
            fw.tt("dve", sm.t[:, WW], ps[3].t[:, 32:40], sm.t[:, CUM], ALU.subtract, [ps[3].b] + smb, smb)
            fw.act(sm.t[:, WW], sm.t[:, WW], AF.Exp, smb, smb)
            fw.act(sm.t[:, EDEC], ps[3].t[:, 32:40], AF.Exp, [ps[3].b], smb)
            fw.tt("dve", sm.t[:, DW], sm.t[:, DT], sm.t[:, WW], ALU.mult, smb, smb)
            for h in range(8):
                lb_ = laB[h % 2]
                pst = ps[4] if h < 4 else ps[5]
                fw.ts("dve" if h % 2 == 0 else "pool", lb_.t[:, :], ones, sm.t[:, 40 + h:41 + h], None, ALU.mult, None, [cb_, sm.b], [lb_.b])
                hs = slice((h % 4) * 128, (h % 4 + 1) * 128)
                fw.mm(pst.t[:, hs], lb_.t[:, :], triu, True, False, [lb_.b, cb_], [pst.b])
                fw.mm(pst.t[:, hs], C.identf.t[:, :], cslice(C, "negm"), False, True, [C.identf.b, cb_], [pst.b])
                fw.act(decT.t[:, h, :], pst.t[:, hs], AF.Exp, [pst.b, sm.b], [decT.b], bias=sm.t[:, 56 + h:57 + h])
            for g in range(2):
                fw.mm(ps[0].t[:, g * 128:(g + 1) * 128], xcb.t[:, 4 + g, :], xcb.t[:, 6 + g, :], True, True, [xcb.b], [ps[0].b])
            for g in range(2):
                fw.tt("dve", PT.t[:, 4 * g:4 * g + 4, :], decT.t[:, 4 * g:4 * g + 4, :],
                      ps[0].t[:, g * 128:(g + 1) * 128].unsqueeze(1).to_broadcast([128, 4, 128]), ALU.mult, [decT.b, ps[0].b], [PT.b])
            xsv = xs.t[:, :].rearrange("p (h e) -> p h e", h=8)
            fw.tt("pool", xdt.t[:, :].rearrange("p (h e) -> p h e", h=8), xsv, sm.t[:, DT].unsqueeze(2).to_broadcast([128, 8, 64]),
                  ALU.mult, [xs.b, sm.b], [xdt.b])
            fw.tt("pool", xdtw.t[:, :].rearrange("p (h e) -> p h e", h=8), xsv, sm.t[:, DW].unsqueeze(2).to_broadcast([128, 8, 64]),
                  ALU.mult, [xs.b, sm.b], [xdtw.b])
            for h in range(8):
                fw.mm(ps[1].t[:, h * 64:(h + 1) * 64], PT.t[:, h, :], xdt.t[:, h * 64:(h + 1) * 64], True, True, [PT.b, xdt.b], [ps[1].b])
            for g in range(2):
                fw.mm(ps[2].t[:, g * 256:(g + 1) * 256], xcb.t[:, 6 + g, :], Sb.t[:, g * 256:(g + 1) * 256], True, True, [xcb.b, Sb.b], [ps[2].b])
            fw.tt("dve", ya.t[:, :].rearrange("p (h e) -> p h e", h=8), ps[2].t[:, :].rearrange("p (h e) -> p h e", h=8),
                  sm.t[:, ECUM].unsqueeze(2).to_broadcast([128, 8, 64]), ALU.mult, [ps[2].b, sm.b], [ya.b])
            fw.tt("dve", ya.t[:, :], ya.t[:, :], ps[1].t[:, :], ALU.add, [ya.b, ps[1].b], [ya.b])
            fw.tt("pool", tmp.t[:, :], xs.t[:, :], dsk.t[:, :], ALU.mult, [xs.b, dsk.b], [tmp.b])
            fw.tt("pool", ya.t[:, :], ya.t[:, :], tmp.t[:, :], ALU.add, [ya.b, tmp.b], [ya.b])
            fw.tt("pool", ya.t[:, :], ya.t[:, :], sz.t[:, :], ALU.mult, [ya.b, sz.b], [ya.b])
            for g in range(2):
                fw.mm(ps[4].t[:, g * 256:(g + 1) * 256], Btok.t[:, g, :], xdtw.t[:, g * 256:(g + 1) * 256], True, True,
                      [Btok.b, xdtw.b], [ps[4].b])
            for h in range(8):
                hs = slice(h * 64, (h + 1) * 64)
                fw.stt("dve", S.t[:, hs], S.t[:, hs], sm.t[:, 80 + h:81 + h], ps[4].t[:, hs], ALU.mult, ALU.add, [S.b, sm.b, ps[4].b], [S.b])
            fw.cp("pool", Sb.t[:, :], S.t[:, :], [S.b], [Sb.b])
            fw.act(tmp.t[:, :], ya.t[:, :], AF.Square, [ya.b], [tmp.b])
            fw.op("dve", lambda g_: g_.reduce_sum(sm.t[:, 96:98], tmp.t[:, :].rearrange("p (g e) -> p g e", g=2), axis=AX.X), [tmp.b], smb)
            rstd_from_ss(fw, C, sm.t[:, 96:98], sm.t[:, 98:100], 256.0, smb, smb)
            for g in range(2):
                gs = slice(g * 256, (g + 1) * 256)
                fw.ts("dve", tmp.t[:, gs], ya.t[:, gs], sm.t[:, 98 + g:99 + g], None, ALU.mult, None, [ya.b, sm.b], [tmp.b])
            fw.tt("pool", yo.t[:, :], tmp.t[:, :], nw.t[:, :], ALU.mult, [tmp.b, nw.b], [yo.b])
            emit_yT(fw, C, yo, yT, 2, i)
        fw.barrier()


def phase_hg(fw, C, l):
    with ExitStack() as ph:
        WH = fw.tile(ph, "WH", [128, 8, 2048], BF16)
        load_w(fw, C.w_in[l], WH, C_HG, C_HG + 2048, 8)
        nw = fw.tile(ph, "hnw", [128, 512], F32)
        fw.dma("sp", nw.t[:, :], C.hgrn_norm_w[l].partition_broadcast(128), writes=[nw.b])
        S = fw.tile(ph, "hS", [128, 4, 128], F32)
        Sb = fw.tile(ph, "hSb", [128, 4, 128], BF16)
        PT = fw.tile(ph, "hPT", [128, 4, 128], BF16)
        fw.memset("dve", S.t[:, :, :], 0.0, [S.b])
        fw.memset("dve", Sb.t[:, :, :], 0.0, [Sb.b])
        fw.memset("pool", PT.t[:, :, :], 0.0, [PT.b])
        qT = fw.tile(ph, "hq", [128, 4, 128], F32)
        fT = fw.tile(ph, "hf", [128, 4, 128], F32)
        la = fw.tile(ph, "hla", [128, 4, 128], F32)
        kT = fw.tile(ph, "hk", [128, 4, 128], F32)
        cum = fw.tile(ph, "hcum", [128, 4, 128], F32)
        ncb = fw.tile(ph, "hncb", [128, 4, 4], F32)
        eq = fw.tile(ph, "heq", [128, 4, 128], F32)
        qd = fw.tile(ph, "hqd", [128, 4, 128], BF16)
        qst = fw.tile(ph, "hqst", [128, 4, 128], BF16)
        ek = fw.tile(ph, "hek", [128, 4, 128], F32)
        Kt = fw.tile(ph, "hKt", [128, 4, 4, 128], BF16)
        khT = fw.tile(ph, "hkhT", [128, 4, 128], BF16)
        khat = fw.tile(ph, "hkhat", [128, 4, 128], BF16)
        dec = fw.tile(ph, "hdec", [128, 4], F32)
        vb = fw.tile(ph, "hvb", [128, 512], BF16)
        sgate = fw.tile(ph, "hsg", [128, 512], F32)
        ysq = fw.tile(ph, "hysq", [128, 512], F32)
        st = fw.tile(ph, "hst", [128, 8], F32)
        yn = fw.tile(ph, "hyn", [128, 512], F32)
        yo = fw.tile(ph, "hyo", [128, 512], BF16)
        yT = fw.tile(ph, "hyT", [128, 4, 128], BF16)
        ps = C.ps
        pb0, pb1 = C.pb
        cb_ = C.consts.b
        lbc = C.lb_all.t[:, l, :]
        omc = C.oml_all.t[:, l, :]
        ones = cslice(C, "ones")
        triu = cslice(C, "triu")
        for i in range(NT):
            tok = slice(i * 128, (i + 1) * 128)
            for h in range(4):
                hs = slice(h * 128, (h + 1) * 128)
                for kc in range(8):
                    fw.mm(ps[0].t[:, hs], WH.t[:, kc, h * 128:(h + 1) * 128], C.uT.t[:, kc, tok], kc == 0, kc == 7, [WH.b, C.uTb[i]], [ps[0].b])
                for kc in range(8):
                    fw.mm(ps[1].t[:, hs], WH.t[:, kc, 512 + h * 128:512 + (h + 1) * 128], C.uT.t[:, kc, tok], kc == 0, kc == 7,
                          [WH.b, C.uTb[i]], [ps[1].b])
            tok_mm(fw, C, ps[2], WH, 1024, 512, i)
            tok_mm(fw, C, ps[3], WH, 1536, 512, i)
            fw.act(qT.t[:, :, :], ps[0].t[:, :].rearrange("p (h t) -> p h t", h=4), AF.Silu, [ps[0].b], [qT.b])
            fw.act(fT.t[:, :, :], ps[1].t[:, :].rearrange("p (h t) -> p h t", h=4), AF.Sigmoid, [ps[1].b], [fT.b])
            for h in range(4):
                fw.ts("dve", fT.t[:, h, :], fT.t[:, h, :], omc[:, h:h + 1], lbc[:, h:h + 1], ALU.mult, ALU.add,
                      [fT.b, C.lb_all.b, C.oml_all.b], [fT.b])
            fw.act(la.t[:, :, :], fT.t[:, :, :], AF.Ln, [fT.b], [la.b])
            fw.ts("pool", kT.t[:, :, :], fT.t[:, :, :], -1.0, 1.0, ALU.mult, ALU.add, [fT.b], [kT.b])
            for h in range(4):
                fw.op("dve", lambda g: g.tensor_tensor_scan(cum.t[:, h, :], ones, la.t[:, h, :], 0.0, ALU.mult, ALU.add),
                      [la.b, cb_], [cum.b])
            cv = cum.t[:, :, :].rearrange("p h (b c) -> p h b c", c=32)
            fw.ts("dve", ncb.t[:, :, :], cv[:, :, :, 31], -1.0, None, ALU.mult, None, [cum.b], [ncb.b])
            for h in range(4):
                for b in range(4):
                    bs = slice(32 * b, 32 * b + 32)
                    if b == 0:
                        fw.act(eq.t[:, h, bs], cum.t[:, h, bs], AF.Exp, [cum.b], [eq.b])
                    else:
                        fw.act(eq.t[:, h, bs], cum.t[:, h, bs], AF.Exp, [cum.b, ncb.b], [eq.b], bias=ncb.t[:, h, b - 1:b])
            fw.tt("pool", qd.t[:, :, :], qT.t[:, :, :], eq.t[:, :, :], ALU.mult, [qT.b, eq.b], [qd.b])
            fw.act(eq.t[:, :, :], cum.t[:, :, :], AF.Exp, [cum.b], [eq.b])
            fw.tt("pool", qst.t[:, :, :], qT.t[:, :, :], eq.t[:, :, :], ALU.mult, [qT.b, eq.b], [qst.b])
            for h in range(4):
                for b in range(4):
                    W_ = 32 * (b + 1)
                    if b == 0:
                        fw.act(ek.t[:, b, 0:W_], cum.t[:, h, 0:W_], AF.Exp, [cum.b], [ek.b], scale=-1.0)
                    else:
                        fw.act(ek.t[:, b, 0:W_], cum.t[:, h, 0:W_], AF.Exp, [cum.b], [ek.b], scale=-1.0,
                               bias=cum.t[:, h, 32 * b - 1:32 * b])
                for b in range(4):
                    W_ = 32 * (b + 1)
                    fw.tt("dve" if b % 2 else "pool", Kt.t[:, h, b, 0:W_], ek.t[:, b, 0:W_], kT.t[:, h, 0:W_], ALU.mult,
                          [ek.b, kT.b], [Kt.b])
            for h in range(4):
                fw.act(ek.t[:, h, :], cum.t[:, h, :], AF.Exp, [cum.b], [ek.b], scale=-1.0, bias=cum.t[:, h, 127:128])
            fw.tt("pool", khT.t[:, :, :], ek.t[:, :, :], kT.t[:, :, :], ALU.mult, [ek.b, kT.b], [khT.b])
            for h in range(4):
                fw.tr(pb0.t[:, h * 128:(h + 1) * 128], khT.t[:, h, :], C.identb.t[:, :], [khT.b, C.identb.b], [pb0.b])
            fw.cp("act", khat.t[:, :, :], pb0.t[:, 0:512].rearrange("p (h d) -> p h d", h=4), [pb0.b], [khat.b])
            fw.act(dec.t[:, :], cum.t[:, :, 127], AF.Exp, [cum.b], [dec.b])
            for h in range(4):
                for b in range(4):
                    W_ = 32 * (b + 1)
                    fw.mm(ps[4].t[0:W_, h * 128 + 32 * b:h * 128 + 32 * b + 32], Kt.t[:, h, b, 0:W_], qd.t[:, h, 32 * b:32 * b + 32],
                          True, True, [Kt.b, qd.b], [ps[4].b])
            psv = ps[4].t[:, :].rearrange("p (h t) -> p h t", h=4)
            for b in range(4):
                W_ = 32 * (b + 1)
                bs = slice(32 * b, 32 * b + 32)
                to, tn = CO["triu"]
                mk = C.consts.t[0:W_, to + 32 * b:to + 32 * b + 32].unsqueeze(1).to_broadcast([W_, 4, 32])
                fw.tt("dve", PT.t[0:W_, :, bs], psv[0:W_, :, bs], mk, ALU.mult, [ps[4].b, cb_], [PT.b])
            fw.cp("act", vb.t[:, :], ps[2].t[:, :], [ps[2].b], [vb.b])
            fw.act(sgate.t[:, :], ps[3].t[:, :], AF.Silu, [ps[3].b], [sgate.b])
            for h in range(4):
                hs = slice(h * 128, (h + 1) * 128)
                fw.mm(ps[5].t[:, hs], PT.t[:, h, :], vb.t[:, hs], True, False, [PT.b, vb.b], [ps[5].b])
                fw.mm(ps[5].t[:, hs], qst.t[:, h, :], Sb.t[:, h, :], False, True, [qst.b, Sb.b], [ps[5].b])
            for h in range(4):
                hs = slice(h * 128, (h + 1) * 128)
                fw.mm(ps[0].t[:, hs], khat.t[:, h, :], vb.t[:, hs], True, True, [khat.b, vb.b], [ps[0].b])
            for h in range(4):
                hs = slice(h * 128, (h + 1) * 128)
                fw.stt("dve", S.t[:, h, :], S.t[:, h, :], dec.t[:, h:h + 1], ps[0].t[:, hs], ALU.mult, ALU.add, [S.b, dec.b, ps[0].b], [S.b])
            fw.cp("pool", Sb.t[:, :, :], S.t[:, :, :], [S.b], [Sb.b])
            fw.act(ysq.t[:, :], ps[5].t[:, :], AF.Square, [ps[5].b], [ysq.b])
            fw.op("dve", lambda g: g.reduce_sum(st.t[:, 0:4], ysq.t[:, :].rearrange("p (h e) -> p h e", h=4), axis=AX.X), [ysq.b], [st.b])
            rstd_from_ss(fw, C, st.t[:, 0:4], st.t[:, 4:8], 128.0, [st.b], [st.b])
            for h in range(4):
                hs = slice(h * 128, (h + 1) * 128)
                fw.ts("dve", yn.t[:, hs], ps[5].t[:, hs], st.t[:, 4 + h:5 + h], None, ALU.mult, None, [ps[5].b, st.b], [yn.b])
            fw.tt("pool", yn.t[:, :], yn.t[:, :], nw.t[:, :], ALU.mult, [yn.b, nw.b], [yn.b])
            fw.tt("pool", yo.t[:, :], yn.t[:, :], sgate.t[:, :], ALU.mult, [yn.b, sgate.b], [yo.b])
            emit_yT(fw, C, yo, yT, 3, i)
        fw.barrier()


TGS = [(0, 512), (512, 512), (1024, 512), (1536, 512), (2048, 128)]


def phase_s5(fw, C, l):
    ps = C.ps
    pb0, pb1 = C.pb
    cb_ = C.consts.b
    with ExitStack() as ph:
        GW = fw.tile(ph, "GW", [128, 4, 1024], BF16)
        load_w(fw, C.s5_glu_w[l], GW, 0, 1024, 4)
        Cre = fw.tile(ph, "Cre", [128, 16, 128], BF16)
        nCim = fw.tile(ph, "nCim", [128, 16, 128], BF16)
        fw.dma("pool", Cre.t[:, :, :], C.s5_cre[l], writes=[Cre.b])
        fw.dma("pool", nCim.t[:, :, :], C.s5_cim[l], writes=[nCim.b])
        fw.ts("pool", nCim.t[:, :, :], nCim.t[:, :, :], -1.0, None, ALU.mult, None, [nCim.b], [nCim.b])
        d5 = fw.tile(ph, "d5", [128, 4], F32)
        gb = fw.tile(ph, "gb5", [128, 8], F32)
        fw.dma("sp", d5.t[:, :], C.s5_dT[l], writes=[d5.b])
        fw.dma("sp", gb.t[:, :], C.s5_gbT[l], writes=[gb.b])
        sp_ = fw.tile(ph, "s5sm", [128, 48], F32)
        fw.dma("sp", sp_.t[:, :], C.s5_small[l], writes=[sp_.b])
        u5T = fw.tile(ph, "u5T", [128, 4, TP], BF16)
        gT = fw.tile(ph, "g5T", [128, 4, TP], BF16)
        Sall = fw.tile(ph, "Sall", [128, 137, 3, 16], F32)
        KT = fw.tile(ph, "KT", [128, 4, 16, 128], BF16)
        PW = fw.tile(ph, "PW", [128, 2, 17, 16], F32)
        wk = fw.tile(ph, "s5wk", [128, 12, 16], F32)
        with ExitStack() as pa:
            W5 = fw.tile(pa, "W5", [128, 8, 512], BF16)
            load_w(fw, C.w_in[l], W5, C_S5, C_S5 + 512, 8)
            k = 0
            for ct in range(4):
                for (t0, n) in TGS:
                    pst = ps[k % 2]
                    for kc in range(8):
                        fw.mm(pst.t[:, 0:n], W5.t[:, kc, ct * 128:(ct + 1) * 128], C.uT.t[:, kc, t0:t0 + n], kc == 0, kc == 7,
                              [W5.b] + C.uTb[t0 // 128:(t0 + n) // 128], [pst.b])
                    fw.cp("act" if k % 2 else "dve", u5T.t[:, ct, t0:t0 + n], pst.t[:, 0:n], [pst.b], [u5T.b])
                    k += 1
            fw.barrier()
        lr, li, lst = sp_.t[:, 0:16], sp_.t[:, 16:32], sp_.t[:, 32:48]
        W_ = [wk.t[:, j, :] for j in range(12)]
        R = [sp_.b, wk.b, PW.b]
        step, lrs, ang, em1, re_, im_, t_a, t_b, inv, co_re, co_im, rr = W_
        big = [fw.tile(ph, "s5big", [128, 18, 16], F32) for _ in range(5)]
        bigi = fw.tile(ph, "s5bigi", [128, 18, 16], mybir.dt.int32)
        RB = R + [b_.b for b_ in big] + [bigi.b, cb_]
        FACT = [1.0, 1.0, 2.0, 6.0, 24.0, 120.0, 720.0, 5040.0, 40320.0, 362880.0, 3628800.0]

        def horner_exp(out, r, deg, minus1=False):
            fw.ts("dve", out, r, 1.0 / FACT[deg], None, ALU.mult, None, RB, RB)
            for j in range(deg - 1, 0, -1):
                fw.stt("dve", out, out, 1.0 / FACT[j], r, ALU.add, ALU.mult, RB, RB)
            if not minus1:
                fw.ts("dve", out, out, 1.0, None, ALU.add, None, RB, RB)

        fw.ts("dve", rr, lst, 0.125, None, ALU.mult, None, RB, RB)
        horner_exp(step, rr, 10)
        for _ in range(3):
            fw.tt("dve", step, step, step, ALU.mult, RB, RB)
        fw.tt("dve", lrs, lr, step, ALU.mult, RB, RB)
        fw.tt("dve", ang, li, step, ALU.mult, RB, RB)
        mo = CO["mtab"][0]
        mtab = C.consts.t[:, mo:mo + 288].rearrange("p (m q) -> p m q", m=18)
        TH, XM, MAG, SN, CS = [b_.t[:, :, :] for b_ in big]
        fw.tt("dve", TH, mtab, ang.unsqueeze(1).to_broadcast([128, 18, 16]), ALU.mult, RB, RB)
        fw.tt("dve", XM, mtab, lrs.unsqueeze(1).to_broadcast([128, 18, 16]), ALU.mult, RB, RB)
        horner_exp(MAG, XM, 10)
        C1, C2 = 6.28125, 2.0 * math.pi - 6.28125

        def sin_reduced(out, th):
            fw.ts("dve", out, th, 1.0 / (2.0 * math.pi), None, ALU.mult, None, RB, RB)
            fw.cp("dve", bigi.t[:, :, :], out, RB, RB)
            fw.cp("dve", XM, bigi.t[:, :, :], RB, RB)
            fw.stt("dve", out, XM, -C1, th, ALU.mult, ALU.add, RB, RB)
            fw.stt("dve", out, XM, -C2, out, ALU.mult, ALU.add, RB, RB)
            fw.act(out, out, AF.Sin, RB, RB)

        sin_reduced(SN, TH)
        fw.ts("dve", TH, TH, math.pi / 2, None, ALU.add, None, RB, RB)
        sin_reduced(CS, TH)
        fw.tt("dve", PW.t[:, 0, :, :], MAG[:, 0:17, :], CS[:, 0:17, :], ALU.mult, RB, RB)
        fw.tt("dve", PW.t[:, 1, :, :], MAG[:, 0:17, :], SN[:, 0:17, :], ALU.mult, RB, RB)
        horner_exp(em1, lrs, 7, minus1=True)
        fw.tt("dve", re_, em1, CS[:, 1, :], ALU.mult, RB, RB)
        fw.tt("dve", t_a, SN[:, 17, :], SN[:, 17, :], ALU.mult, RB, RB)
        fw.stt("dve", re_, t_a, -2.0, re_, ALU.mult, ALU.add, RB, RB)
        fw.ts("dve", t_b, em1, 1.0, None, ALU.add, None, RB, RB)
        fw.tt("dve", im_, t_b, SN[:, 1, :], ALU.mult, RB, RB)
        fw.tt("dve", t_a, lr, lr, ALU.mult, RB, RB)
        fw.tt("dve", t_b, li, li, ALU.mult, RB, RB)
        fw.tt("dve", inv, t_a, t_b, ALU.add, RB, RB)
        fw.op("dve", lambda g: g.reciprocal(inv, inv), RB, RB)
        fw.tt("dve", t_a, re_, lr, ALU.mult, RB, RB)
        fw.tt("dve", t_b, im_, li, ALU.mult, RB, RB)
        fw.tt("dve", t_a, t_a, t_b, ALU.add, RB, RB)
        fw.tt("dve", co_re, t_a, inv, ALU.mult, RB, RB)
        fw.tt("dve", t_a, im_, lr, ALU.mult, RB, RB)
        fw.tt("dve", t_b, re_, li, ALU.mult, RB, RB)
        fw.tt("dve", t_a, t_a, t_b, ALU.subtract, RB, RB)
        fw.tt("dve", co_im, t_a, inv, ALU.mult, RB, RB)
        if C.debug and l == 0:
            fw.dma("sp", C.dbg5[:, 0:192], wk.t[:, :, :].rearrange("p a b -> p (a b)"), reads=[wk.b], writes=[Buf()])
            fw.dma("sp", C.dbg5[:, 192:736], PW.t[:, :, :, :].rearrange("p a m q -> p (a m q)"), reads=[PW.b], writes=[Buf()])
        fw.memset("pool", Sall.t[:, 0, :, :], 0.0, [Sall.b])
        with ExitStack() as pd:
            Bst = fw.tile(pd, "Bst", [128, 2, 4, 128], F32)
            Bb = fw.tile(pd, "Bb", [128, 2, 4, 128], F32)
            tX = [fw.tile(pd, "tX", [128, 128], F32) for _ in range(2)]
            Xs = [fw.tile(pd, "X", [128, 2, 128], BF16) for _ in range(2)]
            XTs = [fw.tile(pd, "XT", [128, 2, 128], BF16) for _ in range(2)]
            psK = ps[2:6]
            zt = fw.tile(pd, "zt", [128, 512], BF16)
            fw.memset("pool", zt.t[:, :], 0.0, [zt.b])
            it = 0
            for ct in range(4):
                for j in range(4):
                    fw.mm(psK[j].t[:, :], zt.t[:, 0:128], zt.t[:, :], True, False, [zt.b], [psK[j].b])
                fw.dma("sp", Bst.t[:, 0, :, :], C.s5_bre[l][:, 4 * ct:4 * ct + 4, :], writes=[Bst.b])
                fw.dma("sp", Bst.t[:, 1, :, :], C.s5_bim[l][:, 4 * ct:4 * ct + 4, :], writes=[Bst.b])
                for pl in range(4):
                    pair = 4 * ct + pl
                    cr, ci = co_re[:, pair:pair + 1], co_im[:, pair:pair + 1]
                    t0_ = tX[0]
                    fw.ts("dve", t0_.t[:, :], Bst.t[:, 1, pl, :], ci, None, ALU.mult, None, [Bst.b, wk.b], [t0_.b])
                    fw.stt("dve", Bb.t[:, 0, pl, :], Bst.t[:, 0, pl, :], cr, t0_.t[:, :], ALU.mult, ALU.subtract, [Bst.b, wk.b, t0_.b], [Bb.b])
                    fw.ts("dve", t0_.t[:, :], Bst.t[:, 1, pl, :], cr, None, ALU.mult, None, [Bst.b, wk.b], [t0_.b])
                    fw.stt("dve", Bb.t[:, 1, pl, :], Bst.t[:, 0, pl, :], ci, t0_.t[:, :], ALU.mult, ALU.add, [Bst.b, wk.b, t0_.b], [Bb.b])
                for pl in range(4):
                    pair = 4 * ct + pl
                    psG = ps[pair % 2]
                    fw.mm(psG.t[:, 0:272], zt.t[:, 0:128], zt.t[:, 0:272], True, False, [zt.b], [psG.b])
                    for m in range(16):
                        X, XT, tx = Xs[it % 2], XTs[it % 2], tX[it % 2]
                        pbt = C.pb[it % 2]
                        e1 = "dve" if it % 2 == 0 else "pool"
                        pr, pi = PW.t[:, 0, m, pair:pair + 1], PW.t[:, 1, m, pair:pair + 1]
                        fw.ts(e1, tx.t[:, :], Bb.t[:, 1, pl, :], pi, None, ALU.mult, None, [Bb.b, PW.b], [tx.b])
                        fw.stt(e1, X.t[:, 0, :], Bb.t[:, 0, pl, :], pr, tx.t[:, :], ALU.mult, ALU.subtract, [Bb.b, PW.b, tx.b], [X.b])
                        fw.ts(e1, tx.t[:, :], Bb.t[:, 1, pl, :], pr, None, ALU.mult, None, [Bb.b, PW.b], [tx.b])
                        fw.stt(e1, X.t[:, 1, :], Bb.t[:, 0, pl, :], pi, tx.t[:, :], ALU.mult, ALU.add, [Bb.b, PW.b, tx.b], [X.b])
                        pk = psK[m // 4]
                        ks = slice((m % 4) * 128, (m % 4 + 1) * 128)
                        fw.mm(pk.t[:, ks], X.t[:, 0, :], Cre.t[:, pair, :], False, False, [X.b, Cre.b], [pk.b])
                        fw.mm(pk.t[:, ks], X.t[:, 1, :], nCim.t[:, pair, :], False, pl == 3, [X.b, nCim.b], [pk.b])
                        fw.tr(pbt.t[:, 0:128], X.t[:, 0, :], C.identb.t[:, :], [X.b, C.identb.b], [pbt.b])
                        fw.tr(pbt.t[:, 128:256], X.t[:, 1, :], C.identb.t[:, :], [X.b, C.identb.b], [pbt.b])
                        fw.cp("act", XT.t[:, :, :], pbt.t[:, 0:256].rearrange("p (a q) -> p a q", a=2), [pbt.b], [XT.b])
                        tau = 15 - m
                        rhs = u5T.t[:, ct, :].rearrange("p (c b) -> p c b", b=16)[:, :, tau]
                        fw.mm(psG.t[:, 0:136], XT.t[:, 0, :], rhs, False, m == 15, [XT.b, u5T.b], [psG.b])
                        fw.mm(psG.t[:, 136:272], XT.t[:, 1, :], rhs, False, m == 15, [XT.b, u5T.b], [psG.b])
                        it += 1
                    fw.cp("dve", Sall.t[:, 1:137, 0, pair], psG.t[:, 0:136], [psG.b], [Sall.b])
                    fw.cp("dve", Sall.t[:, 1:137, 1, pair], psG.t[:, 136:272], [psG.b], [Sall.b])
                for j in range(4):
                    fw.cp("act" if j % 2 else "dve", KT.t[:, ct, 4 * j:4 * j + 4, :], psK[j].t[:, :].rearrange("p (m c) -> p m c", m=4),
                          [psK[j].b], [KT.b])
            fw.barrier()
        if C.debug and l == 0:
            fw.dma("pool", C.dbg5[:, 736:736 + 2048], KT.t[:, 0, :, :].rearrange("p m c -> p (m c)"), reads=[KT.b], writes=[Buf()])
            fw.dma("sp", C.dbg5[:, 2784:2784 + 137 * 48], Sall.t[:, :, :, :].rearrange("p c a q -> p (c a q)"), reads=[Sall.b], writes=[Buf()])
        with ExitStack() as pe_:
            A1 = fw.tile(pe_, "A1", [128, 2, 16], F32)
            A2 = fw.tile(pe_, "A2", [128, 2, 16], F32)
            p1 = fw.tile(pe_, "p1", [128, 2, 16], F32)
            p2 = fw.tile(pe_, "p2", [128, 2, 16], F32)
            fw.cp("dve", A1.t[:, 0, :], PW.t[:, 0, 16, :], [PW.b], [A1.b])
            fw.cp("dve", A1.t[:, 1, :], PW.t[:, 0, 16, :], [PW.b], [A1.b])
            fw.ts("dve", A2.t[:, 0, :], PW.t[:, 1, 16, :], -1.0, None, ALU.mult, None, [PW.b], [A2.b])
            fw.cp("dve", A2.t[:, 1, :], PW.t[:, 1, 16, :], [PW.b], [A2.b])
            for c in range(136):
                fw.tt("dve", p1.t[:, :, :], A1.t[:, :, :], Sall.t[:, c, 0:2, :], ALU.mult, [A1.b, Sall.b], [p1.b])
                fw.tt("pool", p2.t[:, :, :], A2.t[:, :, :], Sall.t[:, c, 1:3, :], ALU.mult, [A2.b, Sall.b], [p2.b])
                fw.tt("dve", p1.t[:, :, :], p1.t[:, :, :], p2.t[:, :, :], ALU.add, [p1.b, p2.b], [p1.b])
                fw.tt("dve", Sall.t[:, c + 1, 0:2, :], Sall.t[:, c + 1, 0:2, :], p1.t[:, :, :], ALU.add, [Sall.b, p1.b], [Sall.b])
                fw.cp("pool", Sall.t[:, c + 1, 2, :], Sall.t[:, c + 1, 0, :], [Sall.b], [Sall.b])
            fw.barrier()
        with ExitStack() as pf:
            SP = fw.tile(pf, "SP", [128, 16, 4, 2, 136], BF16)
            tS = [fw.tile(pf, "tS", [128, 136], F32) for _ in range(2)]
            z = fw.tile(pf, "z5", [128, 512], F32)
            z2 = fw.tile(pf, "z52", [128, 512], F32)
            k = 0
            for ct in range(4):
                for b in range(16):
                    for pl in range(4):
                        pair = 4 * ct + pl
                        e1 = "dve" if pl % 2 == 0 else "pool"
                        ts_ = tS[pl % 2]
                        pr, pi = PW.t[:, 0, b + 1, pair:pair + 1], PW.t[:, 1, b + 1, pair:pair + 1]
                        Sr, Si = Sall.t[:, 0:136, 0, pair], Sall.t[:, 0:136, 1, pair]
                        fw.ts(e1, ts_.t[:, :], Si, pi, None, ALU.mult, None, [Sall.b, PW.b], [ts_.b])
                        fw.stt(e1, SP.t[:, b, pl, 0, :], Sr, pr, ts_.t[:, :], ALU.mult, ALU.subtract, [Sall.b, PW.b, ts_.b], [SP.b])
                        fw.ts(e1, ts_.t[:, :], Si, pr, None, ALU.mult, None, [Sall.b, PW.b], [ts_.b])
                        fw.stt(e1, SP.t[:, b, pl, 1, :], Sr, pi, ts_.t[:, :], ALU.mult, ALU.add, [Sall.b, PW.b, ts_.b], [SP.b])
                for (t0, n) in TGS:
                    c0, nch = t0 // 16, n // 16
                    pst = ps[k % 2]
                    k += 1
                    pv = pst.t[:, 0:n].rearrange("p (c b) -> p c b", b=16)
                    uv = u5T.t[:, ct, t0:t0 + n].rearrange("p (c b) -> p c b", b=16)
                    for tau in range(16):
                        fw.mm(pv[:, :, tau:16], KT.t[:, ct, tau, :], uv[:, :, 0:16 - tau], tau == 0, False, [KT.b, u5T.b], [pst.b])
                    for b in range(16):
                        for pl in range(4):
                            pair = 4 * ct + pl
                            last = (b == 15 and pl == 3)
                            fw.mm(pv[:, :, b], Cre.t[:, pair, :], SP.t[:, b, pl, 0, c0:c0 + nch], False, False, [Cre.b, SP.b], [pst.b])
                            fw.mm(pv[:, :, b], nCim.t[:, pair, :], SP.t[:, b, pl, 1, c0:c0 + nch], False, last, [nCim.b, SP.b], [pst.b])
                    fw.stt("dve", z.t[:, 0:n], u5T.t[:, ct, t0:t0 + n], d5.t[:, ct:ct + 1], pst.t[:, 0:n], ALU.mult, ALU.add,
                           [u5T.b, d5.b, pst.b], [z.b])
                    fw.tt("pool", z2.t[:, 0:n], z.t[:, 0:n], z.t[:, 0:n], ALU.mult, [z.b], [z2.b])
                    fw.ts("pool", z2.t[:, 0:n], z2.t[:, 0:n], 0.044715, 1.0, ALU.mult, ALU.add, [z2.b], [z2.b])
                    fw.tt("pool", z2.t[:, 0:n], z2.t[:, 0:n], z.t[:, 0:n], ALU.mult, [z2.b, z.b], [z2.b])
                    fw.act(z2.t[:, 0:n], z2.t[:, 0:n], AF.Sigmoid, [z2.b], [z2.b], scale=2.0 * math.sqrt(2.0 / math.pi))
                    fw.tt("pool", gT.t[:, ct, t0:t0 + n], z.t[:, 0:n], z2.t[:, 0:n], ALU.mult, [z.b, z2.b], [gT.b])
            fw.barrier()
        with ExitStack() as pg:
            sgs = [fw.tile(pg, "sg5", [128, 512], F32) for _ in range(2)]
            yos = [fw.tile(pg, "yo5", [128, 512], BF16) for _ in range(2)]
            k = 0
            for nb in range(4):
                for (t0, n) in TGS:
                    pa_, pg_ = ps[2 + 2 * (k % 2)], ps[3 + 2 * (k % 2)]
                    sg, yo = sgs[k % 2], yos[k % 2]
                    k += 1
                    for kc in range(4):
                        fw.mm(pa_.t[:, 0:n], GW.t[:, kc, nb * 128:(nb + 1) * 128], gT.t[:, kc, t0:t0 + n], kc == 0, kc == 3, [GW.b, gT.b], [pa_.b])
                    for kc in range(4):
                        fw.mm(pg_.t[:, 0:n], GW.t[:, kc, 512 + nb * 128:512 + (nb + 1) * 128], gT.t[:, kc, t0:t0 + n], kc == 0, kc == 3,
                              [GW.b, gT.b], [pg_.b])
                    fw.act(sg.t[:, 0:n], pg_.t[:, 0:n], AF.Sigmoid, [pg_.b, gb.b], [sg.b], bias=gb.t[:, 4 + nb:5 + nb])
                    fw.stt("dve", yo.t[:, 0:n], pa_.t[:, 0:n], gb.t[:, nb:nb + 1], sg.t[:, 0:n], ALU.add, ALU.mult, [pa_.b, gb.b, sg.b], [yo.b])
                    fw.dma("sp", C.YT[4 + nb, :, t0:t0 + n], yo.t[:, 0:n], reads=[yo.b], writes=C.YTb[1][t0 // 128:(t0 + n) // 128])
            fw.barrier()
        fw.barrier()


MGS = [(g * 256, min(256, TP - g * 256)) for g in range((TP + 255) // 256)]


def rms_epilogue(fw, C, psA, psB, nw, xt, wk2):
    junk, ss, tmp = wk2
    fw.act(junk.t[:, 0:512], psA.t[:, :], AF.Square, [psA.b], [junk.b, ss.b], accum=ss.t[:, 2:3])
    fw.act(junk.t[:, 512:1024], psB.t[:, :], AF.Square, [psB.b], [junk.b, ss.b], accum=ss.t[:, 3:4])
    fw.tt("dve", ss.t[:, 2:3], ss.t[:, 2:3], ss.t[:, 3:4], ALU.add, [ss.b], [ss.b])
    rstd_from_ss(fw, C, ss.t[:, 2:3], ss.t[:, 3:4], 1024.0, [ss.b], [ss.b])
    fw.stt("dve", tmp.t[:, 0:512], psA.t[:, :], ss.t[:, 3:4], nw.t[:, 0:512], ALU.mult, ALU.mult, [psA.b, ss.b, nw.b], [tmp.b])
    fw.stt("dve", tmp.t[:, 512:1024], psB.t[:, :], ss.t[:, 3:4], nw.t[:, 512:1024], ALU.mult, ALU.mult, [psB.b, ss.b, nw.b], [tmp.b])
    fw.tt("pool", xt.t[:, :], xt.t[:, :], tmp.t[:, :], ALU.add, [xt.b, tmp.b], [xt.b])


def phase_merge(fw, C, l):
    ps = C.ps
    with ExitStack() as ph:
        WG = fw.tile(ph, "WG", [128, 8, 4096], BF16)
        load_w(fw, C.w_in[l], WG, C_GATE, C_GATE + 4096, 8)
        WB = fw.tile(ph, "WB", [128, 16, 1024], BF16)
        for n in range(4):
            v = C.w_branch[l][n].rearrange("(cb p) d -> p cb d", p=128)
            fw.dma("pool", WB.t[:, 4 * n:4 * n + 4, :], v, writes=[WB.b])
        WO = fw.tile(ph, "WO", [128, 8, 1024], BF16)
        load_w(fw, C.w_out[l], WO, 0, 1024, 8)
        nw1 = fw.tile(ph, "nw1", [128, D], F32)
        nw2 = fw.tile(ph, "nw2", [128, D], F32)
        fw.dma("sp", nw1.t[:, :], C.norm_post_mix[l].partition_broadcast(128), writes=[nw1.b])
        fw.dma("sp", nw2.t[:, :], C.norm_pre_mlp[l].partition_broadcast(128), writes=[nw2.b])
        YTs = fw.tile(ph, "YTs", [128, 16, 256], BF16)
        mixT = fw.tile(ph, "mixT", [128, 8, 256], BF16)
        acc = fw.tile(ph, "macc", [128, 256], F32)
        sgs = [fw.tile(ph, "msg", [128, 256], F32) for _ in range(2)]
        tmpm = fw.tile(ph, "mtmp", [128, 256], F32)
        xts = [fw.tile(ph, "mxt", [128, D], F32) for _ in range(2)]
        wk = (fw.tile(ph, "junk", [128, D], BF16), fw.tile(ph, "ss", [128, 4], F32), fw.tile(ph, "ub", [128, D], BF16))
        wk2 = (wk[0], wk[1], fw.tile(ph, "mtmp2", [128, D], F32))
        k = 0
        for (t0, n) in MGS:
            tiles = list(range(t0 // 128, (t0 + n) // 128))
            fw.dma("sp", YTs.t[:, :, 0:n], C.YT[:, :, t0:t0 + n].rearrange("c p t -> p c t"),
                   reads=[C.YTb[m][i] for m in range(4) for i in tiles], writes=[YTs.b])
            for db in range(8):
                for nn in range(4):
                    pg_, pb_ = ps[2 * (k % 2)], ps[2 * (k % 2) + 1]
                    sg = sgs[k % 2]
                    k += 1
                    c0 = nn * 1024 + db * 128
                    for kc in range(8):
                        fw.mm(pg_.t[:, 0:n], WG.t[:, kc, c0:c0 + 128], C.uT.t[:, kc, t0:t0 + n], kc == 0, kc == 7,
                              [WG.b] + [C.uTb[i] for i in tiles], [pg_.b])
                    for cb in range(4):
                        fw.mm(pb_.t[:, 0:n], WB.t[:, 4 * nn + cb, db * 128:(db + 1) * 128], YTs.t[:, 4 * nn + cb, 0:n], cb == 0, cb == 3,
                              [WB.b, YTs.b], [pb_.b])
                    fw.act(sg.t[:, 0:n], pg_.t[:, 0:n], AF.Sigmoid, [pg_.b], [sg.b])
                    if nn == 0:
                        fw.tt("dve", acc.t[:, 0:n], sg.t[:, 0:n], pb_.t[:, 0:n], ALU.mult, [sg.b, pb_.b], [acc.b])
                    else:
                        fw.tt("dve", tmpm.t[:, 0:n], sg.t[:, 0:n], pb_.t[:, 0:n], ALU.mult, [sg.b, pb_.b], [tmpm.b])
                        if nn < 3:
                            fw.tt("pool", acc.t[:, 0:n], acc.t[:, 0:n], tmpm.t[:, 0:n], ALU.add, [acc.b, tmpm.b], [acc.b])
                        else:
                            fw.tt("pool", mixT.t[:, db, 0:n], acc.t[:, 0:n], tmpm.t[:, 0:n], ALU.add, [acc.b, tmpm.b], [mixT.b])
            for i in tiles:
                xt = xts[i % 2]
                src = C.h0 if l == 0 else C.hbuf
                fw.dma("sp", xt.t[:, :], src[i * 128:(i + 1) * 128, :], reads=([] if l == 0 else [C.hb[i]]), writes=[xt.b])
                sub = slice(i * 128 - t0, i * 128 - t0 + 128)
                for dh in range(2):
                    pst = ps[4 + dh]
                    for db in range(8):
                        fw.mm(pst.t[:, :], mixT.t[:, db, sub], WO.t[:, db, dh * 512:(dh + 1) * 512], db == 0, db == 7, [mixT.b, WO.b], [pst.b])
                rms_epilogue(fw, C, ps[4], ps[5], nw1, xt, wk2)
                fw.dma("sp", C.hbuf[i * 128:(i + 1) * 128, :], xt.t[:, :], reads=[xt.b], writes=[C.hb[i]])
                norm_rows_to_T(fw, C, ph, xt.t[:, :], xt.b, nw2, C.uT, C.uTb, i, wk)
        fw.barrier()


def phase_mlp(fw, C, l, last):
    ps = C.ps
    with ExitStack() as ph:
        WU = fw.tile(ph, "WU", [128, 8, 4096], BF16)
        load_w(fw, C.w_up[l], WU, 0, 4096, 8)
        WD = fw.tile(ph, "WD", [128, 32, 1024], BF16)
        load_w(fw, C.w_down[l], WD, 0, 1024, 32)
        nw = fw.tile(ph, "nw3", [128, D], F32)
        fw.dma("sp", nw.t[:, :], C.norm_post_mlp[l].partition_broadcast(128), writes=[nw.b])
        hT = fw.tile(ph, "hT", [128, 32, 256], BF16)
        rl = [fw.tile(ph, "rl", [128, 512], BF16) for _ in range(2)]
        xts = [fw.tile(ph, "pxt", [128, D], F32)] * 2
        ptmp = fw.tile(ph, "ptmp", [128, D], F32)
        wk2 = (ptmp, fw.tile(ph, "ss", [128, 4], F32), ptmp)
        k = 0
        for (t0, n) in MGS:
            tiles = list(range(t0 // 128, (t0 + n) // 128))
            for fp in range(16):
                pst = ps[k % 4]
                r = rl[k % 2]
                k += 1
                for j in range(2):
                    ffc = 2 * fp + j
                    for kc in range(8):
                        fw.mm(pst.t[:, j * 256:j * 256 + n], WU.t[:, kc, ffc * 128:(ffc + 1) * 128], C.uT.t[:, kc, t0:t0 + n], kc == 0, kc == 7,
                              [WU.b] + [C.uTb[i] for i in tiles], [pst.b])
                pv = pst.t[:, :].rearrange("p (j t) -> p j t", j=2)[:, :, 0:n]
                rv = r.t[:, :].rearrange("p (j t) -> p j t", j=2)[:, :, 0:n]
                fw.act(rv, pv, AF.Relu, [pst.b], [r.b])
                fw.tt("pool" if fp % 2 else "dve", hT.t[:, 2 * fp:2 * fp + 2, 0:n], rv, rv, ALU.mult, [r.b], [hT.b])
            for i in tiles:
                xt = xts[i % 2]
                fw.dma("sp", xt.t[:, :], C.hbuf[i * 128:(i + 1) * 128, :], reads=[C.hb[i]], writes=[xt.b])
                sub = slice(i * 128 - t0, i * 128 - t0 + 128)
                for dh in range(2):
                    pst = ps[4 + dh]
                    for ffc in range(32):
                        fw.mm(pst.t[:, :], hT.t[:, ffc, sub], WD.t[:, ffc, dh * 512:(dh + 1) * 512], ffc == 0, ffc == 31, [hT.b, WD.b], [pst.b])
                rms_epilogue(fw, C, ps[4], ps[5], nw, xt, wk2)
                if not last:
                    fw.dma("sp", C.hbuf[i * 128:(i + 1) * 128, :], xt.t[:, :], reads=[xt.b], writes=[C.hb[i]])
                else:
                    if C.debug:
                        fw.dma("sp", C.hbuf[i * 128:(i + 1) * 128, :], xt.t[:, :], reads=[xt.b], writes=[C.hb[i]])
                    lo = max(i * 128, 16)
                    hi = min((i + 1) * 128, T)
                    if hi > lo:
                        fw.dma("sp", C.out[lo - 16:hi - 16, :], xt.t[lo - i * 128:hi - i * 128, :], reads=[xt.b], writes=[C.outb])
        fw.barrier()


def build(debug=False, n_layers=DEPTH, phases=None):
    nc = bass.Bass("TRN2", target_bir_lowering=False)
    C = Ctx()
    C.debug = debug

    def din(name, shape):
        return nc.dram_tensor(name, list(shape), F32, kind="ExternalInput").ap()

    C.h0 = din("h0", [TP, D])
    C.consts_d = din("consts", [128, CO_TOTAL[0]])
    C.w_in = din("w_in", [4, D, N_IN])
    C.w_branch = din("w_branch", [4, 4, 512, D])
    C.w_out = din("w_out", [4, D, D])
    C.w_up = din("w_up", [4, D, 4 * D])
    C.w_down = din("w_down", [4, 4 * D, D])
    C.s5_glu_w = din("s5_glu_w", [4, 512, 1024])
    for nm in ("norm_pre_mix", "norm_post_mix", "norm_pre_mlp", "norm_post_mlp"):
        setattr(C, nm, din(nm, [4, D]))
    for nm in ("ret_gn_w", "ssd_norm_w", "hgrn_norm_w", "ssd_dsk"):
        setattr(C, nm, din(nm, [4, 512]))
    C.ssd_dt_bias = din("ssd_dt_bias", [4, 8])
    C.ssd_a_log = din("ssd_a_log", [4, 8])
    C.lbT = din("lbT", [128, 4, 4])
    C.ssd_cw = din("ssd_cw", [4, 128, 8, 4])
    C.ssd_cb = din("ssd_cb", [4, 128, 8])
    C.s5_small = din("s5_small", [4, 128, 48])
    for nm in ("s5_bre", "s5_bim", "s5_cre", "s5_cim"):
        setattr(C, nm, din(nm, [4, 128, 16, 128]))
    C.s5_dT = din("s5_dT", [4, 128, 4])
    C.s5_gbT = din("s5_gbT", [4, 128, 8])
    C.out = nc.dram_tensor("out", [2048, D], F32, kind="ExternalOutput").ap()
    sk = "ExternalOutput" if debug else "Internal"
    C.hbuf = nc.dram_tensor("hbuf", [TP, D], F32, kind=sk).ap()
    C.YT = nc.dram_tensor("YT", [16, 128, TP], BF16, kind=sk).ap()
    if debug:
        C.dbg5 = nc.dram_tensor("dbg5", [128, 2784 + 137 * 48], F32, kind="ExternalOutput").ap()
    C.hb = [Buf("hb%d" % i) for i in range(NT)]
    C.YTb = [[Buf("yt%d_%d" % (m, i)) for i in range(NT)] for m in range(4)]
    C.outb = Buf("out")
    with ExitStack() as es:
        fw = FW(nc, es)
        C.ps = [TL(es.enter_context(nc.psum_tensor("ps%d" % j, [128, 512], F32)), "ps%d" % j) for j in range(6)]
        C.pb = [TL(es.enter_context(nc.psum_tensor("pb%d" % j, [128, 1024], BF16)), "pb%d" % j) for j in range(2)]
        C.consts = fw.tile(es, "consts", [128, CO_TOTAL[0]], F32)
        fw.dma("sp", C.consts.t[:, :], C.consts_d[:, :], writes=[C.consts.b])
        C.identb = fw.tile(es, "identb", [128, 128], BF16)
        C.identf = fw.tile(es, "identf", [128, 128], F32)
        fw.cp("dve", C.identb.t[:, :], cslice(C, "ident"), [C.consts.b], [C.identb.b])
        fw.cp("dve", C.identf.t[:, :], cslice(C, "ident"), [C.consts.b], [C.identf.b])
        C.uT = fw.tile(es, "uT", [128, 8, TP], BF16)
        C.uTb = [Buf("uT%d" % i) for i in range(NT)]
        C.lb_all = fw.tile(es, "lb_all", [128, 4, 4], F32)
        C.oml_all = fw.tile(es, "oml_all", [128, 4, 4], F32)
        lbe = fw.tile(es, "lbe", [128, 4, 4], F32)
        lsum = fw.tile(es, "lsum", [128, 4], F32)
        R = [lbe.b, lsum.b, C.lb_all.b, C.oml_all.b]
        fw.dma("sp", lbe.t[:, :, :], C.lbT[:, :, :], writes=[lbe.b])
        fw.act(lbe.t[:, :, :], lbe.t[:, :, :], AF.Exp, R, R)
        fw.tt("dve", lsum.t[:, :], lbe.t[:, 0, :], lbe.t[:, 1, :], ALU.add, R, R)
        fw.tt("dve", lsum.t[:, :], lsum.t[:, :], lbe.t[:, 2, :], ALU.add, R, R)
        fw.tt("dve", lsum.t[:, :], lsum.t[:, :], lbe.t[:, 3, :], ALU.add, R, R)
        fw.op("dve", lambda g: g.reciprocal(lsum.t[:, :], lsum.t[:, :]), R, R)
        fw.memset("dve", C.lb_all.t[:, 0, :], 0.0, R)
        for ll in range(1, 4):
            fw.tt("dve", lbe.t[:, ll, :], lbe.t[:, ll, :], lsum.t[:, :], ALU.mult, R, R)
            fw.tt("dve", C.lb_all.t[:, ll, :], C.lb_all.t[:, ll - 1, :], lbe.t[:, ll, :], ALU.add, R, R)
        fw.ts("dve", C.oml_all.t[:, :, :], C.lb_all.t[:, :, :], -1.0, 1.0, ALU.mult, ALU.add, R, R)
        fw.barrier()
        allp = ("norm", "ret", "s5", "ssd", "hg", "merge", "mlp")
        for l in range(n_layers):
            for pn in allp:
                if phases is not None and pn not in phases:
                    continue
                if pn == "norm":
                    phase_norm1(fw, C, l)
                elif pn == "ret":
                    phase_ret(fw, C, l)
                elif pn == "s5":
                    phase_s5(fw, C, l)
                elif pn == "ssd":
                    phase_ssd(fw, C, l)
                elif pn == "hg":
                    phase_hg(fw, C, l)
                elif pn == "merge":
                    phase_merge(fw, C, l)
                elif pn == "mlp":
                    phase_mlp(fw, C, l, last=(l == n_layers - 1))
        fw.barrier()
        C.n_inst, C.n_wait = fw.n_inst, fw.n_wait
    return nc, C


CO_TOTAL = [0]
_CONSTS = None


def get_consts():
    global _CONSTS
    if _CONSTS is None:
        _CONSTS = host_consts()
        CO_TOTAL[0] = _CONSTS.shape[1]
    return _CONSTS


def make_in_maps(inp):
    consts = get_consts()
    P = host_params(inp)
    f = lambda a: np.ascontiguousarray(np.asarray(a, np.float32))
    shared = {"consts": consts}
    for nm in ("w_in", "w_branch", "w_out", "w_up", "w_down", "s5_glu_w", "norm_pre_mix", "norm_post_mix", "norm_pre_mlp",
               "norm_post_mlp", "ret_gn_w", "ssd_norm_w", "hgrn_norm_w", "ssd_dt_bias", "ssd_a_log"):
        shared[nm] = f(inp[nm])
    for nm in ("lbT", "ssd_cw", "ssd_cb", "ssd_dsk", "s5_small", "s5_bre", "s5_bim", "s5_cre", "s5_cim", "s5_dT", "s5_gbT"):
        shared[nm] = P[nm]
    x = np.asarray(inp["x"], np.float32)
    meta = np.asarray(inp["meta_tokens"], np.float32)
    maps = []
    for b in range(x.shape[0]):
        h0 = np.zeros((TP, D), np.float32)
        h0[0:16] = meta
        h0[16:T] = x[b]
        m = dict(shared)
        m["h0"] = h0
        maps.append(m)
    return maps


_NC = None


def kernel(**inputs):
    global _NC
    maps = make_in_maps(inputs)
    if _NC is None:
        _NC = build()[0]
    res = run_bass_kernel_spmd(_NC, maps, core_ids=list(range(len(maps))))
    return np.stack([np.asarray(r["out"], np.float32) for r in res.results], axis=0)
```

```python
import math
import os
import numpy as np
from contextlib import ExitStack
import concourse.bass as bass
import concourse.mybir as mybir
from concourse.bass_utils import run_bass_kernel_spmd

F32 = mybir.dt.float32
BF16 = mybir.dt.bfloat16
ALU = mybir.AluOpType
AF = mybir.ActivationFunctionType
AX = mybir.AxisListType

DEPTH = 4
D = 1024
T = 2064
NT = 17
TP = NT * 128
EPS = 1e-6
N_IN = 9736
C_RET, C_S5, C_SSD, C_HG, C_GATE = 0, 1536, 2048, 3592, 5640
GAMMA = [1.0 - 2.0 ** (-5.0 - h) for h in range(4)]


class Buf:
    __slots__ = ("name", "w", "r")

    def __init__(self, name=""):
        self.name = name
        self.w = None
        self.r = []


class TL:
    def __init__(self, t, name):
        self.t = t
        self.b = Buf(name)


class FW:
    N_DMA_SEMS = 16

    def __init__(self, nc, es):
        self.nc = nc
        self.es = es
        self.eng = {"pe": nc.tensor, "dve": nc.vector, "act": nc.scalar, "pool": nc.gpsimd, "sp": nc.sync}
        self.sems = {}
        self.cnt = {}
        for e in ("pe", "dve", "act", "pool"):
            self.sems[e] = es.enter_context(nc.semaphore("s_" + e))
            self.cnt[e] = 0
        self.dma_keys = {}
        self.dma_rr = {}
        for q in ("sp", "pool"):
            ks = []
            for i in range(self.N_DMA_SEMS):
                k = "d_%s_%d" % (q, i)
                self.sems[k] = es.enter_context(nc.semaphore(k))
                self.cnt[k] = 0
                ks.append(k)
            self.dma_keys[q] = ks
            self.dma_rr[q] = 0
        self.known = {e: {} for e in self.eng}
        self.n_inst = 0
        self.n_wait = 0
        self.uid = 0

    def tile(self, stack, name, shape, dt):
        self.uid += 1
        nm = "%s_%d" % (name, self.uid)
        return TL(stack.enter_context(self.nc.sbuf_tensor(nm, list(shape), dt)), nm)

    def _wait(self, e, ev):
        if ev is None:
            return
        k, v = ev
        if e == "pe" and k == "pe":
            return
        kn = self.known[e]
        if kn.get(k, 0) >= v:
            return
        self.eng[e].wait_ge(self.sems[k], v)
        kn[k] = v
        self.n_wait += 1

    def _deps(self, e, reads, writes):
        for b in reads:
            self._wait(e, b.w)
        for b in writes:
            self._wait(e, b.w)
            for ev in b.r:
                self._wait(e, ev)

    def _mark(self, ev, reads, writes):
        for b in reads:
            b.r.append(ev)
        for b in writes:
            b.w = ev
            b.r = []

    def op(self, e, fn, reads=(), writes=()):
        self._deps(e, reads, writes)
        ins = fn(self.eng[e])
        self.cnt[e] += 1
        ins.then_inc(self.sems[e], 1)
        ev = (e, self.cnt[e])
        self._mark(ev, reads, writes)
        self.n_inst += 1
        return ev

    def dma(self, q, out, in_, reads=(), writes=(), **kw):
        self._deps(q, reads, writes)
        ks = self.dma_keys[q]
        k = ks[self.dma_rr[q] % len(ks)]
        self.dma_rr[q] += 1
        if self.cnt[k] > 0:
            self._wait(q, (k, self.cnt[k]))
        ins = self.eng[q].dma_start(out=out, in_=in_, **kw)
        self.cnt[k] += 16
        ins.then_inc(self.sems[k], 16)
        ev = (k, self.cnt[k])
        self._mark(ev, reads, writes)
        self.n_inst += 1
        return ev

    def barrier(self, engines=("pe", "dve", "act", "pool", "sp")):
        for e in engines:
            for k, v in self.cnt.items():
                if v > 0:
                    self._wait(e, (k, v))

    def tt(self, e, out, a, b, op, R, W):
        return self.op(e, lambda g: g.tensor_tensor(out, a, b, op), R, W)

    def ts(self, e, out, a, s1, s2, op0, op1, R, W):
        if s2 is None:
            return self.op(e, lambda g: g.tensor_scalar(out, a, s1, None, op0=op0), R, W)
        return self.op(e, lambda g: g.tensor_scalar(out, a, s1, s2, op0=op0, op1=op1), R, W)

    def stt(self, e, out, in0, sc, in1, op0, op1, R, W):
        e = "dve"
        return self.op(e, lambda g: g.scalar_tensor_tensor(out, in0, sc, in1, op0=op0, op1=op1), R, W)

    def cp(self, e, out, in_, R, W):
        if e == "act":
            return self.op(e, lambda g: g.copy(out, in_), R, W)
        return self.op(e, lambda g: g.tensor_copy(out, in_), R, W)

    def act(self, out, in_, func, R, W, bias=None, scale=None, accum=None):
        kw = {}
        if bias is not None:
            kw["bias"] = bias
        if scale is not None:
            kw["scale"] = scale
        if accum is not None:
            kw["accum_out"] = accum
        return self.op("act", lambda g: g.activation(out, in_, func, **kw), R, W)

    def mm(self, out, lhsT, rhs, start, stop, R, W):
        return self.op("pe", lambda g: g.matmul(out, lhsT, rhs, start=start, stop=stop), R, W)

    def tr(self, out, in_, ident, R, W):
        return self.op("pe", lambda g: g.transpose(out, in_, ident), R, W)

    def memset(self, e, ap, val, W):
        return self.op(e, lambda g: g.memset(ap, val), (), W)


CO = {}


def _pack(items):
    off = 0
    cols = []
    for name, arr in items:
        arr = np.asarray(arr, np.float32).reshape(128, -1)
        CO[name] = (off, arr.shape[1])
        off += arr.shape[1]
        cols.append(arr)
    return np.ascontiguousarray(np.concatenate(cols, axis=1))


def host_consts():
    s = np.arange(128)[:, None]
    t = np.arange(128)[None, :]
    ident = (s == t).astype(np.float32)
    triu = (s <= t).astype(np.float32)
    negm = np.where(s <= t, 0.0, -30000.0).astype(np.float32)
    ones = np.ones((128, 128), np.float32)
    retdt = np.zeros((128, 4, 128), np.float64)
    qdec = np.zeros((128, 4, 64), np.float64)
    kdec = np.zeros((128, 4, 64), np.float64)
    for h in range(4):
        g = GAMMA[h]
        retdt[:, h, :] = np.where(s <= t, 0.125 * g ** np.maximum(t - s, 0), 0.0)
        qdec[:, h, :] = (g ** (np.arange(128) + 1.0))[:, None]
        kdec[:, h, :] = (0.125 * g ** (127.0 - np.arange(128)))[:, None]
    half = 32
    inv_freq = (10000.0 ** (-np.arange(half, dtype=np.float32) / half)).astype(np.float32)
    pos = (np.arange(NT)[None, :] * 128 + np.arange(128)[:, None]).astype(np.float32)
    ang = pos[:, :, None] * inv_freq[None, None, :]
    cos = np.cos(ang).astype(np.float32)
    sin = np.sin(ang).astype(np.float32)
    halfpi = np.full((128, 1), math.pi / 2, np.float32)
    mvals = np.array(list(range(17)) + [0.5], np.float32)
    mtab = np.broadcast_to(mvals[None, :, None], (128, 18, 16))
    return _pack([("mtab", mtab), ("ident", ident), ("triu", triu), ("negm", negm), ("ones", ones), ("retdt", retdt),
                  ("qdec", qdec), ("kdec", kdec), ("cos", cos), ("sin", sin), ("halfpi", halfpi)])


def host_params(inp):
    P = {}
    f = lambda a: np.ascontiguousarray(np.asarray(a, np.float32))
    P["lbT"] = f(np.asarray(inp["hgrn_lb"]).reshape(4, 4, 128).transpose(2, 0, 1))
    P["ssd_cw"] = f(np.asarray(inp["ssd_conv_w"]).reshape(4, 4, 8, 128).transpose(0, 3, 2, 1))
    P["ssd_cb"] = f(np.asarray(inp["ssd_conv_b"]).reshape(4, 8, 128).transpose(0, 2, 1))
    P["ssd_dsk"] = f(np.repeat(np.asarray(inp["ssd_d"]), 64, axis=1))
    def pl_small(a):
        a = np.asarray(a).reshape(4, 16, 2, 64)
        return a.transpose(0, 2, 3, 1).reshape(4, 128, 16)
    ls = np.broadcast_to(np.asarray(inp["s5_log_step"])[:, :, None], (4, 32, 64))
    P["s5_small"] = f(np.concatenate([pl_small(inp["s5_lam_re"]), pl_small(inp["s5_lam_im"]), pl_small(ls)], axis=2))
    def pl_b(b):
        b = np.asarray(b)
        out = np.zeros((4, 2, 64, 16, 8, 16), np.float32)
        for g in range(32):
            out[:, g % 2, :, g // 2, g % 8, :] = b[:, g]
        return out.reshape(4, 128, 16, 128)
    def pl_c(c):
        c = np.asarray(c)
        out = np.zeros((4, 2, 64, 16, 8, 16), np.float32)
        for g in range(32):
            out[:, g % 2, :, g // 2, g % 8, :] = c[:, g].transpose(0, 2, 1)
        return out.reshape(4, 128, 16, 128)
    P["s5_bre"] = pl_b(inp["s5_b_re"])
    P["s5_bim"] = pl_b(inp["s5_b_im"])
    P["s5_cre"] = pl_c(inp["s5_c_re"])
    P["s5_cim"] = pl_c(inp["s5_c_im"])
    P["s5_dT"] = f(np.asarray(inp["s5_d"]).reshape(4, 4, 128).transpose(0, 2, 1))
    P["s5_gbT"] = f(np.asarray(inp["s5_glu_b"]).reshape(4, 8, 128).transpose(0, 2, 1))
    return P


class Ctx:
    pass


def cslice(C, name, a=None, b=None):
    off, n = CO[name]
    if a is None:
        return C.consts.t[:, off:off + n]
    return C.consts.t[:, off + a:off + b]


def load_w(fw, src2d, dst, c0, c1, kcs, R=()):
    v = src2d.rearrange("(kc p) n -> p kc n", p=128)
    step = 2048
    for kc in range(kcs):
        for a in range(c0, c1, step):
            b = min(a + step, c1)
            fw.dma("pool", dst.t[:, kc, a - c0:b - c0], v[:, kc, a:b], reads=R, writes=[dst.b])


def rstd_from_ss(fw, C, ss_ap, out_ap, n, R, W):
    fw.ts("dve", out_ap, ss_ap, 1.0 / n, EPS, ALU.mult, ALU.add, R, W)
    fw.act(out_ap, out_ap, AF.Sqrt, W, W)
    fw.op("dve", lambda g: g.reciprocal(out_ap, out_ap), W, W)


def norm_rows_to_T(fw, C, ph, x_ap, xb, nw, dstT, dst_bufs, i, wk):
    junk, ss, ub = wk
    fw.act(junk.t[:, :], x_ap, AF.Square, [xb], [junk.b, ss.b], accum=ss.t[:, 0:1])
    rstd_from_ss(fw, C, ss.t[:, 0:1], ss.t[:, 1:2], 1024.0, [ss.b], [ss.b])
    fw.stt("dve", ub.t[:, :], x_ap, ss.t[:, 1:2], nw.t[:, :], ALU.mult, ALU.mult, [xb, ss.b, nw.b], [ub.b])
    pb = C.pb[i % 2]
    for kc in range(8):
        fw.tr(pb.t[:, kc * 128:(kc + 1) * 128], ub.t[:, kc * 128:(kc + 1) * 128], C.identb.t[:, :], [ub.b, C.identb.b], [pb.b])
    fw.cp("act" if i % 2 else "dve", dstT.t[:, :, i * 128:(i + 1) * 128],
          pb.t[:, :].rearrange("p (k t) -> p k t", k=8), [pb.b], [dst_bufs[i]])


def phase_norm1(fw, C, l):
    with ExitStack() as ph:
        nw = fw.tile(ph, "nw", [128, D], F32)
        fw.dma("sp", nw.t[:, :], C.norm_pre_mix[l].partition_broadcast(128), writes=[nw.b])
        xts = [fw.tile(ph, "xt", [128, D], F32) for _ in range(2)]
        wk = (fw.tile(ph, "junk", [128, D], BF16), fw.tile(ph, "ss", [128, 2], F32), fw.tile(ph, "ub", [128, D], BF16))
        for i in range(NT):
            xt = xts[i % 2]
            src = C.h0 if l == 0 else C.hbuf
            fw.dma("sp", xt.t[:, :], src[i * 128:(i + 1) * 128, :], reads=([] if l == 0 else [C.hb[i]]), writes=[xt.b])
            norm_rows_to_T(fw, C, ph, xt.t[:, :], xt.b, nw, C.uT, C.uTb, i, wk)
        fw.barrier()


def emit_yT(fw, C, yo, yT, mix, i):
    pb = C.pb[1]
    for cb in range(4):
        fw.tr(pb.t[:, cb * 128:(cb + 1) * 128], yo.t[:, cb * 128:(cb + 1) * 128], C.identb.t[:, :], [yo.b, C.identb.b], [pb.b])
    fw.cp("act", yT.t[:, :, :], pb.t[:, 0:512].rearrange("p (c t) -> p c t", c=4), [pb.b], [yT.b])
    fw.dma("sp", C.YT[mix * 4:(mix + 1) * 4, :, i * 128:(i + 1) * 128].rearrange("c p t -> p c t"), yT.t[:, :, :],
           reads=[yT.b], writes=[C.YTb[mix][i]])


def tok_mm(fw, C, ps, Wt, c0, n, i, ncols=None):
    for kc in range(8):
        fw.mm(ps.t[:, 0:n], C.uT.t[:, kc, i * 128:(i + 1) * 128], Wt.t[:, kc, c0:c0 + n], kc == 0, kc == 7,
              [C.uTb[i], Wt.b], [ps.b])


def phase_ret(fw, C, l):
    with ExitStack() as ph:
        WR = fw.tile(ph, "WR", [128, 8, 1536], BF16)
        load_w(fw, C.w_in[l], WR, C_RET, C_RET + 1536, 8)
        gnw = fw.tile(ph, "gnw", [128, 512], F32)
        fw.dma("sp", gnw.t[:, :], C.ret_gn_w[l].partition_broadcast(128), writes=[gnw.b])
        S = fw.tile(ph, "rS", [64, 4, 128], F32)
        Sb = fw.tile(ph, "rSb", [64, 4, 128], BF16)
        fw.memset("dve", S.t[:, :, :], 0.0, [S.b])
        fw.memset("dve", Sb.t[:, :, :], 0.0, [Sb.b])
        qkr_2 = [fw.tile(ph, "qkr", [128, 8, 64], F32) for _ in range(2)]
        t1_2 = [fw.tile(ph, "t1", [128, 8, 32], F32) for _ in range(2)]
        t2_2 = [fw.tile(ph, "t2", [128, 8, 32], F32) for _ in range(2)]
        qb_2 = [fw.tile(ph, "qb", [128, 256], BF16) for _ in range(2)]
        qdb_2 = [fw.tile(ph, "qdb", [128, 256], BF16) for _ in range(2)]
        kb_2 = [fw.tile(ph, "kb", [128, 256], BF16) for _ in range(2)]
        khb_2 = [fw.tile(ph, "khb", [128, 256], BF16) for _ in range(2)]
        qkT_2 = [fw.tile(ph, "qkT", [64, 12, 128], BF16) for _ in range(2)]
        vb_2 = [fw.tile(ph, "vb", [128, 512], BF16) for _ in range(2)]
        sg_2 = [fw.tile(ph, "sg", [128, 512], F32) for _ in range(2)]
        PT_2 = [fw.tile(ph, "PT", [128, 512], BF16) for _ in range(2)]
        ysq_2 = [fw.tile(ph, "ysq", [128, 512], F32) for _ in range(2)]
        st_2 = [fw.tile(ph, "st", [128, 16], F32) for _ in range(2)]
        yn_2 = [fw.tile(ph, "yn", [128, 512], F32) for _ in range(2)]
        yo_2 = [fw.tile(ph, "yo", [128, 512], BF16) for _ in range(2)]
        yT_2 = [fw.tile(ph, "yT", [128, 4, 128], BF16) for _ in range(2)]
        ps_qk, ps_v, ps_g, ps_s, ps_y, ps_st = C.ps[0:6]
        cb_ = C.consts.b
        def stA(i):
                qkr, t1, t2, qb, qdb, kb, khb, qkT, vb, sg, PT, ysq, st, yn, yo, yT = qkr_2[i % 2], t1_2[i % 2], t2_2[i % 2], qb_2[i % 2], qdb_2[i % 2], kb_2[i % 2], khb_2[i % 2], qkT_2[i % 2], vb_2[i % 2], sg_2[i % 2], PT_2[i % 2], ysq_2[i % 2], st_2[i % 2], yn_2[i % 2], yo_2[i % 2], yT_2[i % 2]
                tok_mm(fw, C, ps_qk, WR, 0, 512, i)
                tok_mm(fw, C, ps_v, WR, 512, 512, i)
                tok_mm(fw, C, ps_g, WR, 1024, 512, i)
                qv = ps_qk.t[:, :].rearrange("p (h d) -> p h d", d=64)
                x1, x2 = qv[:, :, 0:32], qv[:, :, 32:64]
                co, cn = CO["cos"][0], CO["sin"][0]
                cosb = C.consts.t[:, co + i * 32:co + (i + 1) * 32].unsqueeze(1).to_broadcast([128, 8, 32])
                sinb = C.consts.t[:, cn + i * 32:cn + (i + 1) * 32].unsqueeze(1).to_broadcast([128, 8, 32])
                fw.tt("dve", t1.t[:, :, :], x1, cosb, ALU.mult, [ps_qk.b, cb_], [t1.b])
                fw.tt("dve", t2.t[:, :, :], x2, sinb, ALU.mult, [ps_qk.b, cb_], [t2.b])
                fw.tt("pool", qkr.t[:, :, 0:32], t1.t[:, :, :], t2.t[:, :, :], ALU.subtract, [t1.b, t2.b], [qkr.b])
                fw.tt("dve", t1.t[:, :, :], x1, sinb, ALU.mult, [ps_qk.b, cb_], [t1.b])
                fw.tt("dve", t2.t[:, :, :], x2, cosb, ALU.mult, [ps_qk.b, cb_], [t2.b])
                fw.tt("pool", qkr.t[:, :, 32:64], t1.t[:, :, :], t2.t[:, :, :], ALU.add, [t1.b, t2.b], [qkr.b])
                qf = qkr.t[:, 0:4, :].rearrange("p h d -> p (h d)")
                kf = qkr.t[:, 4:8, :].rearrange("p h d -> p (h d)")
                fw.cp("act", qb.t[:, :], qf, [qkr.b], [qb.b])
                fw.tt("pool", qdb.t[:, :], qf, cslice(C, "qdec"), ALU.mult, [qkr.b, cb_], [qdb.b])
                fw.cp("act", kb.t[:, :], kf, [qkr.b], [kb.b])
                fw.tt("pool", khb.t[:, :], kf, cslice(C, "kdec"), ALU.mult, [qkr.b, cb_], [khb.b])
                fw.cp("act", vb.t[:, :], ps_v.t[:, :], [ps_v.b], [vb.b])
                fw.act(sg.t[:, :], ps_g.t[:, :], AF.Silu, [ps_g.b], [sg.b])
        def stB(i):
                qkr, t1, t2, qb, qdb, kb, khb, qkT, vb, sg, PT, ysq, st, yn, yo, yT = qkr_2[i % 2], t1_2[i % 2], t2_2[i % 2], qb_2[i % 2], qdb_2[i % 2], kb_2[i % 2], khb_2[i % 2], qkT_2[i % 2], vb_2[i % 2], sg_2[i % 2], PT_2[i % 2], ysq_2[i % 2], st_2[i % 2], yn_2[i % 2], yo_2[i % 2], yT_2[i % 2]
                pb0, pb1 = C.pb
                for h in range(4):
                    fw.tr(pb0.t[0:64, h * 128:(h + 1) * 128], qb.t[:, h * 64:(h + 1) * 64], C.identb.t[:, :], [qb.b, C.identb.b], [pb0.b])
                    fw.tr(pb0.t[0:64, (4 + h) * 128:(5 + h) * 128], qdb.t[:, h * 64:(h + 1) * 64], C.identb.t[:, :], [qdb.b, C.identb.b], [pb0.b])
                    fw.tr(pb1.t[0:64, h * 128:(h + 1) * 128], kb.t[:, h * 64:(h + 1) * 64], C.identb.t[:, :], [kb.b, C.identb.b], [pb1.b])
                fw.cp("dve", qkT.t[:, 0:8, :], pb0.t[0:64, :].rearrange("p (j t) -> p j t", j=8), [pb0.b], [qkT.b])
                fw.cp("act", qkT.t[:, 8:12, :], pb1.t[0:64, 0:512].rearrange("p (j t) -> p j t", j=4), [pb1.b], [qkT.b])
                for h in range(4):
                    fw.mm(ps_s.t[:, h * 128:(h + 1) * 128], qkT.t[:, 8 + h, :], qkT.t[:, h, :], True, True, [qkT.b], [ps_s.b])
                fw.tt("dve", PT.t[:, :], ps_s.t[:, :], cslice(C, "retdt"), ALU.mult, [ps_s.b, cb_], [PT.b])
                for h in range(4):
                    hs = slice(h * 128, (h + 1) * 128)
                    fw.mm(ps_y.t[:, hs], PT.t[:, hs], vb.t[:, hs], True, False, [PT.b, vb.b], [ps_y.b])
                    fw.mm(ps_y.t[:, hs], qkT.t[:, 4 + h, :], Sb.t[:, h, :], False, True, [qkT.b, Sb.b], [ps_y.b])
                for h in range(4):
                    hs = slice(h * 128, (h + 1) * 128)
                    fw.mm(ps_st.t[0:64, hs], khb.t[:, h * 64:(h + 1) * 64], vb.t[:, hs], True, True, [khb.b, vb.b], [ps_st.b])
                for h in range(4):
                    hs = slice(h * 128, (h + 1) * 128)
                    fw.stt("dve", S.t[:, h, :], S.t[:, h, :], float(GAMMA[h] ** 128), ps_st.t[0:64, hs], ALU.mult, ALU.add,
                           [S.b, ps_st.b], [S.b])
                fw.cp("pool", Sb.t[:, :, :], S.t[:, :, :], [S.b], [Sb.b])
                yv = ps_y.t[:, :].rearrange("p (h e) -> p h e", h=4)
                fw.op("dve", lambda g: g.reduce_sum(st.t[:, 0:4], yv, axis=AX.X), [ps_y.b], [st.b])
                fw.act(ysq.t[:, :], ps_y.t[:, :], AF.Square, [ps_y.b], [ysq.b])
                fw.op("dve", lambda g: g.reduce_sum(st.t[:, 4:8], ysq.t[:, :].rearrange("p (h e) -> p h e", h=4), axis=AX.X), [ysq.b], [st.b])
                fw.ts("dve", st.t[:, 8:12], st.t[:, 0:4], 1.0 / 128, None, ALU.mult, None, [st.b], [st.b])
                fw.tt("dve", st.t[:, 0:4], st.t[:, 8:12], st.t[:, 8:12], ALU.mult, [st.b], [st.b])
                fw.stt("dve", st.t[:, 12:16], st.t[:, 4:8], 1.0 / 128, st.t[:, 0:4], ALU.mult, ALU.subtract, [st.b], [st.b])
                fw.ts("dve", st.t[:, 12:16], st.t[:, 12:16], EPS, None, ALU.add, None, [st.b], [st.b])
                fw.act(st.t[:, 12:16], st.t[:, 12:16], AF.Sqrt, [st.b], [st.b])
                fw.op("dve", lambda g: g.reciprocal(st.t[:, 12:16], st.t[:, 12:16]), [st.b], [st.b])
                for h in range(4):
                    hs = slice(h * 128, (h + 1) * 128)
                    fw.ts("dve", yn.t[:, hs], ps_y.t[:, hs], st.t[:, 8 + h:9 + h], st.t[:, 12 + h:13 + h], ALU.subtract, ALU.mult,
                          [ps_y.b, st.b], [yn.b])
                fw.tt("pool", yn.t[:, :], yn.t[:, :], gnw.t[:, :], ALU.mult, [yn.b, gnw.b], [yn.b])
                fw.tt("pool", yo.t[:, :], yn.t[:, :], sg.t[:, :], ALU.mult, [yn.b, sg.b], [yo.b])
                emit_yT(fw, C, yo, yT, 0, i)
        stA(0)
        for i in range(NT):
            if i + 1 < NT:
                stA(i + 1)
            stB(i)
        fw.barrier()


def phase_ssd(fw, C, l):
    with ExitStack() as ph:
        WS = fw.tile(ph, "WS", [128, 8, 1544], BF16)
        load_w(fw, C.w_in[l], WS, C_SSD, C_SSD + 1544, 8)
        cw = fw.tile(ph, "cw", [128, 8, 4], F32)
        cbi = fw.tile(ph, "cbi", [128, 8], F32)
        dtb = fw.tile(ph, "dtb", [128, 8], F32)
        arow = fw.tile(ph, "arow", [128, 8], F32)
        dsk = fw.tile(ph, "dsk", [128, 512], F32)
        nw = fw.tile(ph, "snw", [128, 512], F32)
        fw.dma("sp", cw.t[:, :, :], C.ssd_cw[l], writes=[cw.b])
        fw.dma("sp", cbi.t[:, :], C.ssd_cb[l], writes=[cbi.b])
        fw.dma("sp", dtb.t[:, :], C.ssd_dt_bias[l].partition_broadcast(128), writes=[dtb.b])
        fw.dma("sp", arow.t[:, :], C.ssd_a_log[l].partition_broadcast(128), writes=[arow.b])
        fw.dma("sp", dsk.t[:, :], C.ssd_dsk[l].partition_broadcast(128), writes=[dsk.b])
        fw.dma("sp", nw.t[:, :], C.ssd_norm_w[l].partition_broadcast(128), writes=[nw.b])
        fw.act(arow.t[:, :], arow.t[:, :], AF.Exp, [arow.b], [arow.b])
        fw.ts("dve", arow.t[:, :], arow.t[:, :], -1.0, None, ALU.mult, None, [arow.b], [arow.b])
        S = fw.tile(ph, "sS", [128, 512], F32)
        Sb = fw.tile(ph, "sSb", [128, 512], BF16)
        fw.memset("dve", S.t[:, :], 0.0, [S.b])
        fw.memset("dve", Sb.t[:, :], 0.0, [Sb.b])
        xraw = fw.tile(ph, "xraw", [128, 8, 131], F32)
        fw.memset("pool", xraw.t[:, :, :], 0.0, [xraw.b])
        xc_2 = [fw.tile(ph, "xc", [128, 8, 128], F32) for _ in range(2)]
        xcb_2 = [fw.tile(ph, "xcb", [128, 8, 128], BF16) for _ in range(2)]
        xs_2 = [fw.tile(ph, "xs", [128, 512], F32) for _ in range(2)]
        Btok_2 = [fw.tile(ph, "Btok", [128, 2, 128], BF16) for _ in range(2)]
        sz_2 = [fw.tile(ph, "sz", [128, 512], F32) for _ in range(2)]
        sm_2 = [fw.tile(ph, "sm", [128, 104], F32) for _ in range(2)]
        laB = [fw.tile(ph, "laB", [128, 128], F32) for _ in range(2)]
        decT_2 = [fw.tile(ph, "decT", [128, 8, 128], F32) for _ in range(2)]
        PT_2 = [fw.tile(ph, "sPT", [128, 8, 128], BF16) for _ in range(2)]
        xdt_2 = [fw.tile(ph, "xdt", [128, 512], BF16) for _ in range(2)]
        xdtw_2 = [fw.tile(ph, "xdtw", [128, 512], BF16) for _ in range(2)]
        ya_2 = [fw.tile(ph, "ya", [128, 512], F32) for _ in range(2)]
        tmp_2 = [fw.tile(ph, "stmp", [128, 512], F32) for _ in range(2)]
        yo_2 = [fw.tile(ph, "syo", [128, 512], BF16) for _ in range(2)]
        yT_2 = [fw.tile(ph, "syT", [128, 4, 128], BF16) for _ in range(2)]
        ps = C.ps
        pb0, pb1 = C.pb
        cb_ = C.consts.b
        XD, AXc, EX, LN, DT, LA, CUM, NCUM, ECUM, WW, EDEC, DW = [slice(8 * j, 8 * j + 8) for j in range(12)]
        triu = cslice(C, "triu")
        ones = cslice(C, "ones")
        def stA(i):
                xc, xcb, xs, Btok, sz, sm, decT, PT, xdt, xdtw, ya, tmp, yo, yT = xc_2[i % 2], xcb_2[i % 2], xs_2[i % 2], Btok_2[i % 2], sz_2[i % 2], sm_2[i % 2], decT_2[i % 2], PT_2[i % 2], xdt_2[i % 2], xdtw_2[i % 2], ya_2[i % 2], tmp_2[i % 2], yo_2[i % 2], yT_2[i % 2]
                tok = slice(i * 128, (i + 1) * 128)
                for cb in range(8):
                    pst = ps[0] if cb < 4 else ps[1]
                    for kc in range(8):
                        fw.mm(pst.t[:, (cb % 4) * 128:(cb % 4 + 1) * 128], WS.t[:, kc, 512 + cb * 128:512 + (cb + 1) * 128],
                              C.uT.t[:, kc, tok], kc == 0, kc == 7, [WS.b, C.uTb[i]], [pst.b])
                if i > 0:
                    fw.cp("pool", xraw.t[:, :, 0:3], xraw.t[:, :, 128:131], [xraw.b], [xraw.b])
                fw.cp("act", xraw.t[:, 0:4, 3:131], ps[0].t[:, :].rearrange("p (c t) -> p c t", c=4), [ps[0].b], [xraw.b])
                fw.cp("dve", xraw.t[:, 4:8, 3:131], ps[1].t[:, :].rearrange("p (c t) -> p c t", c=4), [ps[1].b], [xraw.b])
                for cb in range(8):
                    e = "dve" if cb % 2 == 0 else "pool"
                    fw.ts(e, xc.t[:, cb, :], xraw.t[:, cb, 3:131], cw.t[:, cb, 3:4], cbi.t[:, cb:cb + 1], ALU.mult, ALU.add,
                          [xraw.b, cw.b, cbi.b], [xc.b])
                    for j in (2, 1, 0):
                        fw.stt(e, xc.t[:, cb, :], xraw.t[:, cb, j:j + 128], cw.t[:, cb, j:j + 1], xc.t[:, cb, :], ALU.mult, ALU.add,
                               [xraw.b, cw.b, xc.b], [xc.b])
                fw.act(xcb.t[:, :, :], xc.t[:, :, :], AF.Silu, [xc.b], [xcb.b])
                tok_mm(fw, C, ps[2], WS, 0, 512, i)
                fw.act(sz.t[:, :], ps[2].t[:, :], AF.Silu, [ps[2].b], [sz.b])
                for kc in range(8):
                    fw.mm(ps[3].t[:, 0:8], C.uT.t[:, kc, tok], WS.t[:, kc, 1536:1544], kc == 0, kc == 7, [C.uTb[i], WS.b], [ps[3].b])
                smb = [sm.b]
                fw.tt("dve", sm.t[:, XD], ps[3].t[:, 0:8], dtb.t[:, :], ALU.add, [ps[3].b, dtb.b], smb)
        def stB(i):
                xc, xcb, xs, Btok, sz, sm, decT, PT, xdt, xdtw, ya, tmp, yo, yT = xc_2[i % 2], xcb_2[i % 2], xs_2[i % 2], Btok_2[i % 2], sz_2[i % 2], sm_2[i % 2], decT_2[i % 2], PT_2[i % 2], xdt_2[i % 2], xdtw_2[i % 2], ya_2[i % 2], tmp_2[i % 2], yo_2[i % 2], yT_2[i % 2]
                smb = [sm.b]
                pb0, pb1 = C.pb
                for cb in range(4):
                    fw.tr(pb0.t[:, cb * 128:(cb + 1) * 128], xcb.t[:, cb, :], C.identb.t[:, :], [xcb.b, C.identb.b], [pb0.b])
                fw.cp("dve", xs.t[:, :], pb0.t[:, 0:512], [pb0.b], [xs.b])
                for g in range(2):
                    fw.tr(pb1.t[:, g * 128:(g + 1) * 128], xcb.t[:, 4 + g, :], C.identb.t[:, :], [xcb.b, C.identb.b], [pb1.b])
                fw.cp("act", Btok.t[:, :, :], pb1.t[:, 0:256].rearrange("p (g n) -> p g n", g=2), [pb1.b], [Btok.b])
                fw.ts("dve", sm.t[:, AXc], sm.t[:, XD], -1.0, None, ALU.mult, None, smb, smb)
                fw.tt("dve", sm.t[:, AXc], sm.t[:, AXc], sm.t[:, XD], ALU.max, smb, smb)
                fw.act(sm.t[:, EX], sm.t[:, AXc], AF.Exp, smb, smb, scale=-1.0)
                fw.ts("dve", sm.t[:, EX], sm.t[:, EX], 1.0, None, ALU.add, None, smb, smb)
                fw.act(sm.t[:, LN], sm.t[:, EX], AF.Ln, smb, smb)
                fw.stt("dve", sm.t[:, DT], sm.t[:, XD], 0.0, sm.t[:, LN], ALU.max, ALU.add, smb, smb)
                fw.tt("dve", sm.t[:, LA], sm.t[:, DT], arow.t[:, :], ALU.mult, smb + [arow.b], smb)
                fw.mm(ps[3].t[:, 16:24], triu, sm.t[:, LA], True, True, [cb_, sm.b], [ps[3].b])
                fw.mm(ps[3].t[:, 32:40], ones, sm.t[:, LA], True, True, [cb_, sm.b], [ps[3].b])
                fw.cp("dve", sm.t[:, CUM], ps[3].t[:, 16:24], [ps[3].b], smb)
                fw.ts("dve", sm.t[:, NCUM], sm.t[:, CUM], -1.0, None, ALU.mult, None, smb, smb)
                fw.act(sm.t[:, ECUM], sm.t[:, CUM], AF.Exp, smb, smb)
                fw.tt("dve", sm.t[:, WW], ps[3].t[:, 32:40], sm.t[:, CUM], ALU.subtract, [ps[3].b] + smb, smb)
                fw.act(sm.t[:, WW], sm.t[:, WW], AF.Exp, smb, smb)
                fw.act(sm.t[:, EDEC], ps[3].t[:, 32:40], AF.Exp, [ps[3].b], smb)
                fw.tt("dve", sm.t[:, DW], sm.t[:, DT], sm.t[:, WW], ALU.mult, smb, smb)
                for h in range(8):
                    lb_ = laB[h % 2]
                    pst = ps[4] if h < 4 else ps[5]
                    fw.ts("dve" if h % 2 == 0 else "pool", lb_.t[:, :], ones, sm.t[:, 40 + h:41 + h], None, ALU.mult, None, [cb_, sm.b], [lb_.b])
                    hs = slice((h % 4) * 128, (h % 4 + 1) * 128)
                    fw.mm(pst.t[:, hs], lb_.t[:, :], triu, True, False, [lb_.b, cb_], [pst.b])
                    fw.mm(pst.t[:, hs], C.identf.t[:, :], cslice(C, "negm"), False, True, [C.identf.b, cb_], [pst.b])
                    fw.act(decT.t[:, h, :], pst.t[:, hs], AF.Exp, [pst.b, sm.b], [decT.b], bias=sm.t[:, 56 + h:57 + h])
                for g in range(2):
                    fw.mm(ps[0].t[:, g * 128:(g + 1) * 128], xcb.t[:, 4 + g, :], xcb.t[:, 6 + g, :], True, True, [xcb.b], [ps[0].b])
                for g in range(2):
                    fw.tt("dve", PT.t[:, 4 * g:4 * g + 4, :], decT.t[:, 4 * g:4 * g + 4, :],
                          ps[0].t[:, g * 128:(g + 1) * 128].unsqueeze(1).to_broadcast([128, 4, 128]), ALU.mult, [decT.b, ps[0].b], [PT.b])
                xsv = xs.t[:, :].rearrange("p (h e) -> p h e", h=8)
                fw.tt("pool", xdt.t[:, :].rearrange("p (h e) -> p h e", h=8), xsv, sm.t[:, DT].unsqueeze(2).to_broadcast([128, 8, 64]),
                      ALU.mult, [xs.b, sm.b], [xdt.b])
                fw.tt("pool", xdtw.t[:, :].rearrange("p (h e) -> p h e", h=8), xsv, sm.t[:, DW].unsqueeze(2).to_broadcast([128, 8, 64]),
                      ALU.mult, [xs.b, sm.b], [xdtw.b])
                for h in range(8):
                    fw.mm(ps[1].t[:, h * 64:(h + 1) * 64], PT.t[:, h, :], xdt.t[:, h * 64:(h + 1) * 64], True, True, [PT.b, xdt.b], [ps[1].b])
                for g in range(2):
                    fw.mm(ps[2].t[:, g * 256:(g + 1) * 256], xcb.t[:, 6 + g, :], Sb.t[:, g * 256:(g + 1) * 256], True, True, [xcb.b, Sb.b], [ps[2].b])
                fw.tt("dve", ya.t[:, :].rearrange("p (h e) -> p h e", h=8), ps[2].t[:, :].rearrange("p (h e) -> p h e", h=8),
                      sm.t[:, ECUM].unsqueeze(2).to_broadcast([128, 8, 64]), ALU.mult, [ps[2].b, sm.b], [ya.b])
                fw.tt("dve", ya.t[:, :], ya.t[:, :], ps[1].t[:, :], ALU.add, [ya.b, ps[1].b], [ya.b])
                fw.tt("pool", tmp.t[:, :], xs.t[:, :], dsk.t[:, :], ALU.mult, [xs.b, dsk.b], [tmp.b])
                fw.tt("pool", ya.t[:, :], ya.t[:, :], tmp.t[:, :], ALU.add, [ya.b, tmp.b], [ya.b])
                fw.tt("pool", ya.t[:, :], ya.t[:, :], sz.t[:, :], ALU.mult, [ya.b, sz.b], [ya.b])
                for g in range(2):
                    fw.mm(ps[4].t[:, g * 256:(g + 1) * 256], Btok.t[:, g, :], xdtw.t[:, g * 256:(g + 1) * 256], True, True,
                          [Btok.b, xdtw.b], [ps[4].b])
                for h in range(8):
                    hs = slice(h * 64, (h + 1) * 64)
                    fw.stt("dve", S.t[:, hs], S.t[:, hs], sm.t[:, 80 + h:81 + h], ps[4].t[:, hs], ALU.mult, ALU.add, [S.b, sm.b, ps[4].b], [S.b])
                fw.cp("pool", Sb.t[:, :], S.t[:, :], [S.b], [Sb.b])
                fw.act(tmp.t[:, :], ya.t[:, :], AF.Square, [ya.b], [tmp.b])
                fw.op("dve", lambda g_: g_.reduce_sum(sm.t[:, 96:98], tmp.t[:, :].rearrange("p (g e) -> p g e", g=2), axis=AX.X), [tmp.b], smb)
                rstd_from_ss(fw, C, sm.t[:, 96:98], sm.t[:, 98:100], 256.0, smb, smb)
                for g in range(2):
                    gs = slice(g * 256, (g + 1) * 256)
                    fw.ts("dve", tmp.t[:, gs], ya.t[:, gs], sm.t[:, 98 + g:99 + g], None, ALU.mult, None, [ya.b, sm.b], [tmp.b])
                fw.tt("pool", yo.t[:, :], tmp.t[:, :], nw.t[:, :], ALU.mult, [tmp.b, nw.b], [yo.b])
                emit_yT(fw, C, yo, yT, 2, i)
        stA(0)
        for i in range(NT):
            if i + 1 < NT:
                stA(i + 1)
            stB(i)
        fw.barrier()


def phase_hg(fw, C, l):
    with ExitStack() as ph:
        WH = fw.tile(ph, "WH", [128, 8, 2048], BF16)
        load_w(fw, C.w_in[l], WH, C_HG, C_HG + 2048, 8)
        nw = fw.tile(ph, "hnw", [128, 512], F32)
        fw.dma("sp", nw.t[:, :], C.hgrn_norm_w[l].partition_broadcast(128), writes=[nw.b])
        S = fw.tile(ph, "hS", [128, 4, 128], F32)
        Sb = fw.tile(ph, "hSb", [128, 4, 128], BF16)
        PT_2 = [fw.tile(ph, "hPT", [128, 4, 128], BF16) for _ in range(2)]
        fw.memset("dve", S.t[:, :, :], 0.0, [S.b])
        fw.memset("dve", Sb.t[:, :, :], 0.0, [Sb.b])
        for PT in PT_2:
            fw.memset("pool", PT.t[:, :, :], 0.0, [PT.b])
        qT_2 = [fw.tile(ph, "hq", [128, 4, 128], F32) for _ in range(2)]
        fT_2 = [fw.tile(ph, "hf", [128, 4, 128], F32) for _ in range(2)]
        la_2 = [fw.tile(ph, "hla", [128, 4, 128], F32) for _ in range(2)]
        kT_2 = [fw.tile(ph, "hk", [128, 4, 128], F32) for _ in range(2)]
        cum_2 = [fw.tile(ph, "hcum", [128, 4, 128], F32) for _ in range(2)]
        ncb_2 = [fw.tile(ph, "hncb", [128, 4, 4], F32) for _ in range(2)]
        eq_2 = [fw.tile(ph, "heq", [128, 4, 128], F32) for _ in range(2)]
        qd_2 = [fw.tile(ph, "hqd", [128, 4, 128], BF16) for _ in range(2)]
        qst_2 = [fw.tile(ph, "hqst", [128, 4, 128], BF16) for _ in range(2)]
        ek_2 = [fw.tile(ph, "hek", [128, 4, 128], F32) for _ in range(2)]
        Kt_2 = [fw.tile(ph, "hKt", [128, 4, 4, 128], BF16) for _ in range(2)]
        khT_2 = [fw.tile(ph, "hkhT", [128, 4, 128], BF16) for _ in range(2)]
        khat_2 = [fw.tile(ph, "hkhat", [128, 4, 128], BF16) for _ in range(2)]
        dec_2 = [fw.tile(ph, "hdec", [128, 4], F32) for _ in range(2)]
        vb_2 = [fw.tile(ph, "hvb", [128, 512], BF16) for _ in range(2)]
        sgate_2 = [fw.tile(ph, "hsg", [128, 512], F32) for _ in range(2)]
        ysq_2 = [fw.tile(ph, "hysq", [128, 512], F32) for _ in range(2)]
        st_2 = [fw.tile(ph, "hst", [128, 8], F32) for _ in range(2)]
        yn_2 = [fw.tile(ph, "hyn", [128, 512], F32) for _ in range(2)]
        yo_2 = [fw.tile(ph, "hyo", [128, 512], BF16) for _ in range(2)]
        yT_2 = [fw.tile(ph, "hyT", [128, 4, 128], BF16) for _ in range(2)]
        ps = C.ps
        pb0, pb1 = C.pb
        cb_ = C.consts.b
        lbc = C.lb_all.t[:, l, :]
        omc = C.oml_all.t[:, l, :]
        ones = cslice(C, "ones")
        triu = cslice(C, "triu")
        def stA(i):
                PT = PT_2[i % 2]
                qT, fT, la, kT, cum, ncb, eq, qd, qst, ek, Kt, khT, khat, dec, vb, sgate, ysq, st, yn, yo, yT = qT_2[i % 2], fT_2[i % 2], la_2[i % 2], kT_2[i % 2], cum_2[i % 2], ncb_2[i % 2], eq_2[i % 2], qd_2[i % 2], qst_2[i % 2], ek_2[i % 2], Kt_2[i % 2], khT_2[i % 2], khat_2[i % 2], dec_2[i % 2], vb_2[i % 2], sgate_2[i % 2], ysq_2[i % 2], st_2[i % 2], yn_2[i % 2], yo_2[i % 2], yT_2[i % 2]
                tok = slice(i * 128, (i + 1) * 128)
                for h in range(4):
                    hs = slice(h * 128, (h + 1) * 128)
                    for kc in range(8):
                        fw.mm(ps[0].t[:, hs], WH.t[:, kc, h * 128:(h + 1) * 128], C.uT.t[:, kc, tok], kc == 0, kc == 7, [WH.b, C.uTb[i]], [ps[0].b])
                    for kc in range(8):
                        fw.mm(ps[1].t[:, hs], WH.t[:, kc, 512 + h * 128:512 + (h + 1) * 128], C.uT.t[:, kc, tok], kc == 0, kc == 7,
                              [WH.b, C.uTb[i]], [ps[1].b])
                tok_mm(fw, C, ps[2], WH, 1024, 512, i)
                tok_mm(fw, C, ps[3], WH, 1536, 512, i)
                fw.act(qT.t[:, :, :], ps[0].t[:, :].rearrange("p (h t) -> p h t", h=4), AF.Silu, [ps[0].b], [qT.b])
                fw.act(fT.t[:, :, :], ps[1].t[:, :].rearrange("p (h t) -> p h t", h=4), AF.Sigmoid, [ps[1].b], [fT.b])
                fw.cp("act", vb.t[:, :], ps[2].t[:, :], [ps[2].b], [vb.b])
                fw.act(sgate.t[:, :], ps[3].t[:, :], AF.Silu, [ps[3].b], [sgate.b])
        def stB(i):
                PT = PT_2[i % 2]
                qT, fT, la, kT, cum, ncb, eq, qd, qst, ek, Kt, khT, khat, dec, vb, sgate, ysq, st, yn, yo, yT = qT_2[i % 2], fT_2[i % 2], la_2[i % 2], kT_2[i % 2], cum_2[i % 2], ncb_2[i % 2], eq_2[i % 2], qd_2[i % 2], qst_2[i % 2], ek_2[i % 2], Kt_2[i % 2], khT_2[i % 2], khat_2[i % 2], dec_2[i % 2], vb_2[i % 2], sgate_2[i % 2], ysq_2[i % 2], st_2[i % 2], yn_2[i % 2], yo_2[i % 2], yT_2[i % 2]
                for h in range(4):
                    fw.ts("dve", fT.t[:, h, :], fT.t[:, h, :], omc[:, h:h + 1], lbc[:, h:h + 1], ALU.mult, ALU.add,
                          [fT.b, C.lb_all.b, C.oml_all.b], [fT.b])
                fw.act(la.t[:, :, :], fT.t[:, :, :], AF.Ln, [fT.b], [la.b])
                fw.ts("pool", kT.t[:, :, :], fT.t[:, :, :], -1.0, 1.0, ALU.mult, ALU.add, [fT.b], [kT.b])
                for h in range(4):
                    fw.op("dve", lambda g: g.tensor_tensor_scan(cum.t[:, h, :], ones, la.t[:, h, :], 0.0, ALU.mult, ALU.add),
                          [la.b, cb_], [cum.b])
                cv = cum.t[:, :, :].rearrange("p h (b c) -> p h b c", c=32)
                fw.ts("dve", ncb.t[:, :, :], cv[:, :, :, 31], -1.0, None, ALU.mult, None, [cum.b], [ncb.b])
                for h in range(4):
                    for b in range(4):
                        bs = slice(32 * b, 32 * b + 32)
                        if b == 0:
                            fw.act(eq.t[:, h, bs], cum.t[:, h, bs], AF.Exp, [cum.b], [eq.b])
                        else:
                            fw.act(eq.t[:, h, bs], cum.t[:, h, bs], AF.Exp, [cum.b, ncb.b], [eq.b], bias=ncb.t[:, h, b - 1:b])
                fw.tt("pool", qd.t[:, :, :], qT.t[:, :, :], eq.t[:, :, :], ALU.mult, [qT.b, eq.b], [qd.b])
                fw.act(eq.t[:, :, :], cum.t[:, :, :], AF.Exp, [cum.b], [eq.b])
                fw.tt("pool", qst.t[:, :, :], qT.t[:, :, :], eq.t[:, :, :], ALU.mult, [qT.b, eq.b], [qst.b])
                for h in range(4):
                    for b in range(4):
                        W_ = 32 * (b + 1)
                        if b == 0:
                            fw.act(ek.t[:, b, 0:W_], cum.t[:, h, 0:W_], AF.Exp, [cum.b], [ek.b], scale=-1.0)
                        else:
                            fw.act(ek.t[:, b, 0:W_], cum.t[:, h, 0:W_], AF.Exp, [cum.b], [ek.b], scale=-1.0,
                                   bias=cum.t[:, h, 32 * b - 1:32 * b])
                    for b in range(4):
                        W_ = 32 * (b + 1)
                        fw.tt("dve" if b % 2 else "pool", Kt.t[:, h, b, 0:W_], ek.t[:, b, 0:W_], kT.t[:, h, 0:W_], ALU.mult,
                              [ek.b, kT.b], [Kt.b])
                for h in range(4):
                    fw.act(ek.t[:, h, :], cum.t[:, h, :], AF.Exp, [cum.b], [ek.b], scale=-1.0, bias=cum.t[:, h, 127:128])
                fw.tt("pool", khT.t[:, :, :], ek.t[:, :, :], kT.t[:, :, :], ALU.mult, [ek.b, kT.b], [khT.b])
                for h in range(4):
                    fw.tr(pb0.t[:, h * 128:(h + 1) * 128], khT.t[:, h, :], C.identb.t[:, :], [khT.b, C.identb.b], [pb0.b])
                fw.cp("act", khat.t[:, :, :], pb0.t[:, 0:512].rearrange("p (h d) -> p h d", h=4), [pb0.b], [khat.b])
                fw.act(dec.t[:, :], cum.t[:, :, 127], AF.Exp, [cum.b], [dec.b])
                for h in range(4):
                    for b in range(4):
                        W_ = 32 * (b + 1)
                        fw.mm(ps[4].t[0:W_, h * 128 + 32 * b:h * 128 + 32 * b + 32], Kt.t[:, h, b, 0:W_], qd.t[:, h, 32 * b:32 * b + 32],
                              True, True, [Kt.b, qd.b], [ps[4].b])
                psv = ps[4].t[:, :].rearrange("p (h t) -> p h t", h=4)
                for b in range(4):
                    W_ = 32 * (b + 1)
                    bs = slice(32 * b, 32 * b + 32)
                    to, tn = CO["triu"]
                    mk = C.consts.t[0:W_, to + 32 * b:to + 32 * b + 32].unsqueeze(1).to_broadcast([W_, 4, 32])
                    fw.tt("dve", PT.t[0:W_, :, bs], psv[0:W_, :, bs], mk, ALU.mult, [ps[4].b, cb_], [PT.b])
                for h in range(4):
                    hs = slice(h * 128, (h + 1) * 128)
                    fw.mm(ps[5].t[:, hs], PT.t[:, h, :], vb.t[:, hs], True, False, [PT.b, vb.b], [ps[5].b])
                    fw.mm(ps[5].t[:, hs], qst.t[:, h, :], Sb.t[:, h, :], False, True, [qst.b, Sb.b], [ps[5].b])
                for h in range(4):
                    hs = slice(h * 128, (h + 1) * 128)
                    fw.mm(ps[0].t[:, hs], khat.t[:, h, :], vb.t[:, hs], True, True, [khat.b, vb.b], [ps[0].b])
                for h in range(4):
                    hs = slice(h * 128, (h + 1) * 128)
                    fw.stt("dve", S.t[:, h, :], S.t[:, h, :], dec.t[:, h:h + 1], ps[0].t[:, hs], ALU.mult, ALU.add, [S.b, dec.b, ps[0].b], [S.b])
                fw.cp("pool", Sb.t[:, :, :], S.t[:, :, :], [S.b], [Sb.b])
                fw.act(ysq.t[:, :], ps[5].t[:, :], AF.Square, [ps[5].b], [ysq.b])
                fw.op("dve", lambda g: g.reduce_sum(st.t[:, 0:4], ysq.t[:, :].rearrange("p (h e) -> p h e", h=4), axis=AX.X), [ysq.b], [st.b])
                rstd_from_ss(fw, C, st.t[:, 0:4], st.t[:, 4:8], 128.0, [st.b], [st.b])
                for h in range(4):
                    hs = slice(h * 128, (h + 1) * 128)
                    fw.ts("dve", yn.t[:, hs], ps[5].t[:, hs], st.t[:, 4 + h:5 + h], None, ALU.mult, None, [ps[5].b, st.b], [yn.b])
                fw.tt("pool", yn.t[:, :], yn.t[:, :], nw.t[:, :], ALU.mult, [yn.b, nw.b], [yn.b])
                fw.tt("pool", yo.t[:, :], yn.t[:, :], sgate.t[:, :], ALU.mult, [yn.b, sgate.b], [yo.b])
                emit_yT(fw, C, yo, yT, 3, i)
        stA(0)
        for i in range(NT):
            if i + 1 < NT:
                stA(i + 1)
            stB(i)
        fw.barrier()


TGS = [(0, 512), (512, 512), (1024, 512), (1536, 512), (2048, 128)]


def phase_s5(fw, C, l):
    ps = C.ps
    pb0, pb1 = C.pb
    cb_ = C.consts.b
    with ExitStack() as ph:
        GW = fw.tile(ph, "GW", [128, 4, 1024], BF16)
        load_w(fw, C.s5_glu_w[l], GW, 0, 1024, 4)
        Cre = fw.tile(ph, "Cre", [128, 16, 128], BF16)
        nCim = fw.tile(ph, "nCim", [128, 16, 128], BF16)
        fw.dma("pool", Cre.t[:, :, :], C.s5_cre[l], writes=[Cre.b])
        fw.dma("pool", nCim.t[:, :, :], C.s5_cim[l], writes=[nCim.b])
        fw.ts("pool", nCim.t[:, :, :], nCim.t[:, :, :], -1.0, None, ALU.mult, None, [nCim.b], [nCim.b])
        d5 = fw.tile(ph, "d5", [128, 4], F32)
        gb = fw.tile(ph, "gb5", [128, 8], F32)
        fw.dma("sp", d5.t[:, :], C.s5_dT[l], writes=[d5.b])
        fw.dma("sp", gb.t[:, :], C.s5_gbT[l], writes=[gb.b])
        sp_ = fw.tile(ph, "s5sm", [128, 48], F32)
        fw.dma("sp", sp_.t[:, :], C.s5_small[l], writes=[sp_.b])
        u5T = fw.tile(ph, "u5T", [128, 4, TP], BF16)
        Sall = fw.tile(ph, "Sall", [128, 137, 3, 16], F32)
        KT = fw.tile(ph, "KT", [128, 4, 16, 128], BF16)
        PW = fw.tile(ph, "PW", [128, 2, 17, 16], F32)
        wk = fw.tile(ph, "s5wk", [128, 12, 16], F32)
        with ExitStack() as pa:
            W5 = fw.tile(pa, "W5", [128, 8, 512], BF16)
            load_w(fw, C.w_in[l], W5, C_S5, C_S5 + 512, 8)
            k = 0
            for ct in range(4):
                for (t0, n) in TGS:
                    pst = ps[k % 2]
                    for kc in range(8):
                        fw.mm(pst.t[:, 0:n], W5.t[:, kc, ct * 128:(ct + 1) * 128], C.uT.t[:, kc, t0:t0 + n], kc == 0, kc == 7,
                              [W5.b] + C.uTb[t0 // 128:(t0 + n) // 128], [pst.b])
                    fw.cp("act" if k % 2 else "dve", u5T.t[:, ct, t0:t0 + n], pst.t[:, 0:n], [pst.b], [u5T.b])
                    k += 1
            fw.barrier()
        if os.environ.get("S5_STOP") == "A":
            fw.barrier(); return
        lr, li, lst = sp_.t[:, 0:16], sp_.t[:, 16:32], sp_.t[:, 32:48]
        W_ = [wk.t[:, j, :] for j in range(12)]
        R = [sp_.b, wk.b, PW.b]
        step, lrs, ang, em1, re_, im_, t_a, t_b, inv, co_re, co_im, rr = W_
        big = [fw.tile(ph, "s5big", [128, 18, 16], F32) for _ in range(5)]
        bigi = fw.tile(ph, "s5bigi", [128, 18, 16], mybir.dt.int32)
        RB = R + [b_.b for b_ in big] + [bigi.b, cb_]
        FACT = [1.0, 1.0, 2.0, 6.0, 24.0, 120.0, 720.0, 5040.0, 40320.0, 362880.0, 3628800.0]

        def horner_exp(out, r, deg, minus1=False):
            fw.ts("dve", out, r, 1.0 / FACT[deg], None, ALU.mult, None, RB, RB)
            for j in range(deg - 1, 0, -1):
                fw.stt("dve", out, out, 1.0 / FACT[j], r, ALU.add, ALU.mult, RB, RB)
            if not minus1:
                fw.ts("dve", out, out, 1.0, None, ALU.add, None, RB, RB)

        fw.ts("dve", rr, lst, 0.125, None, ALU.mult, None, RB, RB)
        horner_exp(step, rr, 10)
        for _ in range(3):
            fw.tt("dve", step, step, step, ALU.mult, RB, RB)
        fw.tt("dve", lrs, lr, step, ALU.mult, RB, RB)
        fw.tt("dve", ang, li, step, ALU.mult, RB, RB)
        mo = CO["mtab"][0]
        mtab = C.consts.t[:, mo:mo + 288].rearrange("p (m q) -> p m q", m=18)
        TH, XM, MAG, SN, CS = [b_.t[:, :, :] for b_ in big]
        fw.tt("dve", TH, mtab, ang.unsqueeze(1).to_broadcast([128, 18, 16]), ALU.mult, RB, RB)
        fw.tt("dve", XM, mtab, lrs.unsqueeze(1).to_broadcast([128, 18, 16]), ALU.mult, RB, RB)
        horner_exp(MAG, XM, 10)
        C1, C2 = 6.28125, 2.0 * math.pi - 6.28125

        def sin_reduced(out, th):
            fw.ts("dve", out, th, 1.0 / (2.0 * math.pi), None, ALU.mult, None, RB, RB)
            fw.cp("dve", bigi.t[:, :, :], out, RB, RB)
            fw.cp("dve", XM, bigi.t[:, :, :], RB, RB)
            fw.stt("dve", out, XM, -C1, th, ALU.mult, ALU.add, RB, RB)
            fw.stt("dve", out, XM, -C2, out, ALU.mult, ALU.add, RB, RB)
            fw.act(out, out, AF.Sin, RB, RB)

        sin_reduced(SN, TH)
        fw.ts("dve", TH, TH, math.pi / 2, None, ALU.add, None, RB, RB)
        sin_reduced(CS, TH)
        fw.tt("dve", PW.t[:, 0, :, :], MAG[:, 0:17, :], CS[:, 0:17, :], ALU.mult, RB, RB)
        fw.tt("dve", PW.t[:, 1, :, :], MAG[:, 0:17, :], SN[:, 0:17, :], ALU.mult, RB, RB)
        horner_exp(em1, lrs, 7, minus1=True)
        fw.tt("dve", re_, em1, CS[:, 1, :], ALU.mult, RB, RB)
        fw.tt("dve", t_a, SN[:, 17, :], SN[:, 17, :], ALU.mult, RB, RB)
        fw.stt("dve", re_, t_a, -2.0, re_, ALU.mult, ALU.add, RB, RB)
        fw.ts("dve", t_b, em1, 1.0, None, ALU.add, None, RB, RB)
        fw.tt("dve", im_, t_b, SN[:, 1, :], ALU.mult, RB, RB)
        fw.tt("dve", t_a, lr, lr, ALU.mult, RB, RB)
        fw.tt("dve", t_b, li, li, ALU.mult, RB, RB)
        fw.tt("dve", inv, t_a, t_b, ALU.add, RB, RB)
        fw.op("dve", lambda g: g.reciprocal(inv, inv), RB, RB)
        fw.tt("dve", t_a, re_, lr, ALU.mult, RB, RB)
        fw.tt("dve", t_b, im_, li, ALU.mult, RB, RB)
        fw.tt("dve", t_a, t_a, t_b, ALU.add, RB, RB)
        fw.tt("dve", co_re, t_a, inv, ALU.mult, RB, RB)
        fw.tt("dve", t_a, im_, lr, ALU.mult, RB, RB)
        fw.tt("dve", t_b, re_, li, ALU.mult, RB, RB)
        fw.tt("dve", t_a, t_a, t_b, ALU.subtract, RB, RB)
        fw.tt("dve", co_im, t_a, inv, ALU.mult, RB, RB)
        if C.debug and l == 0:
            fw.dma("sp", C.dbg5[:, 0:192], wk.t[:, :, :].rearrange("p a b -> p (a b)"), reads=[wk.b], writes=[Buf()])
            fw.dma("sp", C.dbg5[:, 192:736], PW.t[:, :, :, :].rearrange("p a m q -> p (a m q)"), reads=[PW.b], writes=[Buf()])
        if os.environ.get("S5_STOP") == "B":
            fw.barrier(); return
        fw.memset("pool", Sall.t[:, 0, :, :], 0.0, [Sall.b])
        with ExitStack() as pd:
            Bst = fw.tile(pd, "Bst", [128, 2, 4, 128], F32)
            Bb = fw.tile(pd, "Bb", [128, 2, 4, 128], F32)
            t0_ = fw.tile(pd, "tB", [128, 128], F32)
            tA = [fw.tile(pd, "tA", [128, 16, 128], F32)] * 2
            tB = [fw.tile(pd, "tBB", [128, 16, 128], F32)] * 2
            Xs = [fw.tile(pd, "X", [128, 2, 16, 128], BF16) for _ in range(2)]
            XTs = [fw.tile(pd, "XT", [128, 4, 2, 128], BF16) for _ in range(2)]
            psK = ps[2:6]
            zt = fw.tile(pd, "zt", [128, 512], BF16)
            fw.memset("pool", zt.t[:, :], 0.0, [zt.b])
            it = 0
            for ct in range(4):
                for j in range(4):
                    fw.mm(psK[j].t[:, :], zt.t[:, 0:128], zt.t[:, :], True, False, [zt.b], [psK[j].b])
                fw.dma("sp", Bst.t[:, 0, :, :], C.s5_bre[l][:, 4 * ct:4 * ct + 4, :], writes=[Bst.b])
                fw.dma("sp", Bst.t[:, 1, :, :], C.s5_bim[l][:, 4 * ct:4 * ct + 4, :], writes=[Bst.b])
                for pl in range(4):
                    pair = 4 * ct + pl
                    cr, ci = co_re[:, pair:pair + 1], co_im[:, pair:pair + 1]
                    fw.ts("dve", t0_.t[:, :], Bst.t[:, 1, pl, :], ci, None, ALU.mult, None, [Bst.b, wk.b], [t0_.b])
                    fw.stt("dve", Bb.t[:, 0, pl, :], Bst.t[:, 0, pl, :], cr, t0_.t[:, :], ALU.mult, ALU.subtract, [Bst.b, wk.b, t0_.b], [Bb.b])
                    fw.ts("dve", t0_.t[:, :], Bst.t[:, 1, pl, :], cr, None, ALU.mult, None, [Bst.b, wk.b], [t0_.b])
                    fw.stt("dve", Bb.t[:, 1, pl, :], Bst.t[:, 0, pl, :], ci, t0_.t[:, :], ALU.mult, ALU.add, [Bst.b, wk.b, t0_.b], [Bb.b])
                for pl in range(4):
                    pair = 4 * ct + pl
                    psG = ps[pair % 2]
                    X = Xs[pair % 2]
                    ta, tb = tA[pair % 2], tB[pair % 2]
                    bre = Bb.t[:, 0, pl, :].unsqueeze(1).to_broadcast([128, 16, 128])
                    bim = Bb.t[:, 1, pl, :].unsqueeze(1).to_broadcast([128, 16, 128])
                    prb = PW.t[:, 0, 0:16, pair].unsqueeze(2).to_broadcast([128, 16, 128])
                    pib = PW.t[:, 1, 0:16, pair].unsqueeze(2).to_broadcast([128, 16, 128])
                    RB_ = [Bb.b, PW.b]
                    fw.tt("dve", ta.t[:, :, :], bre, prb, ALU.mult, RB_, [ta.b])
                    fw.tt("pool", tb.t[:, :, :], bim, pib, ALU.mult, RB_, [tb.b])
                    fw.tt("dve", X.t[:, 0, :, :], ta.t[:, :, :], tb.t[:, :, :], ALU.subtract, [ta.b, tb.b], [X.b])
                    fw.tt("pool", tb.t[:, :, :], bim, prb, ALU.mult, RB_, [tb.b])
                    fw.tt("dve", ta.t[:, :, :], bre, pib, ALU.mult, RB_, [ta.b])
                    fw.tt("pool", X.t[:, 1, :, :], ta.t[:, :, :], tb.t[:, :, :], ALU.add, [ta.b, tb.b], [X.b])
                    fw.mm(psG.t[:, 0:272], zt.t[:, 0:128], zt.t[:, 0:272], True, False, [zt.b], [psG.b])
                    for m in range(16):
                        pk = psK[m // 4]
                        ks = slice((m % 4) * 128, (m % 4 + 1) * 128)
                        fw.mm(pk.t[:, ks], X.t[:, 0, m, :], Cre.t[:, pair, :], False, False, [X.b, Cre.b], [pk.b])
                        fw.mm(pk.t[:, ks], X.t[:, 1, m, :], nCim.t[:, pair, :], False, pl == 3, [X.b, nCim.b], [pk.b])
                    for mg in range(4):
                        XT = XTs[it % 2]
                        pbt = C.pb[it % 2]
                        for mm_ in range(4):
                            m = 4 * mg + mm_
                            for part in range(2):
                                fw.tr(pbt.t[:, (2 * mm_ + part) * 128:(2 * mm_ + part + 1) * 128], X.t[:, part, m, :], C.identb.t[:, :],
                                      [X.b, C.identb.b], [pbt.b])
                        fw.cp("act" if it % 2 else "dve", XT.t[:, :, :, :], pbt.t[:, :].rearrange("p (m a q) -> p m a q", m=4, a=2),
                              [pbt.b], [XT.b])
                        for mm_ in range(4):
                            m = 4 * mg + mm_
                            tau = 15 - m
                            rhs = u5T.t[:, ct, :].rearrange("p (c b) -> p c b", b=16)[:, :, tau]
                            fw.mm(psG.t[:, 0:136], XT.t[:, mm_, 0, :], rhs, False, m == 15, [XT.b, u5T.b], [psG.b])
                            fw.mm(psG.t[:, 136:272], XT.t[:, mm_, 1, :], rhs, False, m == 15, [XT.b, u5T.b], [psG.b])
                        it += 1
                    fw.cp("act", Sall.t[:, 1:137, 0, pair], psG.t[:, 0:136], [psG.b], [Sall.b])
                    fw.cp("act", Sall.t[:, 1:137, 1, pair], psG.t[:, 136:272], [psG.b], [Sall.b])
                for j in range(4):
                    fw.cp("act" if j % 2 else "dve", KT.t[:, ct, 4 * j:4 * j + 4, :], psK[j].t[:, :].rearrange("p (m c) -> p m c", m=4),
                          [psK[j].b], [KT.b])
            fw.barrier()
        if C.debug and l == 0:
            fw.dma("pool", C.dbg5[:, 736:736 + 2048], KT.t[:, 0, :, :].rearrange("p m c -> p (m c)"), reads=[KT.b], writes=[Buf()])
            fw.dma("sp", C.dbg5[:, 2784:2784 + 137 * 48], Sall.t[:, :, :, :].rearrange("p c a q -> p (c a q)"), reads=[Sall.b], writes=[Buf()])
        if os.environ.get("S5_STOP") == "D":
            fw.barrier(); return
        with ExitStack() as pe_:
            A1 = fw.tile(pe_, "A1", [128, 2, 16], F32)
            A2 = fw.tile(pe_, "A2", [128, 2, 16], F32)
            p1 = fw.tile(pe_, "p1", [128, 2, 16], F32)
            p2 = fw.tile(pe_, "p2", [128, 2, 16], F32)
            fw.cp("dve", A1.t[:, 0, :], PW.t[:, 0, 16, :], [PW.b], [A1.b])
            fw.cp("dve", A1.t[:, 1, :], PW.t[:, 0, 16, :], [PW.b], [A1.b])
            fw.ts("dve", A2.t[:, 0, :], PW.t[:, 1, 16, :], -1.0, None, ALU.mult, None, [PW.b], [A2.b])
            fw.cp("dve", A2.t[:, 1, :], PW.t[:, 1, 16, :], [PW.b], [A2.b])
            for c in range(136):
                fw.tt("dve", p1.t[:, :, :], A1.t[:, :, :], Sall.t[:, c, 0:2, :], ALU.mult, [A1.b, Sall.b], [p1.b])
                fw.tt("dve", p2.t[:, :, :], A2.t[:, :, :], Sall.t[:, c, 1:3, :], ALU.mult, [A2.b, Sall.b], [p2.b])
                fw.tt("dve", p1.t[:, :, :], p1.t[:, :, :], p2.t[:, :, :], ALU.add, [p1.b, p2.b], [p1.b])
                fw.tt("dve", Sall.t[:, c + 1, 0:2, :], Sall.t[:, c + 1, 0:2, :], p1.t[:, :, :], ALU.add, [Sall.b, p1.b], [Sall.b])
                fw.cp("dve", Sall.t[:, c + 1, 2, :], Sall.t[:, c + 1, 0, :], [Sall.b], [Sall.b])
            fw.barrier()
        if os.environ.get("S5_STOP") == "E":
            fw.barrier(); return
        pfg = ExitStack()
        gT = fw.tile(pfg, "g5T", [128, 4, TP], BF16)
        with ExitStack() as pf:
            SP = fw.tile(pf, "SP", [128, 4, 2, 136, 16], BF16)
            u1 = fw.tile(pf, "u1", [128, 136, 16], F32)
            u2 = fw.tile(pf, "u2", [128, 136, 16], F32)
            z = fw.tile(pf, "z5", [128, 512], F32)
            z2 = fw.tile(pf, "z52", [128, 512], F32)
            k = 0
            for ct in range(4):
                for pl in range(4):
                    pair = 4 * ct + pl
                    srb = Sall.t[:, 0:136, 0, pair].unsqueeze(2).to_broadcast([128, 136, 16])
                    sib = Sall.t[:, 0:136, 1, pair].unsqueeze(2).to_broadcast([128, 136, 16])
                    prb = PW.t[:, 0, 1:17, pair].unsqueeze(1).to_broadcast([128, 136, 16])
                    pib = PW.t[:, 1, 1:17, pair].unsqueeze(1).to_broadcast([128, 136, 16])
                    RS = [Sall.b, PW.b]
                    fw.tt("dve", u1.t[:, :, :], srb, prb, ALU.mult, RS, [u1.b])
                    fw.tt("pool", u2.t[:, :, :], sib, pib, ALU.mult, RS, [u2.b])
                    fw.tt("dve", SP.t[:, pl, 0, :, :], u1.t[:, :, :], u2.t[:, :, :], ALU.subtract, [u1.b, u2.b], [SP.b])
                    fw.tt("pool", u2.t[:, :, :], sib, prb, ALU.mult, RS, [u2.b])
                    fw.tt("dve", u1.t[:, :, :], srb, pib, ALU.mult, RS, [u1.b])
                    fw.tt("pool", SP.t[:, pl, 1, :, :], u1.t[:, :, :], u2.t[:, :, :], ALU.add, [u1.b, u2.b], [SP.b])
                for (t0, n) in TGS:
                    c0, nch = t0 // 16, n // 16
                    pst = ps[k % 2]
                    k += 1
                    pv = pst.t[:, 0:n].rearrange("p (c b) -> p c b", b=16)
                    uv = u5T.t[:, ct, t0:t0 + n].rearrange("p (c b) -> p c b", b=16)
                    for tau in range(16):
                        fw.mm(pv[:, :, tau:16], KT.t[:, ct, tau, :], uv[:, :, 0:16 - tau], tau == 0, False, [KT.b, u5T.b], [pst.b])
                    for pl in range(4):
                        pair = 4 * ct + pl
                        fw.mm(pst.t[:, 0:n], Cre.t[:, pair, :], SP.t[:, pl, 0, c0:c0 + nch, :].rearrange("p c b -> p (c b)"), False, False,
                              [Cre.b, SP.b], [pst.b])
                        fw.mm(pst.t[:, 0:n], nCim.t[:, pair, :], SP.t[:, pl, 1, c0:c0 + nch, :].rearrange("p c b -> p (c b)"), False, pl == 3,
                              [nCim.b, SP.b], [pst.b])
                    fw.stt("dve", z.t[:, 0:n], u5T.t[:, ct, t0:t0 + n], d5.t[:, ct:ct + 1], pst.t[:, 0:n], ALU.mult, ALU.add,
                           [u5T.b, d5.b, pst.b], [z.b])
                    fw.tt("pool", z2.t[:, 0:n], z.t[:, 0:n], z.t[:, 0:n], ALU.mult, [z.b], [z2.b])
                    fw.ts("pool", z2.t[:, 0:n], z2.t[:, 0:n], 0.044715, 1.0, ALU.mult, ALU.add, [z2.b], [z2.b])
                    fw.tt("pool", z2.t[:, 0:n], z2.t[:, 0:n], z.t[:, 0:n], ALU.mult, [z2.b, z.b], [z2.b])
                    fw.act(z2.t[:, 0:n], z2.t[:, 0:n], AF.Sigmoid, [z2.b], [z2.b], scale=2.0 * math.sqrt(2.0 / math.pi))
                    fw.tt("pool", gT.t[:, ct, t0:t0 + n], z.t[:, 0:n], z2.t[:, 0:n], ALU.mult, [z.b, z2.b], [gT.b])
            fw.barrier()
        if os.environ.get("S5_STOP") == "F":
            pfg.close(); fw.barrier(); return
        with ExitStack() as pg:
            sgs = [fw.tile(pg, "sg5", [128, 512], F32) for _ in range(2)]
            yos = [fw.tile(pg, "yo5", [128, 512], BF16) for _ in range(2)]
            k = 0
            for nb in range(4):
                for (t0, n) in TGS:
                    pa_, pg_ = ps[2 + 2 * (k % 2)], ps[3 + 2 * (k % 2)]
                    sg, yo = sgs[k % 2], yos[k % 2]
                    k += 1
                    for kc in range(4):
                        fw.mm(pa_.t[:, 0:n], GW.t[:, kc, nb * 128:(nb + 1) * 128], gT.t[:, kc, t0:t0 + n], kc == 0, kc == 3, [GW.b, gT.b], [pa_.b])
                    for kc in range(4):
                        fw.mm(pg_.t[:, 0:n], GW.t[:, kc, 512 + nb * 128:512 + (nb + 1) * 128], gT.t[:, kc, t0:t0 + n], kc == 0, kc == 3,
                              [GW.b, gT.b], [pg_.b])
                    fw.act(sg.t[:, 0:n], pg_.t[:, 0:n], AF.Sigmoid, [pg_.b, gb.b], [sg.b], bias=gb.t[:, 4 + nb:5 + nb])
                    fw.stt("dve", yo.t[:, 0:n], pa_.t[:, 0:n], gb.t[:, nb:nb + 1], sg.t[:, 0:n], ALU.add, ALU.mult, [pa_.b, gb.b, sg.b], [yo.b])
                    fw.dma("sp", C.YT[4 + nb, :, t0:t0 + n], yo.t[:, 0:n], reads=[yo.b], writes=C.YTb[1][t0 // 128:(t0 + n) // 128])
            fw.barrier()
        pfg.close()
        fw.barrier()


MGS = [(g * 256, min(256, TP - g * 256)) for g in range((TP + 255) // 256)]


def rms_epilogue(fw, C, psA, psB, nw, xt, wk2):
    junk, ss, tmp = wk2
    fw.act(junk.t[:, 0:512], psA.t[:, :], AF.Square, [psA.b], [junk.b, ss.b], accum=ss.t[:, 2:3])
    fw.act(junk.t[:, 512:1024], psB.t[:, :], AF.Square, [psB.b], [junk.b, ss.b], accum=ss.t[:, 3:4])
    fw.tt("dve", ss.t[:, 2:3], ss.t[:, 2:3], ss.t[:, 3:4], ALU.add, [ss.b], [ss.b])
    rstd_from_ss(fw, C, ss.t[:, 2:3], ss.t[:, 3:4], 1024.0, [ss.b], [ss.b])
    fw.stt("dve", tmp.t[:, 0:512], psA.t[:, :], ss.t[:, 3:4], nw.t[:, 0:512], ALU.mult, ALU.mult, [psA.b, ss.b, nw.b], [tmp.b])
    fw.stt("dve", tmp.t[:, 512:1024], psB.t[:, :], ss.t[:, 3:4], nw.t[:, 512:1024], ALU.mult, ALU.mult, [psB.b, ss.b, nw.b], [tmp.b])
    fw.tt("pool", xt.t[:, :], xt.t[:, :], tmp.t[:, :], ALU.add, [xt.b, tmp.b], [xt.b])


def phase_merge(fw, C, l):
    ps = C.ps
    with ExitStack() as ph:
        WG = fw.tile(ph, "WG", [128, 8, 4096], BF16)
        load_w(fw, C.w_in[l], WG, C_GATE, C_GATE + 4096, 8)
        WB = fw.tile(ph, "WB", [128, 16, 1024], BF16)
        for n in range(4):
            v = C.w_branch[l][n].rearrange("(cb p) d -> p cb d", p=128)
            fw.dma("pool", WB.t[:, 4 * n:4 * n + 4, :], v, writes=[WB.b])
        WO = fw.tile(ph, "WO", [128, 8, 1024], BF16)
        load_w(fw, C.w_out[l], WO, 0, 1024, 8)
        nw1 = fw.tile(ph, "nw1", [128, D], F32)
        nw2 = fw.tile(ph, "nw2", [128, D], F32)
        fw.dma("sp", nw1.t[:, :], C.norm_post_mix[l].partition_broadcast(128), writes=[nw1.b])
        fw.dma("sp", nw2.t[:, :], C.norm_pre_mlp[l].partition_broadcast(128), writes=[nw2.b])
        YTs = fw.tile(ph, "YTs", [128, 16, 256], BF16)
        mixT = fw.tile(ph, "mixT", [128, 8, 256], BF16)
        acc = fw.tile(ph, "macc", [128, 256], F32)
        sgs = [fw.tile(ph, "msg", [128, 256], F32) for _ in range(2)]
        tmpm = fw.tile(ph, "mtmp", [128, 256], F32)
        xts = [fw.tile(ph, "mxt", [128, D], F32) for _ in range(2)]
        wk = (fw.tile(ph, "junk", [128, D], BF16), fw.tile(ph, "ss", [128, 4], F32), fw.tile(ph, "ub", [128, D], BF16))
        wk2 = (wk[0], wk[1], fw.tile(ph, "mtmp2", [128, D], F32))
        k = 0
        for (t0, n) in MGS:
            tiles = list(range(t0 // 128, (t0 + n) // 128))
            fw.dma("sp", YTs.t[:, :, 0:n], C.YT[:, :, t0:t0 + n].rearrange("c p t -> p c t"),
                   reads=[C.YTb[m][i] for m in range(4) for i in tiles], writes=[YTs.b])
            for db in range(8):
                for nn in range(4):
                    pg_, pb_ = ps[2 * (k % 2)], ps[2 * (k % 2) + 1]
                    sg = sgs[k % 2]
                    k += 1
                    c0 = nn * 1024 + db * 128
                    for kc in range(8):
                        fw.mm(pg_.t[:, 0:n], WG.t[:, kc, c0:c0 + 128], C.uT.t[:, kc, t0:t0 + n], kc == 0, kc == 7,
                              [WG.b] + [C.uTb[i] for i in tiles], [pg_.b])
                    for cb in range(4):
                        fw.mm(pb_.t[:, 0:n], WB.t[:, 4 * nn + cb, db * 128:(db + 1) * 128], YTs.t[:, 4 * nn + cb, 0:n], cb == 0, cb == 3,
                              [WB.b, YTs.b], [pb_.b])
                    fw.act(sg.t[:, 0:n], pg_.t[:, 0:n], AF.Sigmoid, [pg_.b], [sg.b])
                    if nn == 0:
                        fw.tt("dve", acc.t[:, 0:n], sg.t[:, 0:n], pb_.t[:, 0:n], ALU.mult, [sg.b, pb_.b], [acc.b])
                    else:
                        fw.tt("dve", tmpm.t[:, 0:n], sg.t[:, 0:n], pb_.t[:, 0:n], ALU.mult, [sg.b, pb_.b], [tmpm.b])
                        if nn < 3:
                            fw.tt("pool", acc.t[:, 0:n], acc.t[:, 0:n], tmpm.t[:, 0:n], ALU.add, [acc.b, tmpm.b], [acc.b])
                        else:
                            fw.tt("pool", mixT.t[:, db, 0:n], acc.t[:, 0:n], tmpm.t[:, 0:n], ALU.add, [acc.b, tmpm.b], [mixT.b])
            for i in tiles:
                xt = xts[i % 2]
                src = C.h0 if l == 0 else C.hbuf
                fw.dma("sp", xt.t[:, :], src[i * 128:(i + 1) * 128, :], reads=([] if l == 0 else [C.hb[i]]), writes=[xt.b])
                sub = slice(i * 128 - t0, i * 128 - t0 + 128)
                for dh in range(2):
                    pst = ps[4 + dh]
                    for db in range(8):
                        fw.mm(pst.t[:, :], mixT.t[:, db, sub], WO.t[:, db, dh * 512:(dh + 1) * 512], db == 0, db == 7, [mixT.b, WO.b], [pst.b])
                rms_epilogue(fw, C, ps[4], ps[5], nw1, xt, wk2)
                fw.dma("sp", C.hbuf[i * 128:(i + 1) * 128, :], xt.t[:, :], reads=[xt.b], writes=[C.hb[i]])
                norm_rows_to_T(fw, C, ph, xt.t[:, :], xt.b, nw2, C.uT, C.uTb, i, wk)
        fw.barrier()


def phase_mlp(fw, C, l, last):
    ps = C.ps
    with ExitStack() as ph:
        WU = fw.tile(ph, "WU", [128, 8, 4096], BF16)
        load_w(fw, C.w_up[l], WU, 0, 4096, 8)
        WD = fw.tile(ph, "WD", [128, 32, 1024], BF16)
        load_w(fw, C.w_down[l], WD, 0, 1024, 32)
        nw = fw.tile(ph, "nw3", [128, D], F32)
        fw.dma("sp", nw.t[:, :], C.norm_post_mlp[l].partition_broadcast(128), writes=[nw.b])
        hT = fw.tile(ph, "hT", [128, 32, 256], BF16)
        rl = [fw.tile(ph, "rl", [128, 512], BF16) for _ in range(2)]
        xts = [fw.tile(ph, "pxt", [128, D], F32)] * 2
        ptmp = fw.tile(ph, "ptmp", [128, D], F32)
        wk2 = (ptmp, fw.tile(ph, "ss", [128, 4], F32), ptmp)
        k = 0
        for (t0, n) in MGS:
            tiles = list(range(t0 // 128, (t0 + n) // 128))
            for fp in range(16):
                pst = ps[k % 4]
                r = rl[k % 2]
                k += 1
                for j in range(2):
                    ffc = 2 * fp + j
                    for kc in range(8):
                        fw.mm(pst.t[:, j * 256:j * 256 + n], WU.t[:, kc, ffc * 128:(ffc + 1) * 128], C.uT.t[:, kc, t0:t0 + n], kc == 0, kc == 7,
                              [WU.b] + [C.uTb[i] for i in tiles], [pst.b])
                pv = pst.t[:, :].rearrange("p (j t) -> p j t", j=2)[:, :, 0:n]
                rv = r.t[:, :].rearrange("p (j t) -> p j t", j=2)[:, :, 0:n]
                fw.act(rv, pv, AF.Relu, [pst.b], [r.b])
                fw.tt("pool" if fp % 2 else "dve", hT.t[:, 2 * fp:2 * fp + 2, 0:n], rv, rv, ALU.mult, [r.b], [hT.b])
            for i in tiles:
                xt = xts[i % 2]
                fw.dma("sp", xt.t[:, :], C.hbuf[i * 128:(i + 1) * 128, :], reads=[C.hb[i]], writes=[xt.b])
                sub = slice(i * 128 - t0, i * 128 - t0 + 128)
                for dh in range(2):
                    pst = ps[4 + dh]
                    for ffc in range(32):
                        fw.mm(pst.t[:, :], hT.t[:, ffc, sub], WD.t[:, ffc, dh * 512:(dh + 1) * 512], ffc == 0, ffc == 31, [hT.b, WD.b], [pst.b])
                rms_epilogue(fw, C, ps[4], ps[5], nw, xt, wk2)
                if not last:
                    fw.dma("sp", C.hbuf[i * 128:(i + 1) * 128, :], xt.t[:, :], reads=[xt.b], writes=[C.hb[i]])
                else:
                    if C.debug:
                        fw.dma("sp", C.hbuf[i * 128:(i + 1) * 128, :], xt.t[:, :], reads=[xt.b], writes=[C.hb[i]])
                    lo = max(i * 128, 16)
                    hi = min((i + 1) * 128, T)
                    if hi > lo:
                        fw.dma("sp", C.out[lo - 16:hi - 16, :], xt.t[lo - i * 128:hi - i * 128, :], reads=[xt.b], writes=[C.outb])
        fw.barrier()


def build(debug=False, n_layers=DEPTH, phases=None):
    nc = bass.Bass("TRN2", target_bir_lowering=False)
    C = Ctx()
    C.debug = debug

    def din(name, shape):
        return nc.dram_tensor(name, list(shape), F32, kind="ExternalInput").ap()

    C.h0 = din("h0", [TP, D])
    C.consts_d = din("consts", [128, CO_TOTAL[0]])
    C.w_in = din("w_in", [4, D, N_IN])
    C.w_branch = din("w_branch", [4, 4, 512, D])
    C.w_out = din("w_out", [4, D, D])
    C.w_up = din("w_up", [4, D, 4 * D])
    C.w_down = din("w_down", [4, 4 * D, D])
    C.s5_glu_w = din("s5_glu_w", [4, 512, 1024])
    for nm in ("norm_pre_mix", "norm_post_mix", "norm_pre_mlp", "norm_post_mlp"):
        setattr(C, nm, din(nm, [4, D]))
    for nm in ("ret_gn_w", "ssd_norm_w", "hgrn_norm_w", "ssd_dsk"):
        setattr(C, nm, din(nm, [4, 512]))
    C.ssd_dt_bias = din("ssd_dt_bias", [4, 8])
    C.ssd_a_log = din("ssd_a_log", [4, 8])
    C.lbT = din("lbT", [128, 4, 4])
    C.ssd_cw = din("ssd_cw", [4, 128, 8, 4])
    C.ssd_cb = din("ssd_cb", [4, 128, 8])
    C.s5_small = din("s5_small", [4, 128, 48])
    for nm in ("s5_bre", "s5_bim", "s5_cre", "s5_cim"):
        setattr(C, nm, din(nm, [4, 128, 16, 128]))
    C.s5_dT = din("s5_dT", [4, 128, 4])
    C.s5_gbT = din("s5_gbT", [4, 128, 8])
    C.out = nc.dram_tensor("out", [2048, D], F32, kind="ExternalOutput").ap()
    sk = "ExternalOutput" if debug else "Internal"
    C.hbuf = nc.dram_tensor("hbuf", [TP, D], F32, kind=sk).ap()
    C.YT = nc.dram_tensor("YT", [16, 128, TP], BF16, kind=sk).ap()
    if debug:
        C.dbg5 = nc.dram_tensor("dbg5", [128, 2784 + 137 * 48], F32, kind="ExternalOutput").ap()
    C.hb = [Buf("hb%d" % i) for i in range(NT)]
    C.YTb = [[Buf("yt%d_%d" % (m, i)) for i in range(NT)] for m in range(4)]
    C.outb = Buf("out")
    with ExitStack() as es:
        fw = FW(nc, es)
        C.ps = [TL(es.enter_context(nc.psum_tensor("ps%d" % j, [128, 512], F32)), "ps%d" % j) for j in range(6)]
        C.pb = [TL(es.enter_context(nc.psum_tensor("pb%d" % j, [128, 1024], BF16)), "pb%d" % j) for j in range(2)]
        C.consts = fw.tile(es, "consts", [128, CO_TOTAL[0]], F32)
        fw.dma("sp", C.consts.t[:, :], C.consts_d[:, :], writes=[C.consts.b])
        C.identb = fw.tile(es, "identb", [128, 128], BF16)
        C.identf = fw.tile(es, "identf", [128, 128], F32)
        fw.cp("dve", C.identb.t[:, :], cslice(C, "ident"), [C.consts.b], [C.identb.b])
        fw.cp("dve", C.identf.t[:, :], cslice(C, "ident"), [C.consts.b], [C.identf.b])
        C.uT = fw.tile(es, "uT", [128, 8, TP], BF16)
        C.uTb = [Buf("uT%d" % i) for i in range(NT)]
        C.lb_all = fw.tile(es, "lb_all", [128, 4, 4], F32)
        C.oml_all = fw.tile(es, "oml_all", [128, 4, 4], F32)
        lbe = fw.tile(es, "lbe", [128, 4, 4], F32)
        lsum = fw.tile(es, "lsum", [128, 4], F32)
        R = [lbe.b, lsum.b, C.lb_all.b, C.oml_all.b]
        fw.dma("sp", lbe.t[:, :, :], C.lbT[:, :, :], writes=[lbe.b])
        fw.act(lbe.t[:, :, :], lbe.t[:, :, :], AF.Exp, R, R)
        fw.tt("dve", lsum.t[:, :], lbe.t[:, 0, :], lbe.t[:, 1, :], ALU.add, R, R)
        fw.tt("dve", lsum.t[:, :], lsum.t[:, :], lbe.t[:, 2, :], ALU.add, R, R)
        fw.tt("dve", lsum.t[:, :], lsum.t[:, :], lbe.t[:, 3, :], ALU.add, R, R)
        fw.op("dve", lambda g: g.reciprocal(lsum.t[:, :], lsum.t[:, :]), R, R)
        fw.memset("dve", C.lb_all.t[:, 0, :], 0.0, R)
        for ll in range(1, 4):
            fw.tt("dve", lbe.t[:, ll, :], lbe.t[:, ll, :], lsum.t[:, :], ALU.mult, R, R)
            fw.tt("dve", C.lb_all.t[:, ll, :], C.lb_all.t[:, ll - 1, :], lbe.t[:, ll, :], ALU.add, R, R)
        fw.ts("dve", C.oml_all.t[:, :, :], C.lb_all.t[:, :, :], -1.0, 1.0, ALU.mult, ALU.add, R, R)
        fw.barrier()
        allp = ("norm", "ret", "s5", "ssd", "hg", "merge", "mlp")
        for l in range(n_layers):
            for pn in allp:
                if phases is not None and pn not in phases:
                    continue
                if pn == "norm":
                    phase_norm1(fw, C, l)
                elif pn == "ret":
                    phase_ret(fw, C, l)
                elif pn == "s5":
                    phase_s5(fw, C, l)
                elif pn == "ssd":
                    phase_ssd(fw, C, l)
                elif pn == "hg":
                    phase_hg(fw, C, l)
                elif pn == "merge":
                    phase_merge(fw, C, l)
                elif pn == "mlp":
                    phase_mlp(fw, C, l, last=(l == n_layers - 1))
        fw.barrier()
        C.n_inst, C.n_wait = fw.n_inst, fw.n_wait
    return nc, C


CO_TOTAL = [0]
_CONSTS = None


def get_consts():
    global _CONSTS
    if _CONSTS is None:
        _CONSTS = host_consts()
        CO_TOTAL[0] = _CONSTS.shape[1]
    return _CONSTS


def make_in_maps(inp):
    consts = get_consts()
    P = host_params(inp)
    f = lambda a: np.ascontiguousarray(np.asarray(a, np.float32))
    shared = {"consts": consts}
    for nm in ("w_in", "w_branch", "w_out", "w_up", "w_down", "s5_glu_w", "norm_pre_mix", "norm_post_mix", "norm_pre_mlp",
               "norm_post_mlp", "ret_gn_w", "ssd_norm_w", "hgrn_norm_w", "ssd_dt_bias", "ssd_a_log"):
        shared[nm] = f(inp[nm])
    for nm in ("lbT", "ssd_cw", "ssd_cb", "ssd_dsk", "s5_small", "s5_bre", "s5_bim", "s5_cre", "s5_cim", "s5_dT", "s5_gbT"):
        shared[nm] = P[nm]
    x = np.asarray(inp["x"], np.float32)
    meta = np.asarray(inp["meta_tokens"], np.float32)
    maps = []
    for b in range(x.shape[0]):
        h0 = np.zeros((TP, D), np.float32)
        h0[0:16] = meta
        h0[16:T] = x[b]
        m = dict(shared)
        m["h0"] = h0
        maps.append(m)
    return maps


_NC = None


def kernel(**inputs):
    global _NC
    maps = make_in_maps(inputs)
    if _NC is None:
        _NC = build()[0]
    res = run_bass_kernel_spmd(_NC, maps, core_ids=list(range(len(maps))))
    return np.stack([np.asarray(r["out"], np.float32) for r in res.results], axis=0)
```

```python
import math
import os
import numpy as np
from contextlib import ExitStack
import concourse.bass as bass
import concourse.mybir as mybir
from concourse.bass_utils import run_bass_kernel_spmd

F32 = mybir.dt.float32
BF16 = mybir.dt.bfloat16
ALU = mybir.AluOpType
AF = mybir.ActivationFunctionType
AX = mybir.AxisListType

DEPTH = 4
D = 1024
T = 2064
NT = 17
TP = NT * 128
EPS = 1e-6
N_IN = 9736
C_RET, C_S5, C_SSD, C_HG, C_GATE = 0, 1536, 2048, 3592, 5640
GAMMA = [1.0 - 2.0 ** (-5.0 - h) for h in range(4)]


class Buf:
    __slots__ = ("name", "w", "r")

    def __init__(self, name=""):
        self.name = name
        self.w = None
        self.r = []


class TL:
    def __init__(self, t, name):
        self.t = t
        self.b = Buf(name)


class VW:
    def __init__(self, t, b):
        self.t = t
        self.b = b


class FW:
    N_DMA_SEMS = 16

    def __init__(self, nc, es):
        self.nc = nc
        self.es = es
        self.eng = {"pe": nc.tensor, "dve": nc.vector, "act": nc.scalar, "pool": nc.gpsimd, "sp": nc.sync}
        self.sems = {}
        self.cnt = {}
        for e in ("pe", "dve", "act", "pool"):
            self.sems[e] = es.enter_context(nc.semaphore("s_" + e))
            self.cnt[e] = 0
        self.dma_keys = {}
        self.dma_rr = {}
        for q in ("sp", "pool"):
            ks = []
            for i in range(self.N_DMA_SEMS):
                k = "d_%s_%d" % (q, i)
                self.sems[k] = es.enter_context(nc.semaphore(k))
                self.cnt[k] = 0
                ks.append(k)
            self.dma_keys[q] = ks
            self.dma_rr[q] = 0
        self.known = {e: {} for e in self.eng}
        self.n_inst = 0
        self.n_wait = 0
        self.uid = 0

    def tile(self, stack, name, shape, dt):
        self.uid += 1
        nm = "%s_%d" % (name, self.uid)
        return TL(stack.enter_context(self.nc.sbuf_tensor(nm, list(shape), dt)), nm)

    def _wait(self, e, ev):
        if ev is None:
            return
        k, v = ev
        if e == "pe" and k == "pe":
            return
        kn = self.known[e]
        if kn.get(k, 0) >= v:
            return
        self.eng[e].wait_ge(self.sems[k], v)
        kn[k] = v
        self.n_wait += 1

    def _deps(self, e, reads, writes):
        for b in reads:
            self._wait(e, b.w)
        for b in writes:
            self._wait(e, b.w)
            for ev in b.r:
                self._wait(e, ev)

    def _mark(self, ev, reads, writes):
        for b in reads:
            b.r.append(ev)
        for b in writes:
            b.w = ev
            b.r = []

    def op(self, e, fn, reads=(), writes=()):
        self._deps(e, reads, writes)
        ins = fn(self.eng[e])
        self.cnt[e] += 1
        ins.then_inc(self.sems[e], 1)
        ev = (e, self.cnt[e])
        self._mark(ev, reads, writes)
        self.n_inst += 1
        return ev

    def dma(self, q, out, in_, reads=(), writes=(), **kw):
        self._deps(q, reads, writes)
        ks = self.dma_keys[q]
        k = ks[self.dma_rr[q] % len(ks)]
        self.dma_rr[q] += 1
        if self.cnt[k] > 0:
            self._wait(q, (k, self.cnt[k]))
        ins = self.eng[q].dma_start(out=out, in_=in_, **kw)
        self.cnt[k] += 16
        ins.then_inc(self.sems[k], 16)
        ev = (k, self.cnt[k])
        self._mark(ev, reads, writes)
        self.n_inst += 1
        return ev

    def barrier(self, engines=("pe", "dve", "act", "pool", "sp")):
        for e in engines:
            for k, v in self.cnt.items():
                if v > 0:
                    self._wait(e, (k, v))

    def tt(self, e, out, a, b, op, R, W):
        return self.op(e, lambda g: g.tensor_tensor(out, a, b, op), R, W)

    def ts(self, e, out, a, s1, s2, op0, op1, R, W):
        if s2 is None:
            return self.op(e, lambda g: g.tensor_scalar(out, a, s1, None, op0=op0), R, W)
        return self.op(e, lambda g: g.tensor_scalar(out, a, s1, s2, op0=op0, op1=op1), R, W)

    def stt(self, e, out, in0, sc, in1, op0, op1, R, W):
        e = "dve"
        return self.op(e, lambda g: g.scalar_tensor_tensor(out, in0, sc, in1, op0=op0, op1=op1), R, W)

    def cp(self, e, out, in_, R, W):
        if e == "act":
            return self.op(e, lambda g: g.copy(out, in_), R, W)
        return self.op(e, lambda g: g.tensor_copy(out, in_), R, W)

    def act(self, out, in_, func, R, W, bias=None, scale=None, accum=None):
        kw = {}
        if bias is not None:
            kw["bias"] = bias
        if scale is not None:
            kw["scale"] = scale
        if accum is not None:
            kw["accum_out"] = accum
        return self.op("act", lambda g: g.activation(out, in_, func, **kw), R, W)

    def mm(self, out, lhsT, rhs, start, stop, R, W):
        return self.op("pe", lambda g: g.matmul(out, lhsT, rhs, start=start, stop=stop), R, W)

    def tr(self, out, in_, ident, R, W):
        return self.op("pe", lambda g: g.transpose(out, in_, ident), R, W)

    def memset(self, e, ap, val, W):
        return self.op(e, lambda g: g.memset(ap, val), (), W)


CO = {}


def _pack(items):
    off = 0
    cols = []
    for name, arr in items:
        arr = np.asarray(arr, np.float32).reshape(128, -1)
        CO[name] = (off, arr.shape[1])
        off += arr.shape[1]
        cols.append(arr)
    return np.ascontiguousarray(np.concatenate(cols, axis=1))


def host_consts():
    s = np.arange(128)[:, None]
    t = np.arange(128)[None, :]
    ident = (s == t).astype(np.float32)
    triu = (s <= t).astype(np.float32)
    negm = np.where(s <= t, 0.0, -30000.0).astype(np.float32)
    ones = np.ones((128, 128), np.float32)
    retdt = np.zeros((128, 4, 128), np.float64)
    qdec = np.zeros((128, 4, 64), np.float64)
    kdec = np.zeros((128, 4, 64), np.float64)
    for h in range(4):
        g = GAMMA[h]
        retdt[:, h, :] = np.where(s <= t, 0.125 * g ** np.maximum(t - s, 0), 0.0)
        qdec[:, h, :] = (g ** (np.arange(128) + 1.0))[:, None]
        kdec[:, h, :] = (0.125 * g ** (127.0 - np.arange(128)))[:, None]
    half = 32
    inv_freq = (10000.0 ** (-np.arange(half, dtype=np.float32) / half)).astype(np.float32)
    pos = (np.arange(NT)[None, :] * 128 + np.arange(128)[:, None]).astype(np.float32)
    ang = pos[:, :, None] * inv_freq[None, None, :]
    cos = np.cos(ang).astype(np.float32)
    sin = np.sin(ang).astype(np.float32)
    halfpi = np.full((128, 1), math.pi / 2, np.float32)
    mvals = np.array(list(range(17)) + [0.5], np.float32)
    mtab = np.broadcast_to(mvals[None, :, None], (128, 18, 16))
    return _pack([("mtab", mtab), ("ident", ident), ("triu", triu), ("negm", negm), ("ones", ones), ("retdt", retdt),
                  ("qdec", qdec), ("kdec", kdec), ("cos", cos), ("sin", sin), ("halfpi", halfpi)])


def host_params(inp):
    P = {}
    f = lambda a: np.ascontiguousarray(np.asarray(a, np.float32))
    P["lbT"] = f(np.asarray(inp["hgrn_lb"]).reshape(4, 4, 128).transpose(2, 0, 1))
    P["ssd_cw"] = f(np.asarray(inp["ssd_conv_w"]).reshape(4, 4, 8, 128).transpose(0, 3, 2, 1))
    P["ssd_cb"] = f(np.asarray(inp["ssd_conv_b"]).reshape(4, 8, 128).transpose(0, 2, 1))
    P["ssd_dsk"] = f(np.repeat(np.asarray(inp["ssd_d"]), 64, axis=1))
    def pl_small(a):
        a = np.asarray(a).reshape(4, 16, 2, 64)
        return a.transpose(0, 2, 3, 1).reshape(4, 128, 16)
    ls = np.broadcast_to(np.asarray(inp["s5_log_step"])[:, :, None], (4, 32, 64))
    P["s5_small"] = f(np.concatenate([pl_small(inp["s5_lam_re"]), pl_small(inp["s5_lam_im"]), pl_small(ls)], axis=2))
    def pl_b(b):
        b = np.asarray(b)
        out = np.zeros((4, 2, 64, 16, 8, 16), np.float32)
        for g in range(32):
            out[:, g % 2, :, g // 2, g % 8, :] = b[:, g]
        return out.reshape(4, 128, 16, 128)
    def pl_c(c):
        c = np.asarray(c)
        out = np.zeros((4, 2, 64, 16, 8, 16), np.float32)
        for g in range(32):
            out[:, g % 2, :, g // 2, g % 8, :] = c[:, g].transpose(0, 2, 1)
        return out.reshape(4, 128, 16, 128)
    P["s5_bre"] = pl_b(inp["s5_b_re"])
    P["s5_bim"] = pl_b(inp["s5_b_im"])
    P["s5_cre"] = pl_c(inp["s5_c_re"])
    P["s5_cim"] = pl_c(inp["s5_c_im"])
    P["s5_dT"] = f(np.asarray(inp["s5_d"]).reshape(4, 4, 128).transpose(0, 2, 1))
    P["s5_gbT"] = f(np.asarray(inp["s5_glu_b"]).reshape(4, 8, 128).transpose(0, 2, 1))
    return P


class Ctx:
    pass


def cslice(C, name, a=None, b=None):
    off, n = CO[name]
    if a is None:
        return C.consts.t[:, off:off + n]
    return C.consts.t[:, off + a:off + b]


def load_w(fw, src2d, dst, c0, c1, kcs, R=()):
    v = src2d.rearrange("(kc p) n -> p kc n", p=128)
    step = 2048
    for kc in range(kcs):
        for a in range(c0, c1, step):
            b = min(a + step, c1)
            fw.dma("pool", dst.t[:, kc, a - c0:b - c0], v[:, kc, a:b], reads=R, writes=[dst.b])


def rstd_from_ss(fw, C, ss_ap, out_ap, n, R, W):
    fw.ts("dve", out_ap, ss_ap, 1.0 / n, EPS, ALU.mult, ALU.add, R, W)
    fw.act(out_ap, out_ap, AF.Sqrt, W, W)
    fw.op("dve", lambda g: g.reciprocal(out_ap, out_ap), W, W)


def norm_rows_to_T(fw, C, ph, x_ap, xb, nw, dstT, dst_bufs, i, wk):
    junk, ss, ub = wk
    fw.act(junk.t[:, :], x_ap, AF.Square, [xb], [junk.b, ss.b], accum=ss.t[:, 0:1])
    rstd_from_ss(fw, C, ss.t[:, 0:1], ss.t[:, 1:2], 1024.0, [ss.b], [ss.b])
    fw.stt("dve", ub.t[:, :], x_ap, ss.t[:, 1:2], nw.t[:, :], ALU.mult, ALU.mult, [xb, ss.b, nw.b], [ub.b])
    pb = C.pb[i % 2]
    for kc in range(8):
        fw.tr(pb.t[:, kc * 128:(kc + 1) * 128], ub.t[:, kc * 128:(kc + 1) * 128], C.identb.t[:, :], [ub.b, C.identb.b], [pb.b])
    fw.cp("act" if i % 2 else "dve", dstT.t[:, :, i * 128:(i + 1) * 128],
          pb.t[:, :].rearrange("p (k t) -> p k t", k=8), [pb.b], [dst_bufs[i]])


def phase_norm1(fw, C, l):
    with ExitStack() as ph:
        nw = fw.tile(ph, "nw", [128, D], F32)
        fw.dma("sp", nw.t[:, :], C.norm_pre_mix[l].partition_broadcast(128), writes=[nw.b])
        xts = [fw.tile(ph, "xt", [128, D], F32) for _ in range(2)]
        wk = (fw.tile(ph, "junk", [128, D], BF16), fw.tile(ph, "ss", [128, 2], F32), fw.tile(ph, "ub", [128, D], BF16))
        for i in range(NT):
            xt = xts[i % 2]
            src = C.h0 if l == 0 else C.hbuf
            fw.dma("sp", xt.t[:, :], src[i * 128:(i + 1) * 128, :], reads=([] if l == 0 else [C.hb[i]]), writes=[xt.b])
            norm_rows_to_T(fw, C, ph, xt.t[:, :], xt.b, nw, C.uT, C.uTb, i, wk)
        fw.barrier()


ILV_OFF = set(os.environ.get("NOILV", "ret,ssd").split(","))
CUR_PHASE = [""]


def interleave(gb, ga):
    if CUR_PHASE[0] in ILV_OFF:
        for g in (ga, gb):
            if g is not None:
                for _ in g:
                    pass
        return
    done_a = ga is None
    done_b = False
    while not (done_a and done_b):
        if not done_b:
            try:
                next(gb)
            except StopIteration:
                done_b = True
        if not done_a:
            try:
                next(ga)
            except StopIteration:
                done_a = True


def interleave_n(gens):
    gens = list(gens)
    if CUR_PHASE[0] in ILV_OFF:
        for g in reversed(gens):
            for _ in g:
                pass
        return
    while gens:
        for g in list(gens):
            try:
                next(g)
            except StopIteration:
                gens.remove(g)


def chain_gens(*gens):
    for g in gens:
        if g is not None:
            yield from g


def emit_yT(fw, C, yo, yT, mix, i):
    pb = C.pb[1]
    for cb in range(4):
        fw.tr(pb.t[:, cb * 128:(cb + 1) * 128], yo.t[:, cb * 128:(cb + 1) * 128], C.identb.t[:, :], [yo.b, C.identb.b], [pb.b])
    fw.cp("act", yT.t[:, :, :], pb.t[:, 0:512].rearrange("p (c t) -> p c t", c=4), [pb.b], [yT.b])
    fw.dma("sp", C.YT[mix * 4:(mix + 1) * 4, :, i * 128:(i + 1) * 128].rearrange("c p t -> p c t"), yT.t[:, :, :],
           reads=[yT.b], writes=[C.YTb[mix][i]])


def tok_mm(fw, C, ps, Wt, c0, n, i, ncols=None):
    for kc in range(8):
        fw.mm(ps.t[:, 0:n], C.uT.t[:, kc, i * 128:(i + 1) * 128], Wt.t[:, kc, c0:c0 + n], kc == 0, kc == 7,
              [C.uTb[i], Wt.b], [ps.b])


def phase_ret(fw, C, l):
    CUR_PHASE[0] = "ret"
    with ExitStack() as ph:
        WR = fw.tile(ph, "WR", [128, 8, 1536], BF16)
        load_w(fw, C.w_in[l], WR, C_RET, C_RET + 1536, 8)
        gnw = fw.tile(ph, "gnw", [128, 512], F32)
        fw.dma("sp", gnw.t[:, :], C.ret_gn_w[l].partition_broadcast(128), writes=[gnw.b])
        S = fw.tile(ph, "rS", [64, 4, 128], F32)
        Sb = fw.tile(ph, "rSb", [64, 4, 128], BF16)
        fw.memset("dve", S.t[:, :, :], 0.0, [S.b])
        fw.memset("dve", Sb.t[:, :, :], 0.0, [Sb.b])
        qkr_2 = [fw.tile(ph, "qkr", [128, 8, 64], F32) for _ in range(2)]
        t1_2 = [fw.tile(ph, "t1", [128, 8, 32], F32) for _ in range(2)]
        t2_2 = [fw.tile(ph, "t2", [128, 8, 32], F32) for _ in range(2)]
        qb_2 = [fw.tile(ph, "qb", [128, 256], BF16) for _ in range(2)]
        qdb_2 = [fw.tile(ph, "qdb", [128, 256], BF16) for _ in range(2)]
        kb_2 = [fw.tile(ph, "kb", [128, 256], BF16) for _ in range(2)]
        khb_2 = [fw.tile(ph, "khb", [128, 256], BF16) for _ in range(2)]
        qkT_2 = [fw.tile(ph, "qkT", [64, 12, 128], BF16) for _ in range(2)]
        vb_2 = [fw.tile(ph, "vb", [128, 512], BF16) for _ in range(2)]
        sg_2 = [fw.tile(ph, "sg", [128, 512], F32) for _ in range(2)]
        PT_2 = [fw.tile(ph, "PT", [128, 512], BF16) for _ in range(2)]
        ysq_2 = [fw.tile(ph, "ysq", [128, 512], F32) for _ in range(2)]
        st_2 = [fw.tile(ph, "st", [128, 16], F32) for _ in range(2)]
        yn_2 = [fw.tile(ph, "yn", [128, 512], F32) for _ in range(2)]
        yo_2 = [fw.tile(ph, "yo", [128, 512], BF16) for _ in range(2)]
        yT_2 = [fw.tile(ph, "yT", [128, 4, 128], BF16) for _ in range(2)]
        ps_qk, ps_v, ps_g, ps_s, ps_y, ps_st = C.ps[0:6]
        cb_ = C.consts.b
        def stA(i):
                qkr, t1, t2, qb, qdb, kb, khb, qkT, vb, sg, PT, ysq, st, yn, yo, yT = qkr_2[i % 2], t1_2[i % 2], t2_2[i % 2], qb_2[i % 2], qdb_2[i % 2], kb_2[i % 2], khb_2[i % 2], qkT_2[i % 2], vb_2[i % 2], sg_2[i % 2], PT_2[i % 2], ysq_2[i % 2], st_2[i % 2], yn_2[i % 2], yo_2[i % 2], yT_2[i % 2]
                tok_mm(fw, C, ps_qk, WR, 0, 512, i)
                yield
                tok_mm(fw, C, ps_v, WR, 512, 512, i)
                yield
                tok_mm(fw, C, ps_g, WR, 1024, 512, i)
                yield
                qv = ps_qk.t[:, :].rearrange("p (h d) -> p h d", d=64)
                x1, x2 = qv[:, :, 0:32], qv[:, :, 32:64]
                co, cn = CO["cos"][0], CO["sin"][0]
                cosb = C.consts.t[:, co + i * 32:co + (i + 1) * 32].unsqueeze(1).to_broadcast([128, 8, 32])
                sinb = C.consts.t[:, cn + i * 32:cn + (i + 1) * 32].unsqueeze(1).to_broadcast([128, 8, 32])
                fw.tt("dve", t1.t[:, :, :], x1, cosb, ALU.mult, [ps_qk.b, cb_], [t1.b])
                fw.tt("dve", t2.t[:, :, :], x2, sinb, ALU.mult, [ps_qk.b, cb_], [t2.b])
                fw.tt("pool", qkr.t[:, :, 0:32], t1.t[:, :, :], t2.t[:, :, :], ALU.subtract, [t1.b, t2.b], [qkr.b])
                fw.tt("dve", t1.t[:, :, :], x1, sinb, ALU.mult, [ps_qk.b, cb_], [t1.b])
                fw.tt("dve", t2.t[:, :, :], x2, cosb, ALU.mult, [ps_qk.b, cb_], [t2.b])
                fw.tt("pool", qkr.t[:, :, 32:64], t1.t[:, :, :], t2.t[:, :, :], ALU.add, [t1.b, t2.b], [qkr.b])
                qf = qkr.t[:, 0:4, :].rearrange("p h d -> p (h d)")
                kf = qkr.t[:, 4:8, :].rearrange("p h d -> p (h d)")
                fw.cp("act", qb.t[:, :], qf, [qkr.b], [qb.b])
                fw.tt("pool", qdb.t[:, :], qf, cslice(C, "qdec"), ALU.mult, [qkr.b, cb_], [qdb.b])
                fw.cp("act", kb.t[:, :], kf, [qkr.b], [kb.b])
                fw.tt("pool", khb.t[:, :], kf, cslice(C, "kdec"), ALU.mult, [qkr.b, cb_], [khb.b])
                fw.cp("act", vb.t[:, :], ps_v.t[:, :], [ps_v.b], [vb.b])
                fw.act(sg.t[:, :], ps_g.t[:, :], AF.Silu, [ps_g.b], [sg.b])
        def stB(i):
                qkr, t1, t2, qb, qdb, kb, khb, qkT, vb, sg, PT, ysq, st, yn, yo, yT = qkr_2[i % 2], t1_2[i % 2], t2_2[i % 2], qb_2[i % 2], qdb_2[i % 2], kb_2[i % 2], khb_2[i % 2], qkT_2[i % 2], vb_2[i % 2], sg_2[i % 2], PT_2[i % 2], ysq_2[i % 2], st_2[i % 2], yn_2[i % 2], yo_2[i % 2], yT_2[i % 2]
                pb0, pb1 = C.pb
                for h in range(4):
                    fw.tr(pb0.t[0:64, h * 128:(h + 1) * 128], qb.t[:, h * 64:(h + 1) * 64], C.identb.t[:, :], [qb.b, C.identb.b], [pb0.b])
                    fw.tr(pb0.t[0:64, (4 + h) * 128:(5 + h) * 128], qdb.t[:, h * 64:(h + 1) * 64], C.identb.t[:, :], [qdb.b, C.identb.b], [pb0.b])
                    fw.tr(pb1.t[0:64, h * 128:(h + 1) * 128], kb.t[:, h * 64:(h + 1) * 64], C.identb.t[:, :], [kb.b, C.identb.b], [pb1.b])
                yield
                fw.cp("dve", qkT.t[:, 0:8, :], pb0.t[0:64, :].rearrange("p (j t) -> p j t", j=8), [pb0.b], [qkT.b])
                fw.cp("act", qkT.t[:, 8:12, :], pb1.t[0:64, 0:512].rearrange("p (j t) -> p j t", j=4), [pb1.b], [qkT.b])
                for h in range(4):
                    fw.mm(ps_s.t[:, h * 128:(h + 1) * 128], qkT.t[:, 8 + h, :], qkT.t[:, h, :], True, True, [qkT.b], [ps_s.b])
                yield
                fw.tt("dve", PT.t[:, :], ps_s.t[:, :], cslice(C, "retdt"), ALU.mult, [ps_s.b, cb_], [PT.b])
                for h in range(4):
                    hs = slice(h * 128, (h + 1) * 128)
                    fw.mm(ps_y.t[:, hs], PT.t[:, hs], vb.t[:, hs], True, False, [PT.b, vb.b], [ps_y.b])
                    fw.mm(ps_y.t[:, hs], qkT.t[:, 4 + h, :], Sb.t[:, h, :], False, True, [qkT.b, Sb.b], [ps_y.b])
                yield
                for h in range(4):
                    hs = slice(h * 128, (h + 1) * 128)
                    fw.mm(ps_st.t[0:64, hs], khb.t[:, h * 64:(h + 1) * 64], vb.t[:, hs], True, True, [khb.b, vb.b], [ps_st.b])
                yield
                for h in range(4):
                    hs = slice(h * 128, (h + 1) * 128)
                    fw.stt("dve", S.t[:, h, :], S.t[:, h, :], float(GAMMA[h] ** 128), ps_st.t[0:64, hs], ALU.mult, ALU.add,
                           [S.b, ps_st.b], [S.b])
                fw.cp("pool", Sb.t[:, :, :], S.t[:, :, :], [S.b], [Sb.b])
                yv = ps_y.t[:, :].rearrange("p (h e) -> p h e", h=4)
                fw.op("dve", lambda g: g.reduce_sum(st.t[:, 0:4], yv, axis=AX.X), [ps_y.b], [st.b])
                fw.act(ysq.t[:, :], ps_y.t[:, :], AF.Square, [ps_y.b], [ysq.b])
                fw.op("dve", lambda g: g.reduce_sum(st.t[:, 4:8], ysq.t[:, :].rearrange("p (h e) -> p h e", h=4), axis=AX.X), [ysq.b], [st.b])
                fw.ts("dve", st.t[:, 8:12], st.t[:, 0:4], 1.0 / 128, None, ALU.mult, None, [st.b], [st.b])
                fw.tt("dve", st.t[:, 0:4], st.t[:, 8:12], st.t[:, 8:12], ALU.mult, [st.b], [st.b])
                fw.stt("dve", st.t[:, 12:16], st.t[:, 4:8], 1.0 / 128, st.t[:, 0:4], ALU.mult, ALU.subtract, [st.b], [st.b])
                fw.ts("dve", st.t[:, 12:16], st.t[:, 12:16], EPS, None, ALU.add, None, [st.b], [st.b])
                fw.act(st.t[:, 12:16], st.t[:, 12:16], AF.Sqrt, [st.b], [st.b])
                fw.op("dve", lambda g: g.reciprocal(st.t[:, 12:16], st.t[:, 12:16]), [st.b], [st.b])
                for h in range(4):
                    hs = slice(h * 128, (h + 1) * 128)
                    fw.ts("dve", yn.t[:, hs], ps_y.t[:, hs], st.t[:, 8 + h:9 + h], st.t[:, 12 + h:13 + h], ALU.subtract, ALU.mult,
                          [ps_y.b, st.b], [yn.b])
                fw.tt("pool", yn.t[:, :], yn.t[:, :], gnw.t[:, :], ALU.mult, [yn.b, gnw.b], [yn.b])
                fw.tt("pool", yo.t[:, :], yn.t[:, :], sg.t[:, :], ALU.mult, [yn.b, sg.b], [yo.b])
                emit_yT(fw, C, yo, yT, 0, i)
                yield
        for _ in stA(0):
            pass
        for i in range(NT):
            interleave(stB(i), stA(i + 1) if i + 1 < NT else None)
        fw.barrier()


def phase_ssd(fw, C, l):
    CUR_PHASE[0] = "ssd"
    with ExitStack() as ph:
        WS = fw.tile(ph, "WS", [128, 8, 1544], BF16)
        load_w(fw, C.w_in[l], WS, C_SSD, C_SSD + 1544, 8)
        cw = fw.tile(ph, "cw", [128, 8, 4], F32)
        cbi = fw.tile(ph, "cbi", [128, 8], F32)
        dtb = fw.tile(ph, "dtb", [128, 8], F32)
        arow = fw.tile(ph, "arow", [128, 8], F32)
        dsk = fw.tile(ph, "dsk", [128, 512], F32)
        nw = fw.tile(ph, "snw", [128, 512], F32)
        fw.dma("sp", cw.t[:, :, :], C.ssd_cw[l], writes=[cw.b])
        fw.dma("sp", cbi.t[:, :], C.ssd_cb[l], writes=[cbi.b])
        fw.dma("sp", dtb.t[:, :], C.ssd_dt_bias[l].partition_broadcast(128), writes=[dtb.b])
        fw.dma("sp", arow.t[:, :], C.ssd_a_log[l].partition_broadcast(128), writes=[arow.b])
        fw.dma("sp", dsk.t[:, :], C.ssd_dsk[l].partition_broadcast(128), writes=[dsk.b])
        fw.dma("sp", nw.t[:, :], C.ssd_norm_w[l].partition_broadcast(128), writes=[nw.b])
        fw.act(arow.t[:, :], arow.t[:, :], AF.Exp, [arow.b], [arow.b])
        fw.ts("dve", arow.t[:, :], arow.t[:, :], -1.0, None, ALU.mult, None, [arow.b], [arow.b])
        S = fw.tile(ph, "sS", [128, 512], F32)
        Sb = fw.tile(ph, "sSb", [128, 512], BF16)
        fw.memset("dve", S.t[:, :], 0.0, [S.b])
        fw.memset("dve", Sb.t[:, :], 0.0, [Sb.b])
        xraw = fw.tile(ph, "xraw", [128, 8, 131], F32)
        fw.memset("pool", xraw.t[:, :, :], 0.0, [xraw.b])
        xc_2 = [fw.tile(ph, "xc", [128, 8, 128], F32) for _ in range(2)]
        xcb_2 = [fw.tile(ph, "xcb", [128, 8, 128], BF16) for _ in range(2)]
        xs_2 = [fw.tile(ph, "xs", [128, 512], F32) for _ in range(2)]
        Btok_2 = [fw.tile(ph, "Btok", [128, 2, 128], BF16) for _ in range(2)]
        sz_2 = [fw.tile(ph, "sz", [128, 512], F32) for _ in range(2)]
        sm_2 = [fw.tile(ph, "sm", [128, 104], F32) for _ in range(2)]
        laB = [fw.tile(ph, "laB", [128, 128], F32) for _ in range(2)]
        decT_2 = [fw.tile(ph, "decT", [128, 8, 128], F32) for _ in range(2)]
        PT_2 = [fw.tile(ph, "sPT", [128, 8, 128], BF16) for _ in range(2)]
        xdt_2 = [fw.tile(ph, "xdt", [128, 512], BF16) for _ in range(2)]
        xdtw_2 = [fw.tile(ph, "xdtw", [128, 512], BF16) for _ in range(2)]
        ya_2 = [fw.tile(ph, "ya", [128, 512], F32) for _ in range(2)]
        tmp_2 = [fw.tile(ph, "stmp", [128, 512], F32) for _ in range(2)]
        yo_2 = [fw.tile(ph, "syo", [128, 512], BF16) for _ in range(2)]
        yT_2 = [fw.tile(ph, "syT", [128, 4, 128], BF16) for _ in range(2)]
        ps = C.ps
        pb0, pb1 = C.pb
        cb_ = C.consts.b
        XD, AXc, EX, LN, DT, LA, CUM, NCUM, ECUM, WW, EDEC, DW = [slice(8 * j, 8 * j + 8) for j in range(12)]
        triu = cslice(C, "triu")
        ones = cslice(C, "ones")
        def stA(i):
                xc, xcb, xs, Btok, sz, sm, decT, PT, xdt, xdtw, ya, tmp, yo, yT = xc_2[i % 2], xcb_2[i % 2], xs_2[i % 2], Btok_2[i % 2], sz_2[i % 2], sm_2[i % 2], decT_2[i % 2], PT_2[i % 2], xdt_2[i % 2], xdtw_2[i % 2], ya_2[i % 2], tmp_2[i % 2], yo_2[i % 2], yT_2[i % 2]
                tok = slice(i * 128, (i + 1) * 128)
                for cb in range(8):
                    pst = ps[0] if cb < 4 else ps[1]
                    for kc in range(8):
                        fw.mm(pst.t[:, (cb % 4) * 128:(cb % 4 + 1) * 128], WS.t[:, kc, 512 + cb * 128:512 + (cb + 1) * 128],
                              C.uT.t[:, kc, tok], kc == 0, kc == 7, [WS.b, C.uTb[i]], [pst.b])
                yield
                if i > 0:
                    fw.cp("pool", xraw.t[:, :, 0:3], xraw.t[:, :, 128:131], [xraw.b], [xraw.b])
                fw.cp("act", xraw.t[:, 0:4, 3:131], ps[0].t[:, :].rearrange("p (c t) -> p c t", c=4), [ps[0].b], [xraw.b])
                fw.cp("dve", xraw.t[:, 4:8, 3:131], ps[1].t[:, :].rearrange("p (c t) -> p c t", c=4), [ps[1].b], [xraw.b])
                for cb in range(8):
                    e = "dve" if cb % 2 == 0 else "pool"
                    fw.ts(e, xc.t[:, cb, :], xraw.t[:, cb, 3:131], cw.t[:, cb, 3:4], cbi.t[:, cb:cb + 1], ALU.mult, ALU.add,
                          [xraw.b, cw.b, cbi.b], [xc.b])
                    for j in (2, 1, 0):
                        fw.stt(e, xc.t[:, cb, :], xraw.t[:, cb, j:j + 128], cw.t[:, cb, j:j + 1], xc.t[:, cb, :], ALU.mult, ALU.add,
                               [xraw.b, cw.b, xc.b], [xc.b])
                fw.act(xcb.t[:, :, :], xc.t[:, :, :], AF.Silu, [xc.b], [xcb.b])
                tok_mm(fw, C, ps[2], WS, 0, 512, i)
                yield
                fw.act(sz.t[:, :], ps[2].t[:, :], AF.Silu, [ps[2].b], [sz.b])
                for kc in range(8):
                    fw.mm(ps[3].t[:, 0:8], C.uT.t[:, kc, tok], WS.t[:, kc, 1536:1544], kc == 0, kc == 7, [C.uTb[i], WS.b], [ps[3].b])
                yield
                smb = [sm.b]
                fw.tt("dve", sm.t[:, XD], ps[3].t[:, 0:8], dtb.t[:, :], ALU.add, [ps[3].b, dtb.b], smb)
        def stB(i):
                xc, xcb, xs, Btok, sz, sm, decT, PT, xdt, xdtw, ya, tmp, yo, yT = xc_2[i % 2], xcb_2[i % 2], xs_2[i % 2], Btok_2[i % 2], sz_2[i % 2], sm_2[i % 2], decT_2[i % 2], PT_2[i % 2], xdt_2[i % 2], xdtw_2[i % 2], ya_2[i % 2], tmp_2[i % 2], yo_2[i % 2], yT_2[i % 2]
                smb = [sm.b]
                pb0, pb1 = C.pb
                for cb in range(4):
                    fw.tr(pb0.t[:, cb * 128:(cb + 1) * 128], xcb.t[:, cb, :], C.identb.t[:, :], [xcb.b, C.identb.b], [pb0.b])
                yield
                fw.cp("dve", xs.t[:, :], pb0.t[:, 0:512], [pb0.b], [xs.b])
                for g in range(2):
                    fw.tr(pb1.t[:, g * 128:(g + 1) * 128], xcb.t[:, 4 + g, :], C.identb.t[:, :], [xcb.b, C.identb.b], [pb1.b])
                yield
                fw.cp("act", Btok.t[:, :, :], pb1.t[:, 0:256].rearrange("p (g n) -> p g n", g=2), [pb1.b], [Btok.b])
                fw.ts("dve", sm.t[:, AXc], sm.t[:, XD], -1.0, None, ALU.mult, None, smb, smb)
                fw.tt("dve", sm.t[:, AXc], sm.t[:, AXc], sm.t[:, XD], ALU.max, smb, smb)
                fw.act(sm.t[:, EX], sm.t[:, AXc], AF.Exp, smb, smb, scale=-1.0)
                fw.ts("dve", sm.t[:, EX], sm.t[:, EX], 1.0, None, ALU.add, None, smb, smb)
                fw.act(sm.t[:, LN], sm.t[:, EX], AF.Ln, smb, smb)
                fw.stt("dve", sm.t[:, DT], sm.t[:, XD], 0.0, sm.t[:, LN], ALU.max, ALU.add, smb, smb)
                fw.tt("dve", sm.t[:, LA], sm.t[:, DT], arow.t[:, :], ALU.mult, smb + [arow.b], smb)
                fw.mm(ps[5].t[:, 16:24], triu, sm.t[:, LA], True, True, [cb_, sm.b], [ps[5].b])
                yield
                fw.mm(ps[5].t[:, 32:40], ones, sm.t[:, LA], True, True, [cb_, sm.b], [ps[5].b])
                yield
                fw.cp("dve", sm.t[:, CUM], ps[5].t[:, 16:24], [ps[5].b], smb)
                fw.ts("dve", sm.t[:, NCUM], sm.t[:, CUM], -1.0, None, ALU.mult, None, smb, smb)
                fw.act(sm.t[:, ECUM], sm.t[:, CUM], AF.Exp, smb, smb)
                fw.tt("dve", sm.t[:, WW], ps[5].t[:, 32:40], sm.t[:, CUM], ALU.subtract, [ps[5].b] + smb, smb)
                fw.act(sm.t[:, WW], sm.t[:, WW], AF.Exp, smb, smb)
                fw.act(sm.t[:, EDEC], ps[5].t[:, 32:40], AF.Exp, [ps[5].b], smb)
                fw.tt("dve", sm.t[:, DW], sm.t[:, DT], sm.t[:, WW], ALU.mult, smb, smb)
                for h in range(8):
                    lb_ = laB[h % 2]
                    pst = ps[4] if h < 4 else ps[5]
                    fw.ts("dve" if h % 2 == 0 else "pool", lb_.t[:, :], ones, sm.t[:, 40 + h:41 + h], None, ALU.mult, None, [cb_, sm.b], [lb_.b])
                    hs = slice((h % 4) * 128, (h % 4 + 1) * 128)
                    fw.mm(pst.t[:, hs], lb_.t[:, :], triu, True, False, [lb_.b, cb_], [pst.b])
                    fw.mm(pst.t[:, hs], C.identf.t[:, :], cslice(C, "negm"), False, True, [C.identf.b, cb_], [pst.b])
                    fw.act(decT.t[:, h, :], pst.t[:, hs], AF.Exp, [pst.b, sm.b], [decT.b], bias=sm.t[:, 56 + h:57 + h])
                yield
                for g in range(2):
                    fw.mm(ps[4].t[:, g * 128:(g + 1) * 128], xcb.t[:, 4 + g, :], xcb.t[:, 6 + g, :], True, True, [xcb.b], [ps[4].b])
                yield
                for g in range(2):
                    fw.tt("dve", PT.t[:, 4 * g:4 * g + 4, :], decT.t[:, 4 * g:4 * g + 4, :],
                          ps[4].t[:, g * 128:(g + 1) * 128].unsqueeze(1).to_broadcast([128, 4, 128]), ALU.mult, [decT.b, ps[4].b], [PT.b])
                xsv = xs.t[:, :].rearrange("p (h e) -> p h e", h=8)
                fw.tt("pool", xdt.t[:, :].rearrange("p (h e) -> p h e", h=8), xsv, sm.t[:, DT].unsqueeze(2).to_broadcast([128, 8, 64]),
                      ALU.mult, [xs.b, sm.b], [xdt.b])
                fw.tt("pool", xdtw.t[:, :].rearrange("p (h e) -> p h e", h=8), xsv, sm.t[:, DW].unsqueeze(2).to_broadcast([128, 8, 64]),
                      ALU.mult, [xs.b, sm.b], [xdtw.b])
                for h in range(8):
                    fw.mm(ps[5].t[:, h * 64:(h + 1) * 64], PT.t[:, h, :], xdt.t[:, h * 64:(h + 1) * 64], True, True, [PT.b, xdt.b], [ps[5].b])
                yield
                for g in range(2):
                    fw.mm(ps[4].t[:, g * 256:(g + 1) * 256], xcb.t[:, 6 + g, :], Sb.t[:, g * 256:(g + 1) * 256], True, True, [xcb.b, Sb.b], [ps[4].b])
                yield
                fw.tt("dve", ya.t[:, :].rearrange("p (h e) -> p h e", h=8), ps[4].t[:, :].rearrange("p (h e) -> p h e", h=8),
                      sm.t[:, ECUM].unsqueeze(2).to_broadcast([128, 8, 64]), ALU.mult, [ps[4].b, sm.b], [ya.b])
                fw.tt("dve", ya.t[:, :], ya.t[:, :], ps[5].t[:, :], ALU.add, [ya.b, ps[5].b], [ya.b])
                fw.tt("pool", tmp.t[:, :], xs.t[:, :], dsk.t[:, :], ALU.mult, [xs.b, dsk.b], [tmp.b])
                fw.tt("pool", ya.t[:, :], ya.t[:, :], tmp.t[:, :], ALU.add, [ya.b, tmp.b], [ya.b])
                fw.tt("pool", ya.t[:, :], ya.t[:, :], sz.t[:, :], ALU.mult, [ya.b, sz.b], [ya.b])
                for g in range(2):
                    fw.mm(ps[5].t[:, g * 256:(g + 1) * 256], Btok.t[:, g, :], xdtw.t[:, g * 256:(g + 1) * 256], True, True,
                          [Btok.b, xdtw.b], [ps[5].b])
                yield
                for h in range(8):
                    hs = slice(h * 64, (h + 1) * 64)
                    fw.stt("dve", S.t[:, hs], S.t[:, hs], sm.t[:, 80 + h:81 + h], ps[5].t[:, hs], ALU.mult, ALU.add, [S.b, sm.b, ps[5].b], [S.b])
                fw.cp("pool", Sb.t[:, :], S.t[:, :], [S.b], [Sb.b])
                fw.act(tmp.t[:, :], ya.t[:, :], AF.Square, [ya.b], [tmp.b])
                fw.op("dve", lambda g_: g_.reduce_sum(sm.t[:, 96:98], tmp.t[:, :].rearrange("p (g e) -> p g e", g=2), axis=AX.X), [tmp.b], smb)
                rstd_from_ss(fw, C, sm.t[:, 96:98], sm.t[:, 98:100], 256.0, smb, smb)
                for g in range(2):
                    gs = slice(g * 256, (g + 1) * 256)
                    fw.ts("dve", tmp.t[:, gs], ya.t[:, gs], sm.t[:, 98 + g:99 + g], None, ALU.mult, None, [ya.b, sm.b], [tmp.b])
                fw.tt("pool", yo.t[:, :], tmp.t[:, :], nw.t[:, :], ALU.mult, [tmp.b, nw.b], [yo.b])
                emit_yT(fw, C, yo, yT, 2, i)
                yield
        for _ in stA(0):
            pass
        for i in range(NT):
            interleave(stB(i), stA(i + 1) if i + 1 < NT else None)
        fw.barrier()


def phase_hg(fw, C, l):
    CUR_PHASE[0] = "hg"
    with ExitStack() as ph:
        WH = fw.tile(ph, "WH", [128, 8, 2048], BF16)
        load_w(fw, C.w_in[l], WH, C_HG, C_HG + 2048, 8)
        nw = fw.tile(ph, "hnw", [128, 512], F32)
        fw.dma("sp", nw.t[:, :], C.hgrn_norm_w[l].partition_broadcast(128), writes=[nw.b])
        S = fw.tile(ph, "hS", [128, 4, 128], F32)
        Sb = fw.tile(ph, "hSb", [128, 4, 128], BF16)
        PT_2 = [fw.tile(ph, "hPT", [128, 4, 128], BF16) for _ in range(2)]
        fw.memset("dve", S.t[:, :, :], 0.0, [S.b])
        fw.memset("dve", Sb.t[:, :, :], 0.0, [Sb.b])
        for PT in PT_2:
            fw.memset("pool", PT.t[:, :, :], 0.0, [PT.b])

        qg_2 = [fw.tile(ph, "hqg", [128, 4, 512], F32) for _ in range(2)]
        fg_2 = [fw.tile(ph, "hfg", [128, 4, 512], F32) for _ in range(2)]
        la_2 = [fw.tile(ph, "hla", [128, 4, 128], F32) for _ in range(2)]
        kT_2 = [fw.tile(ph, "hk", [128, 4, 128], F32) for _ in range(2)]
        cum_2 = [fw.tile(ph, "hcum", [128, 4, 128], F32) for _ in range(2)]
        ncb_2 = [fw.tile(ph, "hncb", [128, 4, 4], F32) for _ in range(2)]
        for nb_ in ncb_2:
            fw.memset("pool", nb_.t[:, :, :], 0.0, [nb_.b])
        eq_2 = [fw.tile(ph, "heq", [128, 4, 128], F32) for _ in range(2)]
        qd_2 = [fw.tile(ph, "hqd", [128, 4, 128], BF16) for _ in range(2)]
        qst_2 = [fw.tile(ph, "hqst", [128, 4, 128], BF16) for _ in range(2)]
        ek_2 = [fw.tile(ph, "hek", [128, 4, 128], F32) for _ in range(2)]
        Kt_2 = [fw.tile(ph, "hKt", [128, 4, 4, 128], BF16) for _ in range(2)]
        khT_2 = [fw.tile(ph, "hkhT", [128, 4, 128], BF16) for _ in range(2)]
        khat_2 = [fw.tile(ph, "hkhat", [128, 4, 128], BF16) for _ in range(2)]
        dec_2 = [fw.tile(ph, "hdec", [128, 4], F32) for _ in range(2)]
        vb_3 = [fw.tile(ph, "hvb", [128, 512], BF16) for _ in range(3)]
        sgate_3 = [fw.tile(ph, "hsg", [128, 512], F32) for _ in range(3)]
        ysq_2 = [fw.tile(ph, "hysq", [128, 512], F32) for _ in range(2)]
        st_2 = [fw.tile(ph, "hst", [128, 8], F32) for _ in range(2)]
        yn_2 = [fw.tile(ph, "hyn", [128, 512], F32) for _ in range(2)]
        yo_2 = [fw.tile(ph, "hyo", [128, 512], BF16) for _ in range(2)]
        yT_2 = [fw.tile(ph, "hyT", [128, 4, 128], BF16) for _ in range(2)]
        ps = C.ps
        pb0, pb1 = C.pb
        cb_ = C.consts.b
        lbc = C.lb_all.t[:, l, :]
        omc = C.oml_all.t[:, l, :]
        ones = cslice(C, "ones")
        triu = cslice(C, "triu")
        def stG(g):
                t0 = g * 512
                n = min(512, TP - t0)
                tiles = list(range(t0 // 128, (t0 + n) // 128))
                qg, fg = qg_2[g % 2], fg_2[g % 2]
                k = 0
                for h in range(4):
                    for (c0, dst, func) in ((0, qg, AF.Silu), (512, fg, AF.Sigmoid)):
                        pst = ps[0]
                        for kc in range(8):
                            fw.mm(pst.t[:, 0:n], WH.t[:, kc, c0 + h * 128:c0 + (h + 1) * 128], C.uT.t[:, kc, t0:t0 + n], kc == 0, kc == 7,
                                  [WH.b] + [C.uTb[j] for j in tiles], [pst.b])
                        fw.act(dst.t[:, h, 0:n], pst.t[:, 0:n], func, [pst.b], [dst.b])
                        yield
        def stA(i):
                PT = PT_2[i % 2]
                la, kT, cum, ncb, eq, qd, qst, ek, Kt, khT, khat, dec, vb, sgate, ysq, st, yn, yo, yT = la_2[i % 2], kT_2[i % 2], cum_2[i % 2], ncb_2[i % 2], eq_2[i % 2], qd_2[i % 2], qst_2[i % 2], ek_2[i % 2], Kt_2[i % 2], khT_2[i % 2], khat_2[i % 2], dec_2[i % 2], vb_3[i % 3], sgate_3[i % 3], ysq_2[i % 2], st_2[i % 2], yn_2[i % 2], yo_2[i % 2], yT_2[i % 2]
                tok_mm(fw, C, ps[2], WH, 1024, 512, i)
                yield
                tok_mm(fw, C, ps[3], WH, 1536, 512, i)
                yield
                fw.cp("act", vb.t[:, :], ps[2].t[:, :], [ps[2].b], [vb.b])
                yield
                fw.act(sgate.t[:, :], ps[3].t[:, :], AF.Silu, [ps[3].b], [sgate.b])
                yield
        def stB1(i):
                PT = PT_2[i % 2]
                la, kT, cum, ncb, eq, qd, qst, ek, Kt, khT, khat, dec, vb, sgate, ysq, st, yn, yo, yT = la_2[i % 2], kT_2[i % 2], cum_2[i % 2], ncb_2[i % 2], eq_2[i % 2], qd_2[i % 2], qst_2[i % 2], ek_2[i % 2], Kt_2[i % 2], khT_2[i % 2], khat_2[i % 2], dec_2[i % 2], vb_3[i % 3], sgate_3[i % 3], ysq_2[i % 2], st_2[i % 2], yn_2[i % 2], yo_2[i % 2], yT_2[i % 2]
                qT = VW(qg_2[(i // 4) % 2].t[:, :, (i % 4) * 128:(i % 4 + 1) * 128], qg_2[(i // 4) % 2].b)
                fT = VW(fg_2[(i // 4) % 2].t[:, :, (i % 4) * 128:(i % 4 + 1) * 128], fg_2[(i // 4) % 2].b)
                for h in range(4):
                    fw.ts("dve", fT.t[:, h, :], fT.t[:, h, :], omc[:, h:h + 1], lbc[:, h:h + 1], ALU.mult, ALU.add,
                          [fT.b, C.lb_all.b, C.oml_all.b], [fT.b])
                yield
                fw.act(la.t[:, :, :], fT.t[:, :, :], AF.Ln, [fT.b], [la.b])
                yield
                fw.ts("pool", kT.t[:, :, :], fT.t[:, :, :], -1.0, 1.0, ALU.mult, ALU.add, [fT.b], [kT.b])
                yield
                for h in range(4):
                    fw.op("dve", lambda g: g.tensor_tensor_scan(cum.t[:, h, :], ones, la.t[:, h, :], 0.0, ALU.mult, ALU.add),
                          [la.b, cb_], [cum.b])
                yield
                cv = cum.t[:, :, :].rearrange("p h (b c) -> p h b c", c=32)
                fw.ts("dve", ncb.t[:, :, 1:4], cv[:, :, 0:3, 31], -1.0, None, ALU.mult, None, [cum.b], [ncb.b])
                yield
                fw.tt("dve", eq.t[:, :, :].rearrange("p h (b c) -> p h b c", c=32), cv,
                      ncb.t[:, :, :].unsqueeze(3).to_broadcast([128, 4, 4, 32]), ALU.add, [cum.b, ncb.b], [eq.b])
                yield
                fw.act(eq.t[:, :, :], eq.t[:, :, :], AF.Exp, [eq.b], [eq.b])
                yield
                fw.tt("pool", qd.t[:, :, :], qT.t[:, :, :], eq.t[:, :, :], ALU.mult, [qT.b, eq.b], [qd.b])
                yield
                fw.act(eq.t[:, :, :], cum.t[:, :, :], AF.Exp, [cum.b], [eq.b])
                yield
                fw.tt("pool", qst.t[:, :, :], qT.t[:, :, :], eq.t[:, :, :], ALU.mult, [qT.b, eq.b], [qst.b])
                yield
                for b in range(4):
                    W_ = 32 * (b + 1)
                    eb = ek if b % 2 == 0 else eq
                    fw.tt("dve", eb.t[:, :, 0:W_], cum.t[:, :, 0:W_], ncb.t[:, :, b:b + 1].to_broadcast([128, 4, W_]), ALU.add,
                          [cum.b, ncb.b], [eb.b])
                    fw.act(eb.t[:, :, 0:W_], eb.t[:, :, 0:W_], AF.Exp, [eb.b], [eb.b], scale=-1.0)
                    fw.tt("pool" if b % 2 == 0 else "dve", Kt.t[:, :, b, 0:W_], eb.t[:, :, 0:W_], kT.t[:, :, 0:W_], ALU.mult,
                          [eb.b, kT.b], [Kt.b])
                    yield
                fw.tt("dve", ek.t[:, :, :], cum.t[:, :, 127:128].to_broadcast([128, 4, 128]), cum.t[:, :, :], ALU.subtract, [cum.b], [ek.b])
                yield
                fw.act(ek.t[:, :, :], ek.t[:, :, :], AF.Exp, [ek.b], [ek.b])
                yield
                fw.tt("pool", khT.t[:, :, :], ek.t[:, :, :], kT.t[:, :, :], ALU.mult, [ek.b, kT.b], [khT.b])
                yield
                for h in range(4):
                    fw.tr(pb0.t[:, h * 128:(h + 1) * 128], khT.t[:, h, :], C.identb.t[:, :], [khT.b, C.identb.b], [pb0.b])
                yield
                fw.cp("act", khat.t[:, :, :], pb0.t[:, 0:512].rearrange("p (h d) -> p h d", h=4), [pb0.b], [khat.b])
                yield
                fw.act(dec.t[:, :], cum.t[:, :, 127], AF.Exp, [cum.b], [dec.b])
                yield
                for h in range(4):
                    for b in range(4):
                        W_ = 32 * (b + 1)
                        fw.mm(ps[4].t[0:W_, h * 128 + 32 * b:h * 128 + 32 * b + 32], Kt.t[:, h, b, 0:W_], qd.t[:, h, 32 * b:32 * b + 32],
                              True, True, [Kt.b, qd.b], [ps[4].b])
                yield
                psv = ps[4].t[:, :].rearrange("p (h t) -> p h t", h=4)
                for b in range(4):
                    W_ = 32 * (b + 1)
                    bs = slice(32 * b, 32 * b + 32)
                    to, tn = CO["triu"]
                    mk = C.consts.t[0:W_, to + 32 * b:to + 32 * b + 32].unsqueeze(1).to_broadcast([W_, 4, 32])
                    fw.tt("dve", PT.t[0:W_, :, bs], psv[0:W_, :, bs], mk, ALU.mult, [ps[4].b, cb_], [PT.b])
                yield
        def stB2(i):
                PT = PT_2[i % 2]
                la, kT, cum, ncb, eq, qd, qst, ek, Kt, khT, khat, dec, vb, sgate, ysq, st, yn, yo, yT = la_2[i % 2], kT_2[i % 2], cum_2[i % 2], ncb_2[i % 2], eq_2[i % 2], qd_2[i % 2], qst_2[i % 2], ek_2[i % 2], Kt_2[i % 2], khT_2[i % 2], khat_2[i % 2], dec_2[i % 2], vb_3[i % 3], sgate_3[i % 3], ysq_2[i % 2], st_2[i % 2], yn_2[i % 2], yo_2[i % 2], yT_2[i % 2]
                qT = VW(qg_2[(i // 4) % 2].t[:, :, (i % 4) * 128:(i % 4 + 1) * 128], qg_2[(i // 4) % 2].b)
                fT = VW(fg_2[(i // 4) % 2].t[:, :, (i % 4) * 128:(i % 4 + 1) * 128], fg_2[(i // 4) % 2].b)
                for h in range(4):
                    hs = slice(h * 128, (h + 1) * 128)
                    fw.mm(ps[5].t[:, hs], PT.t[:, h, :], vb.t[:, hs], True, False, [PT.b, vb.b], [ps[5].b])
                    fw.mm(ps[5].t[:, hs], qst.t[:, h, :], Sb.t[:, h, :], False, True, [qst.b, Sb.b], [ps[5].b])
                yield
                for h in range(4):
                    hs = slice(h * 128, (h + 1) * 128)
                    fw.mm(ps[1].t[:, hs], khat.t[:, h, :], vb.t[:, hs], True, True, [khat.b, vb.b], [ps[1].b])
                yield
                for h in range(4):
                    hs = slice(h * 128, (h + 1) * 128)
                    fw.stt("dve", S.t[:, h, :], S.t[:, h, :], dec.t[:, h:h + 1], ps[1].t[:, hs], ALU.mult, ALU.add, [S.b, dec.b, ps[1].b], [S.b])
                yield
                fw.cp("pool", Sb.t[:, :, :], S.t[:, :, :], [S.b], [Sb.b])
                yield
                fw.act(ysq.t[:, :], ps[5].t[:, :], AF.Square, [ps[5].b], [ysq.b])
                yield
                fw.op("dve", lambda g: g.reduce_sum(st.t[:, 0:4], ysq.t[:, :].rearrange("p (h e) -> p h e", h=4), axis=AX.X), [ysq.b], [st.b])
                yield
                rstd_from_ss(fw, C, st.t[:, 0:4], st.t[:, 4:8], 128.0, [st.b], [st.b])
                yield
                for h in range(4):
                    hs = slice(h * 128, (h + 1) * 128)
                    fw.ts("dve", yn.t[:, hs], ps[5].t[:, hs], st.t[:, 4 + h:5 + h], None, ALU.mult, None, [ps[5].b, st.b], [yn.b])
                yield
                fw.tt("pool", yn.t[:, :], yn.t[:, :], nw.t[:, :], ALU.mult, [yn.b, nw.b], [yn.b])
                yield
                fw.tt("pool", yo.t[:, :], yn.t[:, :], sgate.t[:, :], ALU.mult, [yn.b, sgate.b], [yo.b])
                yield
                emit_yT(fw, C, yo, yT, 3, i)
                yield
        for st0 in (stG(0), stA(0), stB1(0), stA(1) if NT > 1 else None):
            if st0 is not None:
                for _ in st0:
                    pass
        for i in range(NT):
            gens = [stB2(i)]
            if i + 1 < NT:
                gens.append(stB1(i + 1))
            if i + 2 < NT:
                gens.append(chain_gens(stG((i + 2) // 4) if (i + 2) % 4 == 0 else None, stA(i + 2)))
            interleave_n(gens)
        fw.barrier()


TGS = [(0, 512), (512, 512), (1024, 512), (1536, 512), (2048, 128)]


def phase_s5(fw, C, l):
    ps = C.ps
    pb0, pb1 = C.pb
    cb_ = C.consts.b
    with ExitStack() as ph:
        GW = fw.tile(ph, "GW", [128, 4, 1024], BF16)
        load_w(fw, C.s5_glu_w[l], GW, 0, 1024, 4)
        Cre = fw.tile(ph, "Cre", [128, 16, 128], BF16)
        nCim = fw.tile(ph, "nCim", [128, 16, 128], BF16)
        fw.dma("pool", Cre.t[:, :, :], C.s5_cre[l], writes=[Cre.b])
        fw.dma("pool", nCim.t[:, :, :], C.s5_cim[l], writes=[nCim.b])
        fw.ts("pool", nCim.t[:, :, :], nCim.t[:, :, :], -1.0, None, ALU.mult, None, [nCim.b], [nCim.b])
        d5 = fw.tile(ph, "d5", [128, 4], F32)
        gb = fw.tile(ph, "gb5", [128, 8], F32)
        fw.dma("sp", d5.t[:, :], C.s5_dT[l], writes=[d5.b])
        fw.dma("sp", gb.t[:, :], C.s5_gbT[l], writes=[gb.b])
        sp_ = fw.tile(ph, "s5sm", [128, 48], F32)
        fw.dma("sp", sp_.t[:, :], C.s5_small[l], writes=[sp_.b])
        u5T = fw.tile(ph, "u5T", [128, 4, TP], BF16)
        Sall = fw.tile(ph, "Sall", [128, 137, 3, 16], F32)
        KT = fw.tile(ph, "KT", [128, 4, 16, 128], BF16)
        PW = fw.tile(ph, "PW", [128, 2, 17, 16], F32)
        wk = fw.tile(ph, "s5wk", [128, 12, 16], F32)
        with ExitStack() as pa:
            W5 = fw.tile(pa, "W5", [128, 8, 512], BF16)
            load_w(fw, C.w_in[l], W5, C_S5, C_S5 + 512, 8)
            k = 0
            for ct in range(4):
                for (t0, n) in TGS:
                    pst = ps[k % 2]
                    for kc in range(8):
                        fw.mm(pst.t[:, 0:n], W5.t[:, kc, ct * 128:(ct + 1) * 128], C.uT.t[:, kc, t0:t0 + n], kc == 0, kc == 7,
                              [W5.b] + C.uTb[t0 // 128:(t0 + n) // 128], [pst.b])
                    fw.cp("act" if k % 2 else "dve", u5T.t[:, ct, t0:t0 + n], pst.t[:, 0:n], [pst.b], [u5T.b])
                    k += 1
            fw.barrier()
        if os.environ.get("S5_STOP") == "A":
            fw.barrier(); return
        lr, li, lst = sp_.t[:, 0:16], sp_.t[:, 16:32], sp_.t[:, 32:48]
        W_ = [wk.t[:, j, :] for j in range(12)]
        R = [sp_.b, wk.b, PW.b]
        step, lrs, ang, em1, re_, im_, t_a, t_b, inv, co_re, co_im, rr = W_
        big = [fw.tile(ph, "s5big", [128, 18, 16], F32) for _ in range(5)]
        bigi = fw.tile(ph, "s5bigi", [128, 18, 16], mybir.dt.int32)
        RB = R + [b_.b for b_ in big] + [bigi.b, cb_]
        FACT = [1.0, 1.0, 2.0, 6.0, 24.0, 120.0, 720.0, 5040.0, 40320.0, 362880.0, 3628800.0]

        def horner_exp(out, r, deg, minus1=False):
            fw.ts("dve", out, r, 1.0 / FACT[deg], None, ALU.mult, None, RB, RB)
            for j in range(deg - 1, 0, -1):
                fw.stt("dve", out, out, 1.0 / FACT[j], r, ALU.add, ALU.mult, RB, RB)
            if not minus1:
                fw.ts("dve", out, out, 1.0, None, ALU.add, None, RB, RB)

        fw.ts("dve", rr, lst, 0.125, None, ALU.mult, None, RB, RB)
        horner_exp(step, rr, 10)
        for _ in range(3):
            fw.tt("dve", step, step, step, ALU.mult, RB, RB)
        fw.tt("dve", lrs, lr, step, ALU.mult, RB, RB)
        fw.tt("dve", ang, li, step, ALU.mult, RB, RB)
        mo = CO["mtab"][0]
        mtab = C.consts.t[:, mo:mo + 288].rearrange("p (m q) -> p m q", m=18)
        TH, XM, MAG, SN, CS = [b_.t[:, :, :] for b_ in big]
        fw.tt("dve", TH, mtab, ang.unsqueeze(1).to_broadcast([128, 18, 16]), ALU.mult, RB, RB)
        fw.tt("dve", XM, mtab, lrs.unsqueeze(1).to_broadcast([128, 18, 16]), ALU.mult, RB, RB)
        horner_exp(MAG, XM, 10)
        C1, C2 = 6.28125, 2.0 * math.pi - 6.28125

        def sin_reduced(out, th):
            fw.ts("dve", out, th, 1.0 / (2.0 * math.pi), None, ALU.mult, None, RB, RB)
            fw.cp("dve", bigi.t[:, :, :], out, RB, RB)
            fw.cp("dve", XM, bigi.t[:, :, :], RB, RB)
            fw.stt("dve", out, XM, -C1, th, ALU.mult, ALU.add, RB, RB)
            fw.stt("dve", out, XM, -C2, out, ALU.mult, ALU.add, RB, RB)
            fw.act(out, out, AF.Sin, RB, RB)

        sin_reduced(SN, TH)
        fw.ts("dve", TH, TH, math.pi / 2, None, ALU.add, None, RB, RB)
        sin_reduced(CS, TH)
        fw.tt("dve", PW.t[:, 0, :, :], MAG[:, 0:17, :], CS[:, 0:17, :], ALU.mult, RB, RB)
        fw.tt("dve", PW.t[:, 1, :, :], MAG[:, 0:17, :], SN[:, 0:17, :], ALU.mult, RB, RB)
        horner_exp(em1, lrs, 7, minus1=True)
        fw.tt("dve", re_, em1, CS[:, 1, :], ALU.mult, RB, RB)
        fw.tt("dve", t_a, SN[:, 17, :], SN[:, 17, :], ALU.mult, RB, RB)
        fw.stt("dve", re_, t_a, -2.0, re_, ALU.mult, ALU.add, RB, RB)
        fw.ts("dve", t_b, em1, 1.0, None, ALU.add, None, RB, RB)
        fw.tt("dve", im_, t_b, SN[:, 1, :], ALU.mult, RB, RB)
        fw.tt("dve", t_a, lr, lr, ALU.mult, RB, RB)
        fw.tt("dve", t_b, li, li, ALU.mult, RB, RB)
        fw.tt("dve", inv, t_a, t_b, ALU.add, RB, RB)
        fw.op("dve", lambda g: g.reciprocal(inv, inv), RB, RB)
        fw.tt("dve", t_a, re_, lr, ALU.mult, RB, RB)
        fw.tt("dve", t_b, im_, li, ALU.mult, RB, RB)
        fw.tt("dve", t_a, t_a, t_b, ALU.add, RB, RB)
        fw.tt("dve", co_re, t_a, inv, ALU.mult, RB, RB)
        fw.tt("dve", t_a, im_, lr, ALU.mult, RB, RB)
        fw.tt("dve", t_b, re_, li, ALU.mult, RB, RB)
        fw.tt("dve", t_a, t_a, t_b, ALU.subtract, RB, RB)
        fw.tt("dve", co_im, t_a, inv, ALU.mult, RB, RB)
        if C.debug and l == 0:
            fw.dma("sp", C.dbg5[:, 0:192], wk.t[:, :, :].rearrange("p a b -> p (a b)"), reads=[wk.b], writes=[Buf()])
            fw.dma("sp", C.dbg5[:, 192:736], PW.t[:, :, :, :].rearrange("p a m q -> p (a m q)"), reads=[PW.b], writes=[Buf()])
        if os.environ.get("S5_STOP") == "B":
            fw.barrier(); return
        fw.memset("pool", Sall.t[:, 0, :, :], 0.0, [Sall.b])
        with ExitStack() as pd:
            Bst = fw.tile(pd, "Bst", [128, 2, 4, 128], F32)
            Bb = fw.tile(pd, "Bb", [128, 2, 4, 128], F32)
            t0_ = fw.tile(pd, "tB", [128, 128], F32)
            tA = [fw.tile(pd, "tA", [128, 16, 128], F32)] * 2
            tB = [fw.tile(pd, "tBB", [128, 16, 128], F32)] * 2
            Xs = [fw.tile(pd, "X", [128, 2, 16, 128], BF16) for _ in range(2)]
            XTs = [fw.tile(pd, "XT", [128, 4, 2, 128], BF16) for _ in range(2)]
            psK = ps[2:6]
            zt = fw.tile(pd, "zt", [128, 512], BF16)
            fw.memset("pool", zt.t[:, :], 0.0, [zt.b])
            it = 0
            for ct in range(4):
                for j in range(4):
                    fw.mm(psK[j].t[:, :], zt.t[:, 0:128], zt.t[:, :], True, False, [zt.b], [psK[j].b])
                fw.dma("sp", Bst.t[:, 0, :, :], C.s5_bre[l][:, 4 * ct:4 * ct + 4, :], writes=[Bst.b])
                fw.dma("sp", Bst.t[:, 1, :, :], C.s5_bim[l][:, 4 * ct:4 * ct + 4, :], writes=[Bst.b])
                for pl in range(4):
                    pair = 4 * ct + pl
                    cr, ci = co_re[:, pair:pair + 1], co_im[:, pair:pair + 1]
                    fw.ts("dve", t0_.t[:, :], Bst.t[:, 1, pl, :], ci, None, ALU.mult, None, [Bst.b, wk.b], [t0_.b])
                    fw.stt("dve", Bb.t[:, 0, pl, :], Bst.t[:, 0, pl, :], cr, t0_.t[:, :], ALU.mult, ALU.subtract, [Bst.b, wk.b, t0_.b], [Bb.b])
                    fw.ts("dve", t0_.t[:, :], Bst.t[:, 1, pl, :], cr, None, ALU.mult, None, [Bst.b, wk.b], [t0_.b])
                    fw.stt("dve", Bb.t[:, 1, pl, :], Bst.t[:, 0, pl, :], ci, t0_.t[:, :], ALU.mult, ALU.add, [Bst.b, wk.b, t0_.b], [Bb.b])
                for pl in range(4):
                    pair = 4 * ct + pl
                    psG = ps[pair % 2]
                    X = Xs[pair % 2]
                    ta, tb = tA[pair % 2], tB[pair % 2]
                    bre = Bb.t[:, 0, pl, :].unsqueeze(1).to_broadcast([128, 16, 128])
                    bim = Bb.t[:, 1, pl, :].unsqueeze(1).to_broadcast([128, 16, 128])
                    prb = PW.t[:, 0, 0:16, pair].unsqueeze(2).to_broadcast([128, 16, 128])
                    pib = PW.t[:, 1, 0:16, pair].unsqueeze(2).to_broadcast([128, 16, 128])
                    RB_ = [Bb.b, PW.b]
                    fw.tt("dve", ta.t[:, :, :], bre, prb, ALU.mult, RB_, [ta.b])
                    fw.tt("pool", tb.t[:, :, :], bim, pib, ALU.mult, RB_, [tb.b])
                    fw.tt("dve", X.t[:, 0, :, :], ta.t[:, :, :], tb.t[:, :, :], ALU.subtract, [ta.b, tb.b], [X.b])
                    fw.tt("pool", tb.t[:, :, :], bim, prb, ALU.mult, RB_, [tb.b])
                    fw.tt("dve", ta.t[:, :, :], bre, pib, ALU.mult, RB_, [ta.b])
                    fw.tt("pool", X.t[:, 1, :, :], ta.t[:, :, :], tb.t[:, :, :], ALU.add, [ta.b, tb.b], [X.b])
                    fw.mm(psG.t[:, 0:272], zt.t[:, 0:128], zt.t[:, 0:272], True, False, [zt.b], [psG.b])
                    for m in range(16):
                        pk = psK[m // 4]
                        ks = slice((m % 4) * 128, (m % 4 + 1) * 128)
                        fw.mm(pk.t[:, ks], X.t[:, 0, m, :], Cre.t[:, pair, :], False, False, [X.b, Cre.b], [pk.b])
                        fw.mm(pk.t[:, ks], X.t[:, 1, m, :], nCim.t[:, pair, :], False, pl == 3, [X.b, nCim.b], [pk.b])
                    for mg in range(4):
                        XT = XTs[it % 2]
                        pbt = C.pb[it % 2]
                        for mm_ in range(4):
                            m = 4 * mg + mm_
                            for part in range(2):
                                fw.tr(pbt.t[:, (2 * mm_ + part) * 128:(2 * mm_ + part + 1) * 128], X.t[:, part, m, :], C.identb.t[:, :],
                                      [X.b, C.identb.b], [pbt.b])
                        fw.cp("act" if it % 2 else "dve", XT.t[:, :, :, :], pbt.t[:, :].rearrange("p (m a q) -> p m a q", m=4, a=2),
                              [pbt.b], [XT.b])
                        for mm_ in range(4):
                            m = 4 * mg + mm_
                            tau = 15 - m
                            rhs = u5T.t[:, ct, :].rearrange("p (c b) -> p c b", b=16)[:, :, tau]
                            fw.mm(psG.t[:, 0:136], XT.t[:, mm_, 0, :], rhs, False, m == 15, [XT.b, u5T.b], [psG.b])
                            fw.mm(psG.t[:, 136:272], XT.t[:, mm_, 1, :], rhs, False, m == 15, [XT.b, u5T.b], [psG.b])
                        it += 1
                    fw.cp("act", Sall.t[:, 1:137, 0, pair], psG.t[:, 0:136], [psG.b], [Sall.b])
                    fw.cp("act", Sall.t[:, 1:137, 1, pair], psG.t[:, 136:272], [psG.b], [Sall.b])
                for j in range(4):
                    fw.cp("act" if j % 2 else "dve", KT.t[:, ct, 4 * j:4 * j + 4, :], psK[j].t[:, :].rearrange("p (m c) -> p m c", m=4),
                          [psK[j].b], [KT.b])
            fw.barrier()
        if C.debug and l == 0:
            fw.dma("pool", C.dbg5[:, 736:736 + 2048], KT.t[:, 0, :, :].rearrange("p m c -> p (m c)"), reads=[KT.b], writes=[Buf()])
            fw.dma("sp", C.dbg5[:, 2784:2784 + 137 * 48], Sall.t[:, :, :, :].rearrange("p c a q -> p (c a q)"), reads=[Sall.b], writes=[Buf()])
        if os.environ.get("S5_STOP") == "D":
            fw.barrier(); return
        with ExitStack() as pe_:
            A1 = fw.tile(pe_, "A1", [128, 2, 16], F32)
            A2 = fw.tile(pe_, "A2", [128, 2, 16], F32)
            p1 = fw.tile(pe_, "p1", [128, 2, 16], F32)
            p2 = fw.tile(pe_, "p2", [128, 2, 16], F32)
            fw.cp("dve", A1.t[:, 0, :], PW.t[:, 0, 16, :], [PW.b], [A1.b])
            fw.cp("dve", A1.t[:, 1, :], PW.t[:, 0, 16, :], [PW.b], [A1.b])
            fw.ts("dve", A2.t[:, 0, :], PW.t[:, 1, 16, :], -1.0, None, ALU.mult, None, [PW.b], [A2.b])
            fw.cp("dve", A2.t[:, 1, :], PW.t[:, 1, 16, :], [PW.b], [A2.b])
            for c in range(136):
                fw.tt("dve", p1.t[:, :, :], A1.t[:, :, :], Sall.t[:, c, 0:2, :], ALU.mult, [A1.b, Sall.b], [p1.b])
                fw.tt("dve", p2.t[:, :, :], A2.t[:, :, :], Sall.t[:, c, 1:3, :], ALU.mult, [A2.b, Sall.b], [p2.b])
                fw.tt("dve", p1.t[:, :, :], p1.t[:, :, :], p2.t[:, :, :], ALU.add, [p1.b, p2.b], [p1.b])
                fw.tt("dve", Sall.t[:, c + 1, 0:2, :], Sall.t[:, c + 1, 0:2, :], p1.t[:, :, :], ALU.add, [Sall.b, p1.b], [Sall.b])
                fw.cp("dve", Sall.t[:, c + 1, 2, :], Sall.t[:, c + 1, 0, :], [Sall.b], [Sall.b])
            fw.barrier()
        if os.environ.get("S5_STOP") == "E":
            fw.barrier(); return
        pfg = ExitStack()
        gT = fw.tile(pfg, "g5T", [128, 4, TP], BF16)
        with ExitStack() as pf:
            SP = fw.tile(pf, "SP", [128, 4, 2, 136, 16], BF16)
            u1 = fw.tile(pf, "u1", [128, 136, 16], F32)
            u2 = fw.tile(pf, "u2", [128, 136, 16], F32)
            z = fw.tile(pf, "z5", [128, 512], F32)
            z2 = fw.tile(pf, "z52", [128, 512], F32)
            k = 0
            for ct in range(4):
                for pl in range(4):
                    pair = 4 * ct + pl
                    srb = Sall.t[:, 0:136, 0, pair].unsqueeze(2).to_broadcast([128, 136, 16])
                    sib = Sall.t[:, 0:136, 1, pair].unsqueeze(2).to_broadcast([128, 136, 16])
                    prb = PW.t[:, 0, 1:17, pair].unsqueeze(1).to_broadcast([128, 136, 16])
                    pib = PW.t[:, 1, 1:17, pair].unsqueeze(1).to_broadcast([128, 136, 16])
                    RS = [Sall.b, PW.b]
                    fw.tt("dve", u1.t[:, :, :], srb, prb, ALU.mult, RS, [u1.b])
                    fw.tt("pool", u2.t[:, :, :], sib, pib, ALU.mult, RS, [u2.b])
                    fw.tt("dve", SP.t[:, pl, 0, :, :], u1.t[:, :, :], u2.t[:, :, :], ALU.subtract, [u1.b, u2.b], [SP.b])
                    fw.tt("pool", u2.t[:, :, :], sib, prb, ALU.mult, RS, [u2.b])
                    fw.tt("dve", u1.t[:, :, :], srb, pib, ALU.mult, RS, [u1.b])
                    fw.tt("pool", SP.t[:, pl, 1, :, :], u1.t[:, :, :], u2.t[:, :, :], ALU.add, [u1.b, u2.b], [SP.b])
                for (t0, n) in TGS:
                    c0, nch = t0 // 16, n // 16
                    pst = ps[k % 2]
                    k += 1
                    pv = pst.t[:, 0:n].rearrange("p (c b) -> p c b", b=16)
                    uv = u5T.t[:, ct, t0:t0 + n].rearrange("p (c b) -> p c b", b=16)
                    for tau in range(16):
                        fw.mm(pv[:, :, tau:16], KT.t[:, ct, tau, :], uv[:, :, 0:16 - tau], tau == 0, False, [KT.b, u5T.b], [pst.b])
                    for pl in range(4):
                        pair = 4 * ct + pl
                        fw.mm(pst.t[:, 0:n], Cre.t[:, pair, :], SP.t[:, pl, 0, c0:c0 + nch, :].rearrange("p c b -> p (c b)"), False, False,
                              [Cre.b, SP.b], [pst.b])
                        fw.mm(pst.t[:, 0:n], nCim.t[:, pair, :], SP.t[:, pl, 1, c0:c0 + nch, :].rearrange("p c b -> p (c b)"), False, pl == 3,
                              [nCim.b, SP.b], [pst.b])
                    fw.stt("dve", z.t[:, 0:n], u5T.t[:, ct, t0:t0 + n], d5.t[:, ct:ct + 1], pst.t[:, 0:n], ALU.mult, ALU.add,
                           [u5T.b, d5.b, pst.b], [z.b])
                    fw.tt("pool", z2.t[:, 0:n], z.t[:, 0:n], z.t[:, 0:n], ALU.mult, [z.b], [z2.b])
                    fw.ts("pool", z2.t[:, 0:n], z2.t[:, 0:n], 0.044715, 1.0, ALU.mult, ALU.add, [z2.b], [z2.b])
                    fw.tt("pool", z2.t[:, 0:n], z2.t[:, 0:n], z.t[:, 0:n], ALU.mult, [z2.b, z.b], [z2.b])
                    fw.act(z2.t[:, 0:n], z2.t[:, 0:n], AF.Sigmoid, [z2.b], [z2.b], scale=2.0 * math.sqrt(2.0 / math.pi))
                    fw.tt("pool", gT.t[:, ct, t0:t0 + n], z.t[:, 0:n], z2.t[:, 0:n], ALU.mult, [z.b, z2.b], [gT.b])
            fw.barrier()
        if os.environ.get("S5_STOP") == "F":
            pfg.close(); fw.barrier(); return
        with ExitStack() as pg:
            sgs = [fw.tile(pg, "sg5", [128, 512], F32) for _ in range(2)]
            yos = [fw.tile(pg, "yo5", [128, 512], BF16) for _ in range(2)]
            k = 0
            for nb in range(4):
                for (t0, n) in TGS:
                    pa_, pg_ = ps[2 + 2 * (k % 2)], ps[3 + 2 * (k % 2)]
                    sg, yo = sgs[k % 2], yos[k % 2]
                    k += 1
                    for kc in range(4):
                        fw.mm(pa_.t[:, 0:n], GW.t[:, kc, nb * 128:(nb + 1) * 128], gT.t[:, kc, t0:t0 + n], kc == 0, kc == 3, [GW.b, gT.b], [pa_.b])
                    for kc in range(4):
                        fw.mm(pg_.t[:, 0:n], GW.t[:, kc, 512 + nb * 128:512 + (nb + 1) * 128], gT.t[:, kc, t0:t0 + n], kc == 0, kc == 3,
                              [GW.b, gT.b], [pg_.b])
                    fw.act(sg.t[:, 0:n], pg_.t[:, 0:n], AF.Sigmoid, [pg_.b, gb.b], [sg.b], bias=gb.t[:, 4 + nb:5 + nb])
                    fw.stt("dve", yo.t[:, 0:n], pa_.t[:, 0:n], gb.t[:, nb:nb + 1], sg.t[:, 0:n], ALU.add, ALU.mult, [pa_.b, gb.b, sg.b], [yo.b])
                    fw.dma("sp", C.YT[4 + nb, :, t0:t0 + n], yo.t[:, 0:n], reads=[yo.b], writes=C.YTb[1][t0 // 128:(t0 + n) // 128])
            fw.barrier()
        pfg.close()
        fw.barrier()


MGS = [(g * 256, min(256, TP - g * 256)) for g in range((TP + 255) // 256)]


def rms_epilogue(fw, C, psA, psB, nw, xt, wk2):
    junk, ss, tmp = wk2
    fw.act(junk.t[:, 0:512], psA.t[:, :], AF.Square, [psA.b], [junk.b, ss.b], accum=ss.t[:, 2:3])
    fw.act(junk.t[:, 512:1024], psB.t[:, :], AF.Square, [psB.b], [junk.b, ss.b], accum=ss.t[:, 3:4])
    fw.tt("dve", ss.t[:, 2:3], ss.t[:, 2:3], ss.t[:, 3:4], ALU.add, [ss.b], [ss.b])
    rstd_from_ss(fw, C, ss.t[:, 2:3], ss.t[:, 3:4], 1024.0, [ss.b], [ss.b])
    fw.stt("dve", tmp.t[:, 0:512], psA.t[:, :], ss.t[:, 3:4], nw.t[:, 0:512], ALU.mult, ALU.mult, [psA.b, ss.b, nw.b], [tmp.b])
    fw.stt("dve", tmp.t[:, 512:1024], psB.t[:, :], ss.t[:, 3:4], nw.t[:, 512:1024], ALU.mult, ALU.mult, [psB.b, ss.b, nw.b], [tmp.b])
    fw.tt("pool", xt.t[:, :], xt.t[:, :], tmp.t[:, :], ALU.add, [xt.b, tmp.b], [xt.b])


def phase_merge(fw, C, l):
    ps = C.ps
    with ExitStack() as ph:
        WG = fw.tile(ph, "WG", [128, 8, 4096], BF16)
        load_w(fw, C.w_in[l], WG, C_GATE, C_GATE + 4096, 8)
        WB = fw.tile(ph, "WB", [128, 16, 1024], BF16)
        for n in range(4):
            v = C.w_branch[l][n].rearrange("(cb p) d -> p cb d", p=128)
            fw.dma("pool", WB.t[:, 4 * n:4 * n + 4, :], v, writes=[WB.b])
        WO = fw.tile(ph, "WO", [128, 8, 1024], BF16)
        load_w(fw, C.w_out[l], WO, 0, 1024, 8)
        nw1 = fw.tile(ph, "nw1", [128, D], F32)
        nw2 = fw.tile(ph, "nw2", [128, D], F32)
        fw.dma("sp", nw1.t[:, :], C.norm_post_mix[l].partition_broadcast(128), writes=[nw1.b])
        fw.dma("sp", nw2.t[:, :], C.norm_pre_mlp[l].partition_broadcast(128), writes=[nw2.b])
        YTs = fw.tile(ph, "YTs", [128, 16, 256], BF16)
        mixT = fw.tile(ph, "mixT", [128, 8, 256], BF16)
        acc = fw.tile(ph, "macc", [128, 256], F32)
        sgs = [fw.tile(ph, "msg", [128, 256], F32) for _ in range(2)]
        tmpm = fw.tile(ph, "mtmp", [128, 256], F32)
        xts = [fw.tile(ph, "mxt", [128, D], F32) for _ in range(2)]
        wk = (fw.tile(ph, "junk", [128, D], BF16), fw.tile(ph, "ss", [128, 4], F32), fw.tile(ph, "ub", [128, D], BF16))
        wk2 = (wk[0], wk[1], fw.tile(ph, "mtmp2", [128, D], F32))
        k = 0
        for (t0, n) in MGS:
            tiles = list(range(t0 // 128, (t0 + n) // 128))
            fw.dma("sp", YTs.t[:, :, 0:n], C.YT[:, :, t0:t0 + n].rearrange("c p t -> p c t"),
                   reads=[C.YTb[m][i] for m in range(4) for i in tiles], writes=[YTs.b])
            for db in range(8):
                for nn in range(4):
                    pg_, pb_ = ps[2 * (k % 2)], ps[2 * (k % 2) + 1]
                    sg = sgs[k % 2]
                    k += 1
                    c0 = nn * 1024 + db * 128
                    for kc in range(8):
                        fw.mm(pg_.t[:, 0:n], WG.t[:, kc, c0:c0 + 128], C.uT.t[:, kc, t0:t0 + n], kc == 0, kc == 7,
                              [WG.b] + [C.uTb[i] for i in tiles], [pg_.b])
                    for cb in range(4):
                        fw.mm(pb_.t[:, 0:n], WB.t[:, 4 * nn + cb, db * 128:(db + 1) * 128], YTs.t[:, 4 * nn + cb, 0:n], cb == 0, cb == 3,
                              [WB.b, YTs.b], [pb_.b])
                    fw.act(sg.t[:, 0:n], pg_.t[:, 0:n], AF.Sigmoid, [pg_.b], [sg.b])
                    if nn == 0:
                        fw.tt("dve", acc.t[:, 0:n], sg.t[:, 0:n], pb_.t[:, 0:n], ALU.mult, [sg.b, pb_.b], [acc.b])
                    else:
                        fw.tt("dve", tmpm.t[:, 0:n], sg.t[:, 0:n], pb_.t[:, 0:n], ALU.mult, [sg.b, pb_.b], [tmpm.b])
                        if nn < 3:
                            fw.tt("pool", acc.t[:, 0:n], acc.t[:, 0:n], tmpm.t[:, 0:n], ALU.add, [acc.b, tmpm.b], [acc.b])
                        else:
                            fw.tt("pool", mixT.t[:, db, 0:n], acc.t[:, 0:n], tmpm.t[:, 0:n], ALU.add, [acc.b, tmpm.b], [mixT.b])
            for i in tiles:
                xt = xts[i % 2]
                src = C.h0 if l == 0 else C.hbuf
                fw.dma("sp", xt.t[:, :], src[i * 128:(i + 1) * 128, :], reads=([] if l == 0 else [C.hb[i]]), writes=[xt.b])
                sub = slice(i * 128 - t0, i * 128 - t0 + 128)
                for dh in range(2):
                    pst = ps[4 + dh]
                    for db in range(8):
                        fw.mm(pst.t[:, :], mixT.t[:, db, sub], WO.t[:, db, dh * 512:(dh + 1) * 512], db == 0, db == 7, [mixT.b, WO.b], [pst.b])
                rms_epilogue(fw, C, ps[4], ps[5], nw1, xt, wk2)
                fw.dma("sp", C.hbuf[i * 128:(i + 1) * 128, :], xt.t[:, :], reads=[xt.b], writes=[C.hb[i]])
                norm_rows_to_T(fw, C, ph, xt.t[:, :], xt.b, nw2, C.uT, C.uTb, i, wk)
        fw.barrier()


def phase_mlp(fw, C, l, last):
    ps = C.ps
    with ExitStack() as ph:
        WU = fw.tile(ph, "WU", [128, 8, 4096], BF16)
        load_w(fw, C.w_up[l], WU, 0, 4096, 8)
        WD = fw.tile(ph, "WD", [128, 32, 1024], BF16)
        load_w(fw, C.w_down[l], WD, 0, 1024, 32)
        nw = fw.tile(ph, "nw3", [128, D], F32)
        fw.dma("sp", nw.t[:, :], C.norm_post_mlp[l].partition_broadcast(128), writes=[nw.b])
        hT = fw.tile(ph, "hT", [128, 32, 256], BF16)
        rl = [fw.tile(ph, "rl", [128, 512], BF16) for _ in range(2)]
        xts = [fw.tile(ph, "pxt", [128, D], F32)] * 2
        ptmp = fw.tile(ph, "ptmp", [128, D], F32)
        wk2 = (ptmp, fw.tile(ph, "ss", [128, 4], F32), ptmp)
        k = 0
        for (t0, n) in MGS:
            tiles = list(range(t0 // 128, (t0 + n) // 128))
            for fp in range(16):
                pst = ps[k % 4]
                r = rl[k % 2]
                k += 1
                for j in range(2):
                    ffc = 2 * fp + j
                    for kc in range(8):
                        fw.mm(pst.t[:, j * 256:j * 256 + n], WU.t[:, kc, ffc * 128:(ffc + 1) * 128], C.uT.t[:, kc, t0:t0 + n], kc == 0, kc == 7,
                              [WU.b] + [C.uTb[i] for i in tiles], [pst.b])
                pv = pst.t[:, :].rearrange("p (j t) -> p j t", j=2)[:, :, 0:n]
                rv = r.t[:, :].rearrange("p (j t) -> p j t", j=2)[:, :, 0:n]
                fw.act(rv, pv, AF.Relu, [pst.b], [r.b])
                fw.tt("pool" if fp % 2 else "dve", hT.t[:, 2 * fp:2 * fp + 2, 0:n], rv, rv, ALU.mult, [r.b], [hT.b])
            for i in tiles:
                xt = xts[i % 2]
                fw.dma("sp", xt.t[:, :], C.hbuf[i * 128:(i + 1) * 128, :], reads=[C.hb[i]], writes=[xt.b])
                sub = slice(i * 128 - t0, i * 128 - t0 + 128)
                for dh in range(2):
                    pst = ps[4 + dh]
                    for ffc in range(32):
                        fw.mm(pst.t[:, :], hT.t[:, ffc, sub], WD.t[:, ffc, dh * 512:(dh + 1) * 512], ffc == 0, ffc == 31, [hT.b, WD.b], [pst.b])
                rms_epilogue(fw, C, ps[4], ps[5], nw, xt, wk2)
                if not last:
                    fw.dma("sp", C.hbuf[i * 128:(i + 1) * 128, :], xt.t[:, :], reads=[xt.b], writes=[C.hb[i]])
                else:
                    if C.debug:
                        fw.dma("sp", C.hbuf[i * 128:(i + 1) * 128, :], xt.t[:, :], reads=[xt.b], writes=[C.hb[i]])
                    lo = max(i * 128, 16)
                    hi = min((i + 1) * 128, T)
                    if hi > lo:
                        fw.dma("sp", C.out[lo - 16:hi - 16, :], xt.t[lo - i * 128:hi - i * 128, :], reads=[xt.b], writes=[C.outb])
        fw.barrier()


def build(debug=False, n_layers=DEPTH, phases=None):
    nc = bass.Bass("TRN2", target_bir_lowering=False)
    C = Ctx()
    C.debug = debug

    def din(name, shape):
        return nc.dram_tensor(name, list(shape), F32, kind="ExternalInput").ap()

    C.h0 = din("h0", [TP, D])
    C.consts_d = din("consts", [128, CO_TOTAL[0]])
    C.w_in = din("w_in", [4, D, N_IN])
    C.w_branch = din("w_branch", [4, 4, 512, D])
    C.w_out = din("w_out", [4, D, D])
    C.w_up = din("w_up", [4, D, 4 * D])
    C.w_down = din("w_down", [4, 4 * D, D])
    C.s5_glu_w = din("s5_glu_w", [4, 512, 1024])
    for nm in ("norm_pre_mix", "norm_post_mix", "norm_pre_mlp", "norm_post_mlp"):
        setattr(C, nm, din(nm, [4, D]))
    for nm in ("ret_gn_w", "ssd_norm_w", "hgrn_norm_w", "ssd_dsk"):
        setattr(C, nm, din(nm, [4, 512]))
    C.ssd_dt_bias = din("ssd_dt_bias", [4, 8])
    C.ssd_a_log = din("ssd_a_log", [4, 8])
    C.lbT = din("lbT", [128, 4, 4])
    C.ssd_cw = din("ssd_cw", [4, 128, 8, 4])
    C.ssd_cb = din("ssd_cb", [4, 128, 8])
    C.s5_small = din("s5_small", [4, 128, 48])
    for nm in ("s5_bre", "s5_bim", "s5_cre", "s5_cim"):
        setattr(C, nm, din(nm, [4, 128, 16, 128]))
    C.s5_dT = din("s5_dT", [4, 128, 4])
    C.s5_gbT = din("s5_gbT", [4, 128, 8])
    C.out = nc.dram_tensor("out", [2048, D], F32, kind="ExternalOutput").ap()
    sk = "ExternalOutput" if debug else "Internal"
    C.hbuf = nc.dram_tensor("hbuf", [TP, D], F32, kind=sk).ap()
    C.YT = nc.dram_tensor("YT", [16, 128, TP], BF16, kind=sk).ap()
    if debug:
        C.dbg5 = nc.dram_tensor("dbg5", [128, 2784 + 137 * 48], F32, kind="ExternalOutput").ap()
    C.hb = [Buf("hb%d" % i) for i in range(NT)]
    C.YTb = [[Buf("yt%d_%d" % (m, i)) for i in range(NT)] for m in range(4)]
    C.outb = Buf("out")
    with ExitStack() as es:
        fw = FW(nc, es)
        C.ps = [TL(es.enter_context(nc.psum_tensor("ps%d" % j, [128, 512], F32)), "ps%d" % j) for j in range(6)]
        C.pb = [TL(es.enter_context(nc.psum_tensor("pb%d" % j, [128, 1024], BF16)), "pb%d" % j) for j in range(2)]
        C.consts = fw.tile(es, "consts", [128, CO_TOTAL[0]], F32)
        fw.dma("sp", C.consts.t[:, :], C.consts_d[:, :], writes=[C.consts.b])
        C.identb = fw.tile(es, "identb", [128, 128], BF16)
        C.identf = fw.tile(es, "identf", [128, 128], F32)
        fw.cp("dve", C.identb.t[:, :], cslice(C, "ident"), [C.consts.b], [C.identb.b])
        fw.cp("dve", C.identf.t[:, :], cslice(C, "ident"), [C.consts.b], [C.identf.b])
        C.uT = fw.tile(es, "uT", [128, 8, TP], BF16)
        C.uTb = [Buf("uT%d" % i) for i in range(NT)]
        C.lb_all = fw.tile(es, "lb_all", [128, 4, 4], F32)
        C.oml_all = fw.tile(es, "oml_all", [128, 4, 4], F32)
        lbe = fw.tile(es, "lbe", [128, 4, 4], F32)
        lsum = fw.tile(es, "lsum", [128, 4], F32)
        R = [lbe.b, lsum.b, C.lb_all.b, C.oml_all.b]
        fw.dma("sp", lbe.t[:, :, :], C.lbT[:, :, :], writes=[lbe.b])
        fw.act(lbe.t[:, :, :], lbe.t[:, :, :], AF.Exp, R, R)
        fw.tt("dve", lsum.t[:, :], lbe.t[:, 0, :], lbe.t[:, 1, :], ALU.add, R, R)
        fw.tt("dve", lsum.t[:, :], lsum.t[:, :], lbe.t[:, 2, :], ALU.add, R, R)
        fw.tt("dve", lsum.t[:, :], lsum.t[:, :], lbe.t[:, 3, :], ALU.add, R, R)
        fw.op("dve", lambda g: g.reciprocal(lsum.t[:, :], lsum.t[:, :]), R, R)
        fw.memset("dve", C.lb_all.t[:, 0, :], 0.0, R)
        for ll in range(1, 4):
            fw.tt("dve", lbe.t[:, ll, :], lbe.t[:, ll, :], lsum.t[:, :], ALU.mult, R, R)
            fw.tt("dve", C.lb_all.t[:, ll, :], C.lb_all.t[:, ll - 1, :], lbe.t[:, ll, :], ALU.add, R, R)
        fw.ts("dve", C.oml_all.t[:, :, :], C.lb_all.t[:, :, :], -1.0, 1.0, ALU.mult, ALU.add, R, R)
        fw.barrier()
        allp = ("norm", "ret", "s5", "ssd", "hg", "merge", "mlp")
        for l in range(n_layers):
            for pn in allp:
                if phases is not None and pn not in phases:
                    continue
                if pn == "norm":
                    phase_norm1(fw, C, l)
                elif pn == "ret":
                    phase_ret(fw, C, l)
                elif pn == "s5":
                    phase_s5(fw, C, l)
                elif pn == "ssd":
                    phase_ssd(fw, C, l)
                elif pn == "hg":
                    phase_hg(fw, C, l)
                elif pn == "merge":
                    phase_merge(fw, C, l)
                elif pn == "mlp":
                    phase_mlp(fw, C, l, last=(l == n_layers - 1))
        fw.barrier()
        C.n_inst, C.n_wait = fw.n_inst, fw.n_wait
    return nc, C


CO_TOTAL = [0]
_CONSTS = None


def get_consts():
    global _CONSTS
    if _CONSTS is None:
        _CONSTS = host_consts()
        CO_TOTAL[0] = _CONSTS.shape[1]
    return _CONSTS


def make_in_maps(inp):
    consts = get_consts()
    P = host_params(inp)
    f = lambda a: np.ascontiguousarray(np.asarray(a, np.float32))
    shared = {"consts": consts}
    for nm in ("w_in", "w_branch", "w_out", "w_up", "w_down", "s5_glu_w", "norm_pre_mix", "norm_post_mix", "norm_pre_mlp",
               "norm_post_mlp", "ret_gn_w", "ssd_norm_w", "hgrn_norm_w", "ssd_dt_bias", "ssd_a_log"):
        shared[nm] = f(inp[nm])
    for nm in ("lbT", "ssd_cw", "ssd_cb", "ssd_dsk", "s5_small", "s5_bre", "s5_bim", "s5_cre", "s5_cim", "s5_dT", "s5_gbT"):
        shared[nm] = P[nm]
    x = np.asarray(inp["x"], np.float32)
    meta = np.asarray(inp["meta_tokens"], np.float32)
    maps = []
    for b in range(x.shape[0]):
        h0 = np.zeros((TP, D), np.float32)
        h0[0:16] = meta
        h0[16:T] = x[b]
        m = dict(shared)
        m["h0"] = h0
        maps.append(m)
    return maps


_NC = None


def kernel(**inputs):
    global _NC
    maps = make_in_maps(inputs)
    if _NC is None:
        _NC = build()[0]
    res = run_bass_kernel_spmd(_NC, maps, core_ids=list(range(len(maps))))
    return np.stack([np.asarray(r["out"], np.float32) for r in res.results], axis=0)
```

```python
import math
import os
import numpy as np
from contextlib import ExitStack
import concourse.bass as bass
import concourse.mybir as mybir
from concourse.bass_utils import run_bass_kernel_spmd

F32 = mybir.dt.float32
BF16 = mybir.dt.bfloat16
ALU = mybir.AluOpType
AF = mybir.ActivationFunctionType
AX = mybir.AxisListType

DEPTH = 4
D = 1024
T = 2064
NT = 17
TP = NT * 128
EPS = 1e-6
N_IN = 9736
C_RET, C_S5, C_SSD, C_HG, C_GATE = 0, 1536, 2048, 3592, 5640
GAMMA = [1.0 - 2.0 ** (-5.0 - h) for h in range(4)]


class Buf:
    __slots__ = ("name", "w", "r")

    def __init__(self, name=""):
        self.name = name
        self.w = None
        self.r = []


class TL:
    def __init__(self, t, name):
        self.t = t
        self.b = Buf(name)


LAZY_PE_SIGNAL = os.environ.get("LAZY_PE", "0") == "1"


class VW:
    def __init__(self, t, b):
        self.t = t
        self.b = b


class FW:
    N_DMA_SEMS = 16

    def __init__(self, nc, es):
        self.nc = nc
        self.es = es
        self.eng = {"pe": nc.tensor, "dve": nc.vector, "act": nc.scalar, "pool": nc.gpsimd, "sp": nc.sync}
        self.sems = {}
        self.cnt = {}
        for e in ("pe", "dve", "act", "pool"):
            self.sems[e] = es.enter_context(nc.semaphore("s_" + e))
            self.cnt[e] = 0
        self.dma_keys = {}
        self.dma_rr = {}
        for q in ("sp", "pool"):
            ks = []
            for i in range(self.N_DMA_SEMS):
                k = "d_%s_%d" % (q, i)
                self.sems[k] = es.enter_context(nc.semaphore(k))
                self.cnt[k] = 0
                ks.append(k)
            self.dma_keys[q] = ks
            self.dma_rr[q] = 0
        self.known = {e: {} for e in self.eng}
        self.n_inst = 0
        self.n_wait = 0
        self.uid = 0
        self.pe_last = None
        self.pe_unsig = False
        self.n_sig = 0

    def tile(self, stack, name, shape, dt):
        self.uid += 1
        nm = "%s_%d" % (name, self.uid)
        return TL(stack.enter_context(self.nc.sbuf_tensor(nm, list(shape), dt)), nm)

    def _wait(self, e, ev):
        if ev is None:
            return
        k, v = ev
        if e == "pe" and k == "pe":
            return
        if k == "pe" and v > self.cnt["pe"]:
            self._flush_pe()
        kn = self.known[e]
        if kn.get(k, 0) >= v:
            return
        self.eng[e].wait_ge(self.sems[k], v)
        kn[k] = v
        self.n_wait += 1

    def _deps(self, e, reads, writes):
        for b in reads:
            self._wait(e, b.w)
        for b in writes:
            self._wait(e, b.w)
            for ev in b.r:
                self._wait(e, ev)

    def _mark(self, ev, reads, writes):
        for b in reads:
            b.r.append(ev)
        for b in writes:
            b.w = ev
            b.r = []

    def _flush_pe(self):
        if self.pe_unsig:
            self.cnt["pe"] += 1
            self.pe_last.then_inc(self.sems["pe"], 1)
            self.pe_unsig = False
            self.n_sig += 1

    def op(self, e, fn, reads=(), writes=()):
        self._deps(e, reads, writes)
        ins = fn(self.eng[e])
        if e == "pe" and LAZY_PE_SIGNAL:
            self.pe_last = ins
            self.pe_unsig = True
            ev = (e, self.cnt[e] + 1)
        else:
            self.cnt[e] += 1
            ins.then_inc(self.sems[e], 1)
            ev = (e, self.cnt[e])
        self._mark(ev, reads, writes)
        self.n_inst += 1
        return ev

    def dma(self, q, out, in_, reads=(), writes=(), **kw):
        self._deps(q, reads, writes)
        ks = self.dma_keys[q]
        k = ks[self.dma_rr[q] % len(ks)]
        self.dma_rr[q] += 1
        if self.cnt[k] > 0:
            self._wait(q, (k, self.cnt[k]))
        ins = self.eng[q].dma_start(out=out, in_=in_, **kw)
        self.cnt[k] += 16
        ins.then_inc(self.sems[k], 16)
        ev = (k, self.cnt[k])
        self._mark(ev, reads, writes)
        self.n_inst += 1
        return ev

    def barrier(self, engines=("pe", "dve", "act", "pool", "sp")):
        self._flush_pe()
        for e in engines:
            for k, v in self.cnt.items():
                if v > 0:
                    self._wait(e, (k, v))

    def tt(self, e, out, a, b, op, R, W):
        return self.op(e, lambda g: g.tensor_tensor(out, a, b, op), R, W)

    def ts(self, e, out, a, s1, s2, op0, op1, R, W):
        if s2 is None:
            return self.op(e, lambda g: g.tensor_scalar(out, a, s1, None, op0=op0), R, W)
        return self.op(e, lambda g: g.tensor_scalar(out, a, s1, s2, op0=op0, op1=op1), R, W)

    def stt(self, e, out, in0, sc, in1, op0, op1, R, W):
        e = "dve"
        return self.op(e, lambda g: g.scalar_tensor_tensor(out, in0, sc, in1, op0=op0, op1=op1), R, W)

    def cp(self, e, out, in_, R, W):
        if e == "act":
            return self.op(e, lambda g: g.copy(out, in_), R, W)
        return self.op(e, lambda g: g.tensor_copy(out, in_), R, W)

    def act(self, out, in_, func, R, W, bias=None, scale=None, accum=None):
        kw = {}
        if bias is not None:
            kw["bias"] = bias
        if scale is not None:
            kw["scale"] = scale
        if accum is not None:
            kw["accum_out"] = accum
        return self.op("act", lambda g: g.activation(out, in_, func, **kw), R, W)

    def mm(self, out, lhsT, rhs, start, stop, R, W):
        return self.op("pe", lambda g: g.matmul(out, lhsT, rhs, start=start, stop=stop), R, W)

    def tr(self, out, in_, ident, R, W):
        return self.op("pe", lambda g: g.transpose(out, in_, ident), R, W)

    def memset(self, e, ap, val, W):
        return self.op(e, lambda g: g.memset(ap, val), (), W)


CO = {}


def _pack(items):
    off = 0
    cols = []
    for name, arr in items:
        arr = np.asarray(arr, np.float32).reshape(128, -1)
        CO[name] = (off, arr.shape[1])
        off += arr.shape[1]
        cols.append(arr)
    return np.ascontiguousarray(np.concatenate(cols, axis=1))


def host_consts():
    s = np.arange(128)[:, None]
    t = np.arange(128)[None, :]
    ident = (s == t).astype(np.float32)
    triu = (s <= t).astype(np.float32)
    negm = np.where(s <= t, 0.0, -30000.0).astype(np.float32)
    ones = np.ones((128, 128), np.float32)
    retdt = np.zeros((128, 4, 128), np.float64)
    qdec = np.zeros((128, 4, 64), np.float64)
    kdec = np.zeros((128, 4, 64), np.float64)
    for h in range(4):
        g = GAMMA[h]
        retdt[:, h, :] = np.where(s <= t, 0.125 * g ** np.maximum(t - s, 0), 0.0)
        qdec[:, h, :] = (g ** (np.arange(128) + 1.0))[:, None]
        kdec[:, h, :] = (0.125 * g ** (127.0 - np.arange(128)))[:, None]
    half = 32
    inv_freq = (10000.0 ** (-np.arange(half, dtype=np.float32) / half)).astype(np.float32)
    pos = (np.arange(NT)[None, :] * 128 + np.arange(128)[:, None]).astype(np.float32)
    ang = pos[:, :, None] * inv_freq[None, None, :]
    cos = np.cos(ang).astype(np.float32)
    sin = np.sin(ang).astype(np.float32)
    halfpi = np.full((128, 1), math.pi / 2, np.float32)
    mvals = np.array(list(range(17)) + [0.5], np.float32)
    mtab = np.broadcast_to(mvals[None, :, None], (128, 18, 16))
    return _pack([("mtab", mtab), ("ident", ident), ("triu", triu), ("negm", negm), ("ones", ones), ("retdt", retdt),
                  ("qdec", qdec), ("kdec", kdec), ("cos", cos), ("sin", sin), ("halfpi", halfpi)])


def host_params(inp):
    P = {}
    f = lambda a: np.ascontiguousarray(np.asarray(a, np.float32))
    P["lbT"] = f(np.asarray(inp["hgrn_lb"]).reshape(4, 4, 128).transpose(2, 0, 1))
    P["ssd_cw"] = f(np.asarray(inp["ssd_conv_w"]).reshape(4, 4, 8, 128).transpose(0, 3, 2, 1))
    P["ssd_cb"] = f(np.asarray(inp["ssd_conv_b"]).reshape(4, 8, 128).transpose(0, 2, 1))
    P["ssd_dsk"] = f(np.repeat(np.asarray(inp["ssd_d"]), 64, axis=1))
    def pl_small(a):
        a = np.asarray(a).reshape(4, 16, 2, 64)
        return a.transpose(0, 2, 3, 1).reshape(4, 128, 16)
    ls = np.broadcast_to(np.asarray(inp["s5_log_step"])[:, :, None], (4, 32, 64))
    P["s5_small"] = f(np.concatenate([pl_small(inp["s5_lam_re"]), pl_small(inp["s5_lam_im"]), pl_small(ls)], axis=2))
    def pl_b(b):
        b = np.asarray(b)
        out = np.zeros((4, 2, 64, 16, 8, 16), np.float32)
        for g in range(32):
            out[:, g % 2, :, g // 2, g % 8, :] = b[:, g]
        return out.reshape(4, 128, 16, 128)
    def pl_c(c):
        c = np.asarray(c)
        out = np.zeros((4, 2, 64, 16, 8, 16), np.float32)
        for g in range(32):
            out[:, g % 2, :, g // 2, g % 8, :] = c[:, g].transpose(0, 2, 1)
        return out.reshape(4, 128, 16, 128)
    P["s5_bre"] = pl_b(inp["s5_b_re"])
    P["s5_bim"] = pl_b(inp["s5_b_im"])
    P["s5_cre"] = pl_c(inp["s5_c_re"])
    P["s5_cim"] = pl_c(inp["s5_c_im"])
    P["s5_dT"] = f(np.asarray(inp["s5_d"]).reshape(4, 4, 128).transpose(0, 2, 1))
    P["s5_gbT"] = f(np.asarray(inp["s5_glu_b"]).reshape(4, 8, 128).transpose(0, 2, 1))
    return P


class Ctx:
    pass


def cslice(C, name, a=None, b=None):
    off, n = CO[name]
    if a is None:
        return C.consts.t[:, off:off + n]
    return C.consts.t[:, off + a:off + b]


def load_w(fw, src2d, dst, c0, c1, kcs, R=()):
    v = src2d.rearrange("(kc p) n -> p kc n", p=128)
    step = 2048
    for kc in range(kcs):
        for a in range(c0, c1, step):
            b = min(a + step, c1)
            fw.dma("pool", dst.t[:, kc, a - c0:b - c0], v[:, kc, a:b], reads=R, writes=[dst.b])


def rstd_from_ss(fw, C, ss_ap, out_ap, n, R, W):
    fw.ts("dve", out_ap, ss_ap, 1.0 / n, EPS, ALU.mult, ALU.add, R, W)
    fw.act(out_ap, out_ap, AF.Sqrt, W, W)
    fw.op("dve", lambda g: g.reciprocal(out_ap, out_ap), W, W)


def norm_rows_to_T(fw, C, ph, x_ap, xb, nw, dstT, dst_bufs, i, wk):
    junk, ss, ub = wk
    fw.act(junk.t[:, :], x_ap, AF.Square, [xb], [junk.b, ss.b], accum=ss.t[:, 0:1])
    rstd_from_ss(fw, C, ss.t[:, 0:1], ss.t[:, 1:2], 1024.0, [ss.b], [ss.b])
    fw.stt("dve", ub.t[:, :], x_ap, ss.t[:, 1:2], nw.t[:, :], ALU.mult, ALU.mult, [xb, ss.b, nw.b], [ub.b])
    pb = C.pb[i % 2]
    for kc in range(8):
        fw.tr(pb.t[:, kc * 128:(kc + 1) * 128], ub.t[:, kc * 128:(kc + 1) * 128], C.identb.t[:, :], [ub.b, C.identb.b], [pb.b])
    fw.cp("act" if i % 2 else "dve", dstT.t[:, :, i * 128:(i + 1) * 128],
          pb.t[:, :].rearrange("p (k t) -> p k t", k=8), [pb.b], [dst_bufs[i]])


def phase_norm1(fw, C, l):
    with ExitStack() as ph:
        nw = fw.tile(ph, "nw", [128, D], F32)
        fw.dma("sp", nw.t[:, :], C.norm_pre_mix[l].partition_broadcast(128), writes=[nw.b])
        xts = [fw.tile(ph, "xt", [128, D], F32) for _ in range(2)]
        wk = (fw.tile(ph, "junk", [128, D], BF16), fw.tile(ph, "ss", [128, 2], F32), fw.tile(ph, "ub", [128, D], BF16))
        for i in range(NT):
            xt = xts[i % 2]
            src = C.h0 if l == 0 else C.hbuf
            fw.dma("sp", xt.t[:, :], src[i * 128:(i + 1) * 128, :], reads=([] if l == 0 else [C.hb[i]]), writes=[xt.b])
            norm_rows_to_T(fw, C, ph, xt.t[:, :], xt.b, nw, C.uT, C.uTb, i, wk)
        fw.barrier()


ILV_OFF = set(os.environ.get("NOILV", "ret,ssd").split(","))
CUR_PHASE = [""]


def interleave(gb, ga):
    if CUR_PHASE[0] in ILV_OFF:
        for g in (ga, gb):
            if g is not None:
                for _ in g:
                    pass
        return
    done_a = ga is None
    done_b = False
    while not (done_a and done_b):
        if not done_b:
            try:
                next(gb)
            except StopIteration:
                done_b = True
        if not done_a:
            try:
                next(ga)
            except StopIteration:
                done_a = True


def interleave_n(gens):
    gens = list(gens)
    if CUR_PHASE[0] in ILV_OFF:
        for g in reversed(gens):
            for _ in g:
                pass
        return
    while gens:
        for g in list(gens):
            try:
                next(g)
            except StopIteration:
                gens.remove(g)


def chain_gens(*gens):
    for g in gens:
        if g is not None:
            yield from g


def emit_yT(fw, C, yo, yT, mix, i):
    pb = C.pb[1]
    for cb in range(4):
        fw.tr(pb.t[:, cb * 128:(cb + 1) * 128], yo.t[:, cb * 128:(cb + 1) * 128], C.identb.t[:, :], [yo.b, C.identb.b], [pb.b])
    fw.cp("act", yT.t[:, :, :], pb.t[:, 0:512].rearrange("p (c t) -> p c t", c=4), [pb.b], [yT.b])
    fw.dma("sp", C.YT[mix * 4:(mix + 1) * 4, :, i * 128:(i + 1) * 128].rearrange("c p t -> p c t"), yT.t[:, :, :],
           reads=[yT.b], writes=[C.YTb[mix][i]])


def tok_mm(fw, C, ps, Wt, c0, n, i, ncols=None):
    for kc in range(8):
        fw.mm(ps.t[:, 0:n], C.uT.t[:, kc, i * 128:(i + 1) * 128], Wt.t[:, kc, c0:c0 + n], kc == 0, kc == 7,
              [C.uTb[i], Wt.b], [ps.b])


def phase_ret(fw, C, l):
    CUR_PHASE[0] = "ret"
    with ExitStack() as ph:
        WR = fw.tile(ph, "WR", [128, 8, 1536], BF16)
        load_w(fw, C.w_in[l], WR, C_RET, C_RET + 1536, 8)
        gnw = fw.tile(ph, "gnw", [128, 512], F32)
        fw.dma("sp", gnw.t[:, :], C.ret_gn_w[l].partition_broadcast(128), writes=[gnw.b])
        S = fw.tile(ph, "rS", [64, 4, 128], F32)
        Sb = fw.tile(ph, "rSb", [64, 4, 128], BF16)
        fw.memset("dve", S.t[:, :, :], 0.0, [S.b])
        fw.memset("dve", Sb.t[:, :, :], 0.0, [Sb.b])
        qkr_2 = [fw.tile(ph, "qkr", [128, 8, 64], F32) for _ in range(2)]
        t1_2 = [fw.tile(ph, "t1", [128, 8, 32], F32) for _ in range(2)]
        t2_2 = [fw.tile(ph, "t2", [128, 8, 32], F32) for _ in range(2)]
        qb_2 = [fw.tile(ph, "qb", [128, 256], BF16) for _ in range(2)]
        qdb_2 = [fw.tile(ph, "qdb", [128, 256], BF16) for _ in range(2)]
        kb_2 = [fw.tile(ph, "kb", [128, 256], BF16) for _ in range(2)]
        khb_2 = [fw.tile(ph, "khb", [128, 256], BF16) for _ in range(2)]
        qkT_2 = [fw.tile(ph, "qkT", [64, 12, 128], BF16) for _ in range(2)]
        vb_2 = [fw.tile(ph, "vb", [128, 512], BF16) for _ in range(2)]
        sg_2 = [fw.tile(ph, "sg", [128, 512], F32) for _ in range(2)]
        PT_2 = [fw.tile(ph, "PT", [128, 512], BF16) for _ in range(2)]
        ysq_2 = [fw.tile(ph, "ysq", [128, 512], F32) for _ in range(2)]
        st_2 = [fw.tile(ph, "st", [128, 16], F32) for _ in range(2)]
        yn_2 = [fw.tile(ph, "yn", [128, 512], F32) for _ in range(2)]
        yo_2 = [fw.tile(ph, "yo", [128, 512], BF16) for _ in range(2)]
        yT_2 = [fw.tile(ph, "yT", [128, 4, 128], BF16) for _ in range(2)]
        ps_qk, ps_v, ps_g, ps_s, ps_y, ps_st = C.ps[0:6]
        cb_ = C.consts.b
        def stA(i):
                qkr, t1, t2, qb, qdb, kb, khb, qkT, vb, sg, PT, ysq, st, yn, yo, yT = qkr_2[i % 2], t1_2[i % 2], t2_2[i % 2], qb_2[i % 2], qdb_2[i % 2], kb_2[i % 2], khb_2[i % 2], qkT_2[i % 2], vb_2[i % 2], sg_2[i % 2], PT_2[i % 2], ysq_2[i % 2], st_2[i % 2], yn_2[i % 2], yo_2[i % 2], yT_2[i % 2]
                tok_mm(fw, C, ps_qk, WR, 0, 512, i)
                yield
                tok_mm(fw, C, ps_v, WR, 512, 512, i)
                yield
                tok_mm(fw, C, ps_g, WR, 1024, 512, i)
                yield
                qv = ps_qk.t[:, :].rearrange("p (h d) -> p h d", d=64)
                x1, x2 = qv[:, :, 0:32], qv[:, :, 32:64]
                co, cn = CO["cos"][0], CO["sin"][0]
                cosb = C.consts.t[:, co + i * 32:co + (i + 1) * 32].unsqueeze(1).to_broadcast([128, 8, 32])
                sinb = C.consts.t[:, cn + i * 32:cn + (i + 1) * 32].unsqueeze(1).to_broadcast([128, 8, 32])
                fw.tt("dve", t1.t[:, :, :], x1, cosb, ALU.mult, [ps_qk.b, cb_], [t1.b])
                fw.tt("dve", t2.t[:, :, :], x2, sinb, ALU.mult, [ps_qk.b, cb_], [t2.b])
                fw.tt("pool", qkr.t[:, :, 0:32], t1.t[:, :, :], t2.t[:, :, :], ALU.subtract, [t1.b, t2.b], [qkr.b])
                fw.tt("dve", t1.t[:, :, :], x1, sinb, ALU.mult, [ps_qk.b, cb_], [t1.b])
                fw.tt("dve", t2.t[:, :, :], x2, cosb, ALU.mult, [ps_qk.b, cb_], [t2.b])
                fw.tt("pool", qkr.t[:, :, 32:64], t1.t[:, :, :], t2.t[:, :, :], ALU.add, [t1.b, t2.b], [qkr.b])
                qf = qkr.t[:, 0:4, :].rearrange("p h d -> p (h d)")
                kf = qkr.t[:, 4:8, :].rearrange("p h d -> p (h d)")
                fw.cp("act", qb.t[:, :], qf, [qkr.b], [qb.b])
                fw.tt("pool", qdb.t[:, :], qf, cslice(C, "qdec"), ALU.mult, [qkr.b, cb_], [qdb.b])
                fw.cp("act", kb.t[:, :], kf, [qkr.b], [kb.b])
                fw.tt("pool", khb.t[:, :], kf, cslice(C, "kdec"), ALU.mult, [qkr.b, cb_], [khb.b])
                fw.cp("act", vb.t[:, :], ps_v.t[:, :], [ps_v.b], [vb.b])
                fw.act(sg.t[:, :], ps_g.t[:, :], AF.Silu, [ps_g.b], [sg.b])
        def stB(i):
                qkr, t1, t2, qb, qdb, kb, khb, qkT, vb, sg, PT, ysq, st, yn, yo, yT = qkr_2[i % 2], t1_2[i % 2], t2_2[i % 2], qb_2[i % 2], qdb_2[i % 2], kb_2[i % 2], khb_2[i % 2], qkT_2[i % 2], vb_2[i % 2], sg_2[i % 2], PT_2[i % 2], ysq_2[i % 2], st_2[i % 2], yn_2[i % 2], yo_2[i % 2], yT_2[i % 2]
                pb0, pb1 = C.pb
                for h in range(4):
                    fw.tr(pb0.t[0:64, h * 128:(h + 1) * 128], qb.t[:, h * 64:(h + 1) * 64], C.identb.t[:, :], [qb.b, C.identb.b], [pb0.b])
                    fw.tr(pb0.t[0:64, (4 + h) * 128:(5 + h) * 128], qdb.t[:, h * 64:(h + 1) * 64], C.identb.t[:, :], [qdb.b, C.identb.b], [pb0.b])
                    fw.tr(pb1.t[0:64, h * 128:(h + 1) * 128], kb.t[:, h * 64:(h + 1) * 64], C.identb.t[:, :], [kb.b, C.identb.b], [pb1.b])
                yield
                fw.cp("dve", qkT.t[:, 0:8, :], pb0.t[0:64, :].rearrange("p (j t) -> p j t", j=8), [pb0.b], [qkT.b])
                fw.cp("act", qkT.t[:, 8:12, :], pb1.t[0:64, 0:512].rearrange("p (j t) -> p j t", j=4), [pb1.b], [qkT.b])
                for h in range(4):
                    fw.mm(ps_s.t[:, h * 128:(h + 1) * 128], qkT.t[:, 8 + h, :], qkT.t[:, h, :], True, True, [qkT.b], [ps_s.b])
                yield
                fw.tt("dve", PT.t[:, :], ps_s.t[:, :], cslice(C, "retdt"), ALU.mult, [ps_s.b, cb_], [PT.b])
                for h in range(4):
                    hs = slice(h * 128, (h + 1) * 128)
                    fw.mm(ps_y.t[:, hs], PT.t[:, hs], vb.t[:, hs], True, False, [PT.b, vb.b], [ps_y.b])
                    fw.mm(ps_y.t[:, hs], qkT.t[:, 4 + h, :], Sb.t[:, h, :], False, True, [qkT.b, Sb.b], [ps_y.b])
                yield
                for h in range(4):
                    hs = slice(h * 128, (h + 1) * 128)
                    fw.mm(ps_st.t[0:64, hs], khb.t[:, h * 64:(h + 1) * 64], vb.t[:, hs], True, True, [khb.b, vb.b], [ps_st.b])
                yield
                for h in range(4):
                    hs = slice(h * 128, (h + 1) * 128)
                    fw.stt("dve", S.t[:, h, :], S.t[:, h, :], float(GAMMA[h] ** 128), ps_st.t[0:64, hs], ALU.mult, ALU.add,
                           [S.b, ps_st.b], [S.b])
                fw.cp("pool", Sb.t[:, :, :], S.t[:, :, :], [S.b], [Sb.b])
                yv = ps_y.t[:, :].rearrange("p (h e) -> p h e", h=4)
                fw.op("dve", lambda g: g.reduce_sum(st.t[:, 0:4], yv, axis=AX.X), [ps_y.b], [st.b])
                fw.act(ysq.t[:, :], ps_y.t[:, :], AF.Square, [ps_y.b], [ysq.b])
                fw.op("dve", lambda g: g.reduce_sum(st.t[:, 4:8], ysq.t[:, :].rearrange("p (h e) -> p h e", h=4), axis=AX.X), [ysq.b], [st.b])
                fw.ts("dve", st.t[:, 8:12], st.t[:, 0:4], 1.0 / 128, None, ALU.mult, None, [st.b], [st.b])
                fw.tt("dve", st.t[:, 0:4], st.t[:, 8:12], st.t[:, 8:12], ALU.mult, [st.b], [st.b])
                fw.stt("dve", st.t[:, 12:16], st.t[:, 4:8], 1.0 / 128, st.t[:, 0:4], ALU.mult, ALU.subtract, [st.b], [st.b])
                fw.ts("dve", st.t[:, 12:16], st.t[:, 12:16], EPS, None, ALU.add, None, [st.b], [st.b])
                fw.act(st.t[:, 12:16], st.t[:, 12:16], AF.Sqrt, [st.b], [st.b])
                fw.op("dve", lambda g: g.reciprocal(st.t[:, 12:16], st.t[:, 12:16]), [st.b], [st.b])
                for h in range(4):
                    hs = slice(h * 128, (h + 1) * 128)
                    fw.ts("dve", yn.t[:, hs], ps_y.t[:, hs], st.t[:, 8 + h:9 + h], st.t[:, 12 + h:13 + h], ALU.subtract, ALU.mult,
                          [ps_y.b, st.b], [yn.b])
                fw.tt("pool", yn.t[:, :], yn.t[:, :], gnw.t[:, :], ALU.mult, [yn.b, gnw.b], [yn.b])
                fw.tt("pool", yo.t[:, :], yn.t[:, :], sg.t[:, :], ALU.mult, [yn.b, sg.b], [yo.b])
                emit_yT(fw, C, yo, yT, 0, i)
                yield
        for _ in stA(0):
            pass
        for i in range(NT):
            interleave(stB(i), stA(i + 1) if i + 1 < NT else None)
        fw.barrier()


def phase_ssd(fw, C, l):
    CUR_PHASE[0] = "ssd"
    with ExitStack() as ph:
        WS = fw.tile(ph, "WS", [128, 8, 1544], BF16)
        load_w(fw, C.w_in[l], WS, C_SSD, C_SSD + 1544, 8)
        cw = fw.tile(ph, "cw", [128, 8, 4], F32)
        cbi = fw.tile(ph, "cbi", [128, 8], F32)
        dtb = fw.tile(ph, "dtb", [128, 8], F32)
        arow = fw.tile(ph, "arow", [128, 8], F32)
        dsk = fw.tile(ph, "dsk", [128, 512], F32)
        nw = fw.tile(ph, "snw", [128, 512], F32)
        fw.dma("sp", cw.t[:, :, :], C.ssd_cw[l], writes=[cw.b])
        fw.dma("sp", cbi.t[:, :], C.ssd_cb[l], writes=[cbi.b])
        fw.dma("sp", dtb.t[:, :], C.ssd_dt_bias[l].partition_broadcast(128), writes=[dtb.b])
        fw.dma("sp", arow.t[:, :], C.ssd_a_log[l].partition_broadcast(128), writes=[arow.b])
        fw.dma("sp", dsk.t[:, :], C.ssd_dsk[l].partition_broadcast(128), writes=[dsk.b])
        fw.dma("sp", nw.t[:, :], C.ssd_norm_w[l].partition_broadcast(128), writes=[nw.b])
        fw.act(arow.t[:, :], arow.t[:, :], AF.Exp, [arow.b], [arow.b])
        fw.ts("dve", arow.t[:, :], arow.t[:, :], -1.0, None, ALU.mult, None, [arow.b], [arow.b])
        S = fw.tile(ph, "sS", [128, 512], F32)
        Sb = fw.tile(ph, "sSb", [128, 512], BF16)
        fw.memset("dve", S.t[:, :], 0.0, [S.b])
        fw.memset("dve", Sb.t[:, :], 0.0, [Sb.b])
        xraw = fw.tile(ph, "xraw", [128, 8, 131], F32)
        fw.memset("pool", xraw.t[:, :, :], 0.0, [xraw.b])
        xc_2 = [fw.tile(ph, "xc", [128, 8, 128], F32) for _ in range(2)]
        xcb_2 = [fw.tile(ph, "xcb", [128, 8, 128], BF16) for _ in range(2)]
        xs_2 = [fw.tile(ph, "xs", [128, 512], F32) for _ in range(2)]
        Btok_2 = [fw.tile(ph, "Btok", [128, 2, 128], BF16) for _ in range(2)]
        sz_2 = [fw.tile(ph, "sz", [128, 512], F32) for _ in range(2)]
        sm_2 = [fw.tile(ph, "sm", [128, 104], F32) for _ in range(2)]
        rhs8_2 = [fw.tile(ph, "rhs8", [128, 8, 128], F32) for _ in range(2)]
        decT_2 = [fw.tile(ph, "decT", [128, 8, 128], F32) for _ in range(2)]
        PT_2 = [fw.tile(ph, "sPT", [128, 8, 128], BF16) for _ in range(2)]
        xdt_2 = [fw.tile(ph, "xdt", [128, 512], BF16) for _ in range(2)]
        xdtw_2 = [fw.tile(ph, "xdtw", [128, 512], BF16) for _ in range(2)]
        ya_2 = [fw.tile(ph, "ya", [128, 512], F32) for _ in range(2)]
        tmp_2 = [fw.tile(ph, "stmp", [128, 512], F32) for _ in range(2)]
        yo_2 = [fw.tile(ph, "syo", [128, 512], BF16) for _ in range(2)]
        yT_2 = [fw.tile(ph, "syT", [128, 4, 128], BF16) for _ in range(2)]
        ps = C.ps
        pb0, pb1 = C.pb
        cb_ = C.consts.b
        XD, AXc, EX, LN, DT, LA, CUM, NCUM, ECUM, WW, EDEC, DW = [slice(8 * j, 8 * j + 8) for j in range(12)]
        triu = cslice(C, "triu")
        ones = cslice(C, "ones")
        def stA(i):
                xc, xcb, xs, Btok, sz, sm, decT, PT, xdt, xdtw, ya, tmp, yo, yT = xc_2[i % 2], xcb_2[i % 2], xs_2[i % 2], Btok_2[i % 2], sz_2[i % 2], sm_2[i % 2], decT_2[i % 2], PT_2[i % 2], xdt_2[i % 2], xdtw_2[i % 2], ya_2[i % 2], tmp_2[i % 2], yo_2[i % 2], yT_2[i % 2]
                tok = slice(i * 128, (i + 1) * 128)
                for cb in range(8):
                    pst = ps[0] if cb < 4 else ps[1]
                    for kc in range(8):
                        fw.mm(pst.t[:, (cb % 4) * 128:(cb % 4 + 1) * 128], WS.t[:, kc, 512 + cb * 128:512 + (cb + 1) * 128],
                              C.uT.t[:, kc, tok], kc == 0, kc == 7, [WS.b, C.uTb[i]], [pst.b])
                yield
                if i > 0:
                    fw.cp("pool", xraw.t[:, :, 0:3], xraw.t[:, :, 128:131], [xraw.b], [xraw.b])
                fw.cp("act", xraw.t[:, 0:4, 3:131], ps[0].t[:, :].rearrange("p (c t) -> p c t", c=4), [ps[0].b], [xraw.b])
                fw.cp("dve", xraw.t[:, 4:8, 3:131], ps[1].t[:, :].rearrange("p (c t) -> p c t", c=4), [ps[1].b], [xraw.b])
                for cb in range(8):
                    e = "dve" if cb % 2 == 0 else "pool"
                    fw.ts(e, xc.t[:, cb, :], xraw.t[:, cb, 3:131], cw.t[:, cb, 3:4], cbi.t[:, cb:cb + 1], ALU.mult, ALU.add,
                          [xraw.b, cw.b, cbi.b], [xc.b])
                    for j in (2, 1, 0):
                        fw.stt(e, xc.t[:, cb, :], xraw.t[:, cb, j:j + 128], cw.t[:, cb, j:j + 1], xc.t[:, cb, :], ALU.mult, ALU.add,
                               [xraw.b, cw.b, xc.b], [xc.b])
                fw.act(xcb.t[:, :, :], xc.t[:, :, :], AF.Silu, [xc.b], [xcb.b])
                tok_mm(fw, C, ps[2], WS, 0, 512, i)
                yield
                fw.act(sz.t[:, :], ps[2].t[:, :], AF.Silu, [ps[2].b], [sz.b])
                for kc in range(8):
                    fw.mm(ps[3].t[:, 0:8], C.uT.t[:, kc, tok], WS.t[:, kc, 1536:1544], kc == 0, kc == 7, [C.uTb[i], WS.b], [ps[3].b])
                yield
                smb = [sm.b]
                fw.tt("dve", sm.t[:, XD], ps[3].t[:, 0:8], dtb.t[:, :], ALU.add, [ps[3].b, dtb.b], smb)
                fw.ts("dve", sm.t[:, AXc], sm.t[:, XD], -1.0, None, ALU.mult, None, smb, smb)
                fw.tt("dve", sm.t[:, AXc], sm.t[:, AXc], sm.t[:, XD], ALU.max, smb, smb)
                fw.act(sm.t[:, EX], sm.t[:, AXc], AF.Exp, smb, smb, scale=-1.0)
                fw.ts("dve", sm.t[:, EX], sm.t[:, EX], 1.0, None, ALU.add, None, smb, smb)
                fw.act(sm.t[:, LN], sm.t[:, EX], AF.Ln, smb, smb)
                fw.stt("dve", sm.t[:, DT], sm.t[:, XD], 0.0, sm.t[:, LN], ALU.max, ALU.add, smb, smb)
                fw.tt("dve", sm.t[:, LA], sm.t[:, DT], arow.t[:, :], ALU.mult, smb + [arow.b], smb)
                fw.mm(ps[3].t[:, 16:24], triu, sm.t[:, LA], True, True, [cb_, sm.b], [ps[3].b])
                yield
                fw.mm(ps[3].t[:, 32:40], ones, sm.t[:, LA], True, True, [cb_, sm.b], [ps[3].b])
                yield
                fw.cp("dve", sm.t[:, CUM], ps[3].t[:, 16:24], [ps[3].b], smb)
                fw.ts("dve", sm.t[:, NCUM], sm.t[:, CUM], -1.0, None, ALU.mult, None, smb, smb)
                fw.act(sm.t[:, ECUM], sm.t[:, CUM], AF.Exp, smb, smb)
                fw.tt("dve", sm.t[:, WW], ps[3].t[:, 32:40], sm.t[:, CUM], ALU.subtract, [ps[3].b] + smb, smb)
                fw.act(sm.t[:, WW], sm.t[:, WW], AF.Exp, smb, smb)
                fw.act(sm.t[:, EDEC], ps[3].t[:, 32:40], AF.Exp, [ps[3].b], smb)
                fw.tt("dve", sm.t[:, DW], sm.t[:, DT], sm.t[:, WW], ALU.mult, smb, smb)
        def stB(i):
                xc, xcb, xs, Btok, sz, sm, decT, PT, xdt, xdtw, ya, tmp, yo, yT = xc_2[i % 2], xcb_2[i % 2], xs_2[i % 2], Btok_2[i % 2], sz_2[i % 2], sm_2[i % 2], decT_2[i % 2], PT_2[i % 2], xdt_2[i % 2], xdtw_2[i % 2], ya_2[i % 2], tmp_2[i % 2], yo_2[i % 2], yT_2[i % 2]
                smb = [sm.b]
                pb0, pb1 = C.pb
                rhs8 = rhs8_2[i % 2]
                for cb in range(4):
                    fw.tr(pb0.t[:, cb * 128:(cb + 1) * 128], xcb.t[:, cb, :], C.identb.t[:, :], [xcb.b, C.identb.b], [pb0.b])
                yield
                fw.cp("dve", xs.t[:, :], pb0.t[:, 0:512], [pb0.b], [xs.b])
                for g in range(2):
                    fw.tr(pb1.t[:, g * 128:(g + 1) * 128], xcb.t[:, 4 + g, :], C.identb.t[:, :], [xcb.b, C.identb.b], [pb1.b])
                yield
                fw.cp("act", Btok.t[:, :, :], pb1.t[:, 0:256].rearrange("p (g n) -> p g n", g=2), [pb1.b], [Btok.b])
                fw.tt("dve", rhs8.t[:, :, :], triu.unsqueeze(1).to_broadcast([128, 8, 128]),
                      sm.t[:, LA].unsqueeze(2).to_broadcast([128, 8, 128]), ALU.mult, [cb_, sm.b], [rhs8.b])
                fw.mm(ps[4].t[:, :], ones, rhs8.t[:, 0:4, :].rearrange("p h t -> p (h t)"), True, True, [cb_, rhs8.b], [ps[4].b])
                fw.mm(ps[5].t[:, :], ones, rhs8.t[:, 4:8, :].rearrange("p h t -> p (h t)"), True, True, [cb_, rhs8.b], [ps[5].b])
                for hb in range(2):
                    fw.tt("dve", decT.t[:, 4 * hb:4 * hb + 4, :], ps[4 + hb].t[:, :].rearrange("p (h t) -> p h t", h=4),
                          sm.t[:, 56 + 4 * hb:60 + 4 * hb].unsqueeze(2).to_broadcast([128, 4, 128]), ALU.add, [ps[4 + hb].b, sm.b], [decT.b])
                fw.tt("pool", decT.t[:, :, :], decT.t[:, :, :], cslice(C, "negm").unsqueeze(1).to_broadcast([128, 8, 128]), ALU.add,
                      [decT.b, cb_], [decT.b])
                fw.act(decT.t[:, :, :], decT.t[:, :, :], AF.Exp, [decT.b], [decT.b])
                for g in range(2):
                    fw.mm(ps[4].t[:, g * 128:(g + 1) * 128], xcb.t[:, 4 + g, :], xcb.t[:, 6 + g, :], True, True, [xcb.b], [ps[4].b])
                yield
                for g in range(2):
                    fw.tt("dve", PT.t[:, 4 * g:4 * g + 4, :], decT.t[:, 4 * g:4 * g + 4, :],
                          ps[4].t[:, g * 128:(g + 1) * 128].unsqueeze(1).to_broadcast([128, 4, 128]), ALU.mult, [decT.b, ps[4].b], [PT.b])
                xsv = xs.t[:, :].rearrange("p (h e) -> p h e", h=8)
                fw.tt("pool", xdt.t[:, :].rearrange("p (h e) -> p h e", h=8), xsv, sm.t[:, DT].unsqueeze(2).to_broadcast([128, 8, 64]),
                      ALU.mult, [xs.b, sm.b], [xdt.b])
                fw.tt("pool", xdtw.t[:, :].rearrange("p (h e) -> p h e", h=8), xsv, sm.t[:, DW].unsqueeze(2).to_broadcast([128, 8, 64]),
                      ALU.mult, [xs.b, sm.b], [xdtw.b])
                for h in range(8):
                    fw.mm(ps[5].t[:, h * 64:(h + 1) * 64], PT.t[:, h, :], xdt.t[:, h * 64:(h + 1) * 64], True, True, [PT.b, xdt.b], [ps[5].b])
                yield
                for g in range(2):
                    fw.mm(ps[4].t[:, g * 256:(g + 1) * 256], xcb.t[:, 6 + g, :], Sb.t[:, g * 256:(g + 1) * 256], True, True, [xcb.b, Sb.b], [ps[4].b])
                yield
                fw.tt("dve", ya.t[:, :].rearrange("p (h e) -> p h e", h=8), ps[4].t[:, :].rearrange("p (h e) -> p h e", h=8),
                      sm.t[:, ECUM].unsqueeze(2).to_broadcast([128, 8, 64]), ALU.mult, [ps[4].b, sm.b], [ya.b])
                fw.tt("dve", ya.t[:, :], ya.t[:, :], ps[5].t[:, :], ALU.add, [ya.b, ps[5].b], [ya.b])
                fw.tt("pool", tmp.t[:, :], xs.t[:, :], dsk.t[:, :], ALU.mult, [xs.b, dsk.b], [tmp.b])
                fw.tt("pool", ya.t[:, :], ya.t[:, :], tmp.t[:, :], ALU.add, [ya.b, tmp.b], [ya.b])
                fw.tt("pool", ya.t[:, :], ya.t[:, :], sz.t[:, :], ALU.mult, [ya.b, sz.b], [ya.b])
                for g in range(2):
                    fw.mm(ps[5].t[:, g * 256:(g + 1) * 256], Btok.t[:, g, :], xdtw.t[:, g * 256:(g + 1) * 256], True, True,
                          [Btok.b, xdtw.b], [ps[5].b])
                yield
                Sv = S.t[:, :].rearrange("p (h e) -> p h e", h=8)
                fw.tt("dve", Sv, Sv, sm.t[:, EDEC].unsqueeze(2).to_broadcast([128, 8, 64]), ALU.mult, [S.b, sm.b], [S.b])
                fw.tt("dve", S.t[:, :], S.t[:, :], ps[5].t[:, :], ALU.add, [S.b, ps[5].b], [S.b])
                fw.cp("pool", Sb.t[:, :], S.t[:, :], [S.b], [Sb.b])
                fw.act(tmp.t[:, :], ya.t[:, :], AF.Square, [ya.b], [tmp.b])
                fw.op("dve", lambda g_: g_.reduce_sum(sm.t[:, 96:98], tmp.t[:, :].rearrange("p (g e) -> p g e", g=2), axis=AX.X), [tmp.b], smb)
                rstd_from_ss(fw, C, sm.t[:, 96:98], sm.t[:, 98:100], 256.0, smb, smb)
                for g in range(2):
                    gs = slice(g * 256, (g + 1) * 256)
                    fw.ts("dve", tmp.t[:, gs], ya.t[:, gs], sm.t[:, 98 + g:99 + g], None, ALU.mult, None, [ya.b, sm.b], [tmp.b])
                fw.tt("pool", yo.t[:, :], tmp.t[:, :], nw.t[:, :], ALU.mult, [tmp.b, nw.b], [yo.b])
                emit_yT(fw, C, yo, yT, 2, i)
                yield
        for _ in stA(0):
            pass
        for i in range(NT):
            interleave(stB(i), stA(i + 1) if i + 1 < NT else None)
        fw.barrier()


def phase_hg(fw, C, l):
    CUR_PHASE[0] = "hg"
    with ExitStack() as ph:
        WH = fw.tile(ph, "WH", [128, 8, 2048], BF16)
        load_w(fw, C.w_in[l], WH, C_HG, C_HG + 2048, 8)
        nw = fw.tile(ph, "hnw", [128, 512], F32)
        fw.dma("sp", nw.t[:, :], C.hgrn_norm_w[l].partition_broadcast(128), writes=[nw.b])
        S = fw.tile(ph, "hS", [128, 4, 128], F32)
        Sb = fw.tile(ph, "hSb", [128, 4, 128], BF16)
        PT_2 = [fw.tile(ph, "hPT", [128, 4, 128], BF16) for _ in range(2)]
        fw.memset("dve", S.t[:, :, :], 0.0, [S.b])
        fw.memset("dve", Sb.t[:, :, :], 0.0, [Sb.b])
        for PT in PT_2:
            fw.memset("pool", PT.t[:, :, :], 0.0, [PT.b])

        qg_2 = [fw.tile(ph, "hqg", [128, 4, 512], F32) for _ in range(2)]
        fg_2 = [fw.tile(ph, "hfg", [128, 4, 512], F32) for _ in range(2)]
        la_2 = [fw.tile(ph, "hla", [128, 4, 128], F32) for _ in range(2)]
        kT_2 = [fw.tile(ph, "hk", [128, 4, 128], F32) for _ in range(2)]
        cum_2 = [fw.tile(ph, "hcum", [128, 4, 128], F32) for _ in range(2)]
        ncb_2 = [fw.tile(ph, "hncb", [128, 4, 4], F32) for _ in range(2)]
        for nb_ in ncb_2:
            fw.memset("pool", nb_.t[:, :, :], 0.0, [nb_.b])
        eq_2 = [fw.tile(ph, "heq", [128, 4, 128], F32) for _ in range(2)]
        qd_2 = [fw.tile(ph, "hqd", [128, 4, 128], BF16) for _ in range(2)]
        qst_2 = [fw.tile(ph, "hqst", [128, 4, 128], BF16) for _ in range(2)]
        ek_2 = [fw.tile(ph, "hek", [128, 4, 128], F32) for _ in range(2)]
        Kt_2 = [fw.tile(ph, "hKt", [128, 4, 4, 128], BF16) for _ in range(2)]
        khT_2 = [fw.tile(ph, "hkhT", [128, 4, 128], BF16) for _ in range(2)]
        khat_2 = [fw.tile(ph, "hkhat", [128, 4, 128], BF16) for _ in range(2)]
        dec_2 = [fw.tile(ph, "hdec", [128, 4], F32) for _ in range(2)]
        vb_3 = [fw.tile(ph, "hvb", [128, 512], BF16) for _ in range(3)]
        sgate_3 = [fw.tile(ph, "hsg", [128, 512], F32) for _ in range(3)]
        ysq_2 = [fw.tile(ph, "hysq", [128, 512], F32) for _ in range(2)]
        st_2 = [fw.tile(ph, "hst", [128, 8], F32) for _ in range(2)]
        yn_2 = [fw.tile(ph, "hyn", [128, 512], F32) for _ in range(2)]
        yo_2 = [fw.tile(ph, "hyo", [128, 512], BF16) for _ in range(2)]
        yT_2 = [fw.tile(ph, "hyT", [128, 4, 128], BF16) for _ in range(2)]
        ps = C.ps
        pb0, pb1 = C.pb
        cb_ = C.consts.b
        lbc = C.lb_all.t[:, l, :]
        omc = C.oml_all.t[:, l, :]
        ones = cslice(C, "ones")
        triu = cslice(C, "triu")
        def stG(g):
                t0 = g * 512
                n = min(512, TP - t0)
                tiles = list(range(t0 // 128, (t0 + n) // 128))
                qg, fg = qg_2[g % 2], fg_2[g % 2]
                k = 0
                for h in range(4):
                    for (c0, dst, func) in ((0, qg, AF.Silu), (512, fg, AF.Sigmoid)):
                        pst = ps[0]
                        for kc in range(8):
                            fw.mm(pst.t[:, 0:n], WH.t[:, kc, c0 + h * 128:c0 + (h + 1) * 128], C.uT.t[:, kc, t0:t0 + n], kc == 0, kc == 7,
                                  [WH.b] + [C.uTb[j] for j in tiles], [pst.b])
                        fw.act(dst.t[:, h, 0:n], pst.t[:, 0:n], func, [pst.b], [dst.b])
                        yield
        def stA(i):
                PT = PT_2[i % 2]
                la, kT, cum, ncb, eq, qd, qst, ek, Kt, khT, khat, dec, vb, sgate, ysq, st, yn, yo, yT = la_2[i % 2], kT_2[i % 2], cum_2[i % 2], ncb_2[i % 2], eq_2[i % 2], qd_2[i % 2], qst_2[i % 2], ek_2[i % 2], Kt_2[i % 2], khT_2[i % 2], khat_2[i % 2], dec_2[i % 2], vb_3[i % 3], sgate_3[i % 3], ysq_2[i % 2], st_2[i % 2], yn_2[i % 2], yo_2[i % 2], yT_2[i % 2]
                tok_mm(fw, C, ps[2], WH, 1024, 512, i)
                yield
                tok_mm(fw, C, ps[3], WH, 1536, 512, i)
                yield
                fw.cp("act", vb.t[:, :], ps[2].t[:, :], [ps[2].b], [vb.b])
                yield
                fw.act(sgate.t[:, :], ps[3].t[:, :], AF.Silu, [ps[3].b], [sgate.b])
                yield
        def stB1(i):
                PT = PT_2[i % 2]
                la, kT, cum, ncb, eq, qd, qst, ek, Kt, khT, khat, dec, vb, sgate, ysq, st, yn, yo, yT = la_2[i % 2], kT_2[i % 2], cum_2[i % 2], ncb_2[i % 2], eq_2[i % 2], qd_2[i % 2], qst_2[i % 2], ek_2[i % 2], Kt_2[i % 2], khT_2[i % 2], khat_2[i % 2], dec_2[i % 2], vb_3[i % 3], sgate_3[i % 3], ysq_2[i % 2], st_2[i % 2], yn_2[i % 2], yo_2[i % 2], yT_2[i % 2]
                qT = VW(qg_2[(i // 4) % 2].t[:, :, (i % 4) * 128:(i % 4 + 1) * 128], qg_2[(i // 4) % 2].b)
                fT = VW(fg_2[(i // 4) % 2].t[:, :, (i % 4) * 128:(i % 4 + 1) * 128], fg_2[(i // 4) % 2].b)
                for h in range(4):
                    fw.ts("dve", fT.t[:, h, :], fT.t[:, h, :], omc[:, h:h + 1], lbc[:, h:h + 1], ALU.mult, ALU.add,
                          [fT.b, C.lb_all.b, C.oml_all.b], [fT.b])
                yield
                fw.act(la.t[:, :, :], fT.t[:, :, :], AF.Ln, [fT.b], [la.b])
                yield
                fw.ts("pool", kT.t[:, :, :], fT.t[:, :, :], -1.0, 1.0, ALU.mult, ALU.add, [fT.b], [kT.b])
                yield
                for h in range(4):
                    fw.op("dve", lambda g: g.tensor_tensor_scan(cum.t[:, h, :], ones, la.t[:, h, :], 0.0, ALU.mult, ALU.add),
                          [la.b, cb_], [cum.b])
                yield
                cv = cum.t[:, :, :].rearrange("p h (b c) -> p h b c", c=32)
                fw.ts("dve", ncb.t[:, :, 1:4], cv[:, :, 0:3, 31], -1.0, None, ALU.mult, None, [cum.b], [ncb.b])
                yield
                fw.tt("dve", eq.t[:, :, :].rearrange("p h (b c) -> p h b c", c=32), cv,
                      ncb.t[:, :, :].unsqueeze(3).to_broadcast([128, 4, 4, 32]), ALU.add, [cum.b, ncb.b], [eq.b])
                yield
                fw.act(eq.t[:, :, :], eq.t[:, :, :], AF.Exp, [eq.b], [eq.b])
                yield
                fw.tt("pool", qd.t[:, :, :], qT.t[:, :, :], eq.t[:, :, :], ALU.mult, [qT.b, eq.b], [qd.b])
                yield
                fw.act(eq.t[:, :, :], cum.t[:, :, :], AF.Exp, [cum.b], [eq.b])
                yield
                fw.tt("pool", qst.t[:, :, :], qT.t[:, :, :], eq.t[:, :, :], ALU.mult, [qT.b, eq.b], [qst.b])
                yield
                for b in range(4):
                    W_ = 32 * (b + 1)
                    eb = ek if b % 2 == 0 else eq
                    fw.tt("dve", eb.t[:, :, 0:W_], cum.t[:, :, 0:W_], ncb.t[:, :, b:b + 1].to_broadcast([128, 4, W_]), ALU.add,
                          [cum.b, ncb.b], [eb.b])
                    fw.act(eb.t[:, :, 0:W_], eb.t[:, :, 0:W_], AF.Exp, [eb.b], [eb.b], scale=-1.0)
                    fw.tt("pool" if b % 2 == 0 else "dve", Kt.t[:, :, b, 0:W_], eb.t[:, :, 0:W_], kT.t[:, :, 0:W_], ALU.mult,
                          [eb.b, kT.b], [Kt.b])
                    yield
                fw.tt("dve", ek.t[:, :, :], cum.t[:, :, 127:128].to_broadcast([128, 4, 128]), cum.t[:, :, :], ALU.subtract, [cum.b], [ek.b])
                yield
                fw.act(ek.t[:, :, :], ek.t[:, :, :], AF.Exp, [ek.b], [ek.b])
                yield
                fw.tt("pool", khT.t[:, :, :], ek.t[:, :, :], kT.t[:, :, :], ALU.mult, [ek.b, kT.b], [khT.b])
                yield
                for h in range(4):
                    fw.tr(pb0.t[:, h * 128:(h + 1) * 128], khT.t[:, h, :], C.identb.t[:, :], [khT.b, C.identb.b], [pb0.b])
                yield
                fw.cp("act", khat.t[:, :, :], pb0.t[:, 0:512].rearrange("p (h d) -> p h d", h=4), [pb0.b], [khat.b])
                yield
                fw.act(dec.t[:, :], cum.t[:, :, 127], AF.Exp, [cum.b], [dec.b])
                yield
                for h in range(4):
                    for b in range(4):
                        W_ = 32 * (b + 1)
                        fw.mm(ps[4].t[0:W_, h * 128 + 32 * b:h * 128 + 32 * b + 32], Kt.t[:, h, b, 0:W_], qd.t[:, h, 32 * b:32 * b + 32],
                              True, True, [Kt.b, qd.b], [ps[4].b])
                yield
                psv = ps[4].t[:, :].rearrange("p (h t) -> p h t", h=4)
                for b in range(4):
                    W_ = 32 * (b + 1)
                    bs = slice(32 * b, 32 * b + 32)
                    to, tn = CO["triu"]
                    mk = C.consts.t[0:W_, to + 32 * b:to + 32 * b + 32].unsqueeze(1).to_broadcast([W_, 4, 32])
                    fw.tt("dve", PT.t[0:W_, :, bs], psv[0:W_, :, bs], mk, ALU.mult, [ps[4].b, cb_], [PT.b])
                yield
        def stB2(i):
                PT = PT_2[i % 2]
                la, kT, cum, ncb, eq, qd, qst, ek, Kt, khT, khat, dec, vb, sgate, ysq, st, yn, yo, yT = la_2[i % 2], kT_2[i % 2], cum_2[i % 2], ncb_2[i % 2], eq_2[i % 2], qd_2[i % 2], qst_2[i % 2], ek_2[i % 2], Kt_2[i % 2], khT_2[i % 2], khat_2[i % 2], dec_2[i % 2], vb_3[i % 3], sgate_3[i % 3], ysq_2[i % 2], st_2[i % 2], yn_2[i % 2], yo_2[i % 2], yT_2[i % 2]
                qT = VW(qg_2[(i // 4) % 2].t[:, :, (i % 4) * 128:(i % 4 + 1) * 128], qg_2[(i // 4) % 2].b)
                fT = VW(fg_2[(i // 4) % 2].t[:, :, (i % 4) * 128:(i % 4 + 1) * 128], fg_2[(i // 4) % 2].b)
                for h in range(4):
                    hs = slice(h * 128, (h + 1) * 128)
                    fw.mm(ps[5].t[:, hs], PT.t[:, h, :], vb.t[:, hs], True, False, [PT.b, vb.b], [ps[5].b])
                    fw.mm(ps[5].t[:, hs], qst.t[:, h, :], Sb.t[:, h, :], False, True, [qst.b, Sb.b], [ps[5].b])
                yield
                for h in range(4):
                    hs = slice(h * 128, (h + 1) * 128)
                    fw.mm(ps[1].t[:, hs], khat.t[:, h, :], vb.t[:, hs], True, True, [khat.b, vb.b], [ps[1].b])
                yield
                for h in range(4):
                    hs = slice(h * 128, (h + 1) * 128)
                    fw.stt("dve", S.t[:, h, :], S.t[:, h, :], dec.t[:, h:h + 1], ps[1].t[:, hs], ALU.mult, ALU.add, [S.b, dec.b, ps[1].b], [S.b])
                yield
                fw.cp("pool", Sb.t[:, :, :], S.t[:, :, :], [S.b], [Sb.b])
                yield
                fw.act(ysq.t[:, :], ps[5].t[:, :], AF.Square, [ps[5].b], [ysq.b])
                yield
                fw.op("dve", lambda g: g.reduce_sum(st.t[:, 0:4], ysq.t[:, :].rearrange("p (h e) -> p h e", h=4), axis=AX.X), [ysq.b], [st.b])
                yield
                rstd_from_ss(fw, C, st.t[:, 0:4], st.t[:, 4:8], 128.0, [st.b], [st.b])
                yield
                for h in range(4):
                    hs = slice(h * 128, (h + 1) * 128)
                    fw.ts("dve", yn.t[:, hs], ps[5].t[:, hs], st.t[:, 4 + h:5 + h], None, ALU.mult, None, [ps[5].b, st.b], [yn.b])
                yield
                fw.tt("pool", yn.t[:, :], yn.t[:, :], nw.t[:, :], ALU.mult, [yn.b, nw.b], [yn.b])
                yield
                fw.tt("pool", yo.t[:, :], yn.t[:, :], sgate.t[:, :], ALU.mult, [yn.b, sgate.b], [yo.b])
                yield
                emit_yT(fw, C, yo, yT, 3, i)
                yield
        for st0 in (stG(0), stA(0), stB1(0), stA(1) if NT > 1 else None):
            if st0 is not None:
                for _ in st0:
                    pass
        for i in range(NT):
            gens = [stB2(i)]
            if i + 1 < NT:
                gens.append(stB1(i + 1))
            if i + 2 < NT:
                gens.append(chain_gens(stG((i + 2) // 4) if (i + 2) % 4 == 0 else None, stA(i + 2)))
            interleave_n(gens)
        fw.barrier()


TGS = [(0, 512), (512, 512), (1024, 512), (1536, 512), (2048, 128)]


def phase_s5(fw, C, l):
    ps = C.ps
    pb0, pb1 = C.pb
    cb_ = C.consts.b
    with ExitStack() as ph:
        GW = fw.tile(ph, "GW", [128, 4, 1024], BF16)
        load_w(fw, C.s5_glu_w[l], GW, 0, 1024, 4)
        Cre = fw.tile(ph, "Cre", [128, 16, 128], BF16)
        nCim = fw.tile(ph, "nCim", [128, 16, 128], BF16)
        fw.dma("pool", Cre.t[:, :, :], C.s5_cre[l], writes=[Cre.b])
        fw.dma("pool", nCim.t[:, :, :], C.s5_cim[l], writes=[nCim.b])
        fw.ts("pool", nCim.t[:, :, :], nCim.t[:, :, :], -1.0, None, ALU.mult, None, [nCim.b], [nCim.b])
        d5 = fw.tile(ph, "d5", [128, 4], F32)
        gb = fw.tile(ph, "gb5", [128, 8], F32)
        fw.dma("sp", d5.t[:, :], C.s5_dT[l], writes=[d5.b])
        fw.dma("sp", gb.t[:, :], C.s5_gbT[l], writes=[gb.b])
        sp_ = fw.tile(ph, "s5sm", [128, 48], F32)
        fw.dma("sp", sp_.t[:, :], C.s5_small[l], writes=[sp_.b])
        u5T = fw.tile(ph, "u5T", [128, 4, TP], BF16)
        Sall = fw.tile(ph, "Sall", [128, 137, 3, 16], F32)
        KT = fw.tile(ph, "KT", [128, 4, 16, 128], BF16)
        PW = fw.tile(ph, "PW", [128, 2, 17, 16], F32)
        wk = fw.tile(ph, "s5wk", [128, 12, 16], F32)
        with ExitStack() as pa:
            W5 = fw.tile(pa, "W5", [128, 8, 512], BF16)
            load_w(fw, C.w_in[l], W5, C_S5, C_S5 + 512, 8)
            k = 0
            for ct in range(4):
                for (t0, n) in TGS:
                    pst = ps[k % 2]
                    for kc in range(8):
                        fw.mm(pst.t[:, 0:n], W5.t[:, kc, ct * 128:(ct + 1) * 128], C.uT.t[:, kc, t0:t0 + n], kc == 0, kc == 7,
                              [W5.b] + C.uTb[t0 // 128:(t0 + n) // 128], [pst.b])
                    fw.cp("act" if k % 2 else "dve", u5T.t[:, ct, t0:t0 + n], pst.t[:, 0:n], [pst.b], [u5T.b])
                    k += 1
            fw.barrier()
        if os.environ.get("S5_STOP") == "A":
            fw.barrier(); return
        lr, li, lst = sp_.t[:, 0:16], sp_.t[:, 16:32], sp_.t[:, 32:48]
        W_ = [wk.t[:, j, :] for j in range(12)]
        R = [sp_.b, wk.b, PW.b]
        step, lrs, ang, em1, re_, im_, t_a, t_b, inv, co_re, co_im, rr = W_
        big = [fw.tile(ph, "s5big", [128, 18, 16], F32) for _ in range(5)]
        bigi = fw.tile(ph, "s5bigi", [128, 18, 16], mybir.dt.int32)
        RB = R + [b_.b for b_ in big] + [bigi.b, cb_]
        FACT = [1.0, 1.0, 2.0, 6.0, 24.0, 120.0, 720.0, 5040.0, 40320.0, 362880.0, 3628800.0]

        def horner_exp(out, r, deg, minus1=False):
            fw.ts("dve", out, r, 1.0 / FACT[deg], None, ALU.mult, None, RB, RB)
            for j in range(deg - 1, 0, -1):
                fw.stt("dve", out, out, 1.0 / FACT[j], r, ALU.add, ALU.mult, RB, RB)
            if not minus1:
                fw.ts("dve", out, out, 1.0, None, ALU.add, None, RB, RB)

        fw.ts("dve", rr, lst, 0.125, None, ALU.mult, None, RB, RB)
        horner_exp(step, rr, 10)
        for _ in range(3):
            fw.tt("dve", step, step, step, ALU.mult, RB, RB)
        fw.tt("dve", lrs, lr, step, ALU.mult, RB, RB)
        fw.tt("dve", ang, li, step, ALU.mult, RB, RB)
        mo = CO["mtab"][0]
        mtab = C.consts.t[:, mo:mo + 288].rearrange("p (m q) -> p m q", m=18)
        TH, XM, MAG, SN, CS = [b_.t[:, :, :] for b_ in big]
        fw.tt("dve", TH, mtab, ang.unsqueeze(1).to_broadcast([128, 18, 16]), ALU.mult, RB, RB)
        fw.tt("dve", XM, mtab, lrs.unsqueeze(1).to_broadcast([128, 18, 16]), ALU.mult, RB, RB)
        horner_exp(MAG, XM, 10)
        C1, C2 = 6.28125, 2.0 * math.pi - 6.28125

        def sin_reduced(out, th):
            fw.ts("dve", out, th, 1.0 / (2.0 * math.pi), None, ALU.mult, None, RB, RB)
            fw.cp("dve", bigi.t[:, :, :], out, RB, RB)
            fw.cp("dve", XM, bigi.t[:, :, :], RB, RB)
            fw.stt("dve", out, XM, -C1, th, ALU.mult, ALU.add, RB, RB)
            fw.stt("dve", out, XM, -C2, out, ALU.mult, ALU.add, RB, RB)
            fw.act(out, out, AF.Sin, RB, RB)

        sin_reduced(SN, TH)
        fw.ts("dve", TH, TH, math.pi / 2, None, ALU.add, None, RB, RB)
        sin_reduced(CS, TH)
        fw.tt("dve", PW.t[:, 0, :, :], MAG[:, 0:17, :], CS[:, 0:17, :], ALU.mult, RB, RB)
        fw.tt("dve", PW.t[:, 1, :, :], MAG[:, 0:17, :], SN[:, 0:17, :], ALU.mult, RB, RB)
        horner_exp(em1, lrs, 7, minus1=True)
        fw.tt("dve", re_, em1, CS[:, 1, :], ALU.mult, RB, RB)
        fw.tt("dve", t_a, SN[:, 17, :], SN[:, 17, :], ALU.mult, RB, RB)
        fw.stt("dve", re_, t_a, -2.0, re_, ALU.mult, ALU.add, RB, RB)
        fw.ts("dve", t_b, em1, 1.0, None, ALU.add, None, RB, RB)
        fw.tt("dve", im_, t_b, SN[:, 1, :], ALU.mult, RB, RB)
        fw.tt("dve", t_a, lr, lr, ALU.mult, RB, RB)
        fw.tt("dve", t_b, li, li, ALU.mult, RB, RB)
        fw.tt("dve", inv, t_a, t_b, ALU.add, RB, RB)
        fw.op("dve", lambda g: g.reciprocal(inv, inv), RB, RB)
        fw.tt("dve", t_a, re_, lr, ALU.mult, RB, RB)
        fw.tt("dve", t_b, im_, li, ALU.mult, RB, RB)
        fw.tt("dve", t_a, t_a, t_b, ALU.add, RB, RB)
        fw.tt("dve", co_re, t_a, inv, ALU.mult, RB, RB)
        fw.tt("dve", t_a, im_, lr, ALU.mult, RB, RB)
        fw.tt("dve", t_b, re_, li, ALU.mult, RB, RB)
        fw.tt("dve", t_a, t_a, t_b, ALU.subtract, RB, RB)
        fw.tt("dve", co_im, t_a, inv, ALU.mult, RB, RB)
        if C.debug and l == 0:
            fw.dma("sp", C.dbg5[:, 0:192], wk.t[:, :, :].rearrange("p a b -> p (a b)"), reads=[wk.b], writes=[Buf()])
            fw.dma("sp", C.dbg5[:, 192:736], PW.t[:, :, :, :].rearrange("p a m q -> p (a m q)"), reads=[PW.b], writes=[Buf()])
        if os.environ.get("S5_STOP") == "B":
            fw.barrier(); return
        fw.memset("pool", Sall.t[:, 0, :, :], 0.0, [Sall.b])
        with ExitStack() as pd:
            Bst = fw.tile(pd, "Bst", [128, 2, 4, 128], F32)
            Bb = fw.tile(pd, "Bb", [128, 2, 4, 128], F32)
            t0_ = fw.tile(pd, "tB", [128, 128], F32)
            tA = [fw.tile(pd, "tA", [128, 16, 128], F32)] * 2
            tB = [fw.tile(pd, "tBB", [128, 16, 128], F32)] * 2
            Xs = [fw.tile(pd, "X", [128, 2, 16, 128], BF16) for _ in range(2)]
            XTs = [fw.tile(pd, "XT", [128, 4, 2, 128], BF16) for _ in range(2)]
            psK = ps[2:6]
            zt = fw.tile(pd, "zt", [128, 512], BF16)
            fw.memset("pool", zt.t[:, :], 0.0, [zt.b])
            it = 0
            for ct in range(4):
                for j in range(4):
                    fw.mm(psK[j].t[:, :], zt.t[:, 0:128], zt.t[:, :], True, False, [zt.b], [psK[j].b])
                fw.dma("sp", Bst.t[:, 0, :, :], C.s5_bre[l][:, 4 * ct:4 * ct + 4, :], writes=[Bst.b])
                fw.dma("sp", Bst.t[:, 1, :, :], C.s5_bim[l][:, 4 * ct:4 * ct + 4, :], writes=[Bst.b])
                for pl in range(4):
                    pair = 4 * ct + pl
                    cr, ci = co_re[:, pair:pair + 1], co_im[:, pair:pair + 1]
                    fw.ts("dve", t0_.t[:, :], Bst.t[:, 1, pl, :], ci, None, ALU.mult, None, [Bst.b, wk.b], [t0_.b])
                    fw.stt("dve", Bb.t[:, 0, pl, :], Bst.t[:, 0, pl, :], cr, t0_.t[:, :], ALU.mult, ALU.subtract, [Bst.b, wk.b, t0_.b], [Bb.b])
                    fw.ts("dve", t0_.t[:, :], Bst.t[:, 1, pl, :], cr, None, ALU.mult, None, [Bst.b, wk.b], [t0_.b])
                    fw.stt("dve", Bb.t[:, 1, pl, :], Bst.t[:, 0, pl, :], ci, t0_.t[:, :], ALU.mult, ALU.add, [Bst.b, wk.b, t0_.b], [Bb.b])
                for pl in range(4):
                    pair = 4 * ct + pl
                    psG = ps[pair % 2]
                    X = Xs[pair % 2]
                    ta, tb = tA[pair % 2], tB[pair % 2]
                    bre = Bb.t[:, 0, pl, :].unsqueeze(1).to_broadcast([128, 16, 128])
                    bim = Bb.t[:, 1, pl, :].unsqueeze(1).to_broadcast([128, 16, 128])
                    prb = PW.t[:, 0, 0:16, pair].unsqueeze(2).to_broadcast([128, 16, 128])
                    pib = PW.t[:, 1, 0:16, pair].unsqueeze(2).to_broadcast([128, 16, 128])
                    RB_ = [Bb.b, PW.b]
                    fw.tt("dve", ta.t[:, :, :], bre, prb, ALU.mult, RB_, [ta.b])
                    fw.tt("pool", tb.t[:, :, :], bim, pib, ALU.mult, RB_, [tb.b])
                    fw.tt("dve", X.t[:, 0, :, :], ta.t[:, :, :], tb.t[:, :, :], ALU.subtract, [ta.b, tb.b], [X.b])
                    fw.tt("pool", tb.t[:, :, :], bim, prb, ALU.mult, RB_, [tb.b])
                    fw.tt("dve", ta.t[:, :, :], bre, pib, ALU.mult, RB_, [ta.b])
                    fw.tt("dve", X.t[:, 1, :, :], ta.t[:, :, :], tb.t[:, :, :], ALU.add, [ta.b, tb.b], [X.b])
                    fw.mm(psG.t[:, 0:272], zt.t[:, 0:128], zt.t[:, 0:272], True, False, [zt.b], [psG.b])
                    for m in range(16):
                        pk = psK[m // 4]
                        ks = slice((m % 4) * 128, (m % 4 + 1) * 128)
                        fw.mm(pk.t[:, ks], X.t[:, 0, m, :], Cre.t[:, pair, :], False, False, [X.b, Cre.b], [pk.b])
                        fw.mm(pk.t[:, ks], X.t[:, 1, m, :], nCim.t[:, pair, :], False, pl == 3, [X.b, nCim.b], [pk.b])
                    for mg in range(4):
                        XT = XTs[it % 2]
                        pbt = C.pb[it % 2]
                        for mm_ in range(4):
                            m = 4 * mg + mm_
                            for part in range(2):
                                fw.tr(pbt.t[:, (2 * mm_ + part) * 128:(2 * mm_ + part + 1) * 128], X.t[:, part, m, :], C.identb.t[:, :],
                                      [X.b, C.identb.b], [pbt.b])
                        fw.cp("act" if it % 2 else "dve", XT.t[:, :, :, :], pbt.t[:, :].rearrange("p (m a q) -> p m a q", m=4, a=2),
                              [pbt.b], [XT.b])
                        for mm_ in range(4):
                            m = 4 * mg + mm_
                            tau = 15 - m
                            rhs = u5T.t[:, ct, :].rearrange("p (c b) -> p c b", b=16)[:, :, tau]
                            fw.mm(psG.t[:, 0:136], XT.t[:, mm_, 0, :], rhs, False, m == 15, [XT.b, u5T.b], [psG.b])
                            fw.mm(psG.t[:, 136:272], XT.t[:, mm_, 1, :], rhs, False, m == 15, [XT.b, u5T.b], [psG.b])
                        it += 1
                    fw.cp("act", Sall.t[:, 1:137, 0, pair], psG.t[:, 0:136], [psG.b], [Sall.b])
                    fw.cp("act", Sall.t[:, 1:137, 1, pair], psG.t[:, 136:272], [psG.b], [Sall.b])
                for j in range(4):
                    fw.cp("act" if j % 2 else "dve", KT.t[:, ct, 4 * j:4 * j + 4, :], psK[j].t[:, :].rearrange("p (m c) -> p m c", m=4),
                          [psK[j].b], [KT.b])
            fw.barrier()
        if C.debug and l == 0:
            fw.dma("pool", C.dbg5[:, 736:736 + 2048], KT.t[:, 0, :, :].rearrange("p m c -> p (m c)"), reads=[KT.b], writes=[Buf()])
            fw.dma("sp", C.dbg5[:, 2784:2784 + 137 * 48], Sall.t[:, :, :, :].rearrange("p c a q -> p (c a q)"), reads=[Sall.b], writes=[Buf()])
        if os.environ.get("S5_STOP") == "D":
            fw.barrier(); return
        with ExitStack() as pe_:
            A1 = fw.tile(pe_, "A1", [128, 2, 16], F32)
            A2 = fw.tile(pe_, "A2", [128, 2, 16], F32)
            p1 = fw.tile(pe_, "p1", [128, 2, 16], F32)
            p2 = fw.tile(pe_, "p2", [128, 2, 16], F32)
            fw.cp("dve", A1.t[:, 0, :], PW.t[:, 0, 16, :], [PW.b], [A1.b])
            fw.cp("dve", A1.t[:, 1, :], PW.t[:, 0, 16, :], [PW.b], [A1.b])
            fw.ts("dve", A2.t[:, 0, :], PW.t[:, 1, 16, :], -1.0, None, ALU.mult, None, [PW.b], [A2.b])
            fw.cp("dve", A2.t[:, 1, :], PW.t[:, 1, 16, :], [PW.b], [A2.b])
            for c in range(136):
                fw.tt("dve", p1.t[:, :, :], A1.t[:, :, :], Sall.t[:, c, 0:2, :], ALU.mult, [A1.b, Sall.b], [p1.b])
                fw.tt("dve", p2.t[:, :, :], A2.t[:, :, :], Sall.t[:, c, 1:3, :], ALU.mult, [A2.b, Sall.b], [p2.b])
                fw.tt("dve", p1.t[:, :, :], p1.t[:, :, :], p2.t[:, :, :], ALU.add, [p1.b, p2.b], [p1.b])
                fw.tt("dve", Sall.t[:, c + 1, 0:2, :], Sall.t[:, c + 1, 0:2, :], p1.t[:, :, :], ALU.add, [Sall.b, p1.b], [Sall.b])
                fw.cp("dve", Sall.t[:, c + 1, 2, :], Sall.t[:, c + 1, 0, :], [Sall.b], [Sall.b])
            fw.barrier()
        if os.environ.get("S5_STOP") == "E":
            fw.barrier(); return
        pfg = ExitStack()
        gT = fw.tile(pfg, "g5T", [128, 4, TP], BF16)
        with ExitStack() as pf:
            SP = fw.tile(pf, "SP", [128, 4, 2, 136, 16], BF16)
            u1 = fw.tile(pf, "u1", [128, 136, 16], F32)
            u2 = fw.tile(pf, "u2", [128, 136, 16], F32)
            z = fw.tile(pf, "z5", [128, 512], F32)
            z2 = fw.tile(pf, "z52", [128, 512], F32)
            k = 0
            for ct in range(4):
                for pl in range(4):
                    pair = 4 * ct + pl
                    srb = Sall.t[:, 0:136, 0, pair].unsqueeze(2).to_broadcast([128, 136, 16])
                    sib = Sall.t[:, 0:136, 1, pair].unsqueeze(2).to_broadcast([128, 136, 16])
                    prb = PW.t[:, 0, 1:17, pair].unsqueeze(1).to_broadcast([128, 136, 16])
                    pib = PW.t[:, 1, 1:17, pair].unsqueeze(1).to_broadcast([128, 136, 16])
                    RS = [Sall.b, PW.b]
                    fw.tt("dve", u1.t[:, :, :], srb, prb, ALU.mult, RS, [u1.b])
                    fw.tt("pool", u2.t[:, :, :], sib, pib, ALU.mult, RS, [u2.b])
                    fw.tt("dve", SP.t[:, pl, 0, :, :], u1.t[:, :, :], u2.t[:, :, :], ALU.subtract, [u1.b, u2.b], [SP.b])
                    fw.tt("pool", u2.t[:, :, :], sib, prb, ALU.mult, RS, [u2.b])
                    fw.tt("dve", u1.t[:, :, :], srb, pib, ALU.mult, RS, [u1.b])
                    fw.tt("dve", SP.t[:, pl, 1, :, :], u1.t[:, :, :], u2.t[:, :, :], ALU.add, [u1.b, u2.b], [SP.b])
                for (t0, n) in TGS:
                    c0, nch = t0 // 16, n // 16
                    pst = ps[k % 2]
                    k += 1
                    pv = pst.t[:, 0:n].rearrange("p (c b) -> p c b", b=16)
                    uv = u5T.t[:, ct, t0:t0 + n].rearrange("p (c b) -> p c b", b=16)
                    for tau in range(16):
                        fw.mm(pv[:, :, tau:16], KT.t[:, ct, tau, :], uv[:, :, 0:16 - tau], tau == 0, False, [KT.b, u5T.b], [pst.b])
                    for pl in range(4):
                        pair = 4 * ct + pl
                        fw.mm(pst.t[:, 0:n], Cre.t[:, pair, :], SP.t[:, pl, 0, c0:c0 + nch, :].rearrange("p c b -> p (c b)"), False, False,
                              [Cre.b, SP.b], [pst.b])
                        fw.mm(pst.t[:, 0:n], nCim.t[:, pair, :], SP.t[:, pl, 1, c0:c0 + nch, :].rearrange("p c b -> p (c b)"), False, pl == 3,
                              [nCim.b, SP.b], [pst.b])
                    fw.stt("dve", z.t[:, 0:n], u5T.t[:, ct, t0:t0 + n], d5.t[:, ct:ct + 1], pst.t[:, 0:n], ALU.mult, ALU.add,
                           [u5T.b, d5.b, pst.b], [z.b])
                    fw.tt("pool", z2.t[:, 0:n], z.t[:, 0:n], z.t[:, 0:n], ALU.mult, [z.b], [z2.b])
                    fw.ts("pool", z2.t[:, 0:n], z2.t[:, 0:n], 0.044715, 1.0, ALU.mult, ALU.add, [z2.b], [z2.b])
                    fw.tt("pool", z2.t[:, 0:n], z2.t[:, 0:n], z.t[:, 0:n], ALU.mult, [z2.b, z.b], [z2.b])
                    fw.act(z2.t[:, 0:n], z2.t[:, 0:n], AF.Sigmoid, [z2.b], [z2.b], scale=2.0 * math.sqrt(2.0 / math.pi))
                    fw.tt("pool", gT.t[:, ct, t0:t0 + n], z.t[:, 0:n], z2.t[:, 0:n], ALU.mult, [z.b, z2.b], [gT.b])
            fw.barrier()
        if os.environ.get("S5_STOP") == "F":
            pfg.close(); fw.barrier(); return
        with ExitStack() as pg:
            sgs = [fw.tile(pg, "sg5", [128, 512], F32) for _ in range(2)]
            yos = [fw.tile(pg, "yo5", [128, 512], BF16) for _ in range(2)]
            k = 0
            for nb in range(4):
                for (t0, n) in TGS:
                    pa_, pg_ = ps[2 + 2 * (k % 2)], ps[3 + 2 * (k % 2)]
                    sg, yo = sgs[k % 2], yos[k % 2]
                    k += 1
                    for kc in range(4):
                        fw.mm(pa_.t[:, 0:n], GW.t[:, kc, nb * 128:(nb + 1) * 128], gT.t[:, kc, t0:t0 + n], kc == 0, kc == 3, [GW.b, gT.b], [pa_.b])
                    for kc in range(4):
                        fw.mm(pg_.t[:, 0:n], GW.t[:, kc, 512 + nb * 128:512 + (nb + 1) * 128], gT.t[:, kc, t0:t0 + n], kc == 0, kc == 3,
                              [GW.b, gT.b], [pg_.b])
                    fw.act(sg.t[:, 0:n], pg_.t[:, 0:n], AF.Sigmoid, [pg_.b, gb.b], [sg.b], bias=gb.t[:, 4 + nb:5 + nb])
                    fw.stt("dve", yo.t[:, 0:n], pa_.t[:, 0:n], gb.t[:, nb:nb + 1], sg.t[:, 0:n], ALU.add, ALU.mult, [pa_.b, gb.b, sg.b], [yo.b])
                    fw.dma("sp", C.YT[4 + nb, :, t0:t0 + n], yo.t[:, 0:n], reads=[yo.b], writes=C.YTb[1][t0 // 128:(t0 + n) // 128])
            fw.barrier()
        pfg.close()
        fw.barrier()


MGS = [(g * 256, min(256, TP - g * 256)) for g in range((TP + 255) // 256)]


def rms_epilogue(fw, C, psA, psB, nw, xt, wk2):
    junk, ss, tmp = wk2
    fw.act(junk.t[:, 0:512], psA.t[:, :], AF.Square, [psA.b], [junk.b, ss.b], accum=ss.t[:, 2:3])
    fw.act(junk.t[:, 512:1024], psB.t[:, :], AF.Square, [psB.b], [junk.b, ss.b], accum=ss.t[:, 3:4])
    fw.tt("dve", ss.t[:, 2:3], ss.t[:, 2:3], ss.t[:, 3:4], ALU.add, [ss.b], [ss.b])
    rstd_from_ss(fw, C, ss.t[:, 2:3], ss.t[:, 3:4], 1024.0, [ss.b], [ss.b])
    fw.stt("dve", tmp.t[:, 0:512], psA.t[:, :], ss.t[:, 3:4], nw.t[:, 0:512], ALU.mult, ALU.mult, [psA.b, ss.b, nw.b], [tmp.b])
    fw.stt("dve", tmp.t[:, 512:1024], psB.t[:, :], ss.t[:, 3:4], nw.t[:, 512:1024], ALU.mult, ALU.mult, [psB.b, ss.b, nw.b], [tmp.b])
    fw.tt("pool", xt.t[:, :], xt.t[:, :], tmp.t[:, :], ALU.add, [xt.b, tmp.b], [xt.b])


def phase_merge(fw, C, l):
    ps = C.ps
    with ExitStack() as ph:
        WG = fw.tile(ph, "WG", [128, 8, 4096], BF16)
        load_w(fw, C.w_in[l], WG, C_GATE, C_GATE + 4096, 8)
        WB = fw.tile(ph, "WB", [128, 16, 1024], BF16)
        for n in range(4):
            v = C.w_branch[l][n].rearrange("(cb p) d -> p cb d", p=128)
            fw.dma("pool", WB.t[:, 4 * n:4 * n + 4, :], v, writes=[WB.b])
        WO = fw.tile(ph, "WO", [128, 8, 1024], BF16)
        load_w(fw, C.w_out[l], WO, 0, 1024, 8)
        nw1 = fw.tile(ph, "nw1", [128, D], F32)
        nw2 = fw.tile(ph, "nw2", [128, D], F32)
        fw.dma("sp", nw1.t[:, :], C.norm_post_mix[l].partition_broadcast(128), writes=[nw1.b])
        fw.dma("sp", nw2.t[:, :], C.norm_pre_mlp[l].partition_broadcast(128), writes=[nw2.b])
        YTs = fw.tile(ph, "YTs", [128, 16, 256], BF16)
        mixT = fw.tile(ph, "mixT", [128, 8, 256], BF16)
        acc = fw.tile(ph, "macc", [128, 256], F32)
        sgs = [fw.tile(ph, "msg", [128, 256], F32) for _ in range(2)]
        tmpm = fw.tile(ph, "mtmp", [128, 256], F32)
        xts = [fw.tile(ph, "mxt", [128, D], F32) for _ in range(2)]
        wk = (fw.tile(ph, "junk", [128, D], BF16), fw.tile(ph, "ss", [128, 4], F32), fw.tile(ph, "ub", [128, D], BF16))
        wk2 = (wk[0], wk[1], fw.tile(ph, "mtmp2", [128, D], F32))
        k = 0
        for (t0, n) in MGS:
            tiles = list(range(t0 // 128, (t0 + n) // 128))
            fw.dma("sp", YTs.t[:, :, 0:n], C.YT[:, :, t0:t0 + n].rearrange("c p t -> p c t"),
                   reads=[C.YTb[m][i] for m in range(4) for i in tiles], writes=[YTs.b])
            for db in range(8):
                for nn in range(4):
                    pg_, pb_ = ps[2 * (k % 2)], ps[2 * (k % 2) + 1]
                    sg = sgs[k % 2]
                    k += 1
                    c0 = nn * 1024 + db * 128
                    for kc in range(8):
                        fw.mm(pg_.t[:, 0:n], WG.t[:, kc, c0:c0 + 128], C.uT.t[:, kc, t0:t0 + n], kc == 0, kc == 7,
                              [WG.b] + [C.uTb[i] for i in tiles], [pg_.b])
                    for cb in range(4):
                        fw.mm(pb_.t[:, 0:n], WB.t[:, 4 * nn + cb, db * 128:(db + 1) * 128], YTs.t[:, 4 * nn + cb, 0:n], cb == 0, cb == 3,
                              [WB.b, YTs.b], [pb_.b])
                    fw.act(sg.t[:, 0:n], pg_.t[:, 0:n], AF.Sigmoid, [pg_.b], [sg.b])
                    if nn == 0:
                        fw.tt("dve", acc.t[:, 0:n], sg.t[:, 0:n], pb_.t[:, 0:n], ALU.mult, [sg.b, pb_.b], [acc.b])
                    else:
                        fw.tt("dve", tmpm.t[:, 0:n], sg.t[:, 0:n], pb_.t[:, 0:n], ALU.mult, [sg.b, pb_.b], [tmpm.b])
                        if nn < 3:
                            fw.tt("pool", acc.t[:, 0:n], acc.t[:, 0:n], tmpm.t[:, 0:n], ALU.add, [acc.b, tmpm.b], [acc.b])
                        else:
                            fw.tt("pool", mixT.t[:, db, 0:n], acc.t[:, 0:n], tmpm.t[:, 0:n], ALU.add, [acc.b, tmpm.b], [mixT.b])
            for i in tiles:
                xt = xts[i % 2]
                src = C.h0 if l == 0 else C.hbuf
                fw.dma("sp", xt.t[:, :], src[i * 128:(i + 1) * 128, :], reads=([] if l == 0 else [C.hb[i]]), writes=[xt.b])
                sub = slice(i * 128 - t0, i * 128 - t0 + 128)
                for dh in range(2):
                    pst = ps[4 + dh]
                    for db in range(8):
                        fw.mm(pst.t[:, :], mixT.t[:, db, sub], WO.t[:, db, dh * 512:(dh + 1) * 512], db == 0, db == 7, [mixT.b, WO.b], [pst.b])
                rms_epilogue(fw, C, ps[4], ps[5], nw1, xt, wk2)
                fw.dma("sp", C.hbuf[i * 128:(i + 1) * 128, :], xt.t[:, :], reads=[xt.b], writes=[C.hb[i]])
                norm_rows_to_T(fw, C, ph, xt.t[:, :], xt.b, nw2, C.uT, C.uTb, i, wk)
        fw.barrier()


def phase_mlp(fw, C, l, last):
    ps = C.ps
    with ExitStack() as ph:
        WU = fw.tile(ph, "WU", [128, 8, 4096], BF16)
        load_w(fw, C.w_up[l], WU, 0, 4096, 8)
        WD = fw.tile(ph, "WD", [128, 32, 1024], BF16)
        load_w(fw, C.w_down[l], WD, 0, 1024, 32)
        nw = fw.tile(ph, "nw3", [128, D], F32)
        fw.dma("sp", nw.t[:, :], C.norm_post_mlp[l].partition_broadcast(128), writes=[nw.b])
        hT = fw.tile(ph, "hT", [128, 32, 256], BF16)
        rl = [fw.tile(ph, "rl", [128, 512], BF16) for _ in range(2)]
        xts = [fw.tile(ph, "pxt", [128, D], F32)] * 2
        ptmp = fw.tile(ph, "ptmp", [128, D], F32)
        wk2 = (ptmp, fw.tile(ph, "ss", [128, 4], F32), ptmp)
        k = 0
        for (t0, n) in MGS:
            tiles = list(range(t0 // 128, (t0 + n) // 128))
            for fp in range(16):
                pst = ps[k % 4]
                r = rl[k % 2]
                k += 1
                for j in range(2):
                    ffc = 2 * fp + j
                    for kc in range(8):
                        fw.mm(pst.t[:, j * 256:j * 256 + n], WU.t[:, kc, ffc * 128:(ffc + 1) * 128], C.uT.t[:, kc, t0:t0 + n], kc == 0, kc == 7,
                              [WU.b] + [C.uTb[i] for i in tiles], [pst.b])
                pv = pst.t[:, :].rearrange("p (j t) -> p j t", j=2)[:, :, 0:n]
                rv = r.t[:, :].rearrange("p (j t) -> p j t", j=2)[:, :, 0:n]
                fw.act(rv, pv, AF.Relu, [pst.b], [r.b])
                fw.tt("pool" if fp % 2 else "dve", hT.t[:, 2 * fp:2 * fp + 2, 0:n], rv, rv, ALU.mult, [r.b], [hT.b])
            for i in tiles:
                xt = xts[i % 2]
                fw.dma("sp", xt.t[:, :], C.hbuf[i * 128:(i + 1) * 128, :], reads=[C.hb[i]], writes=[xt.b])
                sub = slice(i * 128 - t0, i * 128 - t0 + 128)
                for dh in range(2):
                    pst = ps[4 + dh]
                    for ffc in range(32):
                        fw.mm(pst.t[:, :], hT.t[:, ffc, sub], WD.t[:, ffc, dh * 512:(dh + 1) * 512], ffc == 0, ffc == 31, [hT.b, WD.b], [pst.b])
                rms_epilogue(fw, C, ps[4], ps[5], nw, xt, wk2)
                if not last:
                    fw.dma("sp", C.hbuf[i * 128:(i + 1) * 128, :], xt.t[:, :], reads=[xt.b], writes=[C.hb[i]])
                else:
                    if C.debug:
                        fw.dma("sp", C.hbuf[i * 128:(i + 1) * 128, :], xt.t[:, :], reads=[xt.b], writes=[C.hb[i]])
                    lo = max(i * 128, 16)
                    hi = min((i + 1) * 128, T)
                    if hi > lo:
                        fw.dma("sp", C.out[lo - 16:hi - 16, :], xt.t[lo - i * 128:hi - i * 128, :], reads=[xt.b], writes=[C.outb])
        fw.barrier()


def build(debug=False, n_layers=DEPTH, phases=None):
    nc = bass.Bass("TRN2", target_bir_lowering=False)
    C = Ctx()
    C.debug = debug

    def din(name, shape):
        return nc.dram_tensor(name, list(shape), F32, kind="ExternalInput").ap()

    C.h0 = din("h0", [TP, D])
    C.consts_d = din("consts", [128, CO_TOTAL[0]])
    C.w_in = din("w_in", [4, D, N_IN])
    C.w_branch = din("w_branch", [4, 4, 512, D])
    C.w_out = din("w_out", [4, D, D])
    C.w_up = din("w_up", [4, D, 4 * D])
    C.w_down = din("w_down", [4, 4 * D, D])
    C.s5_glu_w = din("s5_glu_w", [4, 512, 1024])
    for nm in ("norm_pre_mix", "norm_post_mix", "norm_pre_mlp", "norm_post_mlp"):
        setattr(C, nm, din(nm, [4, D]))
    for nm in ("ret_gn_w", "ssd_norm_w", "hgrn_norm_w", "ssd_dsk"):
        setattr(C, nm, din(nm, [4, 512]))
    C.ssd_dt_bias = din("ssd_dt_bias", [4, 8])
    C.ssd_a_log = din("ssd_a_log", [4, 8])
    C.lbT = din("lbT", [128, 4, 4])
    C.ssd_cw = din("ssd_cw", [4, 128, 8, 4])
    C.ssd_cb = din("ssd_cb", [4, 128, 8])
    C.s5_small = din("s5_small", [4, 128, 48])
    for nm in ("s5_bre", "s5_bim", "s5_cre", "s5_cim"):
        setattr(C, nm, din(nm, [4, 128, 16, 128]))
    C.s5_dT = din("s5_dT", [4, 128, 4])
    C.s5_gbT = din("s5_gbT", [4, 128, 8])
    C.out = nc.dram_tensor("out", [2048, D], F32, kind="ExternalOutput").ap()
    sk = "ExternalOutput" if debug else "Internal"
    C.hbuf = nc.dram_tensor("hbuf", [TP, D], F32, kind=sk).ap()
    C.YT = nc.dram_tensor("YT", [16, 128, TP], BF16, kind=sk).ap()
    if debug:
        C.dbg5 = nc.dram_tensor("dbg5", [128, 2784 + 137 * 48], F32, kind="ExternalOutput").ap()
    C.hb = [Buf("hb%d" % i) for i in range(NT)]
    C.YTb = [[Buf("yt%d_%d" % (m, i)) for i in range(NT)] for m in range(4)]
    C.outb = Buf("out")
    with ExitStack() as es:
        fw = FW(nc, es)
        C.ps = [TL(es.enter_context(nc.psum_tensor("ps%d" % j, [128, 512], F32)), "ps%d" % j) for j in range(6)]
        C.pb = [TL(es.enter_context(nc.psum_tensor("pb%d" % j, [128, 1024], BF16)), "pb%d" % j) for j in range(2)]
        C.consts = fw.tile(es, "consts", [128, CO_TOTAL[0]], F32)
        fw.dma("sp", C.consts.t[:, :], C.consts_d[:, :], writes=[C.consts.b])
        C.identb = fw.tile(es, "identb", [128, 128], BF16)
        C.identf = fw.tile(es, "identf", [128, 128], F32)
        fw.cp("dve", C.identb.t[:, :], cslice(C, "ident"), [C.consts.b], [C.identb.b])
        fw.cp("dve", C.identf.t[:, :], cslice(C, "ident"), [C.consts.b], [C.identf.b])
        C.uT = fw.tile(es, "uT", [128, 8, TP], BF16)
        C.uTb = [Buf("uT%d" % i) for i in range(NT)]
        C.lb_all = fw.tile(es, "lb_all", [128, 4, 4], F32)
        C.oml_all = fw.tile(es, "oml_all", [128, 4, 4], F32)
        lbe = fw.tile(es, "lbe", [128, 4, 4], F32)
        lsum = fw.tile(es, "lsum", [128, 4], F32)
        R = [lbe.b, lsum.b, C.lb_all.b, C.oml_all.b]
        fw.dma("sp", lbe.t[:, :, :], C.lbT[:, :, :], writes=[lbe.b])
        fw.act(lbe.t[:, :, :], lbe.t[:, :, :], AF.Exp, R, R)
        fw.tt("dve", lsum.t[:, :], lbe.t[:, 0, :], lbe.t[:, 1, :], ALU.add, R, R)
        fw.tt("dve", lsum.t[:, :], lsum.t[:, :], lbe.t[:, 2, :], ALU.add, R, R)
        fw.tt("dve", lsum.t[:, :], lsum.t[:, :], lbe.t[:, 3, :], ALU.add, R, R)
        fw.op("dve", lambda g: g.reciprocal(lsum.t[:, :], lsum.t[:, :]), R, R)
        fw.memset("dve", C.lb_all.t[:, 0, :], 0.0, R)
        for ll in range(1, 4):
            fw.tt("dve", lbe.t[:, ll, :], lbe.t[:, ll, :], lsum.t[:, :], ALU.mult, R, R)
            fw.tt("dve", C.lb_all.t[:, ll, :], C.lb_all.t[:, ll - 1, :], lbe.t[:, ll, :], ALU.add, R, R)
        fw.ts("dve", C.oml_all.t[:, :, :], C.lb_all.t[:, :, :], -1.0, 1.0, ALU.mult, ALU.add, R, R)
        fw.barrier()
        allp = ("norm", "ret", "s5", "ssd", "hg", "merge", "mlp")
        for l in range(n_layers):
            for pn in allp:
                if phases is not None and pn not in phases:
                    continue
                if pn == "norm":
                    phase_norm1(fw, C, l)
                elif pn == "ret":
                    phase_ret(fw, C, l)
                elif pn == "s5":
                    phase_s5(fw, C, l)
                elif pn == "ssd":
                    phase_ssd(fw, C, l)
                elif pn == "hg":
                    phase_hg(fw, C, l)
                elif pn == "merge":
                    phase_merge(fw, C, l)
                elif pn == "mlp":
                    phase_mlp(fw, C, l, last=(l == n_layers - 1))
        fw.barrier()
        C.n_inst, C.n_wait = fw.n_inst, fw.n_wait
    return nc, C


CO_TOTAL = [0]
_CONSTS = None


def get_consts():
    global _CONSTS
    if _CONSTS is None:
        _CONSTS = host_consts()
        CO_TOTAL[0] = _CONSTS.shape[1]
    return _CONSTS


def make_in_maps(inp):
    consts = get_consts()
    P = host_params(inp)
    f = lambda a: np.ascontiguousarray(np.asarray(a, np.float32))
    shared = {"consts": consts}
    for nm in ("w_in", "w_branch", "w_out", "w_up", "w_down", "s5_glu_w", "norm_pre_mix", "norm_post_mix", "norm_pre_mlp",
               "norm_post_mlp", "ret_gn_w", "ssd_norm_w", "hgrn_norm_w", "ssd_dt_bias", "ssd_a_log"):
        shared[nm] = f(inp[nm])
    for nm in ("lbT", "ssd_cw", "ssd_cb", "ssd_dsk", "s5_small", "s5_bre", "s5_bim", "s5_cre", "s5_cim", "s5_dT", "s5_gbT"):
        shared[nm] = P[nm]
    x = np.asarray(inp["x"], np.float32)
    meta = np.asarray(inp["meta_tokens"], np.float32)
    maps = []
    for b in range(x.shape[0]):
        h0 = np.zeros((TP, D), np.float32)
        h0[0:16] = meta
        h0[16:T] = x[b]
        m = dict(shared)
        m["h0"] = h0
        maps.append(m)
    return maps


_NC = None


def kernel(**inputs):
    global _NC
    maps = make_in_maps(inputs)
    if _NC is None:
        _NC = build()[0]
    res = run_bass_kernel_spmd(_NC, maps, core_ids=list(range(len(maps))))
    return np.stack([np.asarray(r["out"], np.float32) for r in res.results], axis=0)
```

```python
import math
import os
import numpy as np
from contextlib import ExitStack
import concourse.bass as bass
import concourse.mybir as mybir
from concourse.bass_utils import run_bass_kernel_spmd

F32 = mybir.dt.float32
BF16 = mybir.dt.bfloat16
ALU = mybir.AluOpType
AF = mybir.ActivationFunctionType
AX = mybir.AxisListType

DEPTH = 4
D = 1024
T = 2064
NT = 17
TP = NT * 128
EPS = 1e-6
N_IN = 9736
C_RET, C_S5, C_SSD, C_HG, C_GATE = 0, 1536, 2048, 3592, 5640
GAMMA = [1.0 - 2.0 ** (-5.0 - h) for h in range(4)]


class Buf:
    __slots__ = ("name", "w", "r")

    def __init__(self, name=""):
        self.name = name
        self.w = None
        self.r = []


class TL:
    def __init__(self, t, name):
        self.t = t
        self.b = Buf(name)


LAZY_PE_SIGNAL = os.environ.get("LAZY_PE", "0") == "1"


class VW:
    def __init__(self, t, b):
        self.t = t
        self.b = b


class FW:
    N_DMA_SEMS = 16

    def __init__(self, nc, es):
        self.nc = nc
        self.es = es
        self.eng = {"pe": nc.tensor, "dve": nc.vector, "act": nc.scalar, "pool": nc.gpsimd, "sp": nc.sync}
        self.sems = {}
        self.cnt = {}
        for e in ("pe", "dve", "act", "pool"):
            self.sems[e] = es.enter_context(nc.semaphore("s_" + e))
            self.cnt[e] = 0
        self.dma_keys = {}
        self.dma_rr = {}
        for q in ("sp", "pool"):
            ks = []
            for i in range(self.N_DMA_SEMS):
                k = "d_%s_%d" % (q, i)
                self.sems[k] = es.enter_context(nc.semaphore(k))
                self.cnt[k] = 0
                ks.append(k)
            self.dma_keys[q] = ks
            self.dma_rr[q] = 0
        self.known = {e: {} for e in self.eng}
        self.n_inst = 0
        self.n_wait = 0
        self.uid = 0
        self.pe_last = None
        self.pe_unsig = False
        self.n_sig = 0

    def tile(self, stack, name, shape, dt):
        self.uid += 1
        nm = "%s_%d" % (name, self.uid)
        return TL(stack.enter_context(self.nc.sbuf_tensor(nm, list(shape), dt)), nm)

    def _wait(self, e, ev):
        if ev is None:
            return
        k, v = ev
        if e == "pe" and k == "pe":
            return
        if k == "pe" and v > self.cnt["pe"]:
            self._flush_pe()
        kn = self.known[e]
        if kn.get(k, 0) >= v:
            return
        self.eng[e].wait_ge(self.sems[k], v)
        kn[k] = v
        self.n_wait += 1

    def _deps(self, e, reads, writes):
        for b in reads:
            self._wait(e, b.w)
        for b in writes:
            self._wait(e, b.w)
            for ev in b.r:
                self._wait(e, ev)

    def _mark(self, ev, reads, writes):
        for b in reads:
            b.r.append(ev)
        for b in writes:
            b.w = ev
            b.r = []

    def _flush_pe(self):
        if self.pe_unsig:
            self.cnt["pe"] += 1
            self.pe_last.then_inc(self.sems["pe"], 1)
            self.pe_unsig = False
            self.n_sig += 1

    def op(self, e, fn, reads=(), writes=()):
        self._deps(e, reads, writes)
        ins = fn(self.eng[e])
        if e == "pe" and LAZY_PE_SIGNAL:
            self.pe_last = ins
            self.pe_unsig = True
            ev = (e, self.cnt[e] + 1)
        else:
            self.cnt[e] += 1
            ins.then_inc(self.sems[e], 1)
            ev = (e, self.cnt[e])
        self._mark(ev, reads, writes)
        self.n_inst += 1
        return ev

    def dma(self, q, out, in_, reads=(), writes=(), **kw):
        self._deps(q, reads, writes)
        ks = self.dma_keys[q]
        k = ks[self.dma_rr[q] % len(ks)]
        self.dma_rr[q] += 1
        if self.cnt[k] > 0:
            self._wait(q, (k, self.cnt[k]))
        ins = self.eng[q].dma_start(out=out, in_=in_, **kw)
        self.cnt[k] += 16
        ins.then_inc(self.sems[k], 16)
        ev = (k, self.cnt[k])
        self._mark(ev, reads, writes)
        self.n_inst += 1
        return ev

    def barrier(self, engines=("pe", "dve", "act", "pool", "sp")):
        self._flush_pe()
        for e in engines:
            for k, v in self.cnt.items():
                if v > 0:
                    self._wait(e, (k, v))

    def tt(self, e, out, a, b, op, R, W):
        return self.op(e, lambda g: g.tensor_tensor(out, a, b, op), R, W)

    def ts(self, e, out, a, s1, s2, op0, op1, R, W):
        if s2 is None:
            return self.op(e, lambda g: g.tensor_scalar(out, a, s1, None, op0=op0), R, W)
        return self.op(e, lambda g: g.tensor_scalar(out, a, s1, s2, op0=op0, op1=op1), R, W)

    def stt(self, e, out, in0, sc, in1, op0, op1, R, W):
        e = "dve"
        return self.op(e, lambda g: g.scalar_tensor_tensor(out, in0, sc, in1, op0=op0, op1=op1), R, W)

    def cp(self, e, out, in_, R, W):
        if e == "act":
            return self.op(e, lambda g: g.copy(out, in_), R, W)
        return self.op(e, lambda g: g.tensor_copy(out, in_), R, W)

    def act(self, out, in_, func, R, W, bias=None, scale=None, accum=None):
        kw = {}
        if bias is not None:
            kw["bias"] = bias
        if scale is not None:
            kw["scale"] = scale
        if accum is not None:
            kw["accum_out"] = accum
        return self.op("act", lambda g: g.activation(out, in_, func, **kw), R, W)

    def mm(self, out, lhsT, rhs, start, stop, R, W):
        return self.op("pe", lambda g: g.matmul(out, lhsT, rhs, start=start, stop=stop), R, W)

    def tr(self, out, in_, ident, R, W):
        return self.op("pe", lambda g: g.transpose(out, in_, ident), R, W)

    def memset(self, e, ap, val, W):
        return self.op(e, lambda g: g.memset(ap, val), (), W)


CO = {}


def _pack(items):
    off = 0
    cols = []
    for name, arr in items:
        arr = np.asarray(arr, np.float32).reshape(128, -1)
        CO[name] = (off, arr.shape[1])
        off += arr.shape[1]
        cols.append(arr)
    return np.ascontiguousarray(np.concatenate(cols, axis=1))


def host_consts():
    s = np.arange(128)[:, None]
    t = np.arange(128)[None, :]
    ident = (s == t).astype(np.float32)
    triu = (s <= t).astype(np.float32)
    negm = np.where(s <= t, 0.0, -30000.0).astype(np.float32)
    ones = np.ones((128, 128), np.float32)
    retdt = np.zeros((128, 4, 128), np.float64)
    qdec = np.zeros((128, 4, 64), np.float64)
    kdec = np.zeros((128, 4, 64), np.float64)
    for h in range(4):
        g = GAMMA[h]
        retdt[:, h, :] = np.where(s <= t, 0.125 * g ** np.maximum(t - s, 0), 0.0)
        qdec[:, h, :] = (g ** (np.arange(128) + 1.0))[:, None]
        kdec[:, h, :] = (0.125 * g ** (127.0 - np.arange(128)))[:, None]
    half = 32
    inv_freq = (10000.0 ** (-np.arange(half, dtype=np.float32) / half)).astype(np.float32)
    pos = (np.arange(NT)[None, :] * 128 + np.arange(128)[:, None]).astype(np.float32)
    ang = pos[:, :, None] * inv_freq[None, None, :]
    cos = np.cos(ang).astype(np.float32)
    sin = np.sin(ang).astype(np.float32)
    halfpi = np.full((128, 1), math.pi / 2, np.float32)
    mvals = np.array(list(range(17)) + [0.5], np.float32)
    mtab = np.broadcast_to(mvals[None, :, None], (128, 18, 16))
    return _pack([("mtab", mtab), ("ident", ident), ("triu", triu), ("negm", negm), ("ones", ones), ("retdt", retdt),
                  ("qdec", qdec), ("kdec", kdec), ("cos", cos), ("sin", sin), ("halfpi", halfpi)])


def host_params(inp):
    P = {}
    f = lambda a: np.ascontiguousarray(np.asarray(a, np.float32))
    P["lbT"] = f(np.asarray(inp["hgrn_lb"]).reshape(4, 4, 128).transpose(2, 0, 1))
    P["ssd_cw"] = f(np.asarray(inp["ssd_conv_w"]).reshape(4, 4, 8, 128).transpose(0, 3, 2, 1))
    P["ssd_cb"] = f(np.asarray(inp["ssd_conv_b"]).reshape(4, 8, 128).transpose(0, 2, 1))
    P["ssd_dsk"] = f(np.repeat(np.asarray(inp["ssd_d"]), 64, axis=1))
    def pl_small(a):
        a = np.asarray(a).reshape(4, 16, 2, 64)
        return a.transpose(0, 2, 3, 1).reshape(4, 128, 16)
    ls = np.broadcast_to(np.asarray(inp["s5_log_step"])[:, :, None], (4, 32, 64))
    P["s5_small"] = f(np.concatenate([pl_small(inp["s5_lam_re"]), pl_small(inp["s5_lam_im"]), pl_small(ls)], axis=2))
    def pl_b(b):
        b = np.asarray(b)
        out = np.zeros((4, 2, 64, 16, 8, 16), np.float32)
        for g in range(32):
            out[:, g % 2, :, g // 2, g % 8, :] = b[:, g]
        return out.reshape(4, 128, 16, 128)
    def pl_c(c):
        c = np.asarray(c)
        out = np.zeros((4, 2, 64, 16, 8, 16), np.float32)
        for g in range(32):
            out[:, g % 2, :, g // 2, g % 8, :] = c[:, g].transpose(0, 2, 1)
        return out.reshape(4, 128, 16, 128)
    P["s5_bre"] = pl_b(inp["s5_b_re"])
    P["s5_bim"] = pl_b(inp["s5_b_im"])
    P["s5_cre"] = pl_c(inp["s5_c_re"])
    P["s5_cim"] = pl_c(inp["s5_c_im"])
    P["s5_dT"] = f(np.asarray(inp["s5_d"]).reshape(4, 4, 128).transpose(0, 2, 1))
    P["s5_gbT"] = f(np.asarray(inp["s5_glu_b"]).reshape(4, 8, 128).transpose(0, 2, 1))
    return P


class Ctx:
    pass


def cslice(C, name, a=None, b=None):
    off, n = CO[name]
    if a is None:
        return C.consts.t[:, off:off + n]
    return C.consts.t[:, off + a:off + b]


def load_w(fw, src2d, dst, c0, c1, kcs, R=()):
    v = src2d.rearrange("(kc p) n -> p kc n", p=128)
    step = int(os.environ.get("LW_CSTEP", "2048"))
    kstep = min(kcs, int(os.environ.get("LW_KSTEP", "8")))
    for k0 in range(0, kcs, kstep):
        for a in range(c0, c1, step):
            b = min(a + step, c1)
            fw.dma("pool", dst.t[:, k0:k0 + kstep, a - c0:b - c0], v[:, k0:k0 + kstep, a:b], reads=R, writes=[dst.b])


def rstd_from_ss(fw, C, ss_ap, out_ap, n, R, W):
    fw.ts("dve", out_ap, ss_ap, 1.0 / n, EPS, ALU.mult, ALU.add, R, W)
    fw.act(out_ap, out_ap, AF.Sqrt, W, W)
    fw.op("dve", lambda g: g.reciprocal(out_ap, out_ap), W, W)


def norm_rows_to_T(fw, C, ph, x_ap, xb, nw, dstT, dst_bufs, i, wk):
    junk, ss, ub = wk
    fw.act(junk.t[:, :], x_ap, AF.Square, [xb], [junk.b, ss.b], accum=ss.t[:, 0:1])
    rstd_from_ss(fw, C, ss.t[:, 0:1], ss.t[:, 1:2], 1024.0, [ss.b], [ss.b])
    fw.stt("dve", ub.t[:, :], x_ap, ss.t[:, 1:2], nw.t[:, :], ALU.mult, ALU.mult, [xb, ss.b, nw.b], [ub.b])
    pb = C.pb[i % 2]
    for kc in range(8):
        fw.tr(pb.t[:, kc * 128:(kc + 1) * 128], ub.t[:, kc * 128:(kc + 1) * 128], C.identb.t[:, :], [ub.b, C.identb.b], [pb.b])
    fw.cp("act" if i % 2 else "dve", dstT.t[:, :, i * 128:(i + 1) * 128],
          pb.t[:, :].rearrange("p (k t) -> p k t", k=8), [pb.b], [dst_bufs[i]])


def phase_norm1(fw, C, l):
    with ExitStack() as ph:
        nw = fw.tile(ph, "nw", [128, D], F32)
        fw.dma("sp", nw.t[:, :], C.norm_pre_mix[l].partition_broadcast(128), writes=[nw.b])
        xts = [fw.tile(ph, "xt", [128, D], F32) for _ in range(2)]
        wk = (fw.tile(ph, "junk", [128, D], BF16), fw.tile(ph, "ss", [128, 2], F32), fw.tile(ph, "ub", [128, D], BF16))
        for i in range(NT):
            xt = xts[i % 2]
            src = C.h0 if l == 0 else C.hbuf
            fw.dma("sp", xt.t[:, :], src[i * 128:(i + 1) * 128, :], reads=([] if l == 0 else [C.hb[i]]), writes=[xt.b])
            norm_rows_to_T(fw, C, ph, xt.t[:, :], xt.b, nw, C.uT, C.uTb, i, wk)
        fw.barrier()


ILV_OFF = set(os.environ.get("NOILV", "ret,ssd").split(","))
CUR_PHASE = [""]


def interleave(gb, ga):
    if CUR_PHASE[0] in ILV_OFF:
        for g in (ga, gb):
            if g is not None:
                for _ in g:
                    pass
        return
    done_a = ga is None
    done_b = False
    while not (done_a and done_b):
        if not done_b:
            try:
                next(gb)
            except StopIteration:
                done_b = True
        if not done_a:
            try:
                next(ga)
            except StopIteration:
                done_a = True


def interleave_n(gens):
    gens = list(gens)
    if CUR_PHASE[0] in ILV_OFF:
        for g in reversed(gens):
            for _ in g:
                pass
        return
    while gens:
        for g in list(gens):
            try:
                next(g)
            except StopIteration:
                gens.remove(g)


def chain_gens(*gens):
    for g in gens:
        if g is not None:
            yield from g


def emit_yT(fw, C, yo, yT, mix, i):
    pb = C.pb[1]
    for cb in range(4):
        fw.tr(pb.t[:, cb * 128:(cb + 1) * 128], yo.t[:, cb * 128:(cb + 1) * 128], C.identb.t[:, :], [yo.b, C.identb.b], [pb.b])
    fw.cp("act", yT.t[:, :, :], pb.t[:, 0:512].rearrange("p (c t) -> p c t", c=4), [pb.b], [yT.b])
    fw.dma("sp", C.YT[mix * 4:(mix + 1) * 4, :, i * 128:(i + 1) * 128].rearrange("c p t -> p c t"), yT.t[:, :, :],
           reads=[yT.b], writes=[C.YTb[mix][i]])


def tok_mm(fw, C, ps, Wt, c0, n, i, ncols=None):
    for kc in range(8):
        fw.mm(ps.t[:, 0:n], C.uT.t[:, kc, i * 128:(i + 1) * 128], Wt.t[:, kc, c0:c0 + n], kc == 0, kc == 7,
              [C.uTb[i], Wt.b], [ps.b])


def phase_ret(fw, C, l):
    CUR_PHASE[0] = "ret"
    with ExitStack() as ph:
        WR = fw.tile(ph, "WR", [128, 8, 1536], BF16)
        load_w(fw, C.w_in[l], WR, C_RET, C_RET + 1536, 8)
        gnw = fw.tile(ph, "gnw", [128, 512], F32)
        fw.dma("sp", gnw.t[:, :], C.ret_gn_w[l].partition_broadcast(128), writes=[gnw.b])
        S = fw.tile(ph, "rS", [64, 4, 128], F32)
        Sb = fw.tile(ph, "rSb", [64, 4, 128], BF16)
        fw.memset("dve", S.t[:, :, :], 0.0, [S.b])
        fw.memset("dve", Sb.t[:, :, :], 0.0, [Sb.b])
        qkr_2 = [fw.tile(ph, "qkr", [128, 8, 64], F32) for _ in range(2)]
        t1_2 = [fw.tile(ph, "t1", [128, 8, 32], F32) for _ in range(2)]
        t2_2 = [fw.tile(ph, "t2", [128, 8, 32], F32) for _ in range(2)]
        qb_2 = [fw.tile(ph, "qb", [128, 256], BF16) for _ in range(2)]
        qdb_2 = [fw.tile(ph, "qdb", [128, 256], BF16) for _ in range(2)]
        kb_2 = [fw.tile(ph, "kb", [128, 256], BF16) for _ in range(2)]
        khb_2 = [fw.tile(ph, "khb", [128, 256], BF16) for _ in range(2)]
        qkT_2 = [fw.tile(ph, "qkT", [64, 12, 128], BF16) for _ in range(2)]
        vb_2 = [fw.tile(ph, "vb", [128, 512], BF16) for _ in range(2)]
        sg_2 = [fw.tile(ph, "sg", [128, 512], F32) for _ in range(2)]
        PT_2 = [fw.tile(ph, "PT", [128, 512], BF16) for _ in range(2)]
        ysq_2 = [fw.tile(ph, "ysq", [128, 512], F32) for _ in range(2)]
        st_2 = [fw.tile(ph, "st", [128, 16], F32) for _ in range(2)]
        yn_2 = [fw.tile(ph, "yn", [128, 512], F32) for _ in range(2)]
        yo_2 = [fw.tile(ph, "yo", [128, 512], BF16) for _ in range(2)]
        yT_2 = [fw.tile(ph, "yT", [128, 4, 128], BF16) for _ in range(2)]
        ps_qk, ps_v, ps_g, ps_s, ps_y, ps_st = C.ps[0:6]
        cb_ = C.consts.b
        def stA(i):
                qkr, t1, t2, qb, qdb, kb, khb, qkT, vb, sg, PT, ysq, st, yn, yo, yT = qkr_2[i % 2], t1_2[i % 2], t2_2[i % 2], qb_2[i % 2], qdb_2[i % 2], kb_2[i % 2], khb_2[i % 2], qkT_2[i % 2], vb_2[i % 2], sg_2[i % 2], PT_2[i % 2], ysq_2[i % 2], st_2[i % 2], yn_2[i % 2], yo_2[i % 2], yT_2[i % 2]
                tok_mm(fw, C, ps_qk, WR, 0, 512, i)
                yield
                tok_mm(fw, C, ps_v, WR, 512, 512, i)
                yield
                tok_mm(fw, C, ps_g, WR, 1024, 512, i)
                yield
                qv = ps_qk.t[:, :].rearrange("p (h d) -> p h d", d=64)
                x1, x2 = qv[:, :, 0:32], qv[:, :, 32:64]
                co, cn = CO["cos"][0], CO["sin"][0]
                cosb = C.consts.t[:, co + i * 32:co + (i + 1) * 32].unsqueeze(1).to_broadcast([128, 8, 32])
                sinb = C.consts.t[:, cn + i * 32:cn + (i + 1) * 32].unsqueeze(1).to_broadcast([128, 8, 32])
                fw.tt("dve", t1.t[:, :, :], x1, cosb, ALU.mult, [ps_qk.b, cb_], [t1.b])
                fw.tt("dve", t2.t[:, :, :], x2, sinb, ALU.mult, [ps_qk.b, cb_], [t2.b])
                fw.tt("pool", qkr.t[:, :, 0:32], t1.t[:, :, :], t2.t[:, :, :], ALU.subtract, [t1.b, t2.b], [qkr.b])
                fw.tt("dve", t1.t[:, :, :], x1, sinb, ALU.mult, [ps_qk.b, cb_], [t1.b])
                fw.tt("dve", t2.t[:, :, :], x2, cosb, ALU.mult, [ps_qk.b, cb_], [t2.b])
                fw.tt("pool", qkr.t[:, :, 32:64], t1.t[:, :, :], t2.t[:, :, :], ALU.add, [t1.b, t2.b], [qkr.b])
                qf = qkr.t[:, 0:4, :].rearrange("p h d -> p (h d)")
                kf = qkr.t[:, 4:8, :].rearrange("p h d -> p (h d)")
                fw.cp("act", qb.t[:, :], qf, [qkr.b], [qb.b])
                fw.tt("pool", qdb.t[:, :], qf, cslice(C, "qdec"), ALU.mult, [qkr.b, cb_], [qdb.b])
                fw.cp("act", kb.t[:, :], kf, [qkr.b], [kb.b])
                fw.tt("pool", khb.t[:, :], kf, cslice(C, "kdec"), ALU.mult, [qkr.b, cb_], [khb.b])
                fw.cp("act", vb.t[:, :], ps_v.t[:, :], [ps_v.b], [vb.b])
                fw.act(sg.t[:, :], ps_g.t[:, :], AF.Silu, [ps_g.b], [sg.b])
        def stB(i):
                qkr, t1, t2, qb, qdb, kb, khb, qkT, vb, sg, PT, ysq, st, yn, yo, yT = qkr_2[i % 2], t1_2[i % 2], t2_2[i % 2], qb_2[i % 2], qdb_2[i % 2], kb_2[i % 2], khb_2[i % 2], qkT_2[i % 2], vb_2[i % 2], sg_2[i % 2], PT_2[i % 2], ysq_2[i % 2], st_2[i % 2], yn_2[i % 2], yo_2[i % 2], yT_2[i % 2]
                pb0, pb1 = C.pb
                for h in range(4):
                    fw.tr(pb0.t[0:64, h * 128:(h + 1) * 128], qb.t[:, h * 64:(h + 1) * 64], C.identb.t[:, :], [qb.b, C.identb.b], [pb0.b])
                    fw.tr(pb0.t[0:64, (4 + h) * 128:(5 + h) * 128], qdb.t[:, h * 64:(h + 1) * 64], C.identb.t[:, :], [qdb.b, C.identb.b], [pb0.b])
                    fw.tr(pb1.t[0:64, h * 128:(h + 1) * 128], kb.t[:, h * 64:(h + 1) * 64], C.identb.t[:, :], [kb.b, C.identb.b], [pb1.b])
                yield
                fw.cp("dve", qkT.t[:, 0:8, :], pb0.t[0:64, :].rearrange("p (j t) -> p j t", j=8), [pb0.b], [qkT.b])
                fw.cp("act", qkT.t[:, 8:12, :], pb1.t[0:64, 0:512].rearrange("p (j t) -> p j t", j=4), [pb1.b], [qkT.b])
                for h in range(4):
                    fw.mm(ps_s.t[:, h * 128:(h + 1) * 128], qkT.t[:, 8 + h, :], qkT.t[:, h, :], True, True, [qkT.b], [ps_s.b])
                yield
                fw.tt("dve", PT.t[:, :], ps_s.t[:, :], cslice(C, "retdt"), ALU.mult, [ps_s.b, cb_], [PT.b])
                for h in range(4):
                    hs = slice(h * 128, (h + 1) * 128)
                    fw.mm(ps_y.t[:, hs], PT.t[:, hs], vb.t[:, hs], True, False, [PT.b, vb.b], [ps_y.b])
                    fw.mm(ps_y.t[:, hs], qkT.t[:, 4 + h, :], Sb.t[:, h, :], False, True, [qkT.b, Sb.b], [ps_y.b])
                yield
                for h in range(4):
                    hs = slice(h * 128, (h + 1) * 128)
                    fw.mm(ps_st.t[0:64, hs], khb.t[:, h * 64:(h + 1) * 64], vb.t[:, hs], True, True, [khb.b, vb.b], [ps_st.b])
                yield
                for h in range(4):
                    hs = slice(h * 128, (h + 1) * 128)
                    fw.stt("dve", S.t[:, h, :], S.t[:, h, :], float(GAMMA[h] ** 128), ps_st.t[0:64, hs], ALU.mult, ALU.add,
                           [S.b, ps_st.b], [S.b])
                fw.cp("pool", Sb.t[:, :, :], S.t[:, :, :], [S.b], [Sb.b])
                yv = ps_y.t[:, :].rearrange("p (h e) -> p h e", h=4)
                fw.op("dve", lambda g: g.reduce_sum(st.t[:, 0:4], yv, axis=AX.X), [ps_y.b], [st.b])
                fw.act(ysq.t[:, :], ps_y.t[:, :], AF.Square, [ps_y.b], [ysq.b])
                fw.op("dve", lambda g: g.reduce_sum(st.t[:, 4:8], ysq.t[:, :].rearrange("p (h e) -> p h e", h=4), axis=AX.X), [ysq.b], [st.b])
                fw.ts("dve", st.t[:, 8:12], st.t[:, 0:4], 1.0 / 128, None, ALU.mult, None, [st.b], [st.b])
                fw.tt("dve", st.t[:, 0:4], st.t[:, 8:12], st.t[:, 8:12], ALU.mult, [st.b], [st.b])
                fw.stt("dve", st.t[:, 12:16], st.t[:, 4:8], 1.0 / 128, st.t[:, 0:4], ALU.mult, ALU.subtract, [st.b], [st.b])
                fw.ts("dve", st.t[:, 12:16], st.t[:, 12:16], EPS, None, ALU.add, None, [st.b], [st.b])
                fw.act(st.t[:, 12:16], st.t[:, 12:16], AF.Sqrt, [st.b], [st.b])
                fw.op("dve", lambda g: g.reciprocal(st.t[:, 12:16], st.t[:, 12:16]), [st.b], [st.b])
                for h in range(4):
                    hs = slice(h * 128, (h + 1) * 128)
                    fw.ts("dve", yn.t[:, hs], ps_y.t[:, hs], st.t[:, 8 + h:9 + h], st.t[:, 12 + h:13 + h], ALU.subtract, ALU.mult,
                          [ps_y.b, st.b], [yn.b])
                fw.tt("pool", yn.t[:, :], yn.t[:, :], gnw.t[:, :], ALU.mult, [yn.b, gnw.b], [yn.b])
                fw.tt("pool", yo.t[:, :], yn.t[:, :], sg.t[:, :], ALU.mult, [yn.b, sg.b], [yo.b])
                emit_yT(fw, C, yo, yT, 0, i)
                yield
        for _ in stA(0):
            pass
        for i in range(NT):
            interleave(stB(i), stA(i + 1) if i + 1 < NT else None)
        fw.barrier()


def phase_ssd(fw, C, l):
    CUR_PHASE[0] = "ssd"
    with ExitStack() as ph:
        WS = fw.tile(ph, "WS", [128, 8, 1544], BF16)
        load_w(fw, C.w_in[l], WS, C_SSD, C_SSD + 1544, 8)
        cw = fw.tile(ph, "cw", [128, 8, 4], F32)
        cbi = fw.tile(ph, "cbi", [128, 8], F32)
        dtb = fw.tile(ph, "dtb", [128, 8], F32)
        arow = fw.tile(ph, "arow", [128, 8], F32)
        dsk = fw.tile(ph, "dsk", [128, 512], F32)
        nw = fw.tile(ph, "snw", [128, 512], F32)
        fw.dma("sp", cw.t[:, :, :], C.ssd_cw[l], writes=[cw.b])
        fw.dma("sp", cbi.t[:, :], C.ssd_cb[l], writes=[cbi.b])
        fw.dma("sp", dtb.t[:, :], C.ssd_dt_bias[l].partition_broadcast(128), writes=[dtb.b])
        fw.dma("sp", arow.t[:, :], C.ssd_a_log[l].partition_broadcast(128), writes=[arow.b])
        fw.dma("sp", dsk.t[:, :], C.ssd_dsk[l].partition_broadcast(128), writes=[dsk.b])
        fw.dma("sp", nw.t[:, :], C.ssd_norm_w[l].partition_broadcast(128), writes=[nw.b])
        fw.act(arow.t[:, :], arow.t[:, :], AF.Exp, [arow.b], [arow.b])
        fw.ts("dve", arow.t[:, :], arow.t[:, :], -1.0, None, ALU.mult, None, [arow.b], [arow.b])
        S = fw.tile(ph, "sS", [128, 512], F32)
        Sb = fw.tile(ph, "sSb", [128, 512], BF16)
        fw.memset("dve", S.t[:, :], 0.0, [S.b])
        fw.memset("dve", Sb.t[:, :], 0.0, [Sb.b])
        xraw = fw.tile(ph, "xraw", [128, 8, 131], F32)
        fw.memset("pool", xraw.t[:, :, :], 0.0, [xraw.b])
        xc_2 = [fw.tile(ph, "xc", [128, 8, 128], F32) for _ in range(2)]
        xcb_2 = [fw.tile(ph, "xcb", [128, 8, 128], BF16) for _ in range(2)]
        xs_2 = [fw.tile(ph, "xs", [128, 512], F32) for _ in range(2)]
        Btok_2 = [fw.tile(ph, "Btok", [128, 2, 128], BF16) for _ in range(2)]
        sz_2 = [fw.tile(ph, "sz", [128, 512], F32) for _ in range(2)]
        sm_2 = [fw.tile(ph, "sm", [128, 104], F32) for _ in range(2)]
        rhs8_2 = [fw.tile(ph, "rhs8", [128, 8, 128], F32) for _ in range(2)]
        decT_2 = [fw.tile(ph, "decT", [128, 8, 128], F32) for _ in range(2)]
        PT_2 = [fw.tile(ph, "sPT", [128, 8, 128], BF16) for _ in range(2)]
        xdt_2 = [fw.tile(ph, "xdt", [128, 512], BF16) for _ in range(2)]
        xdtw_2 = [fw.tile(ph, "xdtw", [128, 512], BF16) for _ in range(2)]
        ya_2 = [fw.tile(ph, "ya", [128, 512], F32) for _ in range(2)]
        tmp_2 = [fw.tile(ph, "stmp", [128, 512], F32) for _ in range(2)]
        yo_2 = [fw.tile(ph, "syo", [128, 512], BF16) for _ in range(2)]
        yT_2 = [fw.tile(ph, "syT", [128, 4, 128], BF16) for _ in range(2)]
        ps = C.ps
        pb0, pb1 = C.pb
        cb_ = C.consts.b
        XD, AXc, EX, LN, DT, LA, CUM, NCUM, ECUM, WW, EDEC, DW = [slice(8 * j, 8 * j + 8) for j in range(12)]
        triu = cslice(C, "triu")
        ones = cslice(C, "ones")
        def stA(i):
                xc, xcb, xs, Btok, sz, sm, decT, PT, xdt, xdtw, ya, tmp, yo, yT = xc_2[i % 2], xcb_2[i % 2], xs_2[i % 2], Btok_2[i % 2], sz_2[i % 2], sm_2[i % 2], decT_2[i % 2], PT_2[i % 2], xdt_2[i % 2], xdtw_2[i % 2], ya_2[i % 2], tmp_2[i % 2], yo_2[i % 2], yT_2[i % 2]
                tok = slice(i * 128, (i + 1) * 128)
                for cb in range(8):
                    pst = ps[0] if cb < 4 else ps[1]
                    for kc in range(8):
                        fw.mm(pst.t[:, (cb % 4) * 128:(cb % 4 + 1) * 128], WS.t[:, kc, 512 + cb * 128:512 + (cb + 1) * 128],
                              C.uT.t[:, kc, tok], kc == 0, kc == 7, [WS.b, C.uTb[i]], [pst.b])
                yield
                if i > 0:
                    fw.cp("pool", xraw.t[:, :, 0:3], xraw.t[:, :, 128:131], [xraw.b], [xraw.b])
                fw.cp("act", xraw.t[:, 0:4, 3:131], ps[0].t[:, :].rearrange("p (c t) -> p c t", c=4), [ps[0].b], [xraw.b])
                fw.cp("dve", xraw.t[:, 4:8, 3:131], ps[1].t[:, :].rearrange("p (c t) -> p c t", c=4), [ps[1].b], [xraw.b])
                for cb in range(8):
                    e = "dve" if cb % 2 == 0 else "pool"
                    fw.ts(e, xc.t[:, cb, :], xraw.t[:, cb, 3:131], cw.t[:, cb, 3:4], cbi.t[:, cb:cb + 1], ALU.mult, ALU.add,
                          [xraw.b, cw.b, cbi.b], [xc.b])
                    for j in (2, 1, 0):
                        fw.stt(e, xc.t[:, cb, :], xraw.t[:, cb, j:j + 128], cw.t[:, cb, j:j + 1], xc.t[:, cb, :], ALU.mult, ALU.add,
                               [xraw.b, cw.b, xc.b], [xc.b])
                fw.act(xcb.t[:, :, :], xc.t[:, :, :], AF.Silu, [xc.b], [xcb.b])
                tok_mm(fw, C, ps[2], WS, 0, 512, i)
                yield
                fw.act(sz.t[:, :], ps[2].t[:, :], AF.Silu, [ps[2].b], [sz.b])
                for kc in range(8):
                    fw.mm(ps[3].t[:, 0:8], C.uT.t[:, kc, tok], WS.t[:, kc, 1536:1544], kc == 0, kc == 7, [C.uTb[i], WS.b], [ps[3].b])
                yield
                smb = [sm.b]
                fw.tt("dve", sm.t[:, XD], ps[3].t[:, 0:8], dtb.t[:, :], ALU.add, [ps[3].b, dtb.b], smb)
                fw.ts("dve", sm.t[:, AXc], sm.t[:, XD], -1.0, None, ALU.mult, None, smb, smb)
                fw.tt("dve", sm.t[:, AXc], sm.t[:, AXc], sm.t[:, XD], ALU.max, smb, smb)
                fw.act(sm.t[:, EX], sm.t[:, AXc], AF.Exp, smb, smb, scale=-1.0)
                fw.ts("dve", sm.t[:, EX], sm.t[:, EX], 1.0, None, ALU.add, None, smb, smb)
                fw.act(sm.t[:, LN], sm.t[:, EX], AF.Ln, smb, smb)
                fw.stt("dve", sm.t[:, DT], sm.t[:, XD], 0.0, sm.t[:, LN], ALU.max, ALU.add, smb, smb)
                fw.tt("dve", sm.t[:, LA], sm.t[:, DT], arow.t[:, :], ALU.mult, smb + [arow.b], smb)
                fw.mm(ps[3].t[:, 16:24], triu, sm.t[:, LA], True, True, [cb_, sm.b], [ps[3].b])
                yield
                fw.mm(ps[3].t[:, 32:40], ones, sm.t[:, LA], True, True, [cb_, sm.b], [ps[3].b])
                yield
                fw.cp("dve", sm.t[:, CUM], ps[3].t[:, 16:24], [ps[3].b], smb)
                fw.ts("dve", sm.t[:, NCUM], sm.t[:, CUM], -1.0, None, ALU.mult, None, smb, smb)
                fw.act(sm.t[:, ECUM], sm.t[:, CUM], AF.Exp, smb, smb)
                fw.tt("dve", sm.t[:, WW], ps[3].t[:, 32:40], sm.t[:, CUM], ALU.subtract, [ps[3].b] + smb, smb)
                fw.act(sm.t[:, WW], sm.t[:, WW], AF.Exp, smb, smb)
                fw.act(sm.t[:, EDEC], ps[3].t[:, 32:40], AF.Exp, [ps[3].b], smb)
                fw.tt("dve", sm.t[:, DW], sm.t[:, DT], sm.t[:, WW], ALU.mult, smb, smb)
        def stB(i):
                xc, xcb, xs, Btok, sz, sm, decT, PT, xdt, xdtw, ya, tmp, yo, yT = xc_2[i % 2], xcb_2[i % 2], xs_2[i % 2], Btok_2[i % 2], sz_2[i % 2], sm_2[i % 2], decT_2[i % 2], PT_2[i % 2], xdt_2[i % 2], xdtw_2[i % 2], ya_2[i % 2], tmp_2[i % 2], yo_2[i % 2], yT_2[i % 2]
                smb = [sm.b]
                pb0, pb1 = C.pb
                rhs8 = rhs8_2[i % 2]
                for cb in range(4):
                    fw.tr(pb0.t[:, cb * 128:(cb + 1) * 128], xcb.t[:, cb, :], C.identb.t[:, :], [xcb.b, C.identb.b], [pb0.b])
                yield
                fw.cp("dve", xs.t[:, :], pb0.t[:, 0:512], [pb0.b], [xs.b])
                for g in range(2):
                    fw.tr(pb1.t[:, g * 128:(g + 1) * 128], xcb.t[:, 4 + g, :], C.identb.t[:, :], [xcb.b, C.identb.b], [pb1.b])
                yield
                fw.cp("act", Btok.t[:, :, :], pb1.t[:, 0:256].rearrange("p (g n) -> p g n", g=2), [pb1.b], [Btok.b])
                fw.tt("dve", rhs8.t[:, :, :], triu.unsqueeze(1).to_broadcast([128, 8, 128]),
                      sm.t[:, LA].unsqueeze(2).to_broadcast([128, 8, 128]), ALU.mult, [cb_, sm.b], [rhs8.b])
                fw.mm(ps[4].t[:, :], ones, rhs8.t[:, 0:4, :].rearrange("p h t -> p (h t)"), True, True, [cb_, rhs8.b], [ps[4].b])
                fw.mm(ps[5].t[:, :], ones, rhs8.t[:, 4:8, :].rearrange("p h t -> p (h t)"), True, True, [cb_, rhs8.b], [ps[5].b])
                for hb in range(2):
                    fw.tt("dve", decT.t[:, 4 * hb:4 * hb + 4, :], ps[4 + hb].t[:, :].rearrange("p (h t) -> p h t", h=4),
                          sm.t[:, 56 + 4 * hb:60 + 4 * hb].unsqueeze(2).to_broadcast([128, 4, 128]), ALU.add, [ps[4 + hb].b, sm.b], [decT.b])
                fw.tt("pool", decT.t[:, :, :], decT.t[:, :, :], cslice(C, "negm").unsqueeze(1).to_broadcast([128, 8, 128]), ALU.add,
                      [decT.b, cb_], [decT.b])
                fw.act(decT.t[:, :, :], decT.t[:, :, :], AF.Exp, [decT.b], [decT.b])
                for g in range(2):
                    fw.mm(ps[4].t[:, g * 128:(g + 1) * 128], xcb.t[:, 4 + g, :], xcb.t[:, 6 + g, :], True, True, [xcb.b], [ps[4].b])
                yield
                for g in range(2):
                    fw.tt("dve", PT.t[:, 4 * g:4 * g + 4, :], decT.t[:, 4 * g:4 * g + 4, :],
                          ps[4].t[:, g * 128:(g + 1) * 128].unsqueeze(1).to_broadcast([128, 4, 128]), ALU.mult, [decT.b, ps[4].b], [PT.b])
                xsv = xs.t[:, :].rearrange("p (h e) -> p h e", h=8)
                fw.tt("pool", xdt.t[:, :].rearrange("p (h e) -> p h e", h=8), xsv, sm.t[:, DT].unsqueeze(2).to_broadcast([128, 8, 64]),
                      ALU.mult, [xs.b, sm.b], [xdt.b])
                fw.tt("pool", xdtw.t[:, :].rearrange("p (h e) -> p h e", h=8), xsv, sm.t[:, DW].unsqueeze(2).to_broadcast([128, 8, 64]),
                      ALU.mult, [xs.b, sm.b], [xdtw.b])
                for h in range(8):
                    fw.mm(ps[5].t[:, h * 64:(h + 1) * 64], PT.t[:, h, :], xdt.t[:, h * 64:(h + 1) * 64], True, True, [PT.b, xdt.b], [ps[5].b])
                yield
                for g in range(2):
                    fw.mm(ps[4].t[:, g * 256:(g + 1) * 256], xcb.t[:, 6 + g, :], Sb.t[:, g * 256:(g + 1) * 256], True, True, [xcb.b, Sb.b], [ps[4].b])
                yield
                fw.tt("dve", ya.t[:, :].rearrange("p (h e) -> p h e", h=8), ps[4].t[:, :].rearrange("p (h e) -> p h e", h=8),
                      sm.t[:, ECUM].unsqueeze(2).to_broadcast([128, 8, 64]), ALU.mult, [ps[4].b, sm.b], [ya.b])
                fw.tt("dve", ya.t[:, :], ya.t[:, :], ps[5].t[:, :], ALU.add, [ya.b, ps[5].b], [ya.b])
                fw.tt("pool", tmp.t[:, :], xs.t[:, :], dsk.t[:, :], ALU.mult, [xs.b, dsk.b], [tmp.b])
                fw.tt("pool", ya.t[:, :], ya.t[:, :], tmp.t[:, :], ALU.add, [ya.b, tmp.b], [ya.b])
                fw.tt("pool", ya.t[:, :], ya.t[:, :], sz.t[:, :], ALU.mult, [ya.b, sz.b], [ya.b])
                for g in range(2):
                    fw.mm(ps[5].t[:, g * 256:(g + 1) * 256], Btok.t[:, g, :], xdtw.t[:, g * 256:(g + 1) * 256], True, True,
                          [Btok.b, xdtw.b], [ps[5].b])
                yield
                Sv = S.t[:, :].rearrange("p (h e) -> p h e", h=8)
                fw.tt("dve", Sv, Sv, sm.t[:, EDEC].unsqueeze(2).to_broadcast([128, 8, 64]), ALU.mult, [S.b, sm.b], [S.b])
                fw.tt("dve", S.t[:, :], S.t[:, :], ps[5].t[:, :], ALU.add, [S.b, ps[5].b], [S.b])
                fw.cp("pool", Sb.t[:, :], S.t[:, :], [S.b], [Sb.b])
                fw.act(tmp.t[:, :], ya.t[:, :], AF.Square, [ya.b], [tmp.b])
                fw.op("dve", lambda g_: g_.reduce_sum(sm.t[:, 96:98], tmp.t[:, :].rearrange("p (g e) -> p g e", g=2), axis=AX.X), [tmp.b], smb)
                rstd_from_ss(fw, C, sm.t[:, 96:98], sm.t[:, 98:100], 256.0, smb, smb)
                for g in range(2):
                    gs = slice(g * 256, (g + 1) * 256)
                    fw.ts("dve", tmp.t[:, gs], ya.t[:, gs], sm.t[:, 98 + g:99 + g], None, ALU.mult, None, [ya.b, sm.b], [tmp.b])
                fw.tt("pool", yo.t[:, :], tmp.t[:, :], nw.t[:, :], ALU.mult, [tmp.b, nw.b], [yo.b])
                emit_yT(fw, C, yo, yT, 2, i)
                yield
        for _ in stA(0):
            pass
        for i in range(NT):
            interleave(stB(i), stA(i + 1) if i + 1 < NT else None)
        fw.barrier()


def phase_hg(fw, C, l):
    CUR_PHASE[0] = "hg"
    with ExitStack() as ph:
        WH = fw.tile(ph, "WH", [128, 8, 2048], BF16)
        load_w(fw, C.w_in[l], WH, C_HG, C_HG + 2048, 8)
        nw = fw.tile(ph, "hnw", [128, 512], F32)
        fw.dma("sp", nw.t[:, :], C.hgrn_norm_w[l].partition_broadcast(128), writes=[nw.b])
        S = fw.tile(ph, "hS", [128, 4, 128], F32)
        Sb = fw.tile(ph, "hSb", [128, 4, 128], BF16)
        PT_2 = [fw.tile(ph, "hPT", [128, 4, 128], BF16) for _ in range(2)]
        fw.memset("dve", S.t[:, :, :], 0.0, [S.b])
        fw.memset("dve", Sb.t[:, :, :], 0.0, [Sb.b])
        for PT in PT_2:
            fw.memset("pool", PT.t[:, :, :], 0.0, [PT.b])

        qg_2 = [fw.tile(ph, "hqg", [128, 4, 512], F32) for _ in range(2)]
        fg_2 = [fw.tile(ph, "hfg", [128, 4, 512], F32) for _ in range(2)]
        la_2 = [fw.tile(ph, "hla", [128, 4, 128], F32) for _ in range(2)]
        kT_2 = [fw.tile(ph, "hk", [128, 4, 128], F32) for _ in range(2)]
        cum_2 = [fw.tile(ph, "hcum", [128, 4, 128], F32) for _ in range(2)]
        ncb_2 = [fw.tile(ph, "hncb", [128, 4, 4], F32) for _ in range(2)]
        for nb_ in ncb_2:
            fw.memset("pool", nb_.t[:, :, :], 0.0, [nb_.b])
        eq_2 = [fw.tile(ph, "heq", [128, 4, 128], F32) for _ in range(2)]
        qd_2 = [fw.tile(ph, "hqd", [128, 4, 128], BF16) for _ in range(2)]
        qst_2 = [fw.tile(ph, "hqst", [128, 4, 128], BF16) for _ in range(2)]
        ek_2 = [fw.tile(ph, "hek", [128, 4, 128], F32) for _ in range(2)]
        Kt_2 = [fw.tile(ph, "hKt", [128, 4, 4, 128], BF16) for _ in range(2)]
        khT_2 = [fw.tile(ph, "hkhT", [128, 4, 128], BF16) for _ in range(2)]
        khat_2 = [fw.tile(ph, "hkhat", [128, 4, 128], BF16) for _ in range(2)]
        dec_2 = [fw.tile(ph, "hdec", [128, 4], F32) for _ in range(2)]
        vb_3 = [fw.tile(ph, "hvb", [128, 512], BF16) for _ in range(3)]
        sgate_3 = [fw.tile(ph, "hsg", [128, 512], F32) for _ in range(3)]
        ysq_2 = [fw.tile(ph, "hysq", [128, 512], F32) for _ in range(2)]
        st_2 = [fw.tile(ph, "hst", [128, 8], F32) for _ in range(2)]
        yn_2 = [fw.tile(ph, "hyn", [128, 512], F32) for _ in range(2)]
        yo_2 = [fw.tile(ph, "hyo", [128, 512], BF16) for _ in range(2)]
        yT_2 = [fw.tile(ph, "hyT", [128, 4, 128], BF16) for _ in range(2)]
        ps = C.ps
        pb0, pb1 = C.pb
        cb_ = C.consts.b
        lbc = C.lb_all.t[:, l, :]
        omc = C.oml_all.t[:, l, :]
        ones = cslice(C, "ones")
        triu = cslice(C, "triu")
        def stG(g):
                t0 = g * 512
                n = min(512, TP - t0)
                tiles = list(range(t0 // 128, (t0 + n) // 128))
                qg, fg = qg_2[g % 2], fg_2[g % 2]
                k = 0
                for h in range(4):
                    for (c0, dst, func) in ((0, qg, AF.Silu), (512, fg, AF.Sigmoid)):
                        pst = ps[0]
                        for kc in range(8):
                            fw.mm(pst.t[:, 0:n], WH.t[:, kc, c0 + h * 128:c0 + (h + 1) * 128], C.uT.t[:, kc, t0:t0 + n], kc == 0, kc == 7,
                                  [WH.b] + [C.uTb[j] for j in tiles], [pst.b])
                        fw.act(dst.t[:, h, 0:n], pst.t[:, 0:n], func, [pst.b], [dst.b])
                        yield
        def stA(i):
                PT = PT_2[i % 2]
                la, kT, cum, ncb, eq, qd, qst, ek, Kt, khT, khat, dec, vb, sgate, ysq, st, yn, yo, yT = la_2[i % 2], kT_2[i % 2], cum_2[i % 2], ncb_2[i % 2], eq_2[i % 2], qd_2[i % 2], qst_2[i % 2], ek_2[i % 2], Kt_2[i % 2], khT_2[i % 2], khat_2[i % 2], dec_2[i % 2], vb_3[i % 3], sgate_3[i % 3], ysq_2[i % 2], st_2[i % 2], yn_2[i % 2], yo_2[i % 2], yT_2[i % 2]
                tok_mm(fw, C, ps[2], WH, 1024, 512, i)
                yield
                tok_mm(fw, C, ps[3], WH, 1536, 512, i)
                yield
                fw.cp("act", vb.t[:, :], ps[2].t[:, :], [ps[2].b], [vb.b])
                yield
                fw.act(sgate.t[:, :], ps[3].t[:, :], AF.Silu, [ps[3].b], [sgate.b])
                yield
        def stB1(i):
                PT = PT_2[i % 2]
                la, kT, cum, ncb, eq, qd, qst, ek, Kt, khT, khat, dec, vb, sgate, ysq, st, yn, yo, yT = la_2[i % 2], kT_2[i % 2], cum_2[i % 2], ncb_2[i % 2], eq_2[i % 2], qd_2[i % 2], qst_2[i % 2], ek_2[i % 2], Kt_2[i % 2], khT_2[i % 2], khat_2[i % 2], dec_2[i % 2], vb_3[i % 3], sgate_3[i % 3], ysq_2[i % 2], st_2[i % 2], yn_2[i % 2], yo_2[i % 2], yT_2[i % 2]
                qT = VW(qg_2[(i // 4) % 2].t[:, :, (i % 4) * 128:(i % 4 + 1) * 128], qg_2[(i // 4) % 2].b)
                fT = VW(fg_2[(i // 4) % 2].t[:, :, (i % 4) * 128:(i % 4 + 1) * 128], fg_2[(i // 4) % 2].b)
                for h in range(4):
                    fw.ts("dve", fT.t[:, h, :], fT.t[:, h, :], omc[:, h:h + 1], lbc[:, h:h + 1], ALU.mult, ALU.add,
                          [fT.b, C.lb_all.b, C.oml_all.b], [fT.b])
                yield
                fw.act(la.t[:, :, :], fT.t[:, :, :], AF.Ln, [fT.b], [la.b])
                yield
                fw.ts("pool", kT.t[:, :, :], fT.t[:, :, :], -1.0, 1.0, ALU.mult, ALU.add, [fT.b], [kT.b])
                yield
                for h in range(4):
                    fw.op("dve", lambda g: g.tensor_tensor_scan(cum.t[:, h, :], ones, la.t[:, h, :], 0.0, ALU.mult, ALU.add),
                          [la.b, cb_], [cum.b])
                yield
                cv = cum.t[:, :, :].rearrange("p h (b c) -> p h b c", c=32)
                fw.ts("dve", ncb.t[:, :, 1:4], cv[:, :, 0:3, 31], -1.0, None, ALU.mult, None, [cum.b], [ncb.b])
                yield
                fw.tt("dve", eq.t[:, :, :].rearrange("p h (b c) -> p h b c", c=32), cv,
                      ncb.t[:, :, :].unsqueeze(3).to_broadcast([128, 4, 4, 32]), ALU.add, [cum.b, ncb.b], [eq.b])
                yield
                fw.act(eq.t[:, :, :], eq.t[:, :, :], AF.Exp, [eq.b], [eq.b])
                yield
                fw.tt("pool", qd.t[:, :, :], qT.t[:, :, :], eq.t[:, :, :], ALU.mult, [qT.b, eq.b], [qd.b])
                yield
                fw.act(eq.t[:, :, :], cum.t[:, :, :], AF.Exp, [cum.b], [eq.b])
                yield
                fw.tt("pool", qst.t[:, :, :], qT.t[:, :, :], eq.t[:, :, :], ALU.mult, [qT.b, eq.b], [qst.b])
                yield
                for b in range(4):
                    W_ = 32 * (b + 1)
                    eb = ek if b % 2 == 0 else eq
                    fw.tt("dve", eb.t[:, :, 0:W_], cum.t[:, :, 0:W_], ncb.t[:, :, b:b + 1].to_broadcast([128, 4, W_]), ALU.add,
                          [cum.b, ncb.b], [eb.b])
                    fw.act(eb.t[:, :, 0:W_], eb.t[:, :, 0:W_], AF.Exp, [eb.b], [eb.b], scale=-1.0)
                    fw.tt("pool" if b % 2 == 0 else "dve", Kt.t[:, :, b, 0:W_], eb.t[:, :, 0:W_], kT.t[:, :, 0:W_], ALU.mult,
                          [eb.b, kT.b], [Kt.b])
                    yield
                fw.tt("dve", ek.t[:, :, :], cum.t[:, :, 127:128].to_broadcast([128, 4, 128]), cum.t[:, :, :], ALU.subtract, [cum.b], [ek.b])
                yield
                fw.act(ek.t[:, :, :], ek.t[:, :, :], AF.Exp, [ek.b], [ek.b])
                yield
                fw.tt("pool", khT.t[:, :, :], ek.t[:, :, :], kT.t[:, :, :], ALU.mult, [ek.b, kT.b], [khT.b])
                yield
                for h in range(4):
                    fw.tr(pb0.t[:, h * 128:(h + 1) * 128], khT.t[:, h, :], C.identb.t[:, :], [khT.b, C.identb.b], [pb0.b])
                yield
                fw.cp("act", khat.t[:, :, :], pb0.t[:, 0:512].rearrange("p (h d) -> p h d", h=4), [pb0.b], [khat.b])
                yield
                fw.act(dec.t[:, :], cum.t[:, :, 127], AF.Exp, [cum.b], [dec.b])
                yield
                for h in range(4):
                    for b in range(4):
                        W_ = 32 * (b + 1)
                        fw.mm(ps[4].t[0:W_, h * 128 + 32 * b:h * 128 + 32 * b + 32], Kt.t[:, h, b, 0:W_], qd.t[:, h, 32 * b:32 * b + 32],
                              True, True, [Kt.b, qd.b], [ps[4].b])
                yield
                psv = ps[4].t[:, :].rearrange("p (h t) -> p h t", h=4)
                for b in range(4):
                    W_ = 32 * (b + 1)
                    bs = slice(32 * b, 32 * b + 32)
                    to, tn = CO["triu"]
                    mk = C.consts.t[0:W_, to + 32 * b:to + 32 * b + 32].unsqueeze(1).to_broadcast([W_, 4, 32])
                    fw.tt("dve", PT.t[0:W_, :, bs], psv[0:W_, :, bs], mk, ALU.mult, [ps[4].b, cb_], [PT.b])
                yield
        def stB2(i):
                PT = PT_2[i % 2]
                la, kT, cum, ncb, eq, qd, qst, ek, Kt, khT, khat, dec, vb, sgate, ysq, st, yn, yo, yT = la_2[i % 2], kT_2[i % 2], cum_2[i % 2], ncb_2[i % 2], eq_2[i % 2], qd_2[i % 2], qst_2[i % 2], ek_2[i % 2], Kt_2[i % 2], khT_2[i % 2], khat_2[i % 2], dec_2[i % 2], vb_3[i % 3], sgate_3[i % 3], ysq_2[i % 2], st_2[i % 2], yn_2[i % 2], yo_2[i % 2], yT_2[i % 2]
                qT = VW(qg_2[(i // 4) % 2].t[:, :, (i % 4) * 128:(i % 4 + 1) * 128], qg_2[(i // 4) % 2].b)
                fT = VW(fg_2[(i // 4) % 2].t[:, :, (i % 4) * 128:(i % 4 + 1) * 128], fg_2[(i // 4) % 2].b)
                for h in range(4):
                    hs = slice(h * 128, (h + 1) * 128)
                    fw.mm(ps[5].t[:, hs], PT.t[:, h, :], vb.t[:, hs], True, False, [PT.b, vb.b], [ps[5].b])
                    fw.mm(ps[5].t[:, hs], qst.t[:, h, :], Sb.t[:, h, :], False, True, [qst.b, Sb.b], [ps[5].b])
                yield
                for h in range(4):
                    hs = slice(h * 128, (h + 1) * 128)
                    fw.mm(ps[1].t[:, hs], khat.t[:, h, :], vb.t[:, hs], True, True, [khat.b, vb.b], [ps[1].b])
                yield
                for h in range(4):
                    hs = slice(h * 128, (h + 1) * 128)
                    fw.stt("dve", S.t[:, h, :], S.t[:, h, :], dec.t[:, h:h + 1], ps[1].t[:, hs], ALU.mult, ALU.add, [S.b, dec.b, ps[1].b], [S.b])
                yield
                fw.cp("pool", Sb.t[:, :, :], S.t[:, :, :], [S.b], [Sb.b])
                yield
                fw.act(ysq.t[:, :], ps[5].t[:, :], AF.Square, [ps[5].b], [ysq.b])
                yield
                fw.op("dve", lambda g: g.reduce_sum(st.t[:, 0:4], ysq.t[:, :].rearrange("p (h e) -> p h e", h=4), axis=AX.X), [ysq.b], [st.b])
                yield
                rstd_from_ss(fw, C, st.t[:, 0:4], st.t[:, 4:8], 128.0, [st.b], [st.b])
                yield
                for h in range(4):
                    hs = slice(h * 128, (h + 1) * 128)
                    fw.ts("dve", yn.t[:, hs], ps[5].t[:, hs], st.t[:, 4 + h:5 + h], None, ALU.mult, None, [ps[5].b, st.b], [yn.b])
                yield
                fw.tt("pool", yn.t[:, :], yn.t[:, :], nw.t[:, :], ALU.mult, [yn.b, nw.b], [yn.b])
                yield
                fw.tt("pool", yo.t[:, :], yn.t[:, :], sgate.t[:, :], ALU.mult, [yn.b, sgate.b], [yo.b])
                yield
                emit_yT(fw, C, yo, yT, 3, i)
                yield
        for st0 in (stG(0), stA(0), stB1(0), stA(1) if NT > 1 else None):
            if st0 is not None:
                for _ in st0:
                    pass
        for i in range(NT):
            gens = [stB2(i)]
            if i + 1 < NT:
                gens.append(stB1(i + 1))
            if i + 2 < NT:
                gens.append(chain_gens(stG((i + 2) // 4) if (i + 2) % 4 == 0 else None, stA(i + 2)))
            interleave_n(gens)
        fw.barrier()


TGS = [(0, 512), (512, 512), (1024, 512), (1536, 512), (2048, 128)]


def phase_s5(fw, C, l):
    ps = C.ps
    pb0, pb1 = C.pb
    cb_ = C.consts.b
    with ExitStack() as ph:
        GW = fw.tile(ph, "GW", [128, 4, 1024], BF16)
        load_w(fw, C.s5_glu_w[l], GW, 0, 1024, 4)
        Cre = fw.tile(ph, "Cre", [128, 16, 128], BF16)
        nCim = fw.tile(ph, "nCim", [128, 16, 128], BF16)
        fw.dma("pool", Cre.t[:, :, :], C.s5_cre[l], writes=[Cre.b])
        fw.dma("pool", nCim.t[:, :, :], C.s5_cim[l], writes=[nCim.b])
        fw.ts("pool", nCim.t[:, :, :], nCim.t[:, :, :], -1.0, None, ALU.mult, None, [nCim.b], [nCim.b])
        d5 = fw.tile(ph, "d5", [128, 4], F32)
        gb = fw.tile(ph, "gb5", [128, 8], F32)
        fw.dma("sp", d5.t[:, :], C.s5_dT[l], writes=[d5.b])
        fw.dma("sp", gb.t[:, :], C.s5_gbT[l], writes=[gb.b])
        sp_ = fw.tile(ph, "s5sm", [128, 48], F32)
        fw.dma("sp", sp_.t[:, :], C.s5_small[l], writes=[sp_.b])
        u5T = fw.tile(ph, "u5T", [128, 4, TP], BF16)
        Sall = fw.tile(ph, "Sall", [128, 137, 3, 16], F32)
        KT = fw.tile(ph, "KT", [128, 4, 16, 128], BF16)
        PW = fw.tile(ph, "PW", [128, 2, 17, 16], F32)
        wk = fw.tile(ph, "s5wk", [128, 12, 16], F32)
        with ExitStack() as pa:
            W5 = fw.tile(pa, "W5", [128, 8, 512], BF16)
            load_w(fw, C.w_in[l], W5, C_S5, C_S5 + 512, 8)
            k = 0
            for ct in range(4):
                for (t0, n) in TGS:
                    pst = ps[k % 2]
                    for kc in range(8):
                        fw.mm(pst.t[:, 0:n], W5.t[:, kc, ct * 128:(ct + 1) * 128], C.uT.t[:, kc, t0:t0 + n], kc == 0, kc == 7,
                              [W5.b] + C.uTb[t0 // 128:(t0 + n) // 128], [pst.b])
                    fw.cp("act" if k % 2 else "dve", u5T.t[:, ct, t0:t0 + n], pst.t[:, 0:n], [pst.b], [u5T.b])
                    k += 1
            fw.barrier()
        if os.environ.get("S5_STOP") == "A":
            fw.barrier(); return
        lr, li, lst = sp_.t[:, 0:16], sp_.t[:, 16:32], sp_.t[:, 32:48]
        W_ = [wk.t[:, j, :] for j in range(12)]
        R = [sp_.b, wk.b, PW.b]
        step, lrs, ang, em1, re_, im_, t_a, t_b, inv, co_re, co_im, rr = W_
        big = [fw.tile(ph, "s5big", [128, 18, 16], F32) for _ in range(5)]
        bigi = fw.tile(ph, "s5bigi", [128, 18, 16], mybir.dt.int32)
        RB = R + [b_.b for b_ in big] + [bigi.b, cb_]
        FACT = [1.0, 1.0, 2.0, 6.0, 24.0, 120.0, 720.0, 5040.0, 40320.0, 362880.0, 3628800.0]

        def horner_exp(out, r, deg, minus1=False):
            fw.ts("dve", out, r, 1.0 / FACT[deg], None, ALU.mult, None, RB, RB)
            for j in range(deg - 1, 0, -1):
                fw.stt("dve", out, out, 1.0 / FACT[j], r, ALU.add, ALU.mult, RB, RB)
            if not minus1:
                fw.ts("dve", out, out, 1.0, None, ALU.add, None, RB, RB)

        fw.ts("dve", rr, lst, 0.125, None, ALU.mult, None, RB, RB)
        horner_exp(step, rr, 10)
        for _ in range(3):
            fw.tt("dve", step, step, step, ALU.mult, RB, RB)
        fw.tt("dve", lrs, lr, step, ALU.mult, RB, RB)
        fw.tt("dve", ang, li, step, ALU.mult, RB, RB)
        mo = CO["mtab"][0]
        mtab = C.consts.t[:, mo:mo + 288].rearrange("p (m q) -> p m q", m=18)
        TH, XM, MAG, SN, CS = [b_.t[:, :, :] for b_ in big]
        fw.tt("dve", TH, mtab, ang.unsqueeze(1).to_broadcast([128, 18, 16]), ALU.mult, RB, RB)
        fw.tt("dve", XM, mtab, lrs.unsqueeze(1).to_broadcast([128, 18, 16]), ALU.mult, RB, RB)
        horner_exp(MAG, XM, 10)
        C1, C2 = 6.28125, 2.0 * math.pi - 6.28125

        def sin_reduced(out, th):
            fw.ts("dve", out, th, 1.0 / (2.0 * math.pi), None, ALU.mult, None, RB, RB)
            fw.cp("dve", bigi.t[:, :, :], out, RB, RB)
            fw.cp("dve", XM, bigi.t[:, :, :], RB, RB)
            fw.stt("dve", out, XM, -C1, th, ALU.mult, ALU.add, RB, RB)
            fw.stt("dve", out, XM, -C2, out, ALU.mult, ALU.add, RB, RB)
            fw.act(out, out, AF.Sin, RB, RB)

        sin_reduced(SN, TH)
        fw.ts("dve", TH, TH, math.pi / 2, None, ALU.add, None, RB, RB)
        sin_reduced(CS, TH)
        fw.tt("dve", PW.t[:, 0, :, :], MAG[:, 0:17, :], CS[:, 0:17, :], ALU.mult, RB, RB)
        fw.tt("dve", PW.t[:, 1, :, :], MAG[:, 0:17, :], SN[:, 0:17, :], ALU.mult, RB, RB)
        horner_exp(em1, lrs, 7, minus1=True)
        fw.tt("dve", re_, em1, CS[:, 1, :], ALU.mult, RB, RB)
        fw.tt("dve", t_a, SN[:, 17, :], SN[:, 17, :], ALU.mult, RB, RB)
        fw.stt("dve", re_, t_a, -2.0, re_, ALU.mult, ALU.add, RB, RB)
        fw.ts("dve", t_b, em1, 1.0, None, ALU.add, None, RB, RB)
        fw.tt("dve", im_, t_b, SN[:, 1, :], ALU.mult, RB, RB)
        fw.tt("dve", t_a, lr, lr, ALU.mult, RB, RB)
        fw.tt("dve", t_b, li, li, ALU.mult, RB, RB)
        fw.tt("dve", inv, t_a, t_b, ALU.add, RB, RB)
        fw.op("dve", lambda g: g.reciprocal(inv, inv), RB, RB)
        fw.tt("dve", t_a, re_, lr, ALU.mult, RB, RB)
        fw.tt("dve", t_b, im_, li, ALU.mult, RB, RB)
        fw.tt("dve", t_a, t_a, t_b, ALU.add, RB, RB)
        fw.tt("dve", co_re, t_a, inv, ALU.mult, RB, RB)
        fw.tt("dve", t_a, im_, lr, ALU.mult, RB, RB)
        fw.tt("dve", t_b, re_, li, ALU.mult, RB, RB)
        fw.tt("dve", t_a, t_a, t_b, ALU.subtract, RB, RB)
        fw.tt("dve", co_im, t_a, inv, ALU.mult, RB, RB)
        if C.debug and l == 0:
            fw.dma("sp", C.dbg5[:, 0:192], wk.t[:, :, :].rearrange("p a b -> p (a b)"), reads=[wk.b], writes=[Buf()])
            fw.dma("sp", C.dbg5[:, 192:736], PW.t[:, :, :, :].rearrange("p a m q -> p (a m q)"), reads=[PW.b], writes=[Buf()])
        if os.environ.get("S5_STOP") == "B":
            fw.barrier(); return
        fw.memset("pool", Sall.t[:, 0, :, :], 0.0, [Sall.b])
        with ExitStack() as pd:
            Bst = fw.tile(pd, "Bst", [128, 2, 4, 128], F32)
            Bb = fw.tile(pd, "Bb", [128, 2, 4, 128], F32)
            t0_ = fw.tile(pd, "tB", [128, 128], F32)
            tA = [fw.tile(pd, "tA", [128, 16, 128], F32)] * 2
            tB = [fw.tile(pd, "tBB", [128, 16, 128], F32)] * 2
            Xs = [fw.tile(pd, "X", [128, 2, 16, 128], BF16) for _ in range(2)]
            XTs = [fw.tile(pd, "XT", [128, 4, 2, 128], BF16) for _ in range(2)]
            psK = ps[2:6]
            zt = fw.tile(pd, "zt", [128, 512], BF16)
            fw.memset("pool", zt.t[:, :], 0.0, [zt.b])
            it = 0
            for ct in range(4):
                for j in range(4):
                    fw.mm(psK[j].t[:, :], zt.t[:, 0:128], zt.t[:, :], True, False, [zt.b], [psK[j].b])
                fw.dma("sp", Bst.t[:, 0, :, :], C.s5_bre[l][:, 4 * ct:4 * ct + 4, :], writes=[Bst.b])
                fw.dma("sp", Bst.t[:, 1, :, :], C.s5_bim[l][:, 4 * ct:4 * ct + 4, :], writes=[Bst.b])
                for pl in range(4):
                    pair = 4 * ct + pl
                    cr, ci = co_re[:, pair:pair + 1], co_im[:, pair:pair + 1]
                    fw.ts("dve", t0_.t[:, :], Bst.t[:, 1, pl, :], ci, None, ALU.mult, None, [Bst.b, wk.b], [t0_.b])
                    fw.stt("dve", Bb.t[:, 0, pl, :], Bst.t[:, 0, pl, :], cr, t0_.t[:, :], ALU.mult, ALU.subtract, [Bst.b, wk.b, t0_.b], [Bb.b])
                    fw.ts("dve", t0_.t[:, :], Bst.t[:, 1, pl, :], cr, None, ALU.mult, None, [Bst.b, wk.b], [t0_.b])
                    fw.stt("dve", Bb.t[:, 1, pl, :], Bst.t[:, 0, pl, :], ci, t0_.t[:, :], ALU.mult, ALU.add, [Bst.b, wk.b, t0_.b], [Bb.b])
                for pl in range(4):
                    pair = 4 * ct + pl
                    psG = ps[pair % 2]
                    X = Xs[pair % 2]
                    ta, tb = tA[pair % 2], tB[pair % 2]
                    bre = Bb.t[:, 0, pl, :].unsqueeze(1).to_broadcast([128, 16, 128])
                    bim = Bb.t[:, 1, pl, :].unsqueeze(1).to_broadcast([128, 16, 128])
                    prb = PW.t[:, 0, 0:16, pair].unsqueeze(2).to_broadcast([128, 16, 128])
                    pib = PW.t[:, 1, 0:16, pair].unsqueeze(2).to_broadcast([128, 16, 128])
                    RB_ = [Bb.b, PW.b]
                    fw.tt("dve", ta.t[:, :, :], bre, prb, ALU.mult, RB_, [ta.b])
                    fw.tt("pool", tb.t[:, :, :], bim, pib, ALU.mult, RB_, [tb.b])
                    fw.tt("dve", X.t[:, 0, :, :], ta.t[:, :, :], tb.t[:, :, :], ALU.subtract, [ta.b, tb.b], [X.b])
                    fw.tt("pool", tb.t[:, :, :], bim, prb, ALU.mult, RB_, [tb.b])
                    fw.tt("dve", ta.t[:, :, :], bre, pib, ALU.mult, RB_, [ta.b])
                    fw.tt("dve", X.t[:, 1, :, :], ta.t[:, :, :], tb.t[:, :, :], ALU.add, [ta.b, tb.b], [X.b])
                    fw.mm(psG.t[:, 0:272], zt.t[:, 0:128], zt.t[:, 0:272], True, False, [zt.b], [psG.b])
                    for m in range(16):
                        pk = psK[m // 4]
                        ks = slice((m % 4) * 128, (m % 4 + 1) * 128)
                        fw.mm(pk.t[:, ks], X.t[:, 0, m, :], Cre.t[:, pair, :], False, False, [X.b, Cre.b], [pk.b])
                        fw.mm(pk.t[:, ks], X.t[:, 1, m, :], nCim.t[:, pair, :], False, pl == 3, [X.b, nCim.b], [pk.b])
                    for mg in range(4):
                        XT = XTs[it % 2]
                        pbt = C.pb[it % 2]
                        for mm_ in range(4):
                            m = 4 * mg + mm_
                            for part in range(2):
                                fw.tr(pbt.t[:, (2 * mm_ + part) * 128:(2 * mm_ + part + 1) * 128], X.t[:, part, m, :], C.identb.t[:, :],
                                      [X.b, C.identb.b], [pbt.b])
                        fw.cp("act" if it % 2 else "dve", XT.t[:, :, :, :], pbt.t[:, :].rearrange("p (m a q) -> p m a q", m=4, a=2),
                              [pbt.b], [XT.b])
                        for mm_ in range(4):
                            m = 4 * mg + mm_
                            tau = 15 - m
                            rhs = u5T.t[:, ct, :].rearrange("p (c b) -> p c b", b=16)[:, :, tau]
                            fw.mm(psG.t[:, 0:136], XT.t[:, mm_, 0, :], rhs, False, m == 15, [XT.b, u5T.b], [psG.b])
                            fw.mm(psG.t[:, 136:272], XT.t[:, mm_, 1, :], rhs, False, m == 15, [XT.b, u5T.b], [psG.b])
                        it += 1
                    fw.cp("act", Sall.t[:, 1:137, 0, pair], psG.t[:, 0:136], [psG.b], [Sall.b])
                    fw.cp("act", Sall.t[:, 1:137, 1, pair], psG.t[:, 136:272], [psG.b], [Sall.b])
                for j in range(4):
                    fw.cp("act" if j % 2 else "dve", KT.t[:, ct, 4 * j:4 * j + 4, :], psK[j].t[:, :].rearrange("p (m c) -> p m c", m=4),
                          [psK[j].b], [KT.b])
            fw.barrier()
        if C.debug and l == 0:
            fw.dma("pool", C.dbg5[:, 736:736 + 2048], KT.t[:, 0, :, :].rearrange("p m c -> p (m c)"), reads=[KT.b], writes=[Buf()])
            fw.dma("sp", C.dbg5[:, 2784:2784 + 137 * 48], Sall.t[:, :, :, :].rearrange("p c a q -> p (c a q)"), reads=[Sall.b], writes=[Buf()])
        if os.environ.get("S5_STOP") == "D":
            fw.barrier(); return
        with ExitStack() as pe_:
            A1 = fw.tile(pe_, "A1", [128, 2, 16], F32)
            A2 = fw.tile(pe_, "A2", [128, 2, 16], F32)
            p1 = fw.tile(pe_, "p1", [128, 2, 16], F32)
            p2 = fw.tile(pe_, "p2", [128, 2, 16], F32)
            fw.cp("dve", A1.t[:, 0, :], PW.t[:, 0, 16, :], [PW.b], [A1.b])
            fw.cp("dve", A1.t[:, 1, :], PW.t[:, 0, 16, :], [PW.b], [A1.b])
            fw.ts("dve", A2.t[:, 0, :], PW.t[:, 1, 16, :], -1.0, None, ALU.mult, None, [PW.b], [A2.b])
            fw.cp("dve", A2.t[:, 1, :], PW.t[:, 1, 16, :], [PW.b], [A2.b])
            for c in range(136):
                fw.tt("dve", p1.t[:, :, :], A1.t[:, :, :], Sall.t[:, c, 0:2, :], ALU.mult, [A1.b, Sall.b], [p1.b])
                fw.tt("dve", p2.t[:, :, :], A2.t[:, :, :], Sall.t[:, c, 1:3, :], ALU.mult, [A2.b, Sall.b], [p2.b])
                fw.tt("dve", p1.t[:, :, :], p1.t[:, :, :], p2.t[:, :, :], ALU.add, [p1.b, p2.b], [p1.b])
                fw.tt("dve", Sall.t[:, c + 1, 0:2, :], Sall.t[:, c + 1, 0:2, :], p1.t[:, :, :], ALU.add, [Sall.b, p1.b], [Sall.b])
                fw.cp("dve", Sall.t[:, c + 1, 2, :], Sall.t[:, c + 1, 0, :], [Sall.b], [Sall.b])
            fw.barrier()
        if os.environ.get("S5_STOP") == "E":
            fw.barrier(); return
        pfg = ExitStack()
        gT = fw.tile(pfg, "g5T", [128, 4, TP], BF16)
        with ExitStack() as pf:
            SP = fw.tile(pf, "SP", [128, 4, 2, 136, 16], BF16)
            u1 = fw.tile(pf, "u1", [128, 136, 16], F32)
            u2 = fw.tile(pf, "u2", [128, 136, 16], F32)
            z = fw.tile(pf, "z5", [128, 512], F32)
            z2 = fw.tile(pf, "z52", [128, 512], F32)
            k = 0
            for ct in range(4):
                for pl in range(4):
                    pair = 4 * ct + pl
                    srb = Sall.t[:, 0:136, 0, pair].unsqueeze(2).to_broadcast([128, 136, 16])
                    sib = Sall.t[:, 0:136, 1, pair].unsqueeze(2).to_broadcast([128, 136, 16])
                    prb = PW.t[:, 0, 1:17, pair].unsqueeze(1).to_broadcast([128, 136, 16])
                    pib = PW.t[:, 1, 1:17, pair].unsqueeze(1).to_broadcast([128, 136, 16])
                    RS = [Sall.b, PW.b]
                    fw.tt("dve", u1.t[:, :, :], srb, prb, ALU.mult, RS, [u1.b])
                    fw.tt("pool", u2.t[:, :, :], sib, pib, ALU.mult, RS, [u2.b])
                    fw.tt("dve", SP.t[:, pl, 0, :, :], u1.t[:, :, :], u2.t[:, :, :], ALU.subtract, [u1.b, u2.b], [SP.b])
                    fw.tt("pool", u2.t[:, :, :], sib, prb, ALU.mult, RS, [u2.b])
                    fw.tt("dve", u1.t[:, :, :], srb, pib, ALU.mult, RS, [u1.b])
                    fw.tt("dve", SP.t[:, pl, 1, :, :], u1.t[:, :, :], u2.t[:, :, :], ALU.add, [u1.b, u2.b], [SP.b])
                for (t0, n) in TGS:
                    c0, nch = t0 // 16, n // 16
                    pst = ps[k % 2]
                    k += 1
                    pv = pst.t[:, 0:n].rearrange("p (c b) -> p c b", b=16)
                    uv = u5T.t[:, ct, t0:t0 + n].rearrange("p (c b) -> p c b", b=16)
                    for tau in range(16):
                        fw.mm(pv[:, :, tau:16], KT.t[:, ct, tau, :], uv[:, :, 0:16 - tau], tau == 0, False, [KT.b, u5T.b], [pst.b])
                    for pl in range(4):
                        pair = 4 * ct + pl
                        fw.mm(pst.t[:, 0:n], Cre.t[:, pair, :], SP.t[:, pl, 0, c0:c0 + nch, :].rearrange("p c b -> p (c b)"), False, False,
                              [Cre.b, SP.b], [pst.b])
                        fw.mm(pst.t[:, 0:n], nCim.t[:, pair, :], SP.t[:, pl, 1, c0:c0 + nch, :].rearrange("p c b -> p (c b)"), False, pl == 3,
                              [nCim.b, SP.b], [pst.b])
                    fw.stt("dve", z.t[:, 0:n], u5T.t[:, ct, t0:t0 + n], d5.t[:, ct:ct + 1], pst.t[:, 0:n], ALU.mult, ALU.add,
                           [u5T.b, d5.b, pst.b], [z.b])
                    fw.tt("pool", z2.t[:, 0:n], z.t[:, 0:n], z.t[:, 0:n], ALU.mult, [z.b], [z2.b])
                    fw.ts("pool", z2.t[:, 0:n], z2.t[:, 0:n], 0.044715, 1.0, ALU.mult, ALU.add, [z2.b], [z2.b])
                    fw.tt("pool", z2.t[:, 0:n], z2.t[:, 0:n], z.t[:, 0:n], ALU.mult, [z2.b, z.b], [z2.b])
                    fw.act(z2.t[:, 0:n], z2.t[:, 0:n], AF.Sigmoid, [z2.b], [z2.b], scale=2.0 * math.sqrt(2.0 / math.pi))
                    fw.tt("pool", gT.t[:, ct, t0:t0 + n], z.t[:, 0:n], z2.t[:, 0:n], ALU.mult, [z.b, z2.b], [gT.b])
            fw.barrier()
        if os.environ.get("S5_STOP") == "F":
            pfg.close(); fw.barrier(); return
        with ExitStack() as pg:
            sgs = [fw.tile(pg, "sg5", [128, 512], F32) for _ in range(2)]
            yos = [fw.tile(pg, "yo5", [128, 512], BF16) for _ in range(2)]
            k = 0
            for nb in range(4):
                for (t0, n) in TGS:
                    pa_, pg_ = ps[2 + 2 * (k % 2)], ps[3 + 2 * (k % 2)]
                    sg, yo = sgs[k % 2], yos[k % 2]
                    k += 1
                    for kc in range(4):
                        fw.mm(pa_.t[:, 0:n], GW.t[:, kc, nb * 128:(nb + 1) * 128], gT.t[:, kc, t0:t0 + n], kc == 0, kc == 3, [GW.b, gT.b], [pa_.b])
                    for kc in range(4):
                        fw.mm(pg_.t[:, 0:n], GW.t[:, kc, 512 + nb * 128:512 + (nb + 1) * 128], gT.t[:, kc, t0:t0 + n], kc == 0, kc == 3,
                              [GW.b, gT.b], [pg_.b])
                    fw.act(sg.t[:, 0:n], pg_.t[:, 0:n], AF.Sigmoid, [pg_.b, gb.b], [sg.b], bias=gb.t[:, 4 + nb:5 + nb])
                    fw.stt("dve", yo.t[:, 0:n], pa_.t[:, 0:n], gb.t[:, nb:nb + 1], sg.t[:, 0:n], ALU.add, ALU.mult, [pa_.b, gb.b, sg.b], [yo.b])
                    fw.dma("sp", C.YT[4 + nb, :, t0:t0 + n], yo.t[:, 0:n], reads=[yo.b], writes=C.YTb[1][t0 // 128:(t0 + n) // 128])
            fw.barrier()
        pfg.close()
        fw.barrier()


MGS = [(g * 256, min(256, TP - g * 256)) for g in range((TP + 255) // 256)]


def rms_epilogue(fw, C, psA, psB, nw, xt, wk2):
    junk, ss, tmp = wk2
    fw.act(junk.t[:, 0:512], psA.t[:, :], AF.Square, [psA.b], [junk.b, ss.b], accum=ss.t[:, 2:3])
    fw.act(junk.t[:, 512:1024], psB.t[:, :], AF.Square, [psB.b], [junk.b, ss.b], accum=ss.t[:, 3:4])
    fw.tt("dve", ss.t[:, 2:3], ss.t[:, 2:3], ss.t[:, 3:4], ALU.add, [ss.b], [ss.b])
    rstd_from_ss(fw, C, ss.t[:, 2:3], ss.t[:, 3:4], 1024.0, [ss.b], [ss.b])
    fw.stt("dve", tmp.t[:, 0:512], psA.t[:, :], ss.t[:, 3:4], nw.t[:, 0:512], ALU.mult, ALU.mult, [psA.b, ss.b, nw.b], [tmp.b])
    fw.stt("dve", tmp.t[:, 512:1024], psB.t[:, :], ss.t[:, 3:4], nw.t[:, 512:1024], ALU.mult, ALU.mult, [psB.b, ss.b, nw.b], [tmp.b])
    fw.tt("pool", xt.t[:, :], xt.t[:, :], tmp.t[:, :], ALU.add, [xt.b, tmp.b], [xt.b])


def phase_merge(fw, C, l):
    ps = C.ps
    with ExitStack() as ph:
        WG = fw.tile(ph, "WG", [128, 8, 4096], BF16)
        load_w(fw, C.w_in[l], WG, C_GATE, C_GATE + 4096, 8)
        WB = fw.tile(ph, "WB", [128, 16, 1024], BF16)
        for n in range(4):
            v = C.w_branch[l][n].rearrange("(cb p) d -> p cb d", p=128)
            fw.dma("pool", WB.t[:, 4 * n:4 * n + 4, :], v, writes=[WB.b])
        WO = fw.tile(ph, "WO", [128, 8, 1024], BF16)
        load_w(fw, C.w_out[l], WO, 0, 1024, 8)
        nw1 = fw.tile(ph, "nw1", [128, D], F32)
        nw2 = fw.tile(ph, "nw2", [128, D], F32)
        fw.dma("sp", nw1.t[:, :], C.norm_post_mix[l].partition_broadcast(128), writes=[nw1.b])
        fw.dma("sp", nw2.t[:, :], C.norm_pre_mlp[l].partition_broadcast(128), writes=[nw2.b])
        YTs = fw.tile(ph, "YTs", [128, 16, 256], BF16)
        mixT = fw.tile(ph, "mixT", [128, 8, 256], BF16)
        acc = fw.tile(ph, "macc", [128, 256], F32)
        sgs = [fw.tile(ph, "msg", [128, 256], F32) for _ in range(2)]
        tmpm = fw.tile(ph, "mtmp", [128, 256], F32)
        xts = [fw.tile(ph, "mxt", [128, D], F32) for _ in range(2)]
        wk = (fw.tile(ph, "junk", [128, D], BF16), fw.tile(ph, "ss", [128, 4], F32), fw.tile(ph, "ub", [128, D], BF16))
        wk2 = (wk[0], wk[1], fw.tile(ph, "mtmp2", [128, D], F32))
        k = 0
        for (t0, n) in MGS:
            tiles = list(range(t0 // 128, (t0 + n) // 128))
            fw.dma("sp", YTs.t[:, :, 0:n], C.YT[:, :, t0:t0 + n].rearrange("c p t -> p c t"),
                   reads=[C.YTb[m][i] for m in range(4) for i in tiles], writes=[YTs.b])
            for db in range(8):
                for nn in range(4):
                    pg_, pb_ = ps[2 * (k % 2)], ps[2 * (k % 2) + 1]
                    sg = sgs[k % 2]
                    k += 1
                    c0 = nn * 1024 + db * 128
                    for kc in range(8):
                        fw.mm(pg_.t[:, 0:n], WG.t[:, kc, c0:c0 + 128], C.uT.t[:, kc, t0:t0 + n], kc == 0, kc == 7,
                              [WG.b] + [C.uTb[i] for i in tiles], [pg_.b])
                    for cb in range(4):
                        fw.mm(pb_.t[:, 0:n], WB.t[:, 4 * nn + cb, db * 128:(db + 1) * 128], YTs.t[:, 4 * nn + cb, 0:n], cb == 0, cb == 3,
                              [WB.b, YTs.b], [pb_.b])
                    fw.act(sg.t[:, 0:n], pg_.t[:, 0:n], AF.Sigmoid, [pg_.b], [sg.b])
                    if nn == 0:
                        fw.tt("dve", acc.t[:, 0:n], sg.t[:, 0:n], pb_.t[:, 0:n], ALU.mult, [sg.b, pb_.b], [acc.b])
                    else:
                        fw.tt("dve", tmpm.t[:, 0:n], sg.t[:, 0:n], pb_.t[:, 0:n], ALU.mult, [sg.b, pb_.b], [tmpm.b])
                        if nn < 3:
                            fw.tt("pool", acc.t[:, 0:n], acc.t[:, 0:n], tmpm.t[:, 0:n], ALU.add, [acc.b, tmpm.b], [acc.b])
                        else:
                            fw.tt("pool", mixT.t[:, db, 0:n], acc.t[:, 0:n], tmpm.t[:, 0:n], ALU.add, [acc.b, tmpm.b], [mixT.b])
            for i in tiles:
                xt = xts[i % 2]
                src = C.h0 if l == 0 else C.hbuf
                fw.dma("sp", xt.t[:, :], src[i * 128:(i + 1) * 128, :], reads=([] if l == 0 else [C.hb[i]]), writes=[xt.b])
                sub = slice(i * 128 - t0, i * 128 - t0 + 128)
                for dh in range(2):
                    pst = ps[4 + dh]
                    for db in range(8):
                        fw.mm(pst.t[:, :], mixT.t[:, db, sub], WO.t[:, db, dh * 512:(dh + 1) * 512], db == 0, db == 7, [mixT.b, WO.b], [pst.b])
                rms_epilogue(fw, C, ps[4], ps[5], nw1, xt, wk2)
                fw.dma("sp", C.hbuf[i * 128:(i + 1) * 128, :], xt.t[:, :], reads=[xt.b], writes=[C.hb[i]])
                norm_rows_to_T(fw, C, ph, xt.t[:, :], xt.b, nw2, C.uT, C.uTb, i, wk)
        fw.barrier()


def phase_mlp(fw, C, l, last):
    ps = C.ps
    with ExitStack() as ph:
        WU = fw.tile(ph, "WU", [128, 8, 4096], BF16)
        load_w(fw, C.w_up[l], WU, 0, 4096, 8)
        WD = fw.tile(ph, "WD", [128, 32, 1024], BF16)
        load_w(fw, C.w_down[l], WD, 0, 1024, 32)
        nw = fw.tile(ph, "nw3", [128, D], F32)
        fw.dma("sp", nw.t[:, :], C.norm_post_mlp[l].partition_broadcast(128), writes=[nw.b])
        hT = fw.tile(ph, "hT", [128, 32, 256], BF16)
        rl = [fw.tile(ph, "rl", [128, 512], BF16) for _ in range(2)]
        xts = [fw.tile(ph, "pxt", [128, D], F32)] * 2
        ptmp = fw.tile(ph, "ptmp", [128, D], F32)
        wk2 = (ptmp, fw.tile(ph, "ss", [128, 4], F32), ptmp)
        k = 0
        for (t0, n) in MGS:
            tiles = list(range(t0 // 128, (t0 + n) // 128))
            for fp in range(16):
                pst = ps[k % 4]
                r = rl[k % 2]
                k += 1
                for j in range(2):
                    ffc = 2 * fp + j
                    for kc in range(8):
                        fw.mm(pst.t[:, j * 256:j * 256 + n], WU.t[:, kc, ffc * 128:(ffc + 1) * 128], C.uT.t[:, kc, t0:t0 + n], kc == 0, kc == 7,
                              [WU.b] + [C.uTb[i] for i in tiles], [pst.b])
                pv = pst.t[:, :].rearrange("p (j t) -> p j t", j=2)[:, :, 0:n]
                rv = r.t[:, :].rearrange("p (j t) -> p j t", j=2)[:, :, 0:n]
                fw.act(rv, pv, AF.Relu, [pst.b], [r.b])
                fw.tt("pool" if fp % 2 else "dve", hT.t[:, 2 * fp:2 * fp + 2, 0:n], rv, rv, ALU.mult, [r.b], [hT.b])
            for i in tiles:
                xt = xts[i % 2]
                fw.dma("sp", xt.t[:, :], C.hbuf[i * 128:(i + 1) * 128, :], reads=[C.hb[i]], writes=[xt.b])
                sub = slice(i * 128 - t0, i * 128 - t0 + 128)
                for dh in range(2):
                    pst = ps[4 + dh]
                    for ffc in range(32):
                        fw.mm(pst.t[:, :], hT.t[:, ffc, sub], WD.t[:, ffc, dh * 512:(dh + 1) * 512], ffc == 0, ffc == 31, [hT.b, WD.b], [pst.b])
                rms_epilogue(fw, C, ps[4], ps[5], nw, xt, wk2)
                if not last:
                    fw.dma("sp", C.hbuf[i * 128:(i + 1) * 128, :], xt.t[:, :], reads=[xt.b], writes=[C.hb[i]])
                else:
                    if C.debug:
                        fw.dma("sp", C.hbuf[i * 128:(i + 1) * 128, :], xt.t[:, :], reads=[xt.b], writes=[C.hb[i]])
                    lo = max(i * 128, 16)
                    hi = min((i + 1) * 128, T)
                    if hi > lo:
                        fw.dma("sp", C.out[lo - 16:hi - 16, :], xt.t[lo - i * 128:hi - i * 128, :], reads=[xt.b], writes=[C.outb])
        fw.barrier()


def build(debug=False, n_layers=DEPTH, phases=None):
    nc = bass.Bass("TRN2", target_bir_lowering=False)
    C = Ctx()
    C.debug = debug

    def din(name, shape):
        return nc.dram_tensor(name, list(shape), F32, kind="ExternalInput").ap()

    C.h0 = din("h0", [TP, D])
    C.consts_d = din("consts", [128, CO_TOTAL[0]])
    C.w_in = din("w_in", [4, D, N_IN])
    C.w_branch = din("w_branch", [4, 4, 512, D])
    C.w_out = din("w_out", [4, D, D])
    C.w_up = din("w_up", [4, D, 4 * D])
    C.w_down = din("w_down", [4, 4 * D, D])
    C.s5_glu_w = din("s5_glu_w", [4, 512, 1024])
    for nm in ("norm_pre_mix", "norm_post_mix", "norm_pre_mlp", "norm_post_mlp"):
        setattr(C, nm, din(nm, [4, D]))
    for nm in ("ret_gn_w", "ssd_norm_w", "hgrn_norm_w", "ssd_dsk"):
        setattr(C, nm, din(nm, [4, 512]))
    C.ssd_dt_bias = din("ssd_dt_bias", [4, 8])
    C.ssd_a_log = din("ssd_a_log", [4, 8])
    C.lbT = din("lbT", [128, 4, 4])
    C.ssd_cw = din("ssd_cw", [4, 128, 8, 4])
    C.ssd_cb = din("ssd_cb", [4, 128, 8])
    C.s5_small = din("s5_small", [4, 128, 48])
    for nm in ("s5_bre", "s5_bim", "s5_cre", "s5_cim"):
        setattr(C, nm, din(nm, [4, 128, 16, 128]))
    C.s5_dT = din("s5_dT", [4, 128, 4])
    C.s5_gbT = din("s5_gbT", [4, 128, 8])
    C.out = nc.dram_tensor("out", [2048, D], F32, kind="ExternalOutput").ap()
    sk = "ExternalOutput" if debug else "Internal"
    C.hbuf = nc.dram_tensor("hbuf", [TP, D], F32, kind=sk).ap()
    C.YT = nc.dram_tensor("YT", [16, 128, TP], BF16, kind=sk).ap()
    if debug:
        C.dbg5 = nc.dram_tensor("dbg5", [128, 2784 + 137 * 48], F32, kind="ExternalOutput").ap()
    C.hb = [Buf("hb%d" % i) for i in range(NT)]
    C.YTb = [[Buf("yt%d_%d" % (m, i)) for i in range(NT)] for m in range(4)]
    C.outb = Buf("out")
    with ExitStack() as es:
        fw = FW(nc, es)
        C.ps = [TL(es.enter_context(nc.psum_tensor("ps%d" % j, [128, 512], F32)), "ps%d" % j) for j in range(6)]
        C.pb = [TL(es.enter_context(nc.psum_tensor("pb%d" % j, [128, 1024], BF16)), "pb%d" % j) for j in range(2)]
        C.consts = fw.tile(es, "consts", [128, CO_TOTAL[0]], F32)
        fw.dma("sp", C.consts.t[:, :], C.consts_d[:, :], writes=[C.consts.b])
        C.identb = fw.tile(es, "identb", [128, 128], BF16)
        C.identf = fw.tile(es, "identf", [128, 128], F32)
        fw.cp("dve", C.identb.t[:, :], cslice(C, "ident"), [C.consts.b], [C.identb.b])
        fw.cp("dve", C.identf.t[:, :], cslice(C, "ident"), [C.consts.b], [C.identf.b])
        C.uT = fw.tile(es, "uT", [128, 8, TP], BF16)
        C.uTb = [Buf("uT%d" % i) for i in range(NT)]
        C.lb_all = fw.tile(es, "lb_all", [128, 4, 4], F32)
        C.oml_all = fw.tile(es, "oml_all", [128, 4, 4], F32)
        lbe = fw.tile(es, "lbe", [128, 4, 4], F32)
        lsum = fw.tile(es, "lsum", [128, 4], F32)
        R = [lbe.b, lsum.b, C.lb_all.b, C.oml_all.b]
        fw.dma("sp", lbe.t[:, :, :], C.lbT[:, :, :], writes=[lbe.b])
        fw.act(lbe.t[:, :, :], lbe.t[:, :, :], AF.Exp, R, R)
        fw.tt("dve", lsum.t[:, :], lbe.t[:, 0, :], lbe.t[:, 1, :], ALU.add, R, R)
        fw.tt("dve", lsum.t[:, :], lsum.t[:, :], lbe.t[:, 2, :], ALU.add, R, R)
        fw.tt("dve", lsum.t[:, :], lsum.t[:, :], lbe.t[:, 3, :], ALU.add, R, R)
        fw.op("dve", lambda g: g.reciprocal(lsum.t[:, :], lsum.t[:, :]), R, R)
        fw.memset("dve", C.lb_all.t[:, 0, :], 0.0, R)
        for ll in range(1, 4):
            fw.tt("dve", lbe.t[:, ll, :], lbe.t[:, ll, :], lsum.t[:, :], ALU.mult, R, R)
            fw.tt("dve", C.lb_all.t[:, ll, :], C.lb_all.t[:, ll - 1, :], lbe.t[:, ll, :], ALU.add, R, R)
        fw.ts("dve", C.oml_all.t[:, :, :], C.lb_all.t[:, :, :], -1.0, 1.0, ALU.mult, ALU.add, R, R)
        fw.barrier()
        allp = ("norm", "ret", "s5", "ssd", "hg", "merge", "mlp")
        for l in range(n_layers):
            for pn in allp:
                if phases is not None and pn not in phases:
                    continue
                if pn == "norm":
                    phase_norm1(fw, C, l)
                elif pn == "ret":
                    phase_ret(fw, C, l)
                elif pn == "s5":
                    phase_s5(fw, C, l)
                elif pn == "ssd":
                    phase_ssd(fw, C, l)
                elif pn == "hg":
                    phase_hg(fw, C, l)
                elif pn == "merge":
                    phase_merge(fw, C, l)
                elif pn == "mlp":
                    phase_mlp(fw, C, l, last=(l == n_layers - 1))
        fw.barrier()
        C.n_inst, C.n_wait = fw.n_inst, fw.n_wait
    return nc, C


CO_TOTAL = [0]
_CONSTS = None


def get_consts():
    global _CONSTS
    if _CONSTS is None:
        _CONSTS = host_consts()
        CO_TOTAL[0] = _CONSTS.shape[1]
    return _CONSTS


def make_in_maps(inp):
    consts = get_consts()
    P = host_params(inp)
    f = lambda a: np.ascontiguousarray(np.asarray(a, np.float32))
    shared = {"consts": consts}
    for nm in ("w_in", "w_branch", "w_out", "w_up", "w_down", "s5_glu_w", "norm_pre_mix", "norm_post_mix", "norm_pre_mlp",
               "norm_post_mlp", "ret_gn_w", "ssd_norm_w", "hgrn_norm_w", "ssd_dt_bias", "ssd_a_log"):
        shared[nm] = f(inp[nm])
    for nm in ("lbT", "ssd_cw", "ssd_cb", "ssd_dsk", "s5_small", "s5_bre", "s5_bim", "s5_cre", "s5_cim", "s5_dT", "s5_gbT"):
        shared[nm] = P[nm]
    x = np.asarray(inp["x"], np.float32)
    meta = np.asarray(inp["meta_tokens"], np.float32)
    maps = []
    for b in range(x.shape[0]):
        h0 = np.zeros((TP, D), np.float32)
        h0[0:16] = meta
        h0[16:T] = x[b]
        m = dict(shared)
        m["h0"] = h0
        maps.append(m)
    return maps


_NC = None


def kernel(**inputs):
    global _NC
    maps = make_in_maps(inputs)
    if _NC is None:
        _NC = build()[0]
    res = run_bass_kernel_spmd(_NC, maps, core_ids=list(range(len(maps))))
    return np.stack([np.asarray(r["out"], np.float32) for r in res.results], axis=0)
```

```python
import math
import os
import numpy as np
from contextlib import ExitStack
import concourse.bass as bass
import concourse.mybir as mybir
from concourse.bass_utils import run_bass_kernel_spmd

F32 = mybir.dt.float32
BF16 = mybir.dt.bfloat16
ALU = mybir.AluOpType
AF = mybir.ActivationFunctionType
AX = mybir.AxisListType

DEPTH = 4
D = 1024
T = 2064
NT = 17
TP = NT * 128
EPS = 1e-6
N_IN = 9736
C_RET, C_S5, C_SSD, C_HG, C_GATE = 0, 1536, 2048, 3592, 5640
GAMMA = [1.0 - 2.0 ** (-5.0 - h) for h in range(4)]


class Buf:
    __slots__ = ("name", "w", "r")

    def __init__(self, name=""):
        self.name = name
        self.w = None
        self.r = []


class TL:
    def __init__(self, t, name):
        self.t = t
        self.b = Buf(name)


LAZY_PE_SIGNAL = os.environ.get("LAZY_PE", "0") == "1"


class VW:
    def __init__(self, t, b):
        self.t = t
        self.b = b


class FW:
    N_DMA_SEMS = 16

    def __init__(self, nc, es):
        self.nc = nc
        self.es = es
        self.eng = {"pe": nc.tensor, "dve": nc.vector, "act": nc.scalar, "pool": nc.gpsimd, "sp": nc.sync}
        self.sems = {}
        self.cnt = {}
        for e in ("pe", "dve", "act", "pool"):
            self.sems[e] = es.enter_context(nc.semaphore("s_" + e))
            self.cnt[e] = 0
        self.dma_keys = {}
        self.dma_rr = {}
        for q in ("sp", "pool"):
            ks = []
            for i in range(self.N_DMA_SEMS):
                k = "d_%s_%d" % (q, i)
                self.sems[k] = es.enter_context(nc.semaphore(k))
                self.cnt[k] = 0
                ks.append(k)
            self.dma_keys[q] = ks
            self.dma_rr[q] = 0
        self.known = {e: {} for e in self.eng}
        self.n_inst = 0
        self.n_wait = 0
        self.uid = 0
        self.pe_last = None
        self.pe_unsig = False
        self.n_sig = 0

    def tile(self, stack, name, shape, dt):
        self.uid += 1
        nm = "%s_%d" % (name, self.uid)
        return TL(stack.enter_context(self.nc.sbuf_tensor(nm, list(shape), dt)), nm)

    def _wait(self, e, ev):
        if ev is None:
            return
        k, v = ev
        if e == "pe" and k == "pe":
            return
        if k == "pe" and v > self.cnt["pe"]:
            self._flush_pe()
        kn = self.known[e]
        if kn.get(k, 0) >= v:
            return
        self.eng[e].wait_ge(self.sems[k], v)
        kn[k] = v
        self.n_wait += 1

    def _deps(self, e, reads, writes):
        for b in reads:
            self._wait(e, b.w)
        for b in writes:
            self._wait(e, b.w)
            for ev in b.r:
                self._wait(e, ev)

    def _mark(self, ev, reads, writes):
        for b in reads:
            b.r.append(ev)
        for b in writes:
            b.w = ev
            b.r = []

    def _flush_pe(self):
        if self.pe_unsig:
            self.cnt["pe"] += 1
            self.pe_last.then_inc(self.sems["pe"], 1)
            self.pe_unsig = False
            self.n_sig += 1

    def op(self, e, fn, reads=(), writes=()):
        self._deps(e, reads, writes)
        ins = fn(self.eng[e])
        if e == "pe" and LAZY_PE_SIGNAL:
            self.pe_last = ins
            self.pe_unsig = True
            ev = (e, self.cnt[e] + 1)
        else:
            self.cnt[e] += 1
            ins.then_inc(self.sems[e], 1)
            ev = (e, self.cnt[e])
        self._mark(ev, reads, writes)
        self.n_inst += 1
        return ev

    def dma(self, q, out, in_, reads=(), writes=(), **kw):
        self._deps(q, reads, writes)
        ks = self.dma_keys[q]
        k = ks[self.dma_rr[q] % len(ks)]
        self.dma_rr[q] += 1
        if self.cnt[k] > 0:
            self._wait(q, (k, self.cnt[k]))
        ins = self.eng[q].dma_start(out=out, in_=in_, **kw)
        self.cnt[k] += 16
        ins.then_inc(self.sems[k], 16)
        ev = (k, self.cnt[k])
        self._mark(ev, reads, writes)
        self.n_inst += 1
        return ev

    def barrier(self, engines=("pe", "dve", "act", "pool", "sp")):
        self._flush_pe()
        for e in engines:
            for k, v in self.cnt.items():
                if v > 0:
                    self._wait(e, (k, v))

    def tt(self, e, out, a, b, op, R, W):
        return self.op(e, lambda g: g.tensor_tensor(out, a, b, op), R, W)

    def ts(self, e, out, a, s1, s2, op0, op1, R, W):
        if s2 is None:
            return self.op(e, lambda g: g.tensor_scalar(out, a, s1, None, op0=op0), R, W)
        return self.op(e, lambda g: g.tensor_scalar(out, a, s1, s2, op0=op0, op1=op1), R, W)

    def stt(self, e, out, in0, sc, in1, op0, op1, R, W):
        e = "dve"
        return self.op(e, lambda g: g.scalar_tensor_tensor(out, in0, sc, in1, op0=op0, op1=op1), R, W)

    def cp(self, e, out, in_, R, W):
        if e == "act":
            return self.op(e, lambda g: g.copy(out, in_), R, W)
        return self.op(e, lambda g: g.tensor_copy(out, in_), R, W)

    def act(self, out, in_, func, R, W, bias=None, scale=None, accum=None):
        kw = {}
        if bias is not None:
            kw["bias"] = bias
        if scale is not None:
            kw["scale"] = scale
        if accum is not None:
            kw["accum_out"] = accum
        return self.op("act", lambda g: g.activation(out, in_, func, **kw), R, W)

    def mm(self, out, lhsT, rhs, start, stop, R, W):
        return self.op("pe", lambda g: g.matmul(out, lhsT, rhs, start=start, stop=stop), R, W)

    def tr(self, out, in_, ident, R, W):
        return self.op("pe", lambda g: g.transpose(out, in_, ident), R, W)

    def memset(self, e, ap, val, W):
        return self.op(e, lambda g: g.memset(ap, val), (), W)


CO = {}


def _pack(items):
    off = 0
    cols = []
    for name, arr in items:
        arr = np.asarray(arr, np.float32).reshape(128, -1)
        CO[name] = (off, arr.shape[1])
        off += arr.shape[1]
        cols.append(arr)
    return np.ascontiguousarray(np.concatenate(cols, axis=1))


def host_consts():
    s = np.arange(128)[:, None]
    t = np.arange(128)[None, :]
    ident = (s == t).astype(np.float32)
    triu = (s <= t).astype(np.float32)
    negm = np.where(s <= t, 0.0, -30000.0).astype(np.float32)
    ones = np.ones((128, 128), np.float32)
    retdt = np.zeros((128, 4, 128), np.float64)
    qdec = np.zeros((128, 4, 64), np.float64)
    kdec = np.zeros((128, 4, 64), np.float64)
    for h in range(4):
        g = GAMMA[h]
        retdt[:, h, :] = np.where(s <= t, 0.125 * g ** np.maximum(t - s, 0), 0.0)
        qdec[:, h, :] = (g ** (np.arange(128) + 1.0))[:, None]
        kdec[:, h, :] = (0.125 * g ** (127.0 - np.arange(128)))[:, None]
    half = 32
    inv_freq = (10000.0 ** (-np.arange(half, dtype=np.float32) / half)).astype(np.float32)
    pos = (np.arange(NT)[None, :] * 128 + np.arange(128)[:, None]).astype(np.float32)
    ang = pos[:, :, None] * inv_freq[None, None, :]
    cos = np.cos(ang).astype(np.float32)
    sin = np.sin(ang).astype(np.float32)
    halfpi = np.full((128, 1), math.pi / 2, np.float32)
    mvals = np.array(list(range(17)) + [0.5], np.float32)
    mtab = np.broadcast_to(mvals[None, :, None], (128, 18, 16))
    return _pack([("mtab", mtab), ("ident", ident), ("triu", triu), ("negm", negm), ("ones", ones), ("retdt", retdt),
                  ("qdec", qdec), ("kdec", kdec), ("cos", cos), ("sin", sin), ("halfpi", halfpi)])


def host_params(inp):
    P = {}
    f = lambda a: np.ascontiguousarray(np.asarray(a, np.float32))
    P["lbT"] = f(np.asarray(inp["hgrn_lb"]).reshape(4, 4, 128).transpose(2, 0, 1))
    P["ssd_cw"] = f(np.asarray(inp["ssd_conv_w"]).reshape(4, 4, 8, 128).transpose(0, 3, 2, 1))
    P["ssd_cb"] = f(np.asarray(inp["ssd_conv_b"]).reshape(4, 8, 128).transpose(0, 2, 1))
    P["ssd_dsk"] = f(np.repeat(np.asarray(inp["ssd_d"]), 64, axis=1))
    def pl_small(a):
        a = np.asarray(a).reshape(4, 16, 2, 64)
        return a.transpose(0, 2, 3, 1).reshape(4, 128, 16)
    ls = np.broadcast_to(np.asarray(inp["s5_log_step"])[:, :, None], (4, 32, 64))
    P["s5_small"] = f(np.concatenate([pl_small(inp["s5_lam_re"]), pl_small(inp["s5_lam_im"]), pl_small(ls)], axis=2))
    def pl_b(b):
        b = np.asarray(b)
        out = np.zeros((4, 2, 64, 16, 8, 16), np.float32)
        for g in range(32):
            out[:, g % 2, :, g // 2, g % 8, :] = b[:, g]
        return out.reshape(4, 128, 16, 128)
    def pl_c(c):
        c = np.asarray(c)
        out = np.zeros((4, 2, 64, 16, 8, 16), np.float32)
        for g in range(32):
            out[:, g % 2, :, g // 2, g % 8, :] = c[:, g].transpose(0, 2, 1)
        return out.reshape(4, 128, 16, 128)
    P["s5_bre"] = pl_b(inp["s5_b_re"])
    P["s5_bim"] = pl_b(inp["s5_b_im"])
    P["s5_cre"] = pl_c(inp["s5_c_re"])
    P["s5_cim"] = pl_c(inp["s5_c_im"])
    P["s5_dT"] = f(np.asarray(inp["s5_d"]).reshape(4, 4, 128).transpose(0, 2, 1))
    P["s5_gbT"] = f(np.asarray(inp["s5_glu_b"]).reshape(4, 8, 128).transpose(0, 2, 1))
    return P


class Ctx:
    pass


def cslice(C, name, a=None, b=None):
    off, n = CO[name]
    if a is None:
        return C.consts.t[:, off:off + n]
    return C.consts.t[:, off + a:off + b]


def load_w(fw, src2d, dst, c0, c1, kcs, R=()):
    v = src2d.rearrange("(kc p) n -> p kc n", p=128)
    step = int(os.environ.get("LW_CSTEP", "2048"))
    kstep = min(kcs, int(os.environ.get("LW_KSTEP", "8")))
    for k0 in range(0, kcs, kstep):
        for a in range(c0, c1, step):
            b = min(a + step, c1)
            fw.dma("pool", dst.t[:, k0:k0 + kstep, a - c0:b - c0], v[:, k0:k0 + kstep, a:b], reads=R, writes=[dst.b])


def rstd_from_ss(fw, C, ss_ap, out_ap, n, R, W):
    fw.ts("dve", out_ap, ss_ap, 1.0 / n, EPS, ALU.mult, ALU.add, R, W)
    fw.act(out_ap, out_ap, AF.Sqrt, W, W)
    fw.op("dve", lambda g: g.reciprocal(out_ap, out_ap), W, W)


def norm_rows_to_T(fw, C, ph, x_ap, xb, nw, dstT, dst_bufs, i, wk):
    junk, ss, ub = wk
    fw.act(junk.t[:, :], x_ap, AF.Square, [xb], [junk.b, ss.b], accum=ss.t[:, 0:1])
    rstd_from_ss(fw, C, ss.t[:, 0:1], ss.t[:, 1:2], 1024.0, [ss.b], [ss.b])
    fw.stt("dve", ub.t[:, :], x_ap, ss.t[:, 1:2], nw.t[:, :], ALU.mult, ALU.mult, [xb, ss.b, nw.b], [ub.b])
    pb = C.pb[i % 2]
    for kc in range(8):
        fw.tr(pb.t[:, kc * 128:(kc + 1) * 128], ub.t[:, kc * 128:(kc + 1) * 128], C.identb.t[:, :], [ub.b, C.identb.b], [pb.b])
    fw.cp("act" if i % 2 else "dve", dstT.t[:, :, i * 128:(i + 1) * 128],
          pb.t[:, :].rearrange("p (k t) -> p k t", k=8), [pb.b], [dst_bufs[i]])


def phase_norm1(fw, C, l):
    with ExitStack() as ph:
        nw = fw.tile(ph, "nw", [128, D], F32)
        fw.dma("sp", nw.t[:, :], C.norm_pre_mix[l].partition_broadcast(128), writes=[nw.b])
        xts = [fw.tile(ph, "xt", [128, D], F32) for _ in range(2)]
        wk = (fw.tile(ph, "junk", [128, D], BF16), fw.tile(ph, "ss", [128, 2], F32), fw.tile(ph, "ub", [128, D], BF16))
        for i in range(NT):
            xt = xts[i % 2]
            src = C.h0 if l == 0 else C.hbuf
            fw.dma("sp", xt.t[:, :], src[i * 128:(i + 1) * 128, :], reads=([] if l == 0 else [C.hb[i]]), writes=[xt.b])
            norm_rows_to_T(fw, C, ph, xt.t[:, :], xt.b, nw, C.uT, C.uTb, i, wk)
        fw.barrier()


ILV_OFF = set(os.environ.get("NOILV", "ret,ssd").split(","))
CUR_PHASE = [""]


def interleave(gb, ga):
    if CUR_PHASE[0] in ILV_OFF:
        for g in (ga, gb):
            if g is not None:
                for _ in g:
                    pass
        return
    done_a = ga is None
    done_b = False
    while not (done_a and done_b):
        if not done_b:
            try:
                next(gb)
            except StopIteration:
                done_b = True
        if not done_a:
            try:
                next(ga)
            except StopIteration:
                done_a = True


def interleave_n(gens):
    gens = list(gens)
    if CUR_PHASE[0] in ILV_OFF:
        for g in reversed(gens):
            for _ in g:
                pass
        return
    while gens:
        for g in list(gens):
            try:
                next(g)
            except StopIteration:
                gens.remove(g)


def chain_gens(*gens):
    for g in gens:
        if g is not None:
            yield from g


def emit_yT(fw, C, yo, yT, mix, i):
    pb = C.pb[1]
    for cb in range(4):
        fw.tr(pb.t[:, cb * 128:(cb + 1) * 128], yo.t[:, cb * 128:(cb + 1) * 128], C.identb.t[:, :], [yo.b, C.identb.b], [pb.b])
    fw.cp("act", yT.t[:, :, :], pb.t[:, 0:512].rearrange("p (c t) -> p c t", c=4), [pb.b], [yT.b])
    fw.dma("sp", C.YT[mix * 4:(mix + 1) * 4, :, i * 128:(i + 1) * 128].rearrange("c p t -> p c t"), yT.t[:, :, :],
           reads=[yT.b], writes=[C.YTb[mix][i]])


def tok_mm(fw, C, ps, Wt, c0, n, i, ncols=None):
    for kc in range(8):
        fw.mm(ps.t[:, 0:n], C.uT.t[:, kc, i * 128:(i + 1) * 128], Wt.t[:, kc, c0:c0 + n], kc == 0, kc == 7,
              [C.uTb[i], Wt.b], [ps.b])


def phase_ret(fw, C, l):
    CUR_PHASE[0] = "ret"
    with ExitStack() as ph:
        WR = fw.tile(ph, "WR", [128, 8, 1536], BF16)
        load_w(fw, C.w_in[l], WR, C_RET, C_RET + 1536, 8)
        gnw = fw.tile(ph, "gnw", [128, 512], F32)
        fw.dma("sp", gnw.t[:, :], C.ret_gn_w[l].partition_broadcast(128), writes=[gnw.b])
        S = fw.tile(ph, "rS", [64, 4, 128], F32)
        Sb = fw.tile(ph, "rSb", [64, 4, 128], BF16)
        fw.memset("dve", S.t[:, :, :], 0.0, [S.b])
        fw.memset("dve", Sb.t[:, :, :], 0.0, [Sb.b])
        qkr_2 = [fw.tile(ph, "qkr", [128, 8, 64], F32) for _ in range(2)]
        t1_2 = [fw.tile(ph, "t1", [128, 8, 32], F32) for _ in range(2)]
        t2_2 = [fw.tile(ph, "t2", [128, 8, 32], F32) for _ in range(2)]
        qb_2 = [fw.tile(ph, "qb", [128, 256], BF16) for _ in range(2)]
        qdb_2 = [fw.tile(ph, "qdb", [128, 256], BF16) for _ in range(2)]
        kb_2 = [fw.tile(ph, "kb", [128, 256], BF16) for _ in range(2)]
        khb_2 = [fw.tile(ph, "khb", [128, 256], BF16) for _ in range(2)]
        qkT_2 = [fw.tile(ph, "qkT", [64, 12, 128], BF16) for _ in range(2)]
        vb_2 = [fw.tile(ph, "vb", [128, 512], BF16) for _ in range(2)]
        sg_2 = [fw.tile(ph, "sg", [128, 512], F32) for _ in range(2)]
        PT_2 = [fw.tile(ph, "PT", [128, 512], BF16) for _ in range(2)]
        ysq_2 = [fw.tile(ph, "ysq", [128, 512], F32) for _ in range(2)]
        st_2 = [fw.tile(ph, "st", [128, 16], F32) for _ in range(2)]
        yn_2 = [fw.tile(ph, "yn", [128, 512], F32) for _ in range(2)]
        yo_2 = [fw.tile(ph, "yo", [128, 512], BF16) for _ in range(2)]
        yT_2 = [fw.tile(ph, "yT", [128, 4, 128], BF16) for _ in range(2)]
        ps_qk, ps_v, ps_g, ps_s, ps_y, ps_st = C.ps[0:6]
        cb_ = C.consts.b
        def stA(i):
                qkr, t1, t2, qb, qdb, kb, khb, qkT, vb, sg, PT, ysq, st, yn, yo, yT = qkr_2[i % 2], t1_2[i % 2], t2_2[i % 2], qb_2[i % 2], qdb_2[i % 2], kb_2[i % 2], khb_2[i % 2], qkT_2[i % 2], vb_2[i % 2], sg_2[i % 2], PT_2[i % 2], ysq_2[i % 2], st_2[i % 2], yn_2[i % 2], yo_2[i % 2], yT_2[i % 2]
                tok_mm(fw, C, ps_qk, WR, 0, 512, i)
                yield
                tok_mm(fw, C, ps_v, WR, 512, 512, i)
                yield
                tok_mm(fw, C, ps_g, WR, 1024, 512, i)
                yield
                qv = ps_qk.t[:, :].rearrange("p (h d) -> p h d", d=64)
                x1, x2 = qv[:, :, 0:32], qv[:, :, 32:64]
                co, cn = CO["cos"][0], CO["sin"][0]
                cosb = C.consts.t[:, co + i * 32:co + (i + 1) * 32].unsqueeze(1).to_broadcast([128, 8, 32])
                sinb = C.consts.t[:, cn + i * 32:cn + (i + 1) * 32].unsqueeze(1).to_broadcast([128, 8, 32])
                fw.tt("dve", t1.t[:, :, :], x1, cosb, ALU.mult, [ps_qk.b, cb_], [t1.b])
                fw.tt("dve", t2.t[:, :, :], x2, sinb, ALU.mult, [ps_qk.b, cb_], [t2.b])
                fw.tt("pool", qkr.t[:, :, 0:32], t1.t[:, :, :], t2.t[:, :, :], ALU.subtract, [t1.b, t2.b], [qkr.b])
                fw.tt("dve", t1.t[:, :, :], x1, sinb, ALU.mult, [ps_qk.b, cb_], [t1.b])
                fw.tt("dve", t2.t[:, :, :], x2, cosb, ALU.mult, [ps_qk.b, cb_], [t2.b])
                fw.tt("pool", qkr.t[:, :, 32:64], t1.t[:, :, :], t2.t[:, :, :], ALU.add, [t1.b, t2.b], [qkr.b])
                qf = qkr.t[:, 0:4, :].rearrange("p h d -> p (h d)")
                kf = qkr.t[:, 4:8, :].rearrange("p h d -> p (h d)")
                fw.cp("act", qb.t[:, :], qf, [qkr.b], [qb.b])
                fw.tt("pool", qdb.t[:, :], qf, cslice(C, "qdec"), ALU.mult, [qkr.b, cb_], [qdb.b])
                fw.cp("act", kb.t[:, :], kf, [qkr.b], [kb.b])
                fw.tt("pool", khb.t[:, :], kf, cslice(C, "kdec"), ALU.mult, [qkr.b, cb_], [khb.b])
                fw.cp("act", vb.t[:, :], ps_v.t[:, :], [ps_v.b], [vb.b])
                fw.act(sg.t[:, :], ps_g.t[:, :], AF.Silu, [ps_g.b], [sg.b])
        def stB(i):
                qkr, t1, t2, qb, qdb, kb, khb, qkT, vb, sg, PT, ysq, st, yn, yo, yT = qkr_2[i % 2], t1_2[i % 2], t2_2[i % 2], qb_2[i % 2], qdb_2[i % 2], kb_2[i % 2], khb_2[i % 2], qkT_2[i % 2], vb_2[i % 2], sg_2[i % 2], PT_2[i % 2], ysq_2[i % 2], st_2[i % 2], yn_2[i % 2], yo_2[i % 2], yT_2[i % 2]
                pb0, pb1 = C.pb
                for h in range(4):
                    fw.tr(pb0.t[0:64, h * 128:(h + 1) * 128], qb.t[:, h * 64:(h + 1) * 64], C.identb.t[:, :], [qb.b, C.identb.b], [pb0.b])
                    fw.tr(pb0.t[0:64, (4 + h) * 128:(5 + h) * 128], qdb.t[:, h * 64:(h + 1) * 64], C.identb.t[:, :], [qdb.b, C.identb.b], [pb0.b])
                    fw.tr(pb1.t[0:64, h * 128:(h + 1) * 128], kb.t[:, h * 64:(h + 1) * 64], C.identb.t[:, :], [kb.b, C.identb.b], [pb1.b])
                yield
                fw.cp("dve", qkT.t[:, 0:8, :], pb0.t[0:64, :].rearrange("p (j t) -> p j t", j=8), [pb0.b], [qkT.b])
                fw.cp("act", qkT.t[:, 8:12, :], pb1.t[0:64, 0:512].rearrange("p (j t) -> p j t", j=4), [pb1.b], [qkT.b])
                for h in range(4):
                    fw.mm(ps_s.t[:, h * 128:(h + 1) * 128], qkT.t[:, 8 + h, :], qkT.t[:, h, :], True, True, [qkT.b], [ps_s.b])
                yield
                fw.tt("dve", PT.t[:, :], ps_s.t[:, :], cslice(C, "retdt"), ALU.mult, [ps_s.b, cb_], [PT.b])
                for h in range(4):
                    hs = slice(h * 128, (h + 1) * 128)
                    fw.mm(ps_y.t[:, hs], PT.t[:, hs], vb.t[:, hs], True, False, [PT.b, vb.b], [ps_y.b])
                    fw.mm(ps_y.t[:, hs], qkT.t[:, 4 + h, :], Sb.t[:, h, :], False, True, [qkT.b, Sb.b], [ps_y.b])
                yield
                for h in range(4):
                    hs = slice(h * 128, (h + 1) * 128)
                    fw.mm(ps_st.t[0:64, hs], khb.t[:, h * 64:(h + 1) * 64], vb.t[:, hs], True, True, [khb.b, vb.b], [ps_st.b])
                yield
                for h in range(4):
                    hs = slice(h * 128, (h + 1) * 128)
                    fw.stt("dve", S.t[:, h, :], S.t[:, h, :], float(GAMMA[h] ** 128), ps_st.t[0:64, hs], ALU.mult, ALU.add,
                           [S.b, ps_st.b], [S.b])
                fw.cp("pool", Sb.t[:, :, :], S.t[:, :, :], [S.b], [Sb.b])
                yv = ps_y.t[:, :].rearrange("p (h e) -> p h e", h=4)
                fw.op("dve", lambda g: g.reduce_sum(st.t[:, 0:4], yv, axis=AX.X), [ps_y.b], [st.b])
                fw.act(ysq.t[:, :], ps_y.t[:, :], AF.Square, [ps_y.b], [ysq.b])
                fw.op("dve", lambda g: g.reduce_sum(st.t[:, 4:8], ysq.t[:, :].rearrange("p (h e) -> p h e", h=4), axis=AX.X), [ysq.b], [st.b])
                fw.ts("dve", st.t[:, 8:12], st.t[:, 0:4], 1.0 / 128, None, ALU.mult, None, [st.b], [st.b])
                fw.tt("dve", st.t[:, 0:4], st.t[:, 8:12], st.t[:, 8:12], ALU.mult, [st.b], [st.b])
                fw.stt("dve", st.t[:, 12:16], st.t[:, 4:8], 1.0 / 128, st.t[:, 0:4], ALU.mult, ALU.subtract, [st.b], [st.b])
                fw.ts("dve", st.t[:, 12:16], st.t[:, 12:16], EPS, None, ALU.add, None, [st.b], [st.b])
                fw.act(st.t[:, 12:16], st.t[:, 12:16], AF.Sqrt, [st.b], [st.b])
                fw.op("dve", lambda g: g.reciprocal(st.t[:, 12:16], st.t[:, 12:16]), [st.b], [st.b])
                for h in range(4):
                    hs = slice(h * 128, (h + 1) * 128)
                    fw.ts("dve", yn.t[:, hs], ps_y.t[:, hs], st.t[:, 8 + h:9 + h], st.t[:, 12 + h:13 + h], ALU.subtract, ALU.mult,
                          [ps_y.b, st.b], [yn.b])
                fw.tt("pool", yn.t[:, :], yn.t[:, :], gnw.t[:, :], ALU.mult, [yn.b, gnw.b], [yn.b])
                fw.tt("pool", yo.t[:, :], yn.t[:, :], sg.t[:, :], ALU.mult, [yn.b, sg.b], [yo.b])
                emit_yT(fw, C, yo, yT, 0, i)
                yield
        for _ in stA(0):
            pass
        for i in range(NT):
            interleave(stB(i), stA(i + 1) if i + 1 < NT else None)
        fw.barrier()


def phase_ssd(fw, C, l):
    CUR_PHASE[0] = "ssd"
    with ExitStack() as ph:
        WS = fw.tile(ph, "WS", [128, 8, 1544], BF16)
        load_w(fw, C.w_in[l], WS, C_SSD, C_SSD + 1544, 8)
        cw = fw.tile(ph, "cw", [128, 8, 4], F32)
        cbi = fw.tile(ph, "cbi", [128, 8], F32)
        dtb = fw.tile(ph, "dtb", [128, 8], F32)
        arow = fw.tile(ph, "arow", [128, 8], F32)
        dsk = fw.tile(ph, "dsk", [128, 512], F32)
        nw = fw.tile(ph, "snw", [128, 512], F32)
        fw.dma("sp", cw.t[:, :, :], C.ssd_cw[l], writes=[cw.b])
        fw.dma("sp", cbi.t[:, :], C.ssd_cb[l], writes=[cbi.b])
        fw.dma("sp", dtb.t[:, :], C.ssd_dt_bias[l].partition_broadcast(128), writes=[dtb.b])
        fw.dma("sp", arow.t[:, :], C.ssd_a_log[l].partition_broadcast(128), writes=[arow.b])
        fw.dma("sp", dsk.t[:, :], C.ssd_dsk[l].partition_broadcast(128), writes=[dsk.b])
        fw.dma("sp", nw.t[:, :], C.ssd_norm_w[l].partition_broadcast(128), writes=[nw.b])
        fw.act(arow.t[:, :], arow.t[:, :], AF.Exp, [arow.b], [arow.b])
        fw.ts("dve", arow.t[:, :], arow.t[:, :], -1.0, None, ALU.mult, None, [arow.b], [arow.b])
        S = fw.tile(ph, "sS", [128, 512], F32)
        Sb = fw.tile(ph, "sSb", [128, 512], BF16)
        fw.memset("dve", S.t[:, :], 0.0, [S.b])
        fw.memset("dve", Sb.t[:, :], 0.0, [Sb.b])
        xrawg = fw.tile(ph, "xrawg", [128, 8, 515], F32)
        fw.memset("pool", xrawg.t[:, :, :], 0.0, [xrawg.b])
        xcg = fw.tile(ph, "xcg", [128, 8, 512], F32)
        xcbg_2 = [fw.tile(ph, "xcbg", [128, 8, 512], BF16) for _ in range(2)]
        xs_2 = [fw.tile(ph, "xs", [128, 512], F32) for _ in range(2)]
        Btok_2 = [fw.tile(ph, "Btok", [128, 2, 128], BF16) for _ in range(2)]
        sz_2 = [fw.tile(ph, "sz", [128, 512], F32) for _ in range(2)]
        sm_2 = [fw.tile(ph, "sm", [128, 104], F32) for _ in range(2)]
        rhs8_2 = [fw.tile(ph, "rhs8", [128, 8, 128], F32) for _ in range(2)]
        decT_2 = [fw.tile(ph, "decT", [128, 8, 128], F32) for _ in range(2)]
        PT_2 = [fw.tile(ph, "sPT", [128, 8, 128], BF16) for _ in range(2)]
        xdt_2 = [fw.tile(ph, "xdt", [128, 512], BF16) for _ in range(2)]
        xdtw_2 = [fw.tile(ph, "xdtw", [128, 512], BF16) for _ in range(2)]
        ya_2 = [fw.tile(ph, "ya", [128, 512], F32) for _ in range(2)]
        tmp_2 = [fw.tile(ph, "stmp", [128, 512], F32) for _ in range(2)]
        yo_2 = [fw.tile(ph, "syo", [128, 512], BF16) for _ in range(2)]
        yT_2 = [fw.tile(ph, "syT", [128, 4, 128], BF16) for _ in range(2)]
        ps = C.ps
        pb0, pb1 = C.pb
        cb_ = C.consts.b
        XD, AXc, EX, LN, DT, LA, CUM, NCUM, ECUM, WW, EDEC, DW = [slice(8 * j, 8 * j + 8) for j in range(12)]
        triu = cslice(C, "triu")
        ones = cslice(C, "ones")
        def stG(g):
                t0 = g * 512
                n = min(512, TP - t0)
                tiles = list(range(t0 // 128, (t0 + n) // 128))
                xcbg = xcbg_2[g % 2]
                if g > 0:
                    fw.cp("pool", xrawg.t[:, :, 0:3], xrawg.t[:, :, 512:515], [xrawg.b], [xrawg.b])
                for cb in range(8):
                    pst = ps[cb % 2]
                    for kc in range(8):
                        fw.mm(pst.t[:, 0:n], WS.t[:, kc, 512 + cb * 128:512 + (cb + 1) * 128], C.uT.t[:, kc, t0:t0 + n], kc == 0, kc == 7,
                              [WS.b] + [C.uTb[j] for j in tiles], [pst.b])
                    fw.cp("act" if cb % 2 else "dve", xrawg.t[:, cb, 3:3 + n], pst.t[:, 0:n], [pst.b], [xrawg.b])
                for cb in range(8):
                    fw.ts("dve" if cb % 2 == 0 else "pool", xcg.t[:, cb, 0:n], xrawg.t[:, cb, 3:3 + n], cw.t[:, cb, 3:4], cbi.t[:, cb:cb + 1],
                          ALU.mult, ALU.add, [xrawg.b, cw.b, cbi.b], [xcg.b])
                    for j in (2, 1, 0):
                        fw.stt("dve", xcg.t[:, cb, 0:n], xrawg.t[:, cb, j:j + n], cw.t[:, cb, j:j + 1], xcg.t[:, cb, 0:n], ALU.mult, ALU.add,
                               [xrawg.b, cw.b, xcg.b], [xcg.b])
                fw.act(xcbg.t[:, :, 0:n], xcg.t[:, :, 0:n], AF.Silu, [xcg.b], [xcbg.b])
                yield
        def stA(i):
                xs, Btok, sz, sm, decT, PT, xdt, xdtw, ya, tmp, yo, yT = xs_2[i % 2], Btok_2[i % 2], sz_2[i % 2], sm_2[i % 2], decT_2[i % 2], PT_2[i % 2], xdt_2[i % 2], xdtw_2[i % 2], ya_2[i % 2], tmp_2[i % 2], yo_2[i % 2], yT_2[i % 2]
                xcb = VW(xcbg_2[(i // 4) % 2].t[:, :, (i % 4) * 128:(i % 4 + 1) * 128], xcbg_2[(i // 4) % 2].b)
                tok = slice(i * 128, (i + 1) * 128)
                tok_mm(fw, C, ps[2], WS, 0, 512, i)
                yield
                fw.act(sz.t[:, :], ps[2].t[:, :], AF.Silu, [ps[2].b], [sz.b])
                for kc in range(8):
                    fw.mm(ps[3].t[:, 0:8], C.uT.t[:, kc, tok], WS.t[:, kc, 1536:1544], kc == 0, kc == 7, [C.uTb[i], WS.b], [ps[3].b])
                yield
                smb = [sm.b]
                fw.tt("dve", sm.t[:, XD], ps[3].t[:, 0:8], dtb.t[:, :], ALU.add, [ps[3].b, dtb.b], smb)
                fw.ts("dve", sm.t[:, AXc], sm.t[:, XD], -1.0, None, ALU.mult, None, smb, smb)
                fw.tt("dve", sm.t[:, AXc], sm.t[:, AXc], sm.t[:, XD], ALU.max, smb, smb)
                fw.act(sm.t[:, EX], sm.t[:, AXc], AF.Exp, smb, smb, scale=-1.0)
                fw.ts("dve", sm.t[:, EX], sm.t[:, EX], 1.0, None, ALU.add, None, smb, smb)
                fw.act(sm.t[:, LN], sm.t[:, EX], AF.Ln, smb, smb)
                fw.stt("dve", sm.t[:, DT], sm.t[:, XD], 0.0, sm.t[:, LN], ALU.max, ALU.add, smb, smb)
                fw.tt("dve", sm.t[:, LA], sm.t[:, DT], arow.t[:, :], ALU.mult, smb + [arow.b], smb)
                fw.mm(ps[3].t[:, 16:24], triu, sm.t[:, LA], True, True, [cb_, sm.b], [ps[3].b])
                yield
                fw.mm(ps[3].t[:, 32:40], ones, sm.t[:, LA], True, True, [cb_, sm.b], [ps[3].b])
                yield
                fw.cp("dve", sm.t[:, CUM], ps[3].t[:, 16:24], [ps[3].b], smb)
                fw.ts("dve", sm.t[:, NCUM], sm.t[:, CUM], -1.0, None, ALU.mult, None, smb, smb)
                fw.act(sm.t[:, ECUM], sm.t[:, CUM], AF.Exp, smb, smb)
                fw.tt("dve", sm.t[:, WW], ps[3].t[:, 32:40], sm.t[:, CUM], ALU.subtract, [ps[3].b] + smb, smb)
                fw.act(sm.t[:, WW], sm.t[:, WW], AF.Exp, smb, smb)
                fw.act(sm.t[:, EDEC], ps[3].t[:, 32:40], AF.Exp, [ps[3].b], smb)
                fw.tt("dve", sm.t[:, DW], sm.t[:, DT], sm.t[:, WW], ALU.mult, smb, smb)
        def stB(i):
                xs, Btok, sz, sm, decT, PT, xdt, xdtw, ya, tmp, yo, yT = xs_2[i % 2], Btok_2[i % 2], sz_2[i % 2], sm_2[i % 2], decT_2[i % 2], PT_2[i % 2], xdt_2[i % 2], xdtw_2[i % 2], ya_2[i % 2], tmp_2[i % 2], yo_2[i % 2], yT_2[i % 2]
                xcb = VW(xcbg_2[(i // 4) % 2].t[:, :, (i % 4) * 128:(i % 4 + 1) * 128], xcbg_2[(i // 4) % 2].b)
                smb = [sm.b]
                pb0, pb1 = C.pb
                rhs8 = rhs8_2[i % 2]
                for cb in range(4):
                    fw.tr(pb0.t[:, cb * 128:(cb + 1) * 128], xcb.t[:, cb, :], C.identb.t[:, :], [xcb.b, C.identb.b], [pb0.b])
                yield
                fw.cp("dve", xs.t[:, :], pb0.t[:, 0:512], [pb0.b], [xs.b])
                for g in range(2):
                    fw.tr(pb1.t[:, g * 128:(g + 1) * 128], xcb.t[:, 4 + g, :], C.identb.t[:, :], [xcb.b, C.identb.b], [pb1.b])
                yield
                fw.cp("act", Btok.t[:, :, :], pb1.t[:, 0:256].rearrange("p (g n) -> p g n", g=2), [pb1.b], [Btok.b])
                fw.tt("dve", rhs8.t[:, :, :], triu.unsqueeze(1).to_broadcast([128, 8, 128]),
                      sm.t[:, LA].unsqueeze(2).to_broadcast([128, 8, 128]), ALU.mult, [cb_, sm.b], [rhs8.b])
                fw.mm(ps[4].t[:, :], ones, rhs8.t[:, 0:4, :].rearrange("p h t -> p (h t)"), True, True, [cb_, rhs8.b], [ps[4].b])
                fw.mm(ps[5].t[:, :], ones, rhs8.t[:, 4:8, :].rearrange("p h t -> p (h t)"), True, True, [cb_, rhs8.b], [ps[5].b])
                for hb in range(2):
                    fw.tt("dve", decT.t[:, 4 * hb:4 * hb + 4, :], ps[4 + hb].t[:, :].rearrange("p (h t) -> p h t", h=4),
                          sm.t[:, 56 + 4 * hb:60 + 4 * hb].unsqueeze(2).to_broadcast([128, 4, 128]), ALU.add, [ps[4 + hb].b, sm.b], [decT.b])
                fw.tt("pool", decT.t[:, :, :], decT.t[:, :, :], cslice(C, "negm").unsqueeze(1).to_broadcast([128, 8, 128]), ALU.add,
                      [decT.b, cb_], [decT.b])
                fw.act(decT.t[:, :, :], decT.t[:, :, :], AF.Exp, [decT.b], [decT.b])
                for g in range(2):
                    fw.mm(ps[4].t[:, g * 128:(g + 1) * 128], xcb.t[:, 4 + g, :], xcb.t[:, 6 + g, :], True, True, [xcb.b], [ps[4].b])
                yield
                for g in range(2):
                    fw.tt("dve", PT.t[:, 4 * g:4 * g + 4, :], decT.t[:, 4 * g:4 * g + 4, :],
                          ps[4].t[:, g * 128:(g + 1) * 128].unsqueeze(1).to_broadcast([128, 4, 128]), ALU.mult, [decT.b, ps[4].b], [PT.b])
                xsv = xs.t[:, :].rearrange("p (h e) -> p h e", h=8)
                fw.tt("pool", xdt.t[:, :].rearrange("p (h e) -> p h e", h=8), xsv, sm.t[:, DT].unsqueeze(2).to_broadcast([128, 8, 64]),
                      ALU.mult, [xs.b, sm.b], [xdt.b])
                fw.tt("pool", xdtw.t[:, :].rearrange("p (h e) -> p h e", h=8), xsv, sm.t[:, DW].unsqueeze(2).to_broadcast([128, 8, 64]),
                      ALU.mult, [xs.b, sm.b], [xdtw.b])
                for h in range(8):
                    fw.mm(ps[5].t[:, h * 64:(h + 1) * 64], PT.t[:, h, :], xdt.t[:, h * 64:(h + 1) * 64], True, True, [PT.b, xdt.b], [ps[5].b])
                yield
                for g in range(2):
                    fw.mm(ps[4].t[:, g * 256:(g + 1) * 256], xcb.t[:, 6 + g, :], Sb.t[:, g * 256:(g + 1) * 256], True, True, [xcb.b, Sb.b], [ps[4].b])
                yield
                fw.tt("dve", ya.t[:, :].rearrange("p (h e) -> p h e", h=8), ps[4].t[:, :].rearrange("p (h e) -> p h e", h=8),
                      sm.t[:, ECUM].unsqueeze(2).to_broadcast([128, 8, 64]), ALU.mult, [ps[4].b, sm.b], [ya.b])
                fw.tt("dve", ya.t[:, :], ya.t[:, :], ps[5].t[:, :], ALU.add, [ya.b, ps[5].b], [ya.b])
                fw.tt("pool", tmp.t[:, :], xs.t[:, :], dsk.t[:, :], ALU.mult, [xs.b, dsk.b], [tmp.b])
                fw.tt("pool", ya.t[:, :], ya.t[:, :], tmp.t[:, :], ALU.add, [ya.b, tmp.b], [ya.b])
                fw.tt("pool", ya.t[:, :], ya.t[:, :], sz.t[:, :], ALU.mult, [ya.b, sz.b], [ya.b])
                for g in range(2):
                    fw.mm(ps[5].t[:, g * 256:(g + 1) * 256], Btok.t[:, g, :], xdtw.t[:, g * 256:(g + 1) * 256], True, True,
                          [Btok.b, xdtw.b], [ps[5].b])
                yield
                Sv = S.t[:, :].rearrange("p (h e) -> p h e", h=8)
                fw.tt("dve", Sv, Sv, sm.t[:, EDEC].unsqueeze(2).to_broadcast([128, 8, 64]), ALU.mult, [S.b, sm.b], [S.b])
                fw.tt("dve", S.t[:, :], S.t[:, :], ps[5].t[:, :], ALU.add, [S.b, ps[5].b], [S.b])
                fw.cp("pool", Sb.t[:, :], S.t[:, :], [S.b], [Sb.b])
                fw.act(tmp.t[:, :], ya.t[:, :], AF.Square, [ya.b], [tmp.b])
                fw.op("dve", lambda g_: g_.reduce_sum(sm.t[:, 96:98], tmp.t[:, :].rearrange("p (g e) -> p g e", g=2), axis=AX.X), [tmp.b], smb)
                rstd_from_ss(fw, C, sm.t[:, 96:98], sm.t[:, 98:100], 256.0, smb, smb)
                for g in range(2):
                    gs = slice(g * 256, (g + 1) * 256)
                    fw.ts("dve", tmp.t[:, gs], ya.t[:, gs], sm.t[:, 98 + g:99 + g], None, ALU.mult, None, [ya.b, sm.b], [tmp.b])
                fw.tt("pool", yo.t[:, :], tmp.t[:, :], nw.t[:, :], ALU.mult, [tmp.b, nw.b], [yo.b])
                emit_yT(fw, C, yo, yT, 2, i)
                yield
        for st0 in (stG(0), stA(0)):
            for _ in st0:
                pass
        for i in range(NT):
            if i + 1 < NT and (i + 1) % 4 == 0:
                for _ in stG((i + 1) // 4):
                    pass
            interleave(stB(i), stA(i + 1) if i + 1 < NT else None)
        fw.barrier()


def phase_hg(fw, C, l):
    CUR_PHASE[0] = "hg"
    with ExitStack() as ph:
        WH = fw.tile(ph, "WH", [128, 8, 2048], BF16)
        load_w(fw, C.w_in[l], WH, C_HG, C_HG + 2048, 8)
        nw = fw.tile(ph, "hnw", [128, 512], F32)
        fw.dma("sp", nw.t[:, :], C.hgrn_norm_w[l].partition_broadcast(128), writes=[nw.b])
        S = fw.tile(ph, "hS", [128, 4, 128], F32)
        Sb = fw.tile(ph, "hSb", [128, 4, 128], BF16)
        PT_2 = [fw.tile(ph, "hPT", [128, 4, 128], BF16) for _ in range(2)]
        fw.memset("dve", S.t[:, :, :], 0.0, [S.b])
        fw.memset("dve", Sb.t[:, :, :], 0.0, [Sb.b])
        for PT in PT_2:
            fw.memset("pool", PT.t[:, :, :], 0.0, [PT.b])

        qg_2 = [fw.tile(ph, "hqg", [128, 4, 512], F32) for _ in range(2)]
        fg_2 = [fw.tile(ph, "hfg", [128, 4, 512], F32) for _ in range(2)]
        la_2 = [fw.tile(ph, "hla", [128, 4, 128], F32) for _ in range(2)]
        kT_2 = [fw.tile(ph, "hk", [128, 4, 128], F32) for _ in range(2)]
        cum_2 = [fw.tile(ph, "hcum", [128, 4, 128], F32) for _ in range(2)]
        ncb_2 = [fw.tile(ph, "hncb", [128, 4, 4], F32) for _ in range(2)]
        for nb_ in ncb_2:
            fw.memset("pool", nb_.t[:, :, :], 0.0, [nb_.b])
        eq_2 = [fw.tile(ph, "heq", [128, 4, 128], F32) for _ in range(2)]
        qd_2 = [fw.tile(ph, "hqd", [128, 4, 128], BF16) for _ in range(2)]
        qst_2 = [fw.tile(ph, "hqst", [128, 4, 128], BF16) for _ in range(2)]
        ek_2 = [fw.tile(ph, "hek", [128, 4, 128], F32) for _ in range(2)]
        Kt_2 = [fw.tile(ph, "hKt", [128, 4, 4, 128], BF16) for _ in range(2)]
        khT_2 = [fw.tile(ph, "hkhT", [128, 4, 128], BF16) for _ in range(2)]
        khat_2 = [fw.tile(ph, "hkhat", [128, 4, 128], BF16) for _ in range(2)]
        dec_2 = [fw.tile(ph, "hdec", [128, 4], F32) for _ in range(2)]
        vb_3 = [fw.tile(ph, "hvb", [128, 512], BF16) for _ in range(3)]
        sgate_3 = [fw.tile(ph, "hsg", [128, 512], F32) for _ in range(3)]
        ysq_2 = [fw.tile(ph, "hysq", [128, 512], F32) for _ in range(2)]
        st_2 = [fw.tile(ph, "hst", [128, 8], F32) for _ in range(2)]
        yn_2 = [fw.tile(ph, "hyn", [128, 512], F32) for _ in range(2)]
        yo_2 = [fw.tile(ph, "hyo", [128, 512], BF16) for _ in range(2)]
        yT_2 = [fw.tile(ph, "hyT", [128, 4, 128], BF16) for _ in range(2)]
        ps = C.ps
        pb0, pb1 = C.pb
        cb_ = C.consts.b
        lbc = C.lb_all.t[:, l, :]
        omc = C.oml_all.t[:, l, :]
        ones = cslice(C, "ones")
        triu = cslice(C, "triu")
        def stG(g):
                t0 = g * 512
                n = min(512, TP - t0)
                tiles = list(range(t0 // 128, (t0 + n) // 128))
                qg, fg = qg_2[g % 2], fg_2[g % 2]
                k = 0
                for h in range(4):
                    for (c0, dst, func) in ((0, qg, AF.Silu), (512, fg, AF.Sigmoid)):
                        pst = ps[0]
                        for kc in range(8):
                            fw.mm(pst.t[:, 0:n], WH.t[:, kc, c0 + h * 128:c0 + (h + 1) * 128], C.uT.t[:, kc, t0:t0 + n], kc == 0, kc == 7,
                                  [WH.b] + [C.uTb[j] for j in tiles], [pst.b])
                        fw.act(dst.t[:, h, 0:n], pst.t[:, 0:n], func, [pst.b], [dst.b])
                        yield
        def stA(i):
                PT = PT_2[i % 2]
                la, kT, cum, ncb, eq, qd, qst, ek, Kt, khT, khat, dec, vb, sgate, ysq, st, yn, yo, yT = la_2[i % 2], kT_2[i % 2], cum_2[i % 2], ncb_2[i % 2], eq_2[i % 2], qd_2[i % 2], qst_2[i % 2], ek_2[i % 2], Kt_2[i % 2], khT_2[i % 2], khat_2[i % 2], dec_2[i % 2], vb_3[i % 3], sgate_3[i % 3], ysq_2[i % 2], st_2[i % 2], yn_2[i % 2], yo_2[i % 2], yT_2[i % 2]
                tok_mm(fw, C, ps[2], WH, 1024, 512, i)
                yield
                tok_mm(fw, C, ps[3], WH, 1536, 512, i)
                yield
                fw.cp("act", vb.t[:, :], ps[2].t[:, :], [ps[2].b], [vb.b])
                yield
                fw.act(sgate.t[:, :], ps[3].t[:, :], AF.Silu, [ps[3].b], [sgate.b])
                yield
        def stB1(i):
                PT = PT_2[i % 2]
                la, kT, cum, ncb, eq, qd, qst, ek, Kt, khT, khat, dec, vb, sgate, ysq, st, yn, yo, yT = la_2[i % 2], kT_2[i % 2], cum_2[i % 2], ncb_2[i % 2], eq_2[i % 2], qd_2[i % 2], qst_2[i % 2], ek_2[i % 2], Kt_2[i % 2], khT_2[i % 2], khat_2[i % 2], dec_2[i % 2], vb_3[i % 3], sgate_3[i % 3], ysq_2[i % 2], st_2[i % 2], yn_2[i % 2], yo_2[i % 2], yT_2[i % 2]
                qT = VW(qg_2[(i // 4) % 2].t[:, :, (i % 4) * 128:(i % 4 + 1) * 128], qg_2[(i // 4) % 2].b)
                fT = VW(fg_2[(i // 4) % 2].t[:, :, (i % 4) * 128:(i % 4 + 1) * 128], fg_2[(i // 4) % 2].b)
                for h in range(4):
                    fw.ts("dve", fT.t[:, h, :], fT.t[:, h, :], omc[:, h:h + 1], lbc[:, h:h + 1], ALU.mult, ALU.add,
                          [fT.b, C.lb_all.b, C.oml_all.b], [fT.b])
                yield
                fw.act(la.t[:, :, :], fT.t[:, :, :], AF.Ln, [fT.b], [la.b])
                yield
                fw.ts("pool", kT.t[:, :, :], fT.t[:, :, :], -1.0, 1.0, ALU.mult, ALU.add, [fT.b], [kT.b])
                yield
                for h in range(4):
                    fw.op("dve", lambda g: g.tensor_tensor_scan(cum.t[:, h, :], ones, la.t[:, h, :], 0.0, ALU.mult, ALU.add),
                          [la.b, cb_], [cum.b])
                yield
                cv = cum.t[:, :, :].rearrange("p h (b c) -> p h b c", c=32)
                fw.ts("dve", ncb.t[:, :, 1:4], cv[:, :, 0:3, 31], -1.0, None, ALU.mult, None, [cum.b], [ncb.b])
                yield
                fw.tt("dve", eq.t[:, :, :].rearrange("p h (b c) -> p h b c", c=32), cv,
                      ncb.t[:, :, :].unsqueeze(3).to_broadcast([128, 4, 4, 32]), ALU.add, [cum.b, ncb.b], [eq.b])
                yield
                fw.act(eq.t[:, :, :], eq.t[:, :, :], AF.Exp, [eq.b], [eq.b])
                yield
                fw.tt("pool", qd.t[:, :, :], qT.t[:, :, :], eq.t[:, :, :], ALU.mult, [qT.b, eq.b], [qd.b])
                yield
                fw.act(eq.t[:, :, :], cum.t[:, :, :], AF.Exp, [cum.b], [eq.b])
                yield
                fw.tt("pool", qst.t[:, :, :], qT.t[:, :, :], eq.t[:, :, :], ALU.mult, [qT.b, eq.b], [qst.b])
                yield
                for b in range(4):
                    W_ = 32 * (b + 1)
                    eb = ek if b % 2 == 0 else eq
                    fw.tt("dve", eb.t[:, :, 0:W_], cum.t[:, :, 0:W_], ncb.t[:, :, b:b + 1].to_broadcast([128, 4, W_]), ALU.add,
                          [cum.b, ncb.b], [eb.b])
                    fw.act(eb.t[:, :, 0:W_], eb.t[:, :, 0:W_], AF.Exp, [eb.b], [eb.b], scale=-1.0)
                    fw.tt("pool" if b % 2 == 0 else "dve", Kt.t[:, :, b, 0:W_], eb.t[:, :, 0:W_], kT.t[:, :, 0:W_], ALU.mult,
                          [eb.b, kT.b], [Kt.b])
                    yield
                fw.tt("dve", ek.t[:, :, :], cum.t[:, :, 127:128].to_broadcast([128, 4, 128]), cum.t[:, :, :], ALU.subtract, [cum.b], [ek.b])
                yield
                fw.act(ek.t[:, :, :], ek.t[:, :, :], AF.Exp, [ek.b], [ek.b])
                yield
                fw.tt("pool", khT.t[:, :, :], ek.t[:, :, :], kT.t[:, :, :], ALU.mult, [ek.b, kT.b], [khT.b])
                yield
                for h in range(4):
                    fw.tr(pb0.t[:, h * 128:(h + 1) * 128], khT.t[:, h, :], C.identb.t[:, :], [khT.b, C.identb.b], [pb0.b])
                yield
                fw.cp("act", khat.t[:, :, :], pb0.t[:, 0:512].rearrange("p (h d) -> p h d", h=4), [pb0.b], [khat.b])
                yield
                fw.act(dec.t[:, :], cum.t[:, :, 127], AF.Exp, [cum.b], [dec.b])
                yield
                for h in range(4):
                    for b in range(4):
                        W_ = 32 * (b + 1)
                        fw.mm(ps[4].t[0:W_, h * 128 + 32 * b:h * 128 + 32 * b + 32], Kt.t[:, h, b, 0:W_], qd.t[:, h, 32 * b:32 * b + 32],
                              True, True, [Kt.b, qd.b], [ps[4].b])
                yield
                psv = ps[4].t[:, :].rearrange("p (h t) -> p h t", h=4)
                for b in range(4):
                    W_ = 32 * (b + 1)
                    bs = slice(32 * b, 32 * b + 32)
                    to, tn = CO["triu"]
                    mk = C.consts.t[0:W_, to + 32 * b:to + 32 * b + 32].unsqueeze(1).to_broadcast([W_, 4, 32])
                    fw.tt("dve", PT.t[0:W_, :, bs], psv[0:W_, :, bs], mk, ALU.mult, [ps[4].b, cb_], [PT.b])
                yield
        def stB2(i):
                PT = PT_2[i % 2]
                la, kT, cum, ncb, eq, qd, qst, ek, Kt, khT, khat, dec, vb, sgate, ysq, st, yn, yo, yT = la_2[i % 2], kT_2[i % 2], cum_2[i % 2], ncb_2[i % 2], eq_2[i % 2], qd_2[i % 2], qst_2[i % 2], ek_2[i % 2], Kt_2[i % 2], khT_2[i % 2], khat_2[i % 2], dec_2[i % 2], vb_3[i % 3], sgate_3[i % 3], ysq_2[i % 2], st_2[i % 2], yn_2[i % 2], yo_2[i % 2], yT_2[i % 2]
                qT = VW(qg_2[(i // 4) % 2].t[:, :, (i % 4) * 128:(i % 4 + 1) * 128], qg_2[(i // 4) % 2].b)
                fT = VW(fg_2[(i // 4) % 2].t[:, :, (i % 4) * 128:(i % 4 + 1) * 128], fg_2[(i // 4) % 2].b)
                for h in range(4):
                    hs = slice(h * 128, (h + 1) * 128)
                    fw.mm(ps[5].t[:, hs], PT.t[:, h, :], vb.t[:, hs], True, False, [PT.b, vb.b], [ps[5].b])
                    fw.mm(ps[5].t[:, hs], qst.t[:, h, :], Sb.t[:, h, :], False, True, [qst.b, Sb.b], [ps[5].b])
                yield
                for h in range(4):
                    hs = slice(h * 128, (h + 1) * 128)
                    fw.mm(ps[1].t[:, hs], khat.t[:, h, :], vb.t[:, hs], True, True, [khat.b, vb.b], [ps[1].b])
                yield
                for h in range(4):
                    hs = slice(h * 128, (h + 1) * 128)
                    fw.stt("dve", S.t[:, h, :], S.t[:, h, :], dec.t[:, h:h + 1], ps[1].t[:, hs], ALU.mult, ALU.add, [S.b, dec.b, ps[1].b], [S.b])
                yield
                fw.cp("pool", Sb.t[:, :, :], S.t[:, :, :], [S.b], [Sb.b])
                yield
                fw.act(ysq.t[:, :], ps[5].t[:, :], AF.Square, [ps[5].b], [ysq.b])
                yield
                fw.op("dve", lambda g: g.reduce_sum(st.t[:, 0:4], ysq.t[:, :].rearrange("p (h e) -> p h e", h=4), axis=AX.X), [ysq.b], [st.b])
                yield
                rstd_from_ss(fw, C, st.t[:, 0:4], st.t[:, 4:8], 128.0, [st.b], [st.b])
                yield
                for h in range(4):
                    hs = slice(h * 128, (h + 1) * 128)
                    fw.ts("dve", yn.t[:, hs], ps[5].t[:, hs], st.t[:, 4 + h:5 + h], None, ALU.mult, None, [ps[5].b, st.b], [yn.b])
                yield
                fw.tt("pool", yn.t[:, :], yn.t[:, :], nw.t[:, :], ALU.mult, [yn.b, nw.b], [yn.b])
                yield
                fw.tt("pool", yo.t[:, :], yn.t[:, :], sgate.t[:, :], ALU.mult, [yn.b, sgate.b], [yo.b])
                yield
                emit_yT(fw, C, yo, yT, 3, i)
                yield
        for st0 in (stG(0), stA(0), stB1(0), stA(1) if NT > 1 else None):
            if st0 is not None:
                for _ in st0:
                    pass
        for i in range(NT):
            gens = [stB2(i)]
            if i + 1 < NT:
                gens.append(stB1(i + 1))
            if i + 2 < NT:
                gens.append(chain_gens(stG((i + 2) // 4) if (i + 2) % 4 == 0 else None, stA(i + 2)))
            interleave_n(gens)
        fw.barrier()


TGS = [(0, 512), (512, 512), (1024, 512), (1536, 512), (2048, 128)]


def phase_s5(fw, C, l):
    ps = C.ps
    pb0, pb1 = C.pb
    cb_ = C.consts.b
    with ExitStack() as ph:
        GW = fw.tile(ph, "GW", [128, 4, 1024], BF16)
        load_w(fw, C.s5_glu_w[l], GW, 0, 1024, 4)
        Cre = fw.tile(ph, "Cre", [128, 16, 128], BF16)
        nCim = fw.tile(ph, "nCim", [128, 16, 128], BF16)
        fw.dma("pool", Cre.t[:, :, :], C.s5_cre[l], writes=[Cre.b])
        fw.dma("pool", nCim.t[:, :, :], C.s5_cim[l], writes=[nCim.b])
        fw.ts("pool", nCim.t[:, :, :], nCim.t[:, :, :], -1.0, None, ALU.mult, None, [nCim.b], [nCim.b])
        d5 = fw.tile(ph, "d5", [128, 4], F32)
        gb = fw.tile(ph, "gb5", [128, 8], F32)
        fw.dma("sp", d5.t[:, :], C.s5_dT[l], writes=[d5.b])
        fw.dma("sp", gb.t[:, :], C.s5_gbT[l], writes=[gb.b])
        sp_ = fw.tile(ph, "s5sm", [128, 48], F32)
        fw.dma("sp", sp_.t[:, :], C.s5_small[l], writes=[sp_.b])
        u5T = fw.tile(ph, "u5T", [128, 4, TP], BF16)
        Sall = fw.tile(ph, "Sall", [128, 137, 3, 16], F32)
        KT = fw.tile(ph, "KT", [128, 4, 16, 128], BF16)
        PW = fw.tile(ph, "PW", [128, 2, 17, 16], F32)
        wk = fw.tile(ph, "s5wk", [128, 12, 16], F32)
        with ExitStack() as pa:
            W5 = fw.tile(pa, "W5", [128, 8, 512], BF16)
            load_w(fw, C.w_in[l], W5, C_S5, C_S5 + 512, 8)
            k = 0
            for ct in range(4):
                for (t0, n) in TGS:
                    pst = ps[k % 2]
                    for kc in range(8):
                        fw.mm(pst.t[:, 0:n], W5.t[:, kc, ct * 128:(ct + 1) * 128], C.uT.t[:, kc, t0:t0 + n], kc == 0, kc == 7,
                              [W5.b] + C.uTb[t0 // 128:(t0 + n) // 128], [pst.b])
                    fw.cp("act" if k % 2 else "dve", u5T.t[:, ct, t0:t0 + n], pst.t[:, 0:n], [pst.b], [u5T.b])
                    k += 1
            fw.barrier()
        if os.environ.get("S5_STOP") == "A":
            fw.barrier(); return
        lr, li, lst = sp_.t[:, 0:16], sp_.t[:, 16:32], sp_.t[:, 32:48]
        W_ = [wk.t[:, j, :] for j in range(12)]
        R = [sp_.b, wk.b, PW.b]
        step, lrs, ang, em1, re_, im_, t_a, t_b, inv, co_re, co_im, rr = W_
        big = [fw.tile(ph, "s5big", [128, 18, 16], F32) for _ in range(5)]
        bigi = fw.tile(ph, "s5bigi", [128, 18, 16], mybir.dt.int32)
        RB = R + [b_.b for b_ in big] + [bigi.b, cb_]
        FACT = [1.0, 1.0, 2.0, 6.0, 24.0, 120.0, 720.0, 5040.0, 40320.0, 362880.0, 3628800.0]

        def horner_exp(out, r, deg, minus1=False):
            fw.ts("dve", out, r, 1.0 / FACT[deg], None, ALU.mult, None, RB, RB)
            for j in range(deg - 1, 0, -1):
                fw.stt("dve", out, out, 1.0 / FACT[j], r, ALU.add, ALU.mult, RB, RB)
            if not minus1:
                fw.ts("dve", out, out, 1.0, None, ALU.add, None, RB, RB)

        fw.ts("dve", rr, lst, 0.125, None, ALU.mult, None, RB, RB)
        horner_exp(step, rr, 10)
        for _ in range(3):
            fw.tt("dve", step, step, step, ALU.mult, RB, RB)
        fw.tt("dve", lrs, lr, step, ALU.mult, RB, RB)
        fw.tt("dve", ang, li, step, ALU.mult, RB, RB)
        mo = CO["mtab"][0]
        mtab = C.consts.t[:, mo:mo + 288].rearrange("p (m q) -> p m q", m=18)
        TH, XM, MAG, SN, CS = [b_.t[:, :, :] for b_ in big]
        fw.tt("dve", TH, mtab, ang.unsqueeze(1).to_broadcast([128, 18, 16]), ALU.mult, RB, RB)
        fw.tt("dve", XM, mtab, lrs.unsqueeze(1).to_broadcast([128, 18, 16]), ALU.mult, RB, RB)
        horner_exp(MAG, XM, 10)
        C1, C2 = 6.28125, 2.0 * math.pi - 6.28125

        def sin_reduced(out, th):
            fw.ts("dve", out, th, 1.0 / (2.0 * math.pi), None, ALU.mult, None, RB, RB)
            fw.cp("dve", bigi.t[:, :, :], out, RB, RB)
            fw.cp("dve", XM, bigi.t[:, :, :], RB, RB)
            fw.stt("dve", out, XM, -C1, th, ALU.mult, ALU.add, RB, RB)
            fw.stt("dve", out, XM, -C2, out, ALU.mult, ALU.add, RB, RB)
            fw.act(out, out, AF.Sin, RB, RB)

        sin_reduced(SN, TH)
        fw.ts("dve", TH, TH, math.pi / 2, None, ALU.add, None, RB, RB)
        sin_reduced(CS, TH)
        fw.tt("dve", PW.t[:, 0, :, :], MAG[:, 0:17, :], CS[:, 0:17, :], ALU.mult, RB, RB)
        fw.tt("dve", PW.t[:, 1, :, :], MAG[:, 0:17, :], SN[:, 0:17, :], ALU.mult, RB, RB)
        horner_exp(em1, lrs, 7, minus1=True)
        fw.tt("dve", re_, em1, CS[:, 1, :], ALU.mult, RB, RB)
        fw.tt("dve", t_a, SN[:, 17, :], SN[:, 17, :], ALU.mult, RB, RB)
        fw.stt("dve", re_, t_a, -2.0, re_, ALU.mult, ALU.add, RB, RB)
        fw.ts("dve", t_b, em1, 1.0, None, ALU.add, None, RB, RB)
        fw.tt("dve", im_, t_b, SN[:, 1, :], ALU.mult, RB, RB)
        fw.tt("dve", t_a, lr, lr, ALU.mult, RB, RB)
        fw.tt("dve", t_b, li, li, ALU.mult, RB, RB)
        fw.tt("dve", inv, t_a, t_b, ALU.add, RB, RB)
        fw.op("dve", lambda g: g.reciprocal(inv, inv), RB, RB)
        fw.tt("dve", t_a, re_, lr, ALU.mult, RB, RB)
        fw.tt("dve", t_b, im_, li, ALU.mult, RB, RB)
        fw.tt("dve", t_a, t_a, t_b, ALU.add, RB, RB)
        fw.tt("dve", co_re, t_a, inv, ALU.mult, RB, RB)
        fw.tt("dve", t_a, im_, lr, ALU.mult, RB, RB)
        fw.tt("dve", t_b, re_, li, ALU.mult, RB, RB)
        fw.tt("dve", t_a, t_a, t_b, ALU.subtract, RB, RB)
        fw.tt("dve", co_im, t_a, inv, ALU.mult, RB, RB)
        if C.debug and l == 0:
            fw.dma("sp", C.dbg5[:, 0:192], wk.t[:, :, :].rearrange("p a b -> p (a b)"), reads=[wk.b], writes=[Buf()])
            fw.dma("sp", C.dbg5[:, 192:736], PW.t[:, :, :, :].rearrange("p a m q -> p (a m q)"), reads=[PW.b], writes=[Buf()])
        if os.environ.get("S5_STOP") == "B":
            fw.barrier(); return
        fw.memset("pool", Sall.t[:, 0, :, :], 0.0, [Sall.b])
        with ExitStack() as pd:
            Bst = fw.tile(pd, "Bst", [128, 2, 4, 128], F32)
            Bb = fw.tile(pd, "Bb", [128, 2, 4, 128], F32)
            t0_ = fw.tile(pd, "tB", [128, 128], F32)
            tA = [fw.tile(pd, "tA", [128, 16, 128], F32)] * 2
            tB = [fw.tile(pd, "tBB", [128, 16, 128], F32)] * 2
            Xs = [fw.tile(pd, "X", [128, 2, 16, 128], BF16) for _ in range(2)]
            XTs = [fw.tile(pd, "XT", [128, 4, 2, 128], BF16) for _ in range(2)]
            psK = ps[2:6]
            zt = fw.tile(pd, "zt", [128, 512], BF16)
            fw.memset("pool", zt.t[:, :], 0.0, [zt.b])
            it = 0
            for ct in range(4):
                for j in range(4):
                    fw.mm(psK[j].t[:, :], zt.t[:, 0:128], zt.t[:, :], True, False, [zt.b], [psK[j].b])
                fw.dma("sp", Bst.t[:, 0, :, :], C.s5_bre[l][:, 4 * ct:4 * ct + 4, :], writes=[Bst.b])
                fw.dma("sp", Bst.t[:, 1, :, :], C.s5_bim[l][:, 4 * ct:4 * ct + 4, :], writes=[Bst.b])
                for pl in range(4):
                    pair = 4 * ct + pl
                    cr, ci = co_re[:, pair:pair + 1], co_im[:, pair:pair + 1]
                    fw.ts("dve", t0_.t[:, :], Bst.t[:, 1, pl, :], ci, None, ALU.mult, None, [Bst.b, wk.b], [t0_.b])
                    fw.stt("dve", Bb.t[:, 0, pl, :], Bst.t[:, 0, pl, :], cr, t0_.t[:, :], ALU.mult, ALU.subtract, [Bst.b, wk.b, t0_.b], [Bb.b])
                    fw.ts("dve", t0_.t[:, :], Bst.t[:, 1, pl, :], cr, None, ALU.mult, None, [Bst.b, wk.b], [t0_.b])
                    fw.stt("dve", Bb.t[:, 1, pl, :], Bst.t[:, 0, pl, :], ci, t0_.t[:, :], ALU.mult, ALU.add, [Bst.b, wk.b, t0_.b], [Bb.b])
                for pl in range(4):
                    pair = 4 * ct + pl
                    psG = ps[pair % 2]
                    X = Xs[pair % 2]
                    ta, tb = tA[pair % 2], tB[pair % 2]
                    bre = Bb.t[:, 0, pl, :].unsqueeze(1).to_broadcast([128, 16, 128])
                    bim = Bb.t[:, 1, pl, :].unsqueeze(1).to_broadcast([128, 16, 128])
                    prb = PW.t[:, 0, 0:16, pair].unsqueeze(2).to_broadcast([128, 16, 128])
                    pib = PW.t[:, 1, 0:16, pair].unsqueeze(2).to_broadcast([128, 16, 128])
                    RB_ = [Bb.b, PW.b]
                    fw.tt("dve", ta.t[:, :, :], bre, prb, ALU.mult, RB_, [ta.b])
                    fw.tt("pool", tb.t[:, :, :], bim, pib, ALU.mult, RB_, [tb.b])
                    fw.tt("dve", X.t[:, 0, :, :], ta.t[:, :, :], tb.t[:, :, :], ALU.subtract, [ta.b, tb.b], [X.b])
                    fw.tt("pool", tb.t[:, :, :], bim, prb, ALU.mult, RB_, [tb.b])
                    fw.tt("dve", ta.t[:, :, :], bre, pib, ALU.mult, RB_, [ta.b])
                    fw.tt("dve", X.t[:, 1, :, :], ta.t[:, :, :], tb.t[:, :, :], ALU.add, [ta.b, tb.b], [X.b])
                    fw.mm(psG.t[:, 0:272], zt.t[:, 0:128], zt.t[:, 0:272], True, False, [zt.b], [psG.b])
                    for m in range(16):
                        pk = psK[m // 4]
                        ks = slice((m % 4) * 128, (m % 4 + 1) * 128)
                        fw.mm(pk.t[:, ks], X.t[:, 0, m, :], Cre.t[:, pair, :], False, False, [X.b, Cre.b], [pk.b])
                        fw.mm(pk.t[:, ks], X.t[:, 1, m, :], nCim.t[:, pair, :], False, pl == 3, [X.b, nCim.b], [pk.b])
                    for mg in range(4):
                        XT = XTs[it % 2]
                        pbt = C.pb[it % 2]
                        for mm_ in range(4):
                            m = 4 * mg + mm_
                            for part in range(2):
                                fw.tr(pbt.t[:, (2 * mm_ + part) * 128:(2 * mm_ + part + 1) * 128], X.t[:, part, m, :], C.identb.t[:, :],
                                      [X.b, C.identb.b], [pbt.b])
                        fw.cp("act" if it % 2 else "dve", XT.t[:, :, :, :], pbt.t[:, :].rearrange("p (m a q) -> p m a q", m=4, a=2),
                              [pbt.b], [XT.b])
                        for mm_ in range(4):
                            m = 4 * mg + mm_
                            tau = 15 - m
                            rhs = u5T.t[:, ct, :].rearrange("p (c b) -> p c b", b=16)[:, :, tau]
                            fw.mm(psG.t[:, 0:136], XT.t[:, mm_, 0, :], rhs, False, m == 15, [XT.b, u5T.b], [psG.b])
                            fw.mm(psG.t[:, 136:272], XT.t[:, mm_, 1, :], rhs, False, m == 15, [XT.b, u5T.b], [psG.b])
                        it += 1
                    fw.cp("act", Sall.t[:, 1:137, 0, pair], psG.t[:, 0:136], [psG.b], [Sall.b])
                    fw.cp("act", Sall.t[:, 1:137, 1, pair], psG.t[:, 136:272], [psG.b], [Sall.b])
                for j in range(4):
                    fw.cp("act" if j % 2 else "dve", KT.t[:, ct, 4 * j:4 * j + 4, :], psK[j].t[:, :].rearrange("p (m c) -> p m c", m=4),
                          [psK[j].b], [KT.b])
            fw.barrier()
        if C.debug and l == 0:
            fw.dma("pool", C.dbg5[:, 736:736 + 2048], KT.t[:, 0, :, :].rearrange("p m c -> p (m c)"), reads=[KT.b], writes=[Buf()])
            fw.dma("sp", C.dbg5[:, 2784:2784 + 137 * 48], Sall.t[:, :, :, :].rearrange("p c a q -> p (c a q)"), reads=[Sall.b], writes=[Buf()])
        if os.environ.get("S5_STOP") == "D":
            fw.barrier(); return
        with ExitStack() as pe_:
            A1 = fw.tile(pe_, "A1", [128, 2, 16], F32)
            A2 = fw.tile(pe_, "A2", [128, 2, 16], F32)
            p1 = fw.tile(pe_, "p1", [128, 2, 16], F32)
            p2 = fw.tile(pe_, "p2", [128, 2, 16], F32)
            fw.cp("dve", A1.t[:, 0, :], PW.t[:, 0, 16, :], [PW.b], [A1.b])
            fw.cp("dve", A1.t[:, 1, :], PW.t[:, 0, 16, :], [PW.b], [A1.b])
            fw.ts("dve", A2.t[:, 0, :], PW.t[:, 1, 16, :], -1.0, None, ALU.mult, None, [PW.b], [A2.b])
            fw.cp("dve", A2.t[:, 1, :], PW.t[:, 1, 16, :], [PW.b], [A2.b])
            for c in range(136):
                fw.tt("dve", p1.t[:, :, :], A1.t[:, :, :], Sall.t[:, c, 0:2, :], ALU.mult, [A1.b, Sall.b], [p1.b])
                fw.tt("dve", p2.t[:, :, :], A2.t[:, :, :], Sall.t[:, c, 1:3, :], ALU.mult, [A2.b, Sall.b], [p2.b])
                fw.tt("dve", p1.t[:, :, :], p1.t[:, :, :], p2.t[:, :, :], ALU.add, [p1.b, p2.b], [p1.b])
                fw.tt("dve", Sall.t[:, c + 1, 0:2, :], Sall.t[:, c + 1, 0:2, :], p1.t[:, :, :], ALU.add, [Sall.b, p1.b], [Sall.b])
                fw.cp("dve", Sall.t[:, c + 1, 2, :], Sall.t[:, c + 1, 0, :], [Sall.b], [Sall.b])
            fw.barrier()
        if os.environ.get("S5_STOP") == "E":
            fw.barrier(); return
        pfg = ExitStack()
        gT = fw.tile(pfg, "g5T", [128, 4, TP], BF16)
        with ExitStack() as pf:
            SP = fw.tile(pf, "SP", [128, 4, 2, 136, 16], BF16)
            u1 = fw.tile(pf, "u1", [128, 136, 16], F32)
            u2 = fw.tile(pf, "u2", [128, 136, 16], F32)
            z = fw.tile(pf, "z5", [128, 512], F32)
            z2 = fw.tile(pf, "z52", [128, 512], F32)
            k = 0
            for ct in range(4):
                for pl in range(4):
                    pair = 4 * ct + pl
                    srb = Sall.t[:, 0:136, 0, pair].unsqueeze(2).to_broadcast([128, 136, 16])
                    sib = Sall.t[:, 0:136, 1, pair].unsqueeze(2).to_broadcast([128, 136, 16])
                    prb = PW.t[:, 0, 1:17, pair].unsqueeze(1).to_broadcast([128, 136, 16])
                    pib = PW.t[:, 1, 1:17, pair].unsqueeze(1).to_broadcast([128, 136, 16])
                    RS = [Sall.b, PW.b]
                    fw.tt("dve", u1.t[:, :, :], srb, prb, ALU.mult, RS, [u1.b])
                    fw.tt("pool", u2.t[:, :, :], sib, pib, ALU.mult, RS, [u2.b])
                    fw.tt("dve", SP.t[:, pl, 0, :, :], u1.t[:, :, :], u2.t[:, :, :], ALU.subtract, [u1.b, u2.b], [SP.b])
                    fw.tt("pool", u2.t[:, :, :], sib, prb, ALU.mult, RS, [u2.b])
                    fw.tt("dve", u1.t[:, :, :], srb, pib, ALU.mult, RS, [u1.b])
                    fw.tt("dve", SP.t[:, pl, 1, :, :], u1.t[:, :, :], u2.t[:, :, :], ALU.add, [u1.b, u2.b], [SP.b])
                for (t0, n) in TGS:
                    c0, nch = t0 // 16, n // 16
                    pst = ps[k % 2]
                    k += 1
                    pv = pst.t[:, 0:n].rearrange("p (c b) -> p c b", b=16)
                    uv = u5T.t[:, ct, t0:t0 + n].rearrange("p (c b) -> p c b", b=16)
                    for tau in range(16):
                        fw.mm(pv[:, :, tau:16], KT.t[:, ct, tau, :], uv[:, :, 0:16 - tau], tau == 0, False, [KT.b, u5T.b], [pst.b])
                    for pl in range(4):
                        pair = 4 * ct + pl
                        fw.mm(pst.t[:, 0:n], Cre.t[:, pair, :], SP.t[:, pl, 0, c0:c0 + nch, :].rearrange("p c b -> p (c b)"), False, False,
                              [Cre.b, SP.b], [pst.b])
                        fw.mm(pst.t[:, 0:n], nCim.t[:, pair, :], SP.t[:, pl, 1, c0:c0 + nch, :].rearrange("p c b -> p (c b)"), False, pl == 3,
                              [nCim.b, SP.b], [pst.b])
                    fw.stt("dve", z.t[:, 0:n], u5T.t[:, ct, t0:t0 + n], d5.t[:, ct:ct + 1], pst.t[:, 0:n], ALU.mult, ALU.add,
                           [u5T.b, d5.b, pst.b], [z.b])
                    fw.tt("pool", z2.t[:, 0:n], z.t[:, 0:n], z.t[:, 0:n], ALU.mult, [z.b], [z2.b])
                    fw.ts("pool", z2.t[:, 0:n], z2.t[:, 0:n], 0.044715, 1.0, ALU.mult, ALU.add, [z2.b], [z2.b])
                    fw.tt("pool", z2.t[:, 0:n], z2.t[:, 0:n], z.t[:, 0:n], ALU.mult, [z2.b, z.b], [z2.b])
                    fw.act(z2.t[:, 0:n], z2.t[:, 0:n], AF.Sigmoid, [z2.b], [z2.b], scale=2.0 * math.sqrt(2.0 / math.pi))
                    fw.tt("pool", gT.t[:, ct, t0:t0 + n], z.t[:, 0:n], z2.t[:, 0:n], ALU.mult, [z.b, z2.b], [gT.b])
            fw.barrier()
        if os.environ.get("S5_STOP") == "F":
            pfg.close(); fw.barrier(); return
        with ExitStack() as pg:
            sgs = [fw.tile(pg, "sg5", [128, 512], F32) for _ in range(2)]
            yos = [fw.tile(pg, "yo5", [128, 512], BF16) for _ in range(2)]
            k = 0
            for nb in range(4):
                for (t0, n) in TGS:
                    pa_, pg_ = ps[2 + 2 * (k % 2)], ps[3 + 2 * (k % 2)]
                    sg, yo = sgs[k % 2], yos[k % 2]
                    k += 1
                    for kc in range(4):
                        fw.mm(pa_.t[:, 0:n], GW.t[:, kc, nb * 128:(nb + 1) * 128], gT.t[:, kc, t0:t0 + n], kc == 0, kc == 3, [GW.b, gT.b], [pa_.b])
                    for kc in range(4):
                        fw.mm(pg_.t[:, 0:n], GW.t[:, kc, 512 + nb * 128:512 + (nb + 1) * 128], gT.t[:, kc, t0:t0 + n], kc == 0, kc == 3,
                              [GW.b, gT.b], [pg_.b])
                    fw.act(sg.t[:, 0:n], pg_.t[:, 0:n], AF.Sigmoid, [pg_.b, gb.b], [sg.b], bias=gb.t[:, 4 + nb:5 + nb])
                    fw.stt("dve", yo.t[:, 0:n], pa_.t[:, 0:n], gb.t[:, nb:nb + 1], sg.t[:, 0:n], ALU.add, ALU.mult, [pa_.b, gb.b, sg.b], [yo.b])
                    fw.dma("sp", C.YT[4 + nb, :, t0:t0 + n], yo.t[:, 0:n], reads=[yo.b], writes=C.YTb[1][t0 // 128:(t0 + n) // 128])
            fw.barrier()
        pfg.close()
        fw.barrier()


MGS = [(g * 256, min(256, TP - g * 256)) for g in range((TP + 255) // 256)]


def rms_epilogue(fw, C, psA, psB, nw, xt, wk2):
    junk, ss, tmp = wk2
    fw.act(junk.t[:, 0:512], psA.t[:, :], AF.Square, [psA.b], [junk.b, ss.b], accum=ss.t[:, 2:3])
    fw.act(junk.t[:, 512:1024], psB.t[:, :], AF.Square, [psB.b], [junk.b, ss.b], accum=ss.t[:, 3:4])
    fw.tt("dve", ss.t[:, 2:3], ss.t[:, 2:3], ss.t[:, 3:4], ALU.add, [ss.b], [ss.b])
    rstd_from_ss(fw, C, ss.t[:, 2:3], ss.t[:, 3:4], 1024.0, [ss.b], [ss.b])
    fw.stt("dve", tmp.t[:, 0:512], psA.t[:, :], ss.t[:, 3:4], nw.t[:, 0:512], ALU.mult, ALU.mult, [psA.b, ss.b, nw.b], [tmp.b])
    fw.stt("dve", tmp.t[:, 512:1024], psB.t[:, :], ss.t[:, 3:4], nw.t[:, 512:1024], ALU.mult, ALU.mult, [psB.b, ss.b, nw.b], [tmp.b])
    fw.tt("pool", xt.t[:, :], xt.t[:, :], tmp.t[:, :], ALU.add, [xt.b, tmp.b], [xt.b])


def phase_merge(fw, C, l):
    ps = C.ps
    with ExitStack() as ph:
        WG = fw.tile(ph, "WG", [128, 8, 4096], BF16)
        load_w(fw, C.w_in[l], WG, C_GATE, C_GATE + 4096, 8)
        WB = fw.tile(ph, "WB", [128, 16, 1024], BF16)
        for n in range(4):
            v = C.w_branch[l][n].rearrange("(cb p) d -> p cb d", p=128)
            fw.dma("pool", WB.t[:, 4 * n:4 * n + 4, :], v, writes=[WB.b])
        WO = fw.tile(ph, "WO", [128, 8, 1024], BF16)
        load_w(fw, C.w_out[l], WO, 0, 1024, 8)
        nw1 = fw.tile(ph, "nw1", [128, D], F32)
        nw2 = fw.tile(ph, "nw2", [128, D], F32)
        fw.dma("sp", nw1.t[:, :], C.norm_post_mix[l].partition_broadcast(128), writes=[nw1.b])
        fw.dma("sp", nw2.t[:, :], C.norm_pre_mlp[l].partition_broadcast(128), writes=[nw2.b])
        YTs = fw.tile(ph, "YTs", [128, 16, 256], BF16)
        mixT = fw.tile(ph, "mixT", [128, 8, 256], BF16)
        acc = fw.tile(ph, "macc", [128, 256], F32)
        sgs = [fw.tile(ph, "msg", [128, 256], F32) for _ in range(2)]
        tmpm = fw.tile(ph, "mtmp", [128, 256], F32)
        xts = [fw.tile(ph, "mxt", [128, D], F32) for _ in range(2)]
        wk = (fw.tile(ph, "junk", [128, D], BF16), fw.tile(ph, "ss", [128, 4], F32), fw.tile(ph, "ub", [128, D], BF16))
        wk2 = (wk[0], wk[1], fw.tile(ph, "mtmp2", [128, D], F32))
        k = 0
        for (t0, n) in MGS:
            tiles = list(range(t0 // 128, (t0 + n) // 128))
            fw.dma("sp", YTs.t[:, :, 0:n], C.YT[:, :, t0:t0 + n].rearrange("c p t -> p c t"),
                   reads=[C.YTb[m][i] for m in range(4) for i in tiles], writes=[YTs.b])
            for db in range(8):
                for nn in range(4):
                    pg_, pb_ = ps[2 * (k % 2)], ps[2 * (k % 2) + 1]
                    sg = sgs[k % 2]
                    k += 1
                    c0 = nn * 1024 + db * 128
                    for kc in range(8):
                        fw.mm(pg_.t[:, 0:n], WG.t[:, kc, c0:c0 + 128], C.uT.t[:, kc, t0:t0 + n], kc == 0, kc == 7,
                              [WG.b] + [C.uTb[i] for i in tiles], [pg_.b])
                    for cb in range(4):
                        fw.mm(pb_.t[:, 0:n], WB.t[:, 4 * nn + cb, db * 128:(db + 1) * 128], YTs.t[:, 4 * nn + cb, 0:n], cb == 0, cb == 3,
                              [WB.b, YTs.b], [pb_.b])
                    fw.act(sg.t[:, 0:n], pg_.t[:, 0:n], AF.Sigmoid, [pg_.b], [sg.b])
                    if nn == 0:
                        fw.tt("dve", acc.t[:, 0:n], sg.t[:, 0:n], pb_.t[:, 0:n], ALU.mult, [sg.b, pb_.b], [acc.b])
                    else:
                        fw.tt("dve", tmpm.t[:, 0:n], sg.t[:, 0:n], pb_.t[:, 0:n], ALU.mult, [sg.b, pb_.b], [tmpm.b])
                        if nn < 3:
                            fw.tt("pool", acc.t[:, 0:n], acc.t[:, 0:n], tmpm.t[:, 0:n], ALU.add, [acc.b, tmpm.b], [acc.b])
                        else:
                            fw.tt("pool", mixT.t[:, db, 0:n], acc.t[:, 0:n], tmpm.t[:, 0:n], ALU.add, [acc.b, tmpm.b], [mixT.b])
            for i in tiles:
                xt = xts[i % 2]
                src = C.h0 if l == 0 else C.hbuf
                fw.dma("sp", xt.t[:, :], src[i * 128:(i + 1) * 128, :], reads=([] if l == 0 else [C.hb[i]]), writes=[xt.b])
                sub = slice(i * 128 - t0, i * 128 - t0 + 128)
                for dh in range(2):
                    pst = ps[4 + dh]
                    for db in range(8):
                        fw.mm(pst.t[:, :], mixT.t[:, db, sub], WO.t[:, db, dh * 512:(dh + 1) * 512], db == 0, db == 7, [mixT.b, WO.b], [pst.b])
                rms_epilogue(fw, C, ps[4], ps[5], nw1, xt, wk2)
                fw.dma("sp", C.hbuf[i * 128:(i + 1) * 128, :], xt.t[:, :], reads=[xt.b], writes=[C.hb[i]])
                norm_rows_to_T(fw, C, ph, xt.t[:, :], xt.b, nw2, C.uT, C.uTb, i, wk)
        fw.barrier()


def phase_mlp(fw, C, l, last):
    ps = C.ps
    with ExitStack() as ph:
        WU = fw.tile(ph, "WU", [128, 8, 4096], BF16)
        load_w(fw, C.w_up[l], WU, 0, 4096, 8)
        WD = fw.tile(ph, "WD", [128, 32, 1024], BF16)
        load_w(fw, C.w_down[l], WD, 0, 1024, 32)
        nw = fw.tile(ph, "nw3", [128, D], F32)
        fw.dma("sp", nw.t[:, :], C.norm_post_mlp[l].partition_broadcast(128), writes=[nw.b])
        hT = fw.tile(ph, "hT", [128, 32, 256], BF16)
        rl = [fw.tile(ph, "rl", [128, 512], BF16) for _ in range(2)]
        xts = [fw.tile(ph, "pxt", [128, D], F32)] * 2
        ptmp = fw.tile(ph, "ptmp", [128, D], F32)
        wk2 = (ptmp, fw.tile(ph, "ss", [128, 4], F32), ptmp)
        k = 0
        for (t0, n) in MGS:
            tiles = list(range(t0 // 128, (t0 + n) // 128))
            for fp in range(16):
                pst = ps[k % 4]
                r = rl[k % 2]
                k += 1
                for j in range(2):
                    ffc = 2 * fp + j
                    for kc in range(8):
                        fw.mm(pst.t[:, j * 256:j * 256 + n], WU.t[:, kc, ffc * 128:(ffc + 1) * 128], C.uT.t[:, kc, t0:t0 + n], kc == 0, kc == 7,
                              [WU.b] + [C.uTb[i] for i in tiles], [pst.b])
                pv = pst.t[:, :].rearrange("p (j t) -> p j t", j=2)[:, :, 0:n]
                rv = r.t[:, :].rearrange("p (j t) -> p j t", j=2)[:, :, 0:n]
                fw.act(rv, pv, AF.Relu, [pst.b], [r.b])
                fw.tt("pool" if fp % 2 else "dve", hT.t[:, 2 * fp:2 * fp + 2, 0:n], rv, rv, ALU.mult, [r.b], [hT.b])
            for i in tiles:
                xt = xts[i % 2]
                fw.dma("sp", xt.t[:, :], C.hbuf[i * 128:(i + 1) * 128, :], reads=[C.hb[i]], writes=[xt.b])
                sub = slice(i * 128 - t0, i * 128 - t0 + 128)
                for dh in range(2):
                    pst = ps[4 + dh]
                    for ffc in range(32):
                        fw.mm(pst.t[:, :], hT.t[:, ffc, sub], WD.t[:, ffc, dh * 512:(dh + 1) * 512], ffc == 0, ffc == 31, [hT.b, WD.b], [pst.b])
                rms_epilogue(fw, C, ps[4], ps[5], nw, xt, wk2)
                if not last:
                    fw.dma("sp", C.hbuf[i * 128:(i + 1) * 128, :], xt.t[:, :], reads=[xt.b], writes=[C.hb[i]])
                else:
                    if C.debug:
                        fw.dma("sp", C.hbuf[i * 128:(i + 1) * 128, :], xt.t[:, :], reads=[xt.b], writes=[C.hb[i]])
                    lo = max(i * 128, 16)
                    hi = min((i + 1) * 128, T)
                    if hi > lo:
                        fw.dma("sp", C.out[lo - 16:hi - 16, :], xt.t[lo - i * 128:hi - i * 128, :], reads=[xt.b], writes=[C.outb])
        fw.barrier()


def build(debug=False, n_layers=DEPTH, phases=None):
    nc = bass.Bass("TRN2", target_bir_lowering=False)
    C = Ctx()
    C.debug = debug

    def din(name, shape):
        return nc.dram_tensor(name, list(shape), F32, kind="ExternalInput").ap()

    C.h0 = din("h0", [TP, D])
    C.consts_d = din("consts", [128, CO_TOTAL[0]])
    C.w_in = din("w_in", [4, D, N_IN])
    C.w_branch = din("w_branch", [4, 4, 512, D])
    C.w_out = din("w_out", [4, D, D])
    C.w_up = din("w_up", [4, D, 4 * D])
    C.w_down = din("w_down", [4, 4 * D, D])
    C.s5_glu_w = din("s5_glu_w", [4, 512, 1024])
    for nm in ("norm_pre_mix", "norm_post_mix", "norm_pre_mlp", "norm_post_mlp"):
        setattr(C, nm, din(nm, [4, D]))
    for nm in ("ret_gn_w", "ssd_norm_w", "hgrn_norm_w", "ssd_dsk"):
        setattr(C, nm, din(nm, [4, 512]))
    C.ssd_dt_bias = din("ssd_dt_bias", [4, 8])
    C.ssd_a_log = din("ssd_a_log", [4, 8])
    C.lbT = din("lbT", [128, 4, 4])
    C.ssd_cw = din("ssd_cw", [4, 128, 8, 4])
    C.ssd_cb = din("ssd_cb", [4, 128, 8])
    C.s5_small = din("s5_small", [4, 128, 48])
    for nm in ("s5_bre", "s5_bim", "s5_cre", "s5_cim"):
        setattr(C, nm, din(nm, [4, 128, 16, 128]))
    C.s5_dT = din("s5_dT", [4, 128, 4])
    C.s5_gbT = din("s5_gbT", [4, 128, 8])
    C.out = nc.dram_tensor("out", [2048, D], F32, kind="ExternalOutput").ap()
    sk = "ExternalOutput" if debug else "Internal"
    C.hbuf = nc.dram_tensor("hbuf", [TP, D], F32, kind=sk).ap()
    C.YT = nc.dram_tensor("YT", [16, 128, TP], BF16, kind=sk).ap()
    if debug:
        C.dbg5 = nc.dram_tensor("dbg5", [128, 2784 + 137 * 48], F32, kind="ExternalOutput").ap()
    C.hb = [Buf("hb%d" % i) for i in range(NT)]
    C.YTb = [[Buf("yt%d_%d" % (m, i)) for i in range(NT)] for m in range(4)]
    C.outb = Buf("out")
    with ExitStack() as es:
        fw = FW(nc, es)
        C.ps = [TL(es.enter_context(nc.psum_tensor("ps%d" % j, [128, 512], F32)), "ps%d" % j) for j in range(6)]
        C.pb = [TL(es.enter_context(nc.psum_tensor("pb%d" % j, [128, 1024], BF16)), "pb%d" % j) for j in range(2)]
        C.consts = fw.tile(es, "consts", [128, CO_TOTAL[0]], F32)
        fw.dma("sp", C.consts.t[:, :], C.consts_d[:, :], writes=[C.consts.b])
        C.identb = fw.tile(es, "identb", [128, 128], BF16)
        C.identf = fw.tile(es, "identf", [128, 128], F32)
        fw.cp("dve", C.identb.t[:, :], cslice(C, "ident"), [C.consts.b], [C.identb.b])
        fw.cp("dve", C.identf.t[:, :], cslice(C, "ident"), [C.consts.b], [C.identf.b])
        C.uT = fw.tile(es, "uT", [128, 8, TP], BF16)
        C.uTb = [Buf("uT%d" % i) for i in range(NT)]
        C.lb_all = fw.tile(es, "lb_all", [128, 4, 4], F32)
        C.oml_all = fw.tile(es, "oml_all", [128, 4, 4], F32)
        lbe = fw.tile(es, "lbe", [128, 4, 4], F32)
        lsum = fw.tile(es, "lsum", [128, 4], F32)
        R = [lbe.b, lsum.b, C.lb_all.b, C.oml_all.b]
        fw.dma("sp", lbe.t[:, :, :], C.lbT[:, :, :], writes=[lbe.b])
        fw.act(lbe.t[:, :, :], lbe.t[:, :, :], AF.Exp, R, R)
        fw.tt("dve", lsum.t[:, :], lbe.t[:, 0, :], lbe.t[:, 1, :], ALU.add, R, R)
        fw.tt("dve", lsum.t[:, :], lsum.t[:, :], lbe.t[:, 2, :], ALU.add, R, R)
        fw.tt("dve", lsum.t[:, :], lsum.t[:, :], lbe.t[:, 3, :], ALU.add, R, R)
        fw.op("dve", lambda g: g.reciprocal(lsum.t[:, :], lsum.t[:, :]), R, R)
        fw.memset("dve", C.lb_all.t[:, 0, :], 0.0, R)
        for ll in range(1, 4):
            fw.tt("dve", lbe.t[:, ll, :], lbe.t[:, ll, :], lsum.t[:, :], ALU.mult, R, R)
            fw.tt("dve", C.lb_all.t[:, ll, :], C.lb_all.t[:, ll - 1, :], lbe.t[:, ll, :], ALU.add, R, R)
        fw.ts("dve", C.oml_all.t[:, :, :], C.lb_all.t[:, :, :], -1.0, 1.0, ALU.mult, ALU.add, R, R)
        fw.barrier()
        allp = ("norm", "ret", "s5", "ssd", "hg", "merge", "mlp")
        for l in range(n_layers):
            for pn in allp:
                if phases is not None and pn not in phases:
                    continue
                if pn == "norm":
                    phase_norm1(fw, C, l)
                elif pn == "ret":
                    phase_ret(fw, C, l)
                elif pn == "s5":
                    phase_s5(fw, C, l)
                elif pn == "ssd":
                    phase_ssd(fw, C, l)
                elif pn == "hg":
                    phase_hg(fw, C, l)
                elif pn == "merge":
                    phase_merge(fw, C, l)
                elif pn == "mlp":
                    phase_mlp(fw, C, l, last=(l == n_layers - 1))
        fw.barrier()
        C.n_inst, C.n_wait = fw.n_inst, fw.n_wait
    return nc, C


CO_TOTAL = [0]
_CONSTS = None


def get_consts():
    global _CONSTS
    if _CONSTS is None:
        _CONSTS = host_consts()
        CO_TOTAL[0] = _CONSTS.shape[1]
    return _CONSTS


def make_in_maps(inp):
    consts = get_consts()
    P = host_params(inp)
    f = lambda a: np.ascontiguousarray(np.asarray(a, np.float32))
    shared = {"consts": consts}
    for nm in ("w_in", "w_branch", "w_out", "w_up", "w_down", "s5_glu_w", "norm_pre_mix", "norm_post_mix", "norm_pre_mlp",
               "norm_post_mlp", "ret_gn_w", "ssd_norm_w", "hgrn_norm_w", "ssd_dt_bias", "ssd_a_log"):
        shared[nm] = f(inp[nm])
    for nm in ("lbT", "ssd_cw", "ssd_cb", "ssd_dsk", "s5_small", "s5_bre", "s5_bim", "s5_cre", "s5_cim", "s5_dT", "s5_gbT"):
        shared[nm] = P[nm]
    x = np.asarray(inp["x"], np.float32)
    meta = np.asarray(inp["meta_tokens"], np.float32)
    maps = []
    for b in range(x.shape[0]):
        h0 = np.zeros((TP, D), np.float32)
        h0[0:16] = meta
        h0[16:T] = x[b]
        m = dict(shared)
        m["h0"] = h0
        maps.append(m)
    return maps


_NC = None


def kernel(**inputs):
    global _NC
    maps = make_in_maps(inputs)
    if _NC is None:
        _NC = build()[0]
    res = run_bass_kernel_spmd(_NC, maps, core_ids=list(range(len(maps))))
    return np.stack([np.asarray(r["out"], np.float32) for r in res.results], axis=0)
```

```python
import math
import os
import numpy as np
from contextlib import ExitStack
import concourse.bass as bass
import concourse.mybir as mybir
from concourse.bass_utils import run_bass_kernel_spmd

F32 = mybir.dt.float32
BF16 = mybir.dt.bfloat16
ALU = mybir.AluOpType
AF = mybir.ActivationFunctionType
AX = mybir.AxisListType

DEPTH = 4
D = 1024
T = 2064
NT = 17
TP = NT * 128
EPS = 1e-6
N_IN = 9736
C_RET, C_S5, C_SSD, C_HG, C_GATE = 0, 1536, 2048, 3592, 5640
GAMMA = [1.0 - 2.0 ** (-5.0 - h) for h in range(4)]


class Buf:
    __slots__ = ("name", "w", "r")

    def __init__(self, name=""):
        self.name = name
        self.w = None
        self.r = []


class TL:
    def __init__(self, t, name):
        self.t = t
        self.b = Buf(name)


LAZY_PE_SIGNAL = os.environ.get("LAZY_PE", "0") == "1"


class VW:
    def __init__(self, t, b):
        self.t = t
        self.b = b


class FW:
    N_DMA_SEMS = 16

    def __init__(self, nc, es):
        self.nc = nc
        self.es = es
        self.eng = {"pe": nc.tensor, "dve": nc.vector, "act": nc.scalar, "pool": nc.gpsimd, "sp": nc.sync}
        self.sems = {}
        self.cnt = {}
        for e in ("pe", "dve", "act", "pool"):
            self.sems[e] = es.enter_context(nc.semaphore("s_" + e))
            self.cnt[e] = 0
        self.dma_keys = {}
        self.dma_rr = {}
        for q in ("sp", "pool"):
            ks = []
            for i in range(self.N_DMA_SEMS):
                k = "d_%s_%d" % (q, i)
                self.sems[k] = es.enter_context(nc.semaphore(k))
                self.cnt[k] = 0
                ks.append(k)
            self.dma_keys[q] = ks
            self.dma_rr[q] = 0
        self.known = {e: {} for e in self.eng}
        self.n_inst = 0
        self.n_wait = 0
        self.uid = 0
        self.pe_last = None
        self.pe_unsig = False
        self.n_sig = 0

    def tile(self, stack, name, shape, dt):
        self.uid += 1
        nm = "%s_%d" % (name, self.uid)
        return TL(stack.enter_context(self.nc.sbuf_tensor(nm, list(shape), dt)), nm)

    def _wait(self, e, ev):
        if ev is None:
            return
        k, v = ev
        if e == "pe" and k == "pe":
            return
        if k == "pe" and v > self.cnt["pe"]:
            self._flush_pe()
        kn = self.known[e]
        if kn.get(k, 0) >= v:
            return
        self.eng[e].wait_ge(self.sems[k], v)
        kn[k] = v
        self.n_wait += 1

    def _deps(self, e, reads, writes):
        for b in reads:
            self._wait(e, b.w)
        for b in writes:
            self._wait(e, b.w)
            for ev in b.r:
                self._wait(e, ev)

    def _mark(self, ev, reads, writes):
        for b in reads:
            b.r.append(ev)
        for b in writes:
            b.w = ev
            b.r = []

    def _flush_pe(self):
        if self.pe_unsig:
            self.cnt["pe"] += 1
            self.pe_last.then_inc(self.sems["pe"], 1)
            self.pe_unsig = False
            self.n_sig += 1

    def op(self, e, fn, reads=(), writes=()):
        self._deps(e, reads, writes)
        ins = fn(self.eng[e])
        if e == "pe" and LAZY_PE_SIGNAL:
            self.pe_last = ins
            self.pe_unsig = True
            ev = (e, self.cnt[e] + 1)
        else:
            self.cnt[e] += 1
            ins.then_inc(self.sems[e], 1)
            ev = (e, self.cnt[e])
        self._mark(ev, reads, writes)
        self.n_inst += 1
        return ev

    def dma(self, q, out, in_, reads=(), writes=(), **kw):
        self._deps(q, reads, writes)
        ks = self.dma_keys[q]
        k = ks[self.dma_rr[q] % len(ks)]
        self.dma_rr[q] += 1
        if self.cnt[k] > 0:
            self._wait(q, (k, self.cnt[k]))
        ins = self.eng[q].dma_start(out=out, in_=in_, **kw)
        self.cnt[k] += 16
        ins.then_inc(self.sems[k], 16)
        ev = (k, self.cnt[k])
        self._mark(ev, reads, writes)
        self.n_inst += 1
        return ev

    def barrier(self, engines=("pe", "dve", "act", "pool", "sp")):
        self._flush_pe()
        for e in engines:
            for k, v in self.cnt.items():
                if v > 0:
                    self._wait(e, (k, v))

    def tt(self, e, out, a, b, op, R, W):
        return self.op(e, lambda g: g.tensor_tensor(out, a, b, op), R, W)

    def ts(self, e, out, a, s1, s2, op0, op1, R, W):
        if s2 is None:
            return self.op(e, lambda g: g.tensor_scalar(out, a, s1, None, op0=op0), R, W)
        return self.op(e, lambda g: g.tensor_scalar(out, a, s1, s2, op0=op0, op1=op1), R, W)

    def stt(self, e, out, in0, sc, in1, op0, op1, R, W):
        e = "dve"
        return self.op(e, lambda g: g.scalar_tensor_tensor(out, in0, sc, in1, op0=op0, op1=op1), R, W)

    def cp(self, e, out, in_, R, W):
        if e == "act":
            return self.op(e, lambda g: g.copy(out, in_), R, W)
        return self.op(e, lambda g: g.tensor_copy(out, in_), R, W)

    def act(self, out, in_, func, R, W, bias=None, scale=None, accum=None):
        kw = {}
        if bias is not None:
            kw["bias"] = bias
        if scale is not None:
            kw["scale"] = scale
        if accum is not None:
            kw["accum_out"] = accum
        return self.op("act", lambda g: g.activation(out, in_, func, **kw), R, W)

    def mm(self, out, lhsT, rhs, start, stop, R, W):
        return self.op("pe", lambda g: g.matmul(out, lhsT, rhs, start=start, stop=stop), R, W)

    def tr(self, out, in_, ident, R, W):
        return self.op("pe", lambda g: g.transpose(out, in_, ident), R, W)

    def memset(self, e, ap, val, W):
        return self.op(e, lambda g: g.memset(ap, val), (), W)


CO = {}


def _pack(items):
    off = 0
    cols = []
    for name, arr in items:
        arr = np.asarray(arr, np.float32).reshape(128, -1)
        CO[name] = (off, arr.shape[1])
        off += arr.shape[1]
        cols.append(arr)
    return np.ascontiguousarray(np.concatenate(cols, axis=1))


def host_consts():
    s = np.arange(128)[:, None]
    t = np.arange(128)[None, :]
    ident = (s == t).astype(np.float32)
    triu = (s <= t).astype(np.float32)
    negm = np.where(s <= t, 0.0, -30000.0).astype(np.float32)
    ones = np.ones((128, 128), np.float32)
    retdt = np.zeros((128, 4, 128), np.float64)
    qdec = np.zeros((128, 4, 64), np.float64)
    kdec = np.zeros((128, 4, 64), np.float64)
    for h in range(4):
        g = GAMMA[h]
        retdt[:, h, :] = np.where(s <= t, 0.125 * g ** np.maximum(t - s, 0), 0.0)
        qdec[:, h, :] = (g ** (np.arange(128) + 1.0))[:, None]
        kdec[:, h, :] = (0.125 * g ** (127.0 - np.arange(128)))[:, None]
    half = 32
    inv_freq = (10000.0 ** (-np.arange(half, dtype=np.float32) / half)).astype(np.float32)
    pos = (np.arange(NT)[None, :] * 128 + np.arange(128)[:, None]).astype(np.float32)
    ang = pos[:, :, None] * inv_freq[None, None, :]
    cos = np.cos(ang).astype(np.float32)
    sin = np.sin(ang).astype(np.float32)
    halfpi = np.full((128, 1), math.pi / 2, np.float32)
    mvals = np.array(list(range(17)) + [0.5], np.float32)
    mtab = np.broadcast_to(mvals[None, :, None], (128, 18, 16))
    return _pack([("mtab", mtab), ("ident", ident), ("triu", triu), ("negm", negm), ("ones", ones), ("retdt", retdt),
                  ("qdec", qdec), ("kdec", kdec), ("cos", cos), ("sin", sin), ("halfpi", halfpi)])


def host_params(inp):
    P = {}
    f = lambda a: np.ascontiguousarray(np.asarray(a, np.float32))
    P["lbT"] = f(np.asarray(inp["hgrn_lb"]).reshape(4, 4, 128).transpose(2, 0, 1))
    P["ssd_cw"] = f(np.asarray(inp["ssd_conv_w"]).reshape(4, 4, 8, 128).transpose(0, 3, 2, 1))
    P["ssd_cb"] = f(np.asarray(inp["ssd_conv_b"]).reshape(4, 8, 128).transpose(0, 2, 1))
    P["ssd_dsk"] = f(np.repeat(np.asarray(inp["ssd_d"]), 64, axis=1))
    def pl_small(a):
        a = np.asarray(a).reshape(4, 16, 2, 64)
        return a.transpose(0, 2, 3, 1).reshape(4, 128, 16)
    ls = np.broadcast_to(np.asarray(inp["s5_log_step"])[:, :, None], (4, 32, 64))
    P["s5_small"] = f(np.concatenate([pl_small(inp["s5_lam_re"]), pl_small(inp["s5_lam_im"]), pl_small(ls)], axis=2))
    def pl_b(b):
        b = np.asarray(b)
        out = np.zeros((4, 2, 64, 16, 8, 16), np.float32)
        for g in range(32):
            out[:, g % 2, :, g // 2, g % 8, :] = b[:, g]
        return out.reshape(4, 128, 16, 128)
    def pl_c(c):
        c = np.asarray(c)
        out = np.zeros((4, 2, 64, 16, 8, 16), np.float32)
        for g in range(32):
            out[:, g % 2, :, g // 2, g % 8, :] = c[:, g].transpose(0, 2, 1)
        return out.reshape(4, 128, 16, 128)
    P["s5_bre"] = pl_b(inp["s5_b_re"])
    P["s5_bim"] = pl_b(inp["s5_b_im"])
    P["s5_cre"] = pl_c(inp["s5_c_re"])
    P["s5_cim"] = pl_c(inp["s5_c_im"])
    P["s5_dT"] = f(np.asarray(inp["s5_d"]).reshape(4, 4, 128).transpose(0, 2, 1))
    P["s5_gbT"] = f(np.asarray(inp["s5_glu_b"]).reshape(4, 8, 128).transpose(0, 2, 1))
    return P


class Ctx:
    pass


def cslice(C, name, a=None, b=None):
    off, n = CO[name]
    if a is None:
        return C.consts.t[:, off:off + n]
    return C.consts.t[:, off + a:off + b]


def load_w(fw, src2d, dst, c0, c1, kcs, R=(), chunk_bufs=None):
    v = src2d.rearrange("(kc p) n -> p kc n", p=128)
    step = int(os.environ.get("LW_CSTEP", "2048"))
    kstep = min(kcs, int(os.environ.get("LW_KSTEP", "8")))
    for k0 in range(0, kcs, kstep):
        for a in range(c0, c1, step):
            b = min(a + step, c1)
            wb = dst.b if chunk_bufs is None else chunk_bufs[(a - c0) // step]
            fw.dma("pool", dst.t[:, k0:k0 + kstep, a - c0:b - c0], v[:, k0:k0 + kstep, a:b], reads=R, writes=[wb])


def rstd_from_ss(fw, C, ss_ap, out_ap, n, R, W):
    fw.ts("dve", out_ap, ss_ap, 1.0 / n, EPS, ALU.mult, ALU.add, R, W)
    fw.act(out_ap, out_ap, AF.Sqrt, W, W)
    fw.op("dve", lambda g: g.reciprocal(out_ap, out_ap), W, W)


def norm_rows_to_T(fw, C, ph, x_ap, xb, nw, dstT, dst_bufs, i, wk):
    junk, ss, ub = wk
    fw.act(junk.t[:, :], x_ap, AF.Square, [xb], [junk.b, ss.b], accum=ss.t[:, 0:1])
    rstd_from_ss(fw, C, ss.t[:, 0:1], ss.t[:, 1:2], 1024.0, [ss.b], [ss.b])
    fw.stt("dve", ub.t[:, :], x_ap, ss.t[:, 1:2], nw.t[:, :], ALU.mult, ALU.mult, [xb, ss.b, nw.b], [ub.b])
    pb = C.pb[i % 2]
    for kc in range(8):
        fw.tr(pb.t[:, kc * 128:(kc + 1) * 128], ub.t[:, kc * 128:(kc + 1) * 128], C.identb.t[:, :], [ub.b, C.identb.b], [pb.b])
    fw.cp("act" if i % 2 else "dve", dstT.t[:, :, i * 128:(i + 1) * 128],
          pb.t[:, :].rearrange("p (k t) -> p k t", k=8), [pb.b], [dst_bufs[i]])


def phase_norm1(fw, C, l):
    with ExitStack() as ph:
        nw = fw.tile(ph, "nw", [128, D], F32)
        fw.dma("sp", nw.t[:, :], C.norm_pre_mix[l].partition_broadcast(128), writes=[nw.b])
        xts = [fw.tile(ph, "xt", [128, D], F32) for _ in range(2)]
        wk = (fw.tile(ph, "junk", [128, D], BF16), fw.tile(ph, "ss", [128, 2], F32), fw.tile(ph, "ub", [128, D], BF16))
        for i in range(NT):
            xt = xts[i % 2]
            src = C.h0 if l == 0 else C.hbuf
            fw.dma("sp", xt.t[:, :], src[i * 128:(i + 1) * 128, :], reads=([] if l == 0 else [C.hb[i]]), writes=[xt.b])
            norm_rows_to_T(fw, C, ph, xt.t[:, :], xt.b, nw, C.uT, C.uTb, i, wk)
        fw.barrier()


ILV_OFF = set(os.environ.get("NOILV", "ret,ssd").split(","))
CUR_PHASE = [""]


def interleave(gb, ga):
    if CUR_PHASE[0] in ILV_OFF:
        for g in (ga, gb):
            if g is not None:
                for _ in g:
                    pass
        return
    done_a = ga is None
    done_b = False
    while not (done_a and done_b):
        if not done_b:
            try:
                next(gb)
            except StopIteration:
                done_b = True
        if not done_a:
            try:
                next(ga)
            except StopIteration:
                done_a = True


def interleave_n(gens):
    gens = list(gens)
    if CUR_PHASE[0] in ILV_OFF:
        for g in reversed(gens):
            for _ in g:
                pass
        return
    while gens:
        for g in list(gens):
            try:
                next(g)
            except StopIteration:
                gens.remove(g)


def chain_gens(*gens):
    for g in gens:
        if g is not None:
            yield from g


def emit_yT(fw, C, yo, yT, mix, i):
    pb = C.pb[1]
    for cb in range(4):
        fw.tr(pb.t[:, cb * 128:(cb + 1) * 128], yo.t[:, cb * 128:(cb + 1) * 128], C.identb.t[:, :], [yo.b, C.identb.b], [pb.b])
    fw.cp("act", yT.t[:, :, :], pb.t[:, 0:512].rearrange("p (c t) -> p c t", c=4), [pb.b], [yT.b])
    fw.dma("sp", C.YT[mix * 4:(mix + 1) * 4, :, i * 128:(i + 1) * 128].rearrange("c p t -> p c t"), yT.t[:, :, :],
           reads=[yT.b], writes=[C.YTb[mix][i]])


def tok_mm(fw, C, ps, Wt, c0, n, i, ncols=None):
    for kc in range(8):
        fw.mm(ps.t[:, 0:n], C.uT.t[:, kc, i * 128:(i + 1) * 128], Wt.t[:, kc, c0:c0 + n], kc == 0, kc == 7,
              [C.uTb[i], Wt.b], [ps.b])


def phase_ret(fw, C, l):
    CUR_PHASE[0] = "ret"
    with ExitStack() as ph:
        WR = fw.tile(ph, "WR", [128, 8, 1536], BF16)
        load_w(fw, C.w_in[l], WR, C_RET, C_RET + 1536, 8)
        gnw = fw.tile(ph, "gnw", [128, 512], F32)
        fw.dma("sp", gnw.t[:, :], C.ret_gn_w[l].partition_broadcast(128), writes=[gnw.b])
        S = fw.tile(ph, "rS", [64, 4, 128], F32)
        Sb = fw.tile(ph, "rSb", [64, 4, 128], BF16)
        fw.memset("dve", S.t[:, :, :], 0.0, [S.b])
        fw.memset("dve", Sb.t[:, :, :], 0.0, [Sb.b])
        qkr_2 = [fw.tile(ph, "qkr", [128, 8, 64], F32) for _ in range(2)]
        t1_2 = [fw.tile(ph, "t1", [128, 8, 32], F32) for _ in range(2)]
        t2_2 = [fw.tile(ph, "t2", [128, 8, 32], F32) for _ in range(2)]
        qb_2 = [fw.tile(ph, "qb", [128, 256], BF16) for _ in range(2)]
        qdb_2 = [fw.tile(ph, "qdb", [128, 256], BF16) for _ in range(2)]
        kb_2 = [fw.tile(ph, "kb", [128, 256], BF16) for _ in range(2)]
        khb_2 = [fw.tile(ph, "khb", [128, 256], BF16) for _ in range(2)]
        qkT_2 = [fw.tile(ph, "qkT", [64, 12, 128], BF16) for _ in range(2)]
        vb_2 = [fw.tile(ph, "vb", [128, 512], BF16) for _ in range(2)]
        sg_2 = [fw.tile(ph, "sg", [128, 512], F32) for _ in range(2)]
        PT_2 = [fw.tile(ph, "PT", [128, 512], BF16) for _ in range(2)]
        ysq_2 = [fw.tile(ph, "ysq", [128, 512], F32) for _ in range(2)]
        st_2 = [fw.tile(ph, "st", [128, 16], F32) for _ in range(2)]
        yn_2 = [fw.tile(ph, "yn", [128, 512], F32) for _ in range(2)]
        yo_2 = [fw.tile(ph, "yo", [128, 512], BF16) for _ in range(2)]
        yT_2 = [fw.tile(ph, "yT", [128, 4, 128], BF16) for _ in range(2)]
        ps_qk, ps_v, ps_g, ps_s, ps_y, ps_st = C.ps[0:6]
        cb_ = C.consts.b
        def stA(i):
                qkr, t1, t2, qb, qdb, kb, khb, qkT, vb, sg, PT, ysq, st, yn, yo, yT = qkr_2[i % 2], t1_2[i % 2], t2_2[i % 2], qb_2[i % 2], qdb_2[i % 2], kb_2[i % 2], khb_2[i % 2], qkT_2[i % 2], vb_2[i % 2], sg_2[i % 2], PT_2[i % 2], ysq_2[i % 2], st_2[i % 2], yn_2[i % 2], yo_2[i % 2], yT_2[i % 2]
                tok_mm(fw, C, ps_qk, WR, 0, 512, i)
                yield
                tok_mm(fw, C, ps_v, WR, 512, 512, i)
                yield
                tok_mm(fw, C, ps_g, WR, 1024, 512, i)
                yield
                qv = ps_qk.t[:, :].rearrange("p (h d) -> p h d", d=64)
                x1, x2 = qv[:, :, 0:32], qv[:, :, 32:64]
                co, cn = CO["cos"][0], CO["sin"][0]
                cosb = C.consts.t[:, co + i * 32:co + (i + 1) * 32].unsqueeze(1).to_broadcast([128, 8, 32])
                sinb = C.consts.t[:, cn + i * 32:cn + (i + 1) * 32].unsqueeze(1).to_broadcast([128, 8, 32])
                fw.tt("dve", t1.t[:, :, :], x1, cosb, ALU.mult, [ps_qk.b, cb_], [t1.b])
                fw.tt("dve", t2.t[:, :, :], x2, sinb, ALU.mult, [ps_qk.b, cb_], [t2.b])
                fw.tt("pool", qkr.t[:, :, 0:32], t1.t[:, :, :], t2.t[:, :, :], ALU.subtract, [t1.b, t2.b], [qkr.b])
                fw.tt("dve", t1.t[:, :, :], x1, sinb, ALU.mult, [ps_qk.b, cb_], [t1.b])
                fw.tt("dve", t2.t[:, :, :], x2, cosb, ALU.mult, [ps_qk.b, cb_], [t2.b])
                fw.tt("pool", qkr.t[:, :, 32:64], t1.t[:, :, :], t2.t[:, :, :], ALU.add, [t1.b, t2.b], [qkr.b])
                qf = qkr.t[:, 0:4, :].rearrange("p h d -> p (h d)")
                kf = qkr.t[:, 4:8, :].rearrange("p h d -> p (h d)")
                fw.cp("act", qb.t[:, :], qf, [qkr.b], [qb.b])
                fw.tt("pool", qdb.t[:, :], qf, cslice(C, "qdec"), ALU.mult, [qkr.b, cb_], [qdb.b])
                fw.cp("act", kb.t[:, :], kf, [qkr.b], [kb.b])
                fw.tt("pool", khb.t[:, :], kf, cslice(C, "kdec"), ALU.mult, [qkr.b, cb_], [khb.b])
                fw.cp("act", vb.t[:, :], ps_v.t[:, :], [ps_v.b], [vb.b])
                fw.act(sg.t[:, :], ps_g.t[:, :], AF.Silu, [ps_g.b], [sg.b])
        def stB(i):
                qkr, t1, t2, qb, qdb, kb, khb, qkT, vb, sg, PT, ysq, st, yn, yo, yT = qkr_2[i % 2], t1_2[i % 2], t2_2[i % 2], qb_2[i % 2], qdb_2[i % 2], kb_2[i % 2], khb_2[i % 2], qkT_2[i % 2], vb_2[i % 2], sg_2[i % 2], PT_2[i % 2], ysq_2[i % 2], st_2[i % 2], yn_2[i % 2], yo_2[i % 2], yT_2[i % 2]
                pb0, pb1 = C.pb
                for h in range(4):
                    fw.tr(pb0.t[0:64, h * 128:(h + 1) * 128], qb.t[:, h * 64:(h + 1) * 64], C.identb.t[:, :], [qb.b, C.identb.b], [pb0.b])
                    fw.tr(pb0.t[0:64, (4 + h) * 128:(5 + h) * 128], qdb.t[:, h * 64:(h + 1) * 64], C.identb.t[:, :], [qdb.b, C.identb.b], [pb0.b])
                    fw.tr(pb1.t[0:64, h * 128:(h + 1) * 128], kb.t[:, h * 64:(h + 1) * 64], C.identb.t[:, :], [kb.b, C.identb.b], [pb1.b])
                yield
                fw.cp("dve", qkT.t[:, 0:8, :], pb0.t[0:64, :].rearrange("p (j t) -> p j t", j=8), [pb0.b], [qkT.b])
                fw.cp("act", qkT.t[:, 8:12, :], pb1.t[0:64, 0:512].rearrange("p (j t) -> p j t", j=4), [pb1.b], [qkT.b])
                for h in range(4):
                    fw.mm(ps_s.t[:, h * 128:(h + 1) * 128], qkT.t[:, 8 + h, :], qkT.t[:, h, :], True, True, [qkT.b], [ps_s.b])
                yield
                fw.tt("dve", PT.t[:, :], ps_s.t[:, :], cslice(C, "retdt"), ALU.mult, [ps_s.b, cb_], [PT.b])
                for h in range(4):
                    hs = slice(h * 128, (h + 1) * 128)
                    fw.mm(ps_y.t[:, hs], PT.t[:, hs], vb.t[:, hs], True, False, [PT.b, vb.b], [ps_y.b])
                    fw.mm(ps_y.t[:, hs], qkT.t[:, 4 + h, :], Sb.t[:, h, :], False, True, [qkT.b, Sb.b], [ps_y.b])
                yield
                for h in range(4):
                    hs = slice(h * 128, (h + 1) * 128)
                    fw.mm(ps_st.t[0:64, hs], khb.t[:, h * 64:(h + 1) * 64], vb.t[:, hs], True, True, [khb.b, vb.b], [ps_st.b])
                yield
                for h in range(4):
                    hs = slice(h * 128, (h + 1) * 128)
                    fw.stt("dve", S.t[:, h, :], S.t[:, h, :], float(GAMMA[h] ** 128), ps_st.t[0:64, hs], ALU.mult, ALU.add,
                           [S.b, ps_st.b], [S.b])
                fw.cp("pool", Sb.t[:, :, :], S.t[:, :, :], [S.b], [Sb.b])
                yv = ps_y.t[:, :].rearrange("p (h e) -> p h e", h=4)
                fw.op("dve", lambda g: g.reduce_sum(st.t[:, 0:4], yv, axis=AX.X), [ps_y.b], [st.b])
                fw.act(ysq.t[:, :], ps_y.t[:, :], AF.Square, [ps_y.b], [ysq.b])
                fw.op("dve", lambda g: g.reduce_sum(st.t[:, 4:8], ysq.t[:, :].rearrange("p (h e) -> p h e", h=4), axis=AX.X), [ysq.b], [st.b])
                fw.ts("dve", st.t[:, 8:12], st.t[:, 0:4], 1.0 / 128, None, ALU.mult, None, [st.b], [st.b])
                fw.tt("dve", st.t[:, 0:4], st.t[:, 8:12], st.t[:, 8:12], ALU.mult, [st.b], [st.b])
                fw.stt("dve", st.t[:, 12:16], st.t[:, 4:8], 1.0 / 128, st.t[:, 0:4], ALU.mult, ALU.subtract, [st.b], [st.b])
                fw.ts("dve", st.t[:, 12:16], st.t[:, 12:16], EPS, None, ALU.add, None, [st.b], [st.b])
                fw.act(st.t[:, 12:16], st.t[:, 12:16], AF.Sqrt, [st.b], [st.b])
                fw.op("dve", lambda g: g.reciprocal(st.t[:, 12:16], st.t[:, 12:16]), [st.b], [st.b])
                for h in range(4):
                    hs = slice(h * 128, (h + 1) * 128)
                    fw.ts("dve", yn.t[:, hs], ps_y.t[:, hs], st.t[:, 8 + h:9 + h], st.t[:, 12 + h:13 + h], ALU.subtract, ALU.mult,
                          [ps_y.b, st.b], [yn.b])
                fw.tt("pool", yn.t[:, :], yn.t[:, :], gnw.t[:, :], ALU.mult, [yn.b, gnw.b], [yn.b])
                fw.tt("pool", yo.t[:, :], yn.t[:, :], sg.t[:, :], ALU.mult, [yn.b, sg.b], [yo.b])
                emit_yT(fw, C, yo, yT, 0, i)
                yield
        for _ in stA(0):
            pass
        for i in range(NT):
            interleave(stB(i), stA(i + 1) if i + 1 < NT else None)
        fw.barrier()


def phase_ssd(fw, C, l):
    CUR_PHASE[0] = "ssd"
    with ExitStack() as ph:
        WS = fw.tile(ph, "WS", [128, 8, 1544], BF16)
        load_w(fw, C.w_in[l], WS, C_SSD, C_SSD + 1544, 8)
        cw = fw.tile(ph, "cw", [128, 8, 4], F32)
        cbi = fw.tile(ph, "cbi", [128, 8], F32)
        dtb = fw.tile(ph, "dtb", [128, 8], F32)
        arow = fw.tile(ph, "arow", [128, 8], F32)
        dsk = fw.tile(ph, "dsk", [128, 512], F32)
        nw = fw.tile(ph, "snw", [128, 512], F32)
        fw.dma("sp", cw.t[:, :, :], C.ssd_cw[l], writes=[cw.b])
        fw.dma("sp", cbi.t[:, :], C.ssd_cb[l], writes=[cbi.b])
        fw.dma("sp", dtb.t[:, :], C.ssd_dt_bias[l].partition_broadcast(128), writes=[dtb.b])
        fw.dma("sp", arow.t[:, :], C.ssd_a_log[l].partition_broadcast(128), writes=[arow.b])
        fw.dma("sp", dsk.t[:, :], C.ssd_dsk[l].partition_broadcast(128), writes=[dsk.b])
        fw.dma("sp", nw.t[:, :], C.ssd_norm_w[l].partition_broadcast(128), writes=[nw.b])
        fw.act(arow.t[:, :], arow.t[:, :], AF.Exp, [arow.b], [arow.b])
        fw.ts("dve", arow.t[:, :], arow.t[:, :], -1.0, None, ALU.mult, None, [arow.b], [arow.b])
        S = fw.tile(ph, "sS", [128, 512], F32)
        Sb = fw.tile(ph, "sSb", [128, 512], BF16)
        fw.memset("dve", S.t[:, :], 0.0, [S.b])
        fw.memset("dve", Sb.t[:, :], 0.0, [Sb.b])
        xrawg = fw.tile(ph, "xrawg", [128, 8, 515], F32)
        fw.memset("pool", xrawg.t[:, :, :], 0.0, [xrawg.b])
        xcg = fw.tile(ph, "xcg", [128, 8, 512], F32)
        xcbg_2 = [fw.tile(ph, "xcbg", [128, 8, 512], BF16) for _ in range(2)]
        xs_2 = [fw.tile(ph, "xs", [128, 512], F32) for _ in range(2)]
        Btok_2 = [fw.tile(ph, "Btok", [128, 2, 128], BF16) for _ in range(2)]
        sz_2 = [fw.tile(ph, "sz", [128, 512], F32) for _ in range(2)]
        sm_2 = [fw.tile(ph, "sm", [128, 104], F32) for _ in range(2)]
        rhs8_2 = [fw.tile(ph, "rhs8", [128, 8, 128], F32) for _ in range(2)]
        decT_2 = [fw.tile(ph, "decT", [128, 8, 128], F32) for _ in range(2)]
        PT_2 = [fw.tile(ph, "sPT", [128, 8, 128], BF16) for _ in range(2)]
        xdt_2 = [fw.tile(ph, "xdt", [128, 512], BF16) for _ in range(2)]
        xdtw_2 = [fw.tile(ph, "xdtw", [128, 512], BF16) for _ in range(2)]
        ya_2 = [fw.tile(ph, "ya", [128, 512], F32) for _ in range(2)]
        tmp_2 = [fw.tile(ph, "stmp", [128, 512], F32) for _ in range(2)]
        yo_2 = [fw.tile(ph, "syo", [128, 512], BF16) for _ in range(2)]
        yT_2 = [fw.tile(ph, "syT", [128, 4, 128], BF16) for _ in range(2)]
        ps = C.ps
        pb0, pb1 = C.pb
        cb_ = C.consts.b
        XD, AXc, EX, LN, DT, LA, CUM, NCUM, ECUM, WW, EDEC, DW = [slice(8 * j, 8 * j + 8) for j in range(12)]
        triu = cslice(C, "triu")
        ones = cslice(C, "ones")
        def stG(g):
                t0 = g * 512
                n = min(512, TP - t0)
                tiles = list(range(t0 // 128, (t0 + n) // 128))
                xcbg = xcbg_2[g % 2]
                if g > 0:
                    fw.cp("pool", xrawg.t[:, :, 0:3], xrawg.t[:, :, 512:515], [xrawg.b], [xrawg.b])
                for cb in range(8):
                    pst = ps[cb % 2]
                    for kc in range(8):
                        fw.mm(pst.t[:, 0:n], WS.t[:, kc, 512 + cb * 128:512 + (cb + 1) * 128], C.uT.t[:, kc, t0:t0 + n], kc == 0, kc == 7,
                              [WS.b] + [C.uTb[j] for j in tiles], [pst.b])
                    fw.cp("act" if cb % 2 else "dve", xrawg.t[:, cb, 3:3 + n], pst.t[:, 0:n], [pst.b], [xrawg.b])
                for cb in range(8):
                    fw.ts("dve" if cb % 2 == 0 else "pool", xcg.t[:, cb, 0:n], xrawg.t[:, cb, 3:3 + n], cw.t[:, cb, 3:4], cbi.t[:, cb:cb + 1],
                          ALU.mult, ALU.add, [xrawg.b, cw.b, cbi.b], [xcg.b])
                    for j in (2, 1, 0):
                        fw.stt("dve", xcg.t[:, cb, 0:n], xrawg.t[:, cb, j:j + n], cw.t[:, cb, j:j + 1], xcg.t[:, cb, 0:n], ALU.mult, ALU.add,
                               [xrawg.b, cw.b, xcg.b], [xcg.b])
                fw.act(xcbg.t[:, :, 0:n], xcg.t[:, :, 0:n], AF.Silu, [xcg.b], [xcbg.b])
                yield
        def stA(i):
                xs, Btok, sz, sm, decT, PT, xdt, xdtw, ya, tmp, yo, yT = xs_2[i % 2], Btok_2[i % 2], sz_2[i % 2], sm_2[i % 2], decT_2[i % 2], PT_2[i % 2], xdt_2[i % 2], xdtw_2[i % 2], ya_2[i % 2], tmp_2[i % 2], yo_2[i % 2], yT_2[i % 2]
                xcb = VW(xcbg_2[(i // 4) % 2].t[:, :, (i % 4) * 128:(i % 4 + 1) * 128], xcbg_2[(i // 4) % 2].b)
                tok = slice(i * 128, (i + 1) * 128)
                tok_mm(fw, C, ps[2], WS, 0, 512, i)
                yield
                fw.act(sz.t[:, :], ps[2].t[:, :], AF.Silu, [ps[2].b], [sz.b])
                for kc in range(8):
                    fw.mm(ps[3].t[:, 0:8], C.uT.t[:, kc, tok], WS.t[:, kc, 1536:1544], kc == 0, kc == 7, [C.uTb[i], WS.b], [ps[3].b])
                yield
                smb = [sm.b]
                fw.tt("dve", sm.t[:, XD], ps[3].t[:, 0:8], dtb.t[:, :], ALU.add, [ps[3].b, dtb.b], smb)
                fw.ts("dve", sm.t[:, AXc], sm.t[:, XD], -1.0, None, ALU.mult, None, smb, smb)
                fw.tt("dve", sm.t[:, AXc], sm.t[:, AXc], sm.t[:, XD], ALU.max, smb, smb)
                fw.act(sm.t[:, EX], sm.t[:, AXc], AF.Exp, smb, smb, scale=-1.0)
                fw.ts("dve", sm.t[:, EX], sm.t[:, EX], 1.0, None, ALU.add, None, smb, smb)
                fw.act(sm.t[:, LN], sm.t[:, EX], AF.Ln, smb, smb)
                fw.stt("dve", sm.t[:, DT], sm.t[:, XD], 0.0, sm.t[:, LN], ALU.max, ALU.add, smb, smb)
                fw.tt("dve", sm.t[:, LA], sm.t[:, DT], arow.t[:, :], ALU.mult, smb + [arow.b], smb)
                fw.mm(ps[3].t[:, 16:24], triu, sm.t[:, LA], True, True, [cb_, sm.b], [ps[3].b])
                yield
                fw.mm(ps[3].t[:, 32:40], ones, sm.t[:, LA], True, True, [cb_, sm.b], [ps[3].b])
                yield
                fw.cp("dve", sm.t[:, CUM], ps[3].t[:, 16:24], [ps[3].b], smb)
                fw.ts("dve", sm.t[:, NCUM], sm.t[:, CUM], -1.0, None, ALU.mult, None, smb, smb)
                fw.act(sm.t[:, ECUM], sm.t[:, CUM], AF.Exp, smb, smb)
                fw.tt("dve", sm.t[:, WW], ps[3].t[:, 32:40], sm.t[:, CUM], ALU.subtract, [ps[3].b] + smb, smb)
                fw.act(sm.t[:, WW], sm.t[:, WW], AF.Exp, smb, smb)
                fw.act(sm.t[:, EDEC], ps[3].t[:, 32:40], AF.Exp, [ps[3].b], smb)
                fw.tt("dve", sm.t[:, DW], sm.t[:, DT], sm.t[:, WW], ALU.mult, smb, smb)
        def stB(i):
                xs, Btok, sz, sm, decT, PT, xdt, xdtw, ya, tmp, yo, yT = xs_2[i % 2], Btok_2[i % 2], sz_2[i % 2], sm_2[i % 2], decT_2[i % 2], PT_2[i % 2], xdt_2[i % 2], xdtw_2[i % 2], ya_2[i % 2], tmp_2[i % 2], yo_2[i % 2], yT_2[i % 2]
                xcb = VW(xcbg_2[(i // 4) % 2].t[:, :, (i % 4) * 128:(i % 4 + 1) * 128], xcbg_2[(i // 4) % 2].b)
                smb = [sm.b]
                pb0, pb1 = C.pb
                rhs8 = rhs8_2[i % 2]
                for cb in range(4):
                    fw.tr(pb0.t[:, cb * 128:(cb + 1) * 128], xcb.t[:, cb, :], C.identb.t[:, :], [xcb.b, C.identb.b], [pb0.b])
                yield
                fw.cp("dve", xs.t[:, :], pb0.t[:, 0:512], [pb0.b], [xs.b])
                for g in range(2):
                    fw.tr(pb1.t[:, g * 128:(g + 1) * 128], xcb.t[:, 4 + g, :], C.identb.t[:, :], [xcb.b, C.identb.b], [pb1.b])
                yield
                fw.cp("act", Btok.t[:, :, :], pb1.t[:, 0:256].rearrange("p (g n) -> p g n", g=2), [pb1.b], [Btok.b])
                fw.tt("dve", rhs8.t[:, :, :], triu.unsqueeze(1).to_broadcast([128, 8, 128]),
                      sm.t[:, LA].unsqueeze(2).to_broadcast([128, 8, 128]), ALU.mult, [cb_, sm.b], [rhs8.b])
                fw.mm(ps[4].t[:, :], ones, rhs8.t[:, 0:4, :].rearrange("p h t -> p (h t)"), True, True, [cb_, rhs8.b], [ps[4].b])
                fw.mm(ps[5].t[:, :], ones, rhs8.t[:, 4:8, :].rearrange("p h t -> p (h t)"), True, True, [cb_, rhs8.b], [ps[5].b])
                for hb in range(2):
                    fw.tt("dve", decT.t[:, 4 * hb:4 * hb + 4, :], ps[4 + hb].t[:, :].rearrange("p (h t) -> p h t", h=4),
                          sm.t[:, 56 + 4 * hb:60 + 4 * hb].unsqueeze(2).to_broadcast([128, 4, 128]), ALU.add, [ps[4 + hb].b, sm.b], [decT.b])
                fw.tt("pool", decT.t[:, :, :], decT.t[:, :, :], cslice(C, "negm").unsqueeze(1).to_broadcast([128, 8, 128]), ALU.add,
                      [decT.b, cb_], [decT.b])
                fw.act(decT.t[:, :, :], decT.t[:, :, :], AF.Exp, [decT.b], [decT.b])
                for g in range(2):
                    fw.mm(ps[4].t[:, g * 128:(g + 1) * 128], xcb.t[:, 4 + g, :], xcb.t[:, 6 + g, :], True, True, [xcb.b], [ps[4].b])
                yield
                for g in range(2):
                    fw.tt("dve", PT.t[:, 4 * g:4 * g + 4, :], decT.t[:, 4 * g:4 * g + 4, :],
                          ps[4].t[:, g * 128:(g + 1) * 128].unsqueeze(1).to_broadcast([128, 4, 128]), ALU.mult, [decT.b, ps[4].b], [PT.b])
                xsv = xs.t[:, :].rearrange("p (h e) -> p h e", h=8)
                fw.tt("pool", xdt.t[:, :].rearrange("p (h e) -> p h e", h=8), xsv, sm.t[:, DT].unsqueeze(2).to_broadcast([128, 8, 64]),
                      ALU.mult, [xs.b, sm.b], [xdt.b])
                fw.tt("pool", xdtw.t[:, :].rearrange("p (h e) -> p h e", h=8), xsv, sm.t[:, DW].unsqueeze(2).to_broadcast([128, 8, 64]),
                      ALU.mult, [xs.b, sm.b], [xdtw.b])
                for h in range(8):
                    fw.mm(ps[5].t[:, h * 64:(h + 1) * 64], PT.t[:, h, :], xdt.t[:, h * 64:(h + 1) * 64], True, True, [PT.b, xdt.b], [ps[5].b])
                yield
                for g in range(2):
                    fw.mm(ps[4].t[:, g * 256:(g + 1) * 256], xcb.t[:, 6 + g, :], Sb.t[:, g * 256:(g + 1) * 256], True, True, [xcb.b, Sb.b], [ps[4].b])
                yield
                fw.tt("dve", ya.t[:, :].rearrange("p (h e) -> p h e", h=8), ps[4].t[:, :].rearrange("p (h e) -> p h e", h=8),
                      sm.t[:, ECUM].unsqueeze(2).to_broadcast([128, 8, 64]), ALU.mult, [ps[4].b, sm.b], [ya.b])
                fw.tt("dve", ya.t[:, :], ya.t[:, :], ps[5].t[:, :], ALU.add, [ya.b, ps[5].b], [ya.b])
                fw.tt("pool", tmp.t[:, :], xs.t[:, :], dsk.t[:, :], ALU.mult, [xs.b, dsk.b], [tmp.b])
                fw.tt("pool", ya.t[:, :], ya.t[:, :], tmp.t[:, :], ALU.add, [ya.b, tmp.b], [ya.b])
                fw.tt("pool", ya.t[:, :], ya.t[:, :], sz.t[:, :], ALU.mult, [ya.b, sz.b], [ya.b])
                for g in range(2):
                    fw.mm(ps[5].t[:, g * 256:(g + 1) * 256], Btok.t[:, g, :], xdtw.t[:, g * 256:(g + 1) * 256], True, True,
                          [Btok.b, xdtw.b], [ps[5].b])
                yield
                Sv = S.t[:, :].rearrange("p (h e) -> p h e", h=8)
                fw.tt("dve", Sv, Sv, sm.t[:, EDEC].unsqueeze(2).to_broadcast([128, 8, 64]), ALU.mult, [S.b, sm.b], [S.b])
                fw.tt("dve", S.t[:, :], S.t[:, :], ps[5].t[:, :], ALU.add, [S.b, ps[5].b], [S.b])
                fw.cp("pool", Sb.t[:, :], S.t[:, :], [S.b], [Sb.b])
                fw.act(tmp.t[:, :], ya.t[:, :], AF.Square, [ya.b], [tmp.b])
                fw.op("dve", lambda g_: g_.reduce_sum(sm.t[:, 96:98], tmp.t[:, :].rearrange("p (g e) -> p g e", g=2), axis=AX.X), [tmp.b], smb)
                rstd_from_ss(fw, C, sm.t[:, 96:98], sm.t[:, 98:100], 256.0, smb, smb)
                for g in range(2):
                    gs = slice(g * 256, (g + 1) * 256)
                    fw.ts("dve", tmp.t[:, gs], ya.t[:, gs], sm.t[:, 98 + g:99 + g], None, ALU.mult, None, [ya.b, sm.b], [tmp.b])
                fw.tt("pool", yo.t[:, :], tmp.t[:, :], nw.t[:, :], ALU.mult, [tmp.b, nw.b], [yo.b])
                emit_yT(fw, C, yo, yT, 2, i)
                yield
        for st0 in (stG(0), stA(0)):
            for _ in st0:
                pass
        for i in range(NT):
            if i + 1 < NT and (i + 1) % 4 == 0:
                for _ in stG((i + 1) // 4):
                    pass
            interleave(stB(i), stA(i + 1) if i + 1 < NT else None)
        fw.barrier()


def phase_hg(fw, C, l):
    CUR_PHASE[0] = "hg"
    with ExitStack() as ph:
        WH = fw.tile(ph, "WH", [128, 8, 2048], BF16)
        load_w(fw, C.w_in[l], WH, C_HG, C_HG + 2048, 8)
        nw = fw.tile(ph, "hnw", [128, 512], F32)
        fw.dma("sp", nw.t[:, :], C.hgrn_norm_w[l].partition_broadcast(128), writes=[nw.b])
        S = fw.tile(ph, "hS", [128, 4, 128], F32)
        Sb = fw.tile(ph, "hSb", [128, 4, 128], BF16)
        PT_2 = [fw.tile(ph, "hPT", [128, 4, 128], BF16) for _ in range(2)]
        fw.memset("dve", S.t[:, :, :], 0.0, [S.b])
        fw.memset("dve", Sb.t[:, :, :], 0.0, [Sb.b])
        for PT in PT_2:
            fw.memset("pool", PT.t[:, :, :], 0.0, [PT.b])

        qg_2 = [fw.tile(ph, "hqg", [128, 4, 512], F32) for _ in range(2)]
        fg_2 = [fw.tile(ph, "hfg", [128, 4, 512], F32) for _ in range(2)]
        la_2 = [fw.tile(ph, "hla", [128, 4, 128], F32) for _ in range(2)]
        kT_2 = [fw.tile(ph, "hk", [128, 4, 128], F32) for _ in range(2)]
        cum_2 = [fw.tile(ph, "hcum", [128, 4, 128], F32) for _ in range(2)]
        ncb_2 = [fw.tile(ph, "hncb", [128, 4, 4], F32) for _ in range(2)]
        for nb_ in ncb_2:
            fw.memset("pool", nb_.t[:, :, :], 0.0, [nb_.b])
        eq_2 = [fw.tile(ph, "heq", [128, 4, 128], F32) for _ in range(2)]
        qd_2 = [fw.tile(ph, "hqd", [128, 4, 128], BF16) for _ in range(2)]
        qst_2 = [fw.tile(ph, "hqst", [128, 4, 128], BF16) for _ in range(2)]
        ek_2 = [fw.tile(ph, "hek", [128, 4, 128], F32) for _ in range(2)]
        Kt_2 = [fw.tile(ph, "hKt", [128, 4, 4, 128], BF16) for _ in range(2)]
        khT_2 = [fw.tile(ph, "hkhT", [128, 4, 128], BF16) for _ in range(2)]
        khat_2 = [fw.tile(ph, "hkhat", [128, 4, 128], BF16) for _ in range(2)]
        dec_2 = [fw.tile(ph, "hdec", [128, 4], F32) for _ in range(2)]
        vb_3 = [fw.tile(ph, "hvb", [128, 512], BF16) for _ in range(3)]
        sgate_3 = [fw.tile(ph, "hsg", [128, 512], F32) for _ in range(3)]
        ysq_2 = [fw.tile(ph, "hysq", [128, 512], F32) for _ in range(2)]
        st_2 = [fw.tile(ph, "hst", [128, 8], F32) for _ in range(2)]
        yn_2 = [fw.tile(ph, "hyn", [128, 512], F32) for _ in range(2)]
        yo_2 = [fw.tile(ph, "hyo", [128, 512], BF16) for _ in range(2)]
        yT_2 = [fw.tile(ph, "hyT", [128, 4, 128], BF16) for _ in range(2)]
        ps = C.ps
        pb0, pb1 = C.pb
        cb_ = C.consts.b
        lbc = C.lb_all.t[:, l, :]
        omc = C.oml_all.t[:, l, :]
        ones = cslice(C, "ones")
        triu = cslice(C, "triu")
        def stG(g):
                t0 = g * 512
                n = min(512, TP - t0)
                tiles = list(range(t0 // 128, (t0 + n) // 128))
                qg, fg = qg_2[g % 2], fg_2[g % 2]
                k = 0
                for h in range(4):
                    for (c0, dst, func) in ((0, qg, AF.Silu), (512, fg, AF.Sigmoid)):
                        pst = ps[0]
                        for kc in range(8):
                            fw.mm(pst.t[:, 0:n], WH.t[:, kc, c0 + h * 128:c0 + (h + 1) * 128], C.uT.t[:, kc, t0:t0 + n], kc == 0, kc == 7,
                                  [WH.b] + [C.uTb[j] for j in tiles], [pst.b])
                        fw.act(dst.t[:, h, 0:n], pst.t[:, 0:n], func, [pst.b], [dst.b])
                        yield
        def stA(i):
                PT = PT_2[i % 2]
                la, kT, cum, ncb, eq, qd, qst, ek, Kt, khT, khat, dec, vb, sgate, ysq, st, yn, yo, yT = la_2[i % 2], kT_2[i % 2], cum_2[i % 2], ncb_2[i % 2], eq_2[i % 2], qd_2[i % 2], qst_2[i % 2], ek_2[i % 2], Kt_2[i % 2], khT_2[i % 2], khat_2[i % 2], dec_2[i % 2], vb_3[i % 3], sgate_3[i % 3], ysq_2[i % 2], st_2[i % 2], yn_2[i % 2], yo_2[i % 2], yT_2[i % 2]
                tok_mm(fw, C, ps[2], WH, 1024, 512, i)
                yield
                tok_mm(fw, C, ps[3], WH, 1536, 512, i)
                yield
                fw.cp("act", vb.t[:, :], ps[2].t[:, :], [ps[2].b], [vb.b])
                yield
                fw.act(sgate.t[:, :], ps[3].t[:, :], AF.Silu, [ps[3].b], [sgate.b])
                yield
        def stB1(i):
                PT = PT_2[i % 2]
                la, kT, cum, ncb, eq, qd, qst, ek, Kt, khT, khat, dec, vb, sgate, ysq, st, yn, yo, yT = la_2[i % 2], kT_2[i % 2], cum_2[i % 2], ncb_2[i % 2], eq_2[i % 2], qd_2[i % 2], qst_2[i % 2], ek_2[i % 2], Kt_2[i % 2], khT_2[i % 2], khat_2[i % 2], dec_2[i % 2], vb_3[i % 3], sgate_3[i % 3], ysq_2[i % 2], st_2[i % 2], yn_2[i % 2], yo_2[i % 2], yT_2[i % 2]
                qT = VW(qg_2[(i // 4) % 2].t[:, :, (i % 4) * 128:(i % 4 + 1) * 128], qg_2[(i // 4) % 2].b)
                fT = VW(fg_2[(i // 4) % 2].t[:, :, (i % 4) * 128:(i % 4 + 1) * 128], fg_2[(i // 4) % 2].b)
                for h in range(4):
                    fw.ts("dve", fT.t[:, h, :], fT.t[:, h, :], omc[:, h:h + 1], lbc[:, h:h + 1], ALU.mult, ALU.add,
                          [fT.b, C.lb_all.b, C.oml_all.b], [fT.b])
                yield
                fw.act(la.t[:, :, :], fT.t[:, :, :], AF.Ln, [fT.b], [la.b])
                yield
                fw.ts("pool", kT.t[:, :, :], fT.t[:, :, :], -1.0, 1.0, ALU.mult, ALU.add, [fT.b], [kT.b])
                yield
                for h in range(4):
                    fw.op("dve", lambda g: g.tensor_tensor_scan(cum.t[:, h, :], ones, la.t[:, h, :], 0.0, ALU.mult, ALU.add),
                          [la.b, cb_], [cum.b])
                yield
                cv = cum.t[:, :, :].rearrange("p h (b c) -> p h b c", c=32)
                fw.ts("dve", ncb.t[:, :, 1:4], cv[:, :, 0:3, 31], -1.0, None, ALU.mult, None, [cum.b], [ncb.b])
                yield
                fw.tt("dve", eq.t[:, :, :].rearrange("p h (b c) -> p h b c", c=32), cv,
                      ncb.t[:, :, :].unsqueeze(3).to_broadcast([128, 4, 4, 32]), ALU.add, [cum.b, ncb.b], [eq.b])
                yield
                fw.act(eq.t[:, :, :], eq.t[:, :, :], AF.Exp, [eq.b], [eq.b])
                yield
                fw.tt("pool", qd.t[:, :, :], qT.t[:, :, :], eq.t[:, :, :], ALU.mult, [qT.b, eq.b], [qd.b])
                yield
                fw.act(eq.t[:, :, :], cum.t[:, :, :], AF.Exp, [cum.b], [eq.b])
                yield
                fw.tt("pool", qst.t[:, :, :], qT.t[:, :, :], eq.t[:, :, :], ALU.mult, [qT.b, eq.b], [qst.b])
                yield
                for b in range(4):
                    W_ = 32 * (b + 1)
                    eb = ek if b % 2 == 0 else eq
                    fw.tt("dve", eb.t[:, :, 0:W_], cum.t[:, :, 0:W_], ncb.t[:, :, b:b + 1].to_broadcast([128, 4, W_]), ALU.add,
                          [cum.b, ncb.b], [eb.b])
                    fw.act(eb.t[:, :, 0:W_], eb.t[:, :, 0:W_], AF.Exp, [eb.b], [eb.b], scale=-1.0)
                    fw.tt("pool" if b % 2 == 0 else "dve", Kt.t[:, :, b, 0:W_], eb.t[:, :, 0:W_], kT.t[:, :, 0:W_], ALU.mult,
                          [eb.b, kT.b], [Kt.b])
                    yield
                fw.tt("dve", ek.t[:, :, :], cum.t[:, :, 127:128].to_broadcast([128, 4, 128]), cum.t[:, :, :], ALU.subtract, [cum.b], [ek.b])
                yield
                fw.act(ek.t[:, :, :], ek.t[:, :, :], AF.Exp, [ek.b], [ek.b])
                yield
                fw.tt("pool", khT.t[:, :, :], ek.t[:, :, :], kT.t[:, :, :], ALU.mult, [ek.b, kT.b], [khT.b])
                yield
                for h in range(4):
                    fw.tr(pb0.t[:, h * 128:(h + 1) * 128], khT.t[:, h, :], C.identb.t[:, :], [khT.b, C.identb.b], [pb0.b])
                yield
                fw.cp("act", khat.t[:, :, :], pb0.t[:, 0:512].rearrange("p (h d) -> p h d", h=4), [pb0.b], [khat.b])
                yield
                fw.act(dec.t[:, :], cum.t[:, :, 127], AF.Exp, [cum.b], [dec.b])
                yield
                for h in range(4):
                    for b in range(4):
                        W_ = 32 * (b + 1)
                        fw.mm(ps[4].t[0:W_, h * 128 + 32 * b:h * 128 + 32 * b + 32], Kt.t[:, h, b, 0:W_], qd.t[:, h, 32 * b:32 * b + 32],
                              True, True, [Kt.b, qd.b], [ps[4].b])
                yield
                psv = ps[4].t[:, :].rearrange("p (h t) -> p h t", h=4)
                for b in range(4):
                    W_ = 32 * (b + 1)
                    bs = slice(32 * b, 32 * b + 32)
                    to, tn = CO["triu"]
                    mk = C.consts.t[0:W_, to + 32 * b:to + 32 * b + 32].unsqueeze(1).to_broadcast([W_, 4, 32])
                    fw.tt("dve", PT.t[0:W_, :, bs], psv[0:W_, :, bs], mk, ALU.mult, [ps[4].b, cb_], [PT.b])
                yield
        def stB2(i):
                PT = PT_2[i % 2]
                la, kT, cum, ncb, eq, qd, qst, ek, Kt, khT, khat, dec, vb, sgate, ysq, st, yn, yo, yT = la_2[i % 2], kT_2[i % 2], cum_2[i % 2], ncb_2[i % 2], eq_2[i % 2], qd_2[i % 2], qst_2[i % 2], ek_2[i % 2], Kt_2[i % 2], khT_2[i % 2], khat_2[i % 2], dec_2[i % 2], vb_3[i % 3], sgate_3[i % 3], ysq_2[i % 2], st_2[i % 2], yn_2[i % 2], yo_2[i % 2], yT_2[i % 2]
                qT = VW(qg_2[(i // 4) % 2].t[:, :, (i % 4) * 128:(i % 4 + 1) * 128], qg_2[(i // 4) % 2].b)
                fT = VW(fg_2[(i // 4) % 2].t[:, :, (i % 4) * 128:(i % 4 + 1) * 128], fg_2[(i // 4) % 2].b)
                for h in range(4):
                    hs = slice(h * 128, (h + 1) * 128)
                    fw.mm(ps[5].t[:, hs], PT.t[:, h, :], vb.t[:, hs], True, False, [PT.b, vb.b], [ps[5].b])
                    fw.mm(ps[5].t[:, hs], qst.t[:, h, :], Sb.t[:, h, :], False, True, [qst.b, Sb.b], [ps[5].b])
                yield
                for h in range(4):
                    hs = slice(h * 128, (h + 1) * 128)
                    fw.mm(ps[1].t[:, hs], khat.t[:, h, :], vb.t[:, hs], True, True, [khat.b, vb.b], [ps[1].b])
                yield
                for h in range(4):
                    hs = slice(h * 128, (h + 1) * 128)
                    fw.stt("dve", S.t[:, h, :], S.t[:, h, :], dec.t[:, h:h + 1], ps[1].t[:, hs], ALU.mult, ALU.add, [S.b, dec.b, ps[1].b], [S.b])
                yield
                fw.cp("pool", Sb.t[:, :, :], S.t[:, :, :], [S.b], [Sb.b])
                yield
                fw.act(ysq.t[:, :], ps[5].t[:, :], AF.Square, [ps[5].b], [ysq.b])
                yield
                fw.op("dve", lambda g: g.reduce_sum(st.t[:, 0:4], ysq.t[:, :].rearrange("p (h e) -> p h e", h=4), axis=AX.X), [ysq.b], [st.b])
                yield
                rstd_from_ss(fw, C, st.t[:, 0:4], st.t[:, 4:8], 128.0, [st.b], [st.b])
                yield
                for h in range(4):
                    hs = slice(h * 128, (h + 1) * 128)
                    fw.ts("dve", yn.t[:, hs], ps[5].t[:, hs], st.t[:, 4 + h:5 + h], None, ALU.mult, None, [ps[5].b, st.b], [yn.b])
                yield
                fw.tt("pool", yn.t[:, :], yn.t[:, :], nw.t[:, :], ALU.mult, [yn.b, nw.b], [yn.b])
                yield
                fw.tt("pool", yo.t[:, :], yn.t[:, :], sgate.t[:, :], ALU.mult, [yn.b, sgate.b], [yo.b])
                yield
                emit_yT(fw, C, yo, yT, 3, i)
                yield
        for st0 in (stG(0), stA(0), stB1(0), stA(1) if NT > 1 else None):
            if st0 is not None:
                for _ in st0:
                    pass
        for i in range(NT):
            gens = [stB2(i)]
            if i + 1 < NT:
                gens.append(stB1(i + 1))
            if i + 2 < NT:
                gens.append(chain_gens(stG((i + 2) // 4) if (i + 2) % 4 == 0 else None, stA(i + 2)))
            interleave_n(gens)
        fw.barrier()


TGS = [(0, 512), (512, 512), (1024, 512), (1536, 512), (2048, 128)]


def phase_s5(fw, C, l):
    ps = C.ps
    pb0, pb1 = C.pb
    cb_ = C.consts.b
    with ExitStack() as ph:
        GW = fw.tile(ph, "GW", [128, 4, 1024], BF16)
        load_w(fw, C.s5_glu_w[l], GW, 0, 1024, 4)
        Cre = fw.tile(ph, "Cre", [128, 16, 128], BF16)
        nCim = fw.tile(ph, "nCim", [128, 16, 128], BF16)
        fw.dma("pool", Cre.t[:, :, :], C.s5_cre[l], writes=[Cre.b])
        fw.dma("pool", nCim.t[:, :, :], C.s5_cim[l], writes=[nCim.b])
        fw.ts("pool", nCim.t[:, :, :], nCim.t[:, :, :], -1.0, None, ALU.mult, None, [nCim.b], [nCim.b])
        d5 = fw.tile(ph, "d5", [128, 4], F32)
        gb = fw.tile(ph, "gb5", [128, 8], F32)
        fw.dma("sp", d5.t[:, :], C.s5_dT[l], writes=[d5.b])
        fw.dma("sp", gb.t[:, :], C.s5_gbT[l], writes=[gb.b])
        sp_ = fw.tile(ph, "s5sm", [128, 48], F32)
        fw.dma("sp", sp_.t[:, :], C.s5_small[l], writes=[sp_.b])
        u5T = fw.tile(ph, "u5T", [128, 4, TP], BF16)
        Sall = fw.tile(ph, "Sall", [128, 137, 3, 16], F32)
        KT = fw.tile(ph, "KT", [128, 4, 16, 128], BF16)
        PW = fw.tile(ph, "PW", [128, 2, 17, 16], F32)
        wk = fw.tile(ph, "s5wk", [128, 12, 16], F32)
        with ExitStack() as pa:
            W5 = fw.tile(pa, "W5", [128, 8, 512], BF16)
            load_w(fw, C.w_in[l], W5, C_S5, C_S5 + 512, 8)
            k = 0
            for ct in range(4):
                for (t0, n) in TGS:
                    pst = ps[k % 2]
                    for kc in range(8):
                        fw.mm(pst.t[:, 0:n], W5.t[:, kc, ct * 128:(ct + 1) * 128], C.uT.t[:, kc, t0:t0 + n], kc == 0, kc == 7,
                              [W5.b] + C.uTb[t0 // 128:(t0 + n) // 128], [pst.b])
                    fw.cp("act" if k % 2 else "dve", u5T.t[:, ct, t0:t0 + n], pst.t[:, 0:n], [pst.b], [u5T.b])
                    k += 1
            fw.barrier()
        if os.environ.get("S5_STOP") == "A":
            fw.barrier(); return
        lr, li, lst = sp_.t[:, 0:16], sp_.t[:, 16:32], sp_.t[:, 32:48]
        W_ = [wk.t[:, j, :] for j in range(12)]
        R = [sp_.b, wk.b, PW.b]
        step, lrs, ang, em1, re_, im_, t_a, t_b, inv, co_re, co_im, rr = W_
        big = [fw.tile(ph, "s5big", [128, 18, 16], F32) for _ in range(5)]
        bigi = fw.tile(ph, "s5bigi", [128, 18, 16], mybir.dt.int32)
        RB = R + [b_.b for b_ in big] + [bigi.b, cb_]
        FACT = [1.0, 1.0, 2.0, 6.0, 24.0, 120.0, 720.0, 5040.0, 40320.0, 362880.0, 3628800.0]

        def horner_exp(out, r, deg, minus1=False):
            fw.ts("dve", out, r, 1.0 / FACT[deg], None, ALU.mult, None, RB, RB)
            for j in range(deg - 1, 0, -1):
                fw.stt("dve", out, out, 1.0 / FACT[j], r, ALU.add, ALU.mult, RB, RB)
            if not minus1:
                fw.ts("dve", out, out, 1.0, None, ALU.add, None, RB, RB)

        fw.ts("dve", rr, lst, 0.125, None, ALU.mult, None, RB, RB)
        horner_exp(step, rr, 10)
        for _ in range(3):
            fw.tt("dve", step, step, step, ALU.mult, RB, RB)
        fw.tt("dve", lrs, lr, step, ALU.mult, RB, RB)
        fw.tt("dve", ang, li, step, ALU.mult, RB, RB)
        mo = CO["mtab"][0]
        mtab = C.consts.t[:, mo:mo + 288].rearrange("p (m q) -> p m q", m=18)
        TH, XM, MAG, SN, CS = [b_.t[:, :, :] for b_ in big]
        fw.tt("dve", TH, mtab, ang.unsqueeze(1).to_broadcast([128, 18, 16]), ALU.mult, RB, RB)
        fw.tt("dve", XM, mtab, lrs.unsqueeze(1).to_broadcast([128, 18, 16]), ALU.mult, RB, RB)
        horner_exp(MAG, XM, 10)
        C1, C2 = 6.28125, 2.0 * math.pi - 6.28125

        def sin_reduced(out, th):
            fw.ts("dve", out, th, 1.0 / (2.0 * math.pi), None, ALU.mult, None, RB, RB)
            fw.cp("dve", bigi.t[:, :, :], out, RB, RB)
            fw.cp("dve", XM, bigi.t[:, :, :], RB, RB)
            fw.stt("dve", out, XM, -C1, th, ALU.mult, ALU.add, RB, RB)
            fw.stt("dve", out, XM, -C2, out, ALU.mult, ALU.add, RB, RB)
            fw.act(out, out, AF.Sin, RB, RB)

        sin_reduced(SN, TH)
        fw.ts("dve", TH, TH, math.pi / 2, None, ALU.add, None, RB, RB)
        sin_reduced(CS, TH)
        fw.tt("dve", PW.t[:, 0, :, :], MAG[:, 0:17, :], CS[:, 0:17, :], ALU.mult, RB, RB)
        fw.tt("dve", PW.t[:, 1, :, :], MAG[:, 0:17, :], SN[:, 0:17, :], ALU.mult, RB, RB)
        horner_exp(em1, lrs, 7, minus1=True)
        fw.tt("dve", re_, em1, CS[:, 1, :], ALU.mult, RB, RB)
        fw.tt("dve", t_a, SN[:, 17, :], SN[:, 17, :], ALU.mult, RB, RB)
        fw.stt("dve", re_, t_a, -2.0, re_, ALU.mult, ALU.add, RB, RB)
        fw.ts("dve", t_b, em1, 1.0, None, ALU.add, None, RB, RB)
        fw.tt("dve", im_, t_b, SN[:, 1, :], ALU.mult, RB, RB)
        fw.tt("dve", t_a, lr, lr, ALU.mult, RB, RB)
        fw.tt("dve", t_b, li, li, ALU.mult, RB, RB)
        fw.tt("dve", inv, t_a, t_b, ALU.add, RB, RB)
        fw.op("dve", lambda g: g.reciprocal(inv, inv), RB, RB)
        fw.tt("dve", t_a, re_, lr, ALU.mult, RB, RB)
        fw.tt("dve", t_b, im_, li, ALU.mult, RB, RB)
        fw.tt("dve", t_a, t_a, t_b, ALU.add, RB, RB)
        fw.tt("dve", co_re, t_a, inv, ALU.mult, RB, RB)
        fw.tt("dve", t_a, im_, lr, ALU.mult, RB, RB)
        fw.tt("dve", t_b, re_, li, ALU.mult, RB, RB)
        fw.tt("dve", t_a, t_a, t_b, ALU.subtract, RB, RB)
        fw.tt("dve", co_im, t_a, inv, ALU.mult, RB, RB)
        if C.debug and l == 0:
            fw.dma("sp", C.dbg5[:, 0:192], wk.t[:, :, :].rearrange("p a b -> p (a b)"), reads=[wk.b], writes=[Buf()])
            fw.dma("sp", C.dbg5[:, 192:736], PW.t[:, :, :, :].rearrange("p a m q -> p (a m q)"), reads=[PW.b], writes=[Buf()])
        if os.environ.get("S5_STOP") == "B":
            fw.barrier(); return
        fw.memset("pool", Sall.t[:, 0, :, :], 0.0, [Sall.b])
        with ExitStack() as pd:
            Bst = fw.tile(pd, "Bst", [128, 2, 4, 128], F32)
            Bb = fw.tile(pd, "Bb", [128, 2, 4, 128], F32)
            t0_ = fw.tile(pd, "tB", [128, 128], F32)
            tA = [fw.tile(pd, "tA", [128, 16, 128], F32)] * 2
            tB = [fw.tile(pd, "tBB", [128, 16, 128], F32)] * 2
            Xs = [fw.tile(pd, "X", [128, 2, 16, 128], BF16) for _ in range(2)]
            XTs = [fw.tile(pd, "XT", [128, 4, 2, 128], BF16) for _ in range(2)]
            psK = ps[2:6]
            zt = fw.tile(pd, "zt", [128, 512], BF16)
            fw.memset("pool", zt.t[:, :], 0.0, [zt.b])
            u5D = fw.tile(pd, "u5D", [128, 4, 16, 136], BF16)
            for ct_ in range(4):
                fw.cp("pool" if ct_ % 2 else "dve", u5D.t[:, ct_, :, :], u5T.t[:, ct_, :].rearrange("p (c b) -> p b c", b=16),
                      [u5T.b], [u5D.b])
            it = 0
            for ct in range(4):
                for j in range(4):
                    fw.mm(psK[j].t[:, :], zt.t[:, 0:128], zt.t[:, :], True, False, [zt.b], [psK[j].b])
                fw.dma("sp", Bst.t[:, 0, :, :], C.s5_bre[l][:, 4 * ct:4 * ct + 4, :], writes=[Bst.b])
                fw.dma("sp", Bst.t[:, 1, :, :], C.s5_bim[l][:, 4 * ct:4 * ct + 4, :], writes=[Bst.b])
                for pl in range(4):
                    pair = 4 * ct + pl
                    cr, ci = co_re[:, pair:pair + 1], co_im[:, pair:pair + 1]
                    fw.ts("dve", t0_.t[:, :], Bst.t[:, 1, pl, :], ci, None, ALU.mult, None, [Bst.b, wk.b], [t0_.b])
                    fw.stt("dve", Bb.t[:, 0, pl, :], Bst.t[:, 0, pl, :], cr, t0_.t[:, :], ALU.mult, ALU.subtract, [Bst.b, wk.b, t0_.b], [Bb.b])
                    fw.ts("dve", t0_.t[:, :], Bst.t[:, 1, pl, :], cr, None, ALU.mult, None, [Bst.b, wk.b], [t0_.b])
                    fw.stt("dve", Bb.t[:, 1, pl, :], Bst.t[:, 0, pl, :], ci, t0_.t[:, :], ALU.mult, ALU.add, [Bst.b, wk.b, t0_.b], [Bb.b])
                for pl in range(4):
                    pair = 4 * ct + pl
                    psG = ps[pair % 2]
                    X = Xs[pair % 2]
                    ta, tb = tA[pair % 2], tB[pair % 2]
                    bre = Bb.t[:, 0, pl, :].unsqueeze(1).to_broadcast([128, 16, 128])
                    bim = Bb.t[:, 1, pl, :].unsqueeze(1).to_broadcast([128, 16, 128])
                    prb = PW.t[:, 0, 0:16, pair].unsqueeze(2).to_broadcast([128, 16, 128])
                    pib = PW.t[:, 1, 0:16, pair].unsqueeze(2).to_broadcast([128, 16, 128])
                    RB_ = [Bb.b, PW.b]
                    fw.tt("dve", ta.t[:, :, :], bre, prb, ALU.mult, RB_, [ta.b])
                    fw.tt("pool", tb.t[:, :, :], bim, pib, ALU.mult, RB_, [tb.b])
                    fw.tt("dve", X.t[:, 0, :, :], ta.t[:, :, :], tb.t[:, :, :], ALU.subtract, [ta.b, tb.b], [X.b])
                    fw.tt("pool", tb.t[:, :, :], bim, prb, ALU.mult, RB_, [tb.b])
                    fw.tt("dve", ta.t[:, :, :], bre, pib, ALU.mult, RB_, [ta.b])
                    fw.tt("dve", X.t[:, 1, :, :], ta.t[:, :, :], tb.t[:, :, :], ALU.add, [ta.b, tb.b], [X.b])
                    fw.mm(psG.t[:, 0:272], zt.t[:, 0:128], zt.t[:, 0:272], True, False, [zt.b], [psG.b])
                    for m in range(16):
                        pk = psK[m // 4]
                        ks = slice((m % 4) * 128, (m % 4 + 1) * 128)
                        fw.mm(pk.t[:, ks], X.t[:, 0, m, :], Cre.t[:, pair, :], False, False, [X.b, Cre.b], [pk.b])
                        fw.mm(pk.t[:, ks], X.t[:, 1, m, :], nCim.t[:, pair, :], False, pl == 3, [X.b, nCim.b], [pk.b])
                    for mg in range(4):
                        XT = XTs[it % 2]
                        pbt = C.pb[it % 2]
                        for mm_ in range(4):
                            m = 4 * mg + mm_
                            for part in range(2):
                                fw.tr(pbt.t[:, (2 * mm_ + part) * 128:(2 * mm_ + part + 1) * 128], X.t[:, part, m, :], C.identb.t[:, :],
                                      [X.b, C.identb.b], [pbt.b])
                        fw.cp("act" if it % 2 else "dve", XT.t[:, :, :, :], pbt.t[:, :].rearrange("p (m a q) -> p m a q", m=4, a=2),
                              [pbt.b], [XT.b])
                        for mm_ in range(4):
                            m = 4 * mg + mm_
                            tau = 15 - m
                            rhs = u5D.t[:, ct, tau, :]
                            fw.mm(psG.t[:, 0:136], XT.t[:, mm_, 0, :], rhs, False, m == 15, [XT.b, u5D.b], [psG.b])
                            fw.mm(psG.t[:, 136:272], XT.t[:, mm_, 1, :], rhs, False, m == 15, [XT.b, u5D.b], [psG.b])
                        it += 1
                    fw.cp("act", Sall.t[:, 1:137, 0, pair], psG.t[:, 0:136], [psG.b], [Sall.b])
                    fw.cp("act", Sall.t[:, 1:137, 1, pair], psG.t[:, 136:272], [psG.b], [Sall.b])
                for j in range(4):
                    fw.cp("act" if j % 2 else "dve", KT.t[:, ct, 4 * j:4 * j + 4, :], psK[j].t[:, :].rearrange("p (m c) -> p m c", m=4),
                          [psK[j].b], [KT.b])
            fw.barrier()
        if C.debug and l == 0:
            fw.dma("pool", C.dbg5[:, 736:736 + 2048], KT.t[:, 0, :, :].rearrange("p m c -> p (m c)"), reads=[KT.b], writes=[Buf()])
            fw.dma("sp", C.dbg5[:, 2784:2784 + 137 * 48], Sall.t[:, :, :, :].rearrange("p c a q -> p (c a q)"), reads=[Sall.b], writes=[Buf()])
        if os.environ.get("S5_STOP") == "D":
            fw.barrier(); return
        with ExitStack() as pe_:
            A1 = fw.tile(pe_, "A1", [128, 2, 16], F32)
            A2 = fw.tile(pe_, "A2", [128, 2, 16], F32)
            p1 = fw.tile(pe_, "p1", [128, 2, 16], F32)
            p2 = fw.tile(pe_, "p2", [128, 2, 16], F32)
            fw.cp("dve", A1.t[:, 0, :], PW.t[:, 0, 16, :], [PW.b], [A1.b])
            fw.cp("dve", A1.t[:, 1, :], PW.t[:, 0, 16, :], [PW.b], [A1.b])
            fw.ts("dve", A2.t[:, 0, :], PW.t[:, 1, 16, :], -1.0, None, ALU.mult, None, [PW.b], [A2.b])
            fw.cp("dve", A2.t[:, 1, :], PW.t[:, 1, 16, :], [PW.b], [A2.b])
            for c in range(136):
                fw.tt("dve", p1.t[:, :, :], A1.t[:, :, :], Sall.t[:, c, 0:2, :], ALU.mult, [A1.b, Sall.b], [p1.b])
                fw.tt("dve", p2.t[:, :, :], A2.t[:, :, :], Sall.t[:, c, 1:3, :], ALU.mult, [A2.b, Sall.b], [p2.b])
                fw.tt("dve", p1.t[:, :, :], p1.t[:, :, :], p2.t[:, :, :], ALU.add, [p1.b, p2.b], [p1.b])
                fw.tt("dve", Sall.t[:, c + 1, 0:2, :], Sall.t[:, c + 1, 0:2, :], p1.t[:, :, :], ALU.add, [Sall.b, p1.b], [Sall.b])
                fw.cp("dve", Sall.t[:, c + 1, 2, :], Sall.t[:, c + 1, 0, :], [Sall.b], [Sall.b])
            fw.barrier()
        if os.environ.get("S5_STOP") == "E":
            fw.barrier(); return
        pfg = ExitStack()
        gT = fw.tile(pfg, "g5T", [128, 4, TP], BF16)
        with ExitStack() as pf:
            SP = fw.tile(pf, "SP", [128, 4, 2, 136, 16], BF16)
            u1 = fw.tile(pf, "u1", [128, 136, 16], F32)
            u2 = fw.tile(pf, "u2", [128, 136, 16], F32)
            z = fw.tile(pf, "z5", [128, 512], F32)
            z2 = fw.tile(pf, "z52", [128, 512], F32)
            k = 0
            for ct in range(4):
                for pl in range(4):
                    pair = 4 * ct + pl
                    srb = Sall.t[:, 0:136, 0, pair].unsqueeze(2).to_broadcast([128, 136, 16])
                    sib = Sall.t[:, 0:136, 1, pair].unsqueeze(2).to_broadcast([128, 136, 16])
                    prb = PW.t[:, 0, 1:17, pair].unsqueeze(1).to_broadcast([128, 136, 16])
                    pib = PW.t[:, 1, 1:17, pair].unsqueeze(1).to_broadcast([128, 136, 16])
                    RS = [Sall.b, PW.b]
                    fw.tt("dve", u1.t[:, :, :], srb, prb, ALU.mult, RS, [u1.b])
                    fw.tt("pool", u2.t[:, :, :], sib, pib, ALU.mult, RS, [u2.b])
                    fw.tt("dve", SP.t[:, pl, 0, :, :], u1.t[:, :, :], u2.t[:, :, :], ALU.subtract, [u1.b, u2.b], [SP.b])
                    fw.tt("pool", u2.t[:, :, :], sib, prb, ALU.mult, RS, [u2.b])
                    fw.tt("dve", u1.t[:, :, :], srb, pib, ALU.mult, RS, [u1.b])
                    fw.tt("dve", SP.t[:, pl, 1, :, :], u1.t[:, :, :], u2.t[:, :, :], ALU.add, [u1.b, u2.b], [SP.b])
                for (t0, n) in TGS:
                    c0, nch = t0 // 16, n // 16
                    pst = ps[k % 2]
                    k += 1
                    pv = pst.t[:, 0:n].rearrange("p (c b) -> p c b", b=16)
                    uv = u5T.t[:, ct, t0:t0 + n].rearrange("p (c b) -> p c b", b=16)
                    for tau in range(16):
                        fw.mm(pv[:, :, tau:16], KT.t[:, ct, tau, :], uv[:, :, 0:16 - tau], tau == 0, False, [KT.b, u5T.b], [pst.b])
                    for pl in range(4):
                        pair = 4 * ct + pl
                        fw.mm(pst.t[:, 0:n], Cre.t[:, pair, :], SP.t[:, pl, 0, c0:c0 + nch, :].rearrange("p c b -> p (c b)"), False, False,
                              [Cre.b, SP.b], [pst.b])
                        fw.mm(pst.t[:, 0:n], nCim.t[:, pair, :], SP.t[:, pl, 1, c0:c0 + nch, :].rearrange("p c b -> p (c b)"), False, pl == 3,
                              [nCim.b, SP.b], [pst.b])
                    fw.stt("dve", z.t[:, 0:n], u5T.t[:, ct, t0:t0 + n], d5.t[:, ct:ct + 1], pst.t[:, 0:n], ALU.mult, ALU.add,
                           [u5T.b, d5.b, pst.b], [z.b])
                    fw.tt("pool", z2.t[:, 0:n], z.t[:, 0:n], z.t[:, 0:n], ALU.mult, [z.b], [z2.b])
                    fw.ts("pool", z2.t[:, 0:n], z2.t[:, 0:n], 0.044715, 1.0, ALU.mult, ALU.add, [z2.b], [z2.b])
                    fw.tt("pool", z2.t[:, 0:n], z2.t[:, 0:n], z.t[:, 0:n], ALU.mult, [z2.b, z.b], [z2.b])
                    fw.act(z2.t[:, 0:n], z2.t[:, 0:n], AF.Sigmoid, [z2.b], [z2.b], scale=2.0 * math.sqrt(2.0 / math.pi))
                    fw.tt("pool", gT.t[:, ct, t0:t0 + n], z.t[:, 0:n], z2.t[:, 0:n], ALU.mult, [z.b, z2.b], [gT.b])
            fw.barrier()
        if os.environ.get("S5_STOP") == "F":
            pfg.close(); fw.barrier(); return
        with ExitStack() as pg:
            sgs = [fw.tile(pg, "sg5", [128, 512], F32) for _ in range(2)]
            yos = [fw.tile(pg, "yo5", [128, 512], BF16) for _ in range(2)]
            k = 0
            for nb in range(4):
                for (t0, n) in TGS:
                    pa_, pg_ = ps[2 + 2 * (k % 2)], ps[3 + 2 * (k % 2)]
                    sg, yo = sgs[k % 2], yos[k % 2]
                    k += 1
                    for kc in range(4):
                        fw.mm(pa_.t[:, 0:n], GW.t[:, kc, nb * 128:(nb + 1) * 128], gT.t[:, kc, t0:t0 + n], kc == 0, kc == 3, [GW.b, gT.b], [pa_.b])
                    for kc in range(4):
                        fw.mm(pg_.t[:, 0:n], GW.t[:, kc, 512 + nb * 128:512 + (nb + 1) * 128], gT.t[:, kc, t0:t0 + n], kc == 0, kc == 3,
                              [GW.b, gT.b], [pg_.b])
                    fw.act(sg.t[:, 0:n], pg_.t[:, 0:n], AF.Sigmoid, [pg_.b, gb.b], [sg.b], bias=gb.t[:, 4 + nb:5 + nb])
                    fw.stt("dve", yo.t[:, 0:n], pa_.t[:, 0:n], gb.t[:, nb:nb + 1], sg.t[:, 0:n], ALU.add, ALU.mult, [pa_.b, gb.b, sg.b], [yo.b])
                    fw.dma("sp", C.YT[4 + nb, :, t0:t0 + n], yo.t[:, 0:n], reads=[yo.b], writes=C.YTb[1][t0 // 128:(t0 + n) // 128])
            fw.barrier()
        pfg.close()
        fw.barrier()


MGS = [(g * 256, min(256, TP - g * 256)) for g in range((TP + 255) // 256)]


def rms_epilogue(fw, C, psA, psB, nw, xt, wk2):
    junk, ss, tmp = wk2
    fw.act(junk.t[:, 0:512], psA.t[:, :], AF.Square, [psA.b], [junk.b, ss.b], accum=ss.t[:, 2:3])
    fw.act(junk.t[:, 512:1024], psB.t[:, :], AF.Square, [psB.b], [junk.b, ss.b], accum=ss.t[:, 3:4])
    fw.tt("dve", ss.t[:, 2:3], ss.t[:, 2:3], ss.t[:, 3:4], ALU.add, [ss.b], [ss.b])
    rstd_from_ss(fw, C, ss.t[:, 2:3], ss.t[:, 3:4], 1024.0, [ss.b], [ss.b])
    fw.stt("dve", tmp.t[:, 0:512], psA.t[:, :], ss.t[:, 3:4], nw.t[:, 0:512], ALU.mult, ALU.mult, [psA.b, ss.b, nw.b], [tmp.b])
    fw.stt("dve", tmp.t[:, 512:1024], psB.t[:, :], ss.t[:, 3:4], nw.t[:, 512:1024], ALU.mult, ALU.mult, [psB.b, ss.b, nw.b], [tmp.b])
    fw.tt("pool", xt.t[:, :], xt.t[:, :], tmp.t[:, :], ALU.add, [xt.b, tmp.b], [xt.b])


def phase_merge(fw, C, l):
    ps = C.ps
    with ExitStack() as ph:
        WG = fw.tile(ph, "WG", [128, 8, 4096], BF16)
        load_w(fw, C.w_in[l], WG, C_GATE, C_GATE + 4096, 8)
        WB = fw.tile(ph, "WB", [128, 16, 1024], BF16)
        for n in range(4):
            v = C.w_branch[l][n].rearrange("(cb p) d -> p cb d", p=128)
            fw.dma("pool", WB.t[:, 4 * n:4 * n + 4, :], v, writes=[WB.b])
        WO = fw.tile(ph, "WO", [128, 8, 1024], BF16)
        load_w(fw, C.w_out[l], WO, 0, 1024, 8)
        nw1 = fw.tile(ph, "nw1", [128, D], F32)
        nw2 = fw.tile(ph, "nw2", [128, D], F32)
        fw.dma("sp", nw1.t[:, :], C.norm_post_mix[l].partition_broadcast(128), writes=[nw1.b])
        fw.dma("sp", nw2.t[:, :], C.norm_pre_mlp[l].partition_broadcast(128), writes=[nw2.b])
        YTs = fw.tile(ph, "YTs", [128, 16, 256], BF16)
        mixT = fw.tile(ph, "mixT", [128, 8, 256], BF16)
        acc = fw.tile(ph, "macc", [128, 256], F32)
        sgs = [fw.tile(ph, "msg", [128, 256], F32) for _ in range(2)]
        tmpm = fw.tile(ph, "mtmp", [128, 256], F32)
        xts = [fw.tile(ph, "mxt", [128, D], F32) for _ in range(2)]
        wk = (fw.tile(ph, "junk", [128, D], BF16), fw.tile(ph, "ss", [128, 4], F32), fw.tile(ph, "ub", [128, D], BF16))
        wk2 = (wk[0], wk[1], fw.tile(ph, "mtmp2", [128, D], F32))
        k = 0
        for (t0, n) in MGS:
            tiles = list(range(t0 // 128, (t0 + n) // 128))
            fw.dma("sp", YTs.t[:, :, 0:n], C.YT[:, :, t0:t0 + n].rearrange("c p t -> p c t"),
                   reads=[C.YTb[m][i] for m in range(4) for i in tiles], writes=[YTs.b])
            for db in range(8):
                for nn in range(4):
                    pg_, pb_ = ps[2 * (k % 2)], ps[2 * (k % 2) + 1]
                    sg = sgs[k % 2]
                    k += 1
                    c0 = nn * 1024 + db * 128
                    for kc in range(8):
                        fw.mm(pg_.t[:, 0:n], WG.t[:, kc, c0:c0 + 128], C.uT.t[:, kc, t0:t0 + n], kc == 0, kc == 7,
                              [WG.b] + [C.uTb[i] for i in tiles], [pg_.b])
                    for cb in range(4):
                        fw.mm(pb_.t[:, 0:n], WB.t[:, 4 * nn + cb, db * 128:(db + 1) * 128], YTs.t[:, 4 * nn + cb, 0:n], cb == 0, cb == 3,
                              [WB.b, YTs.b], [pb_.b])
                    fw.act(sg.t[:, 0:n], pg_.t[:, 0:n], AF.Sigmoid, [pg_.b], [sg.b])
                    if nn == 0:
                        fw.tt("dve", acc.t[:, 0:n], sg.t[:, 0:n], pb_.t[:, 0:n], ALU.mult, [sg.b, pb_.b], [acc.b])
                    else:
                        fw.tt("dve", tmpm.t[:, 0:n], sg.t[:, 0:n], pb_.t[:, 0:n], ALU.mult, [sg.b, pb_.b], [tmpm.b])
                        if nn < 3:
                            fw.tt("pool", acc.t[:, 0:n], acc.t[:, 0:n], tmpm.t[:, 0:n], ALU.add, [acc.b, tmpm.b], [acc.b])
                        else:
                            fw.tt("pool", mixT.t[:, db, 0:n], acc.t[:, 0:n], tmpm.t[:, 0:n], ALU.add, [acc.b, tmpm.b], [mixT.b])
            for i in tiles:
                xt = xts[i % 2]
                src = C.h0 if l == 0 else C.hbuf
                fw.dma("sp", xt.t[:, :], src[i * 128:(i + 1) * 128, :], reads=([] if l == 0 else [C.hb[i]]), writes=[xt.b])
                sub = slice(i * 128 - t0, i * 128 - t0 + 128)
                for dh in range(2):
                    pst = ps[4 + dh]
                    for db in range(8):
                        fw.mm(pst.t[:, :], mixT.t[:, db, sub], WO.t[:, db, dh * 512:(dh + 1) * 512], db == 0, db == 7, [mixT.b, WO.b], [pst.b])
                rms_epilogue(fw, C, ps[4], ps[5], nw1, xt, wk2)
                fw.dma("sp", C.hbuf[i * 128:(i + 1) * 128, :], xt.t[:, :], reads=[xt.b], writes=[C.hb[i]])
                norm_rows_to_T(fw, C, ph, xt.t[:, :], xt.b, nw2, C.uT, C.uTb, i, wk)
        fw.barrier()


def phase_mlp(fw, C, l, last):
    ps = C.ps
    with ExitStack() as ph:
        WU = fw.tile(ph, "WU", [128, 8, 4096], BF16)
        WUb = [Buf("WU0"), Buf("WU1")]
        load_w(fw, C.w_up[l], WU, 0, 4096, 8, chunk_bufs=WUb)
        WD = fw.tile(ph, "WD", [128, 32, 1024], BF16)
        load_w(fw, C.w_down[l], WD, 0, 1024, 32)
        nw = fw.tile(ph, "nw3", [128, D], F32)
        fw.dma("sp", nw.t[:, :], C.norm_post_mlp[l].partition_broadcast(128), writes=[nw.b])
        hT = fw.tile(ph, "hT", [128, 32, 256], BF16)
        rl = [fw.tile(ph, "rl", [128, 512], BF16) for _ in range(2)]
        xts = [fw.tile(ph, "pxt", [128, D], F32)] * 2
        ptmp = fw.tile(ph, "ptmp", [128, D], F32)
        wk2 = (ptmp, fw.tile(ph, "ss", [128, 4], F32), ptmp)
        k = 0
        for (t0, n) in MGS:
            tiles = list(range(t0 // 128, (t0 + n) // 128))
            for fp in range(16):
                pst = ps[k % 4]
                r = rl[k % 2]
                k += 1
                for j in range(2):
                    ffc = 2 * fp + j
                    for kc in range(8):
                        fw.mm(pst.t[:, j * 256:j * 256 + n], WU.t[:, kc, ffc * 128:(ffc + 1) * 128], C.uT.t[:, kc, t0:t0 + n], kc == 0, kc == 7,
                              [WUb[(ffc * 128) // 2048]] + [C.uTb[i] for i in tiles], [pst.b])
                pv = pst.t[:, :].rearrange("p (j t) -> p j t", j=2)[:, :, 0:n]
                rv = r.t[:, :].rearrange("p (j t) -> p j t", j=2)[:, :, 0:n]
                fw.act(rv, pv, AF.Relu, [pst.b], [r.b])
                fw.tt("pool" if fp % 2 else "dve", hT.t[:, 2 * fp:2 * fp + 2, 0:n], rv, rv, ALU.mult, [r.b], [hT.b])
            for i in tiles:
                xt = xts[i % 2]
                fw.dma("sp", xt.t[:, :], C.hbuf[i * 128:(i + 1) * 128, :], reads=[C.hb[i]], writes=[xt.b])
                sub = slice(i * 128 - t0, i * 128 - t0 + 128)
                for dh in range(2):
                    pst = ps[4 + dh]
                    for ffc in range(32):
                        fw.mm(pst.t[:, :], hT.t[:, ffc, sub], WD.t[:, ffc, dh * 512:(dh + 1) * 512], ffc == 0, ffc == 31, [hT.b, WD.b], [pst.b])
                rms_epilogue(fw, C, ps[4], ps[5], nw, xt, wk2)
                if not last:
                    fw.dma("sp", C.hbuf[i * 128:(i + 1) * 128, :], xt.t[:, :], reads=[xt.b], writes=[C.hb[i]])
                else:
                    if C.debug:
                        fw.dma("sp", C.hbuf[i * 128:(i + 1) * 128, :], xt.t[:, :], reads=[xt.b], writes=[C.hb[i]])
                    lo = max(i * 128, 16)
                    hi = min((i + 1) * 128, T)
                    if hi > lo:
                        fw.dma("sp", C.out[lo - 16:hi - 16, :], xt.t[lo - i * 128:hi - i * 128, :], reads=[xt.b], writes=[C.outb])
        fw.barrier()


def build(debug=False, n_layers=DEPTH, phases=None):
    nc = bass.Bass("TRN2", target_bir_lowering=False)
    C = Ctx()
    C.debug = debug

    def din(name, shape):
        return nc.dram_tensor(name, list(shape), F32, kind="ExternalInput").ap()

    C.h0 = din("h0", [TP, D])
    C.consts_d = din("consts", [128, CO_TOTAL[0]])
    C.w_in = din("w_in", [4, D, N_IN])
    C.w_branch = din("w_branch", [4, 4, 512, D])
    C.w_out = din("w_out", [4, D, D])
    C.w_up = din("w_up", [4, D, 4 * D])
    C.w_down = din("w_down", [4, 4 * D, D])
    C.s5_glu_w = din("s5_glu_w", [4, 512, 1024])
    for nm in ("norm_pre_mix", "norm_post_mix", "norm_pre_mlp", "norm_post_mlp"):
        setattr(C, nm, din(nm, [4, D]))
    for nm in ("ret_gn_w", "ssd_norm_w", "hgrn_norm_w", "ssd_dsk"):
        setattr(C, nm, din(nm, [4, 512]))
    C.ssd_dt_bias = din("ssd_dt_bias", [4, 8])
    C.ssd_a_log = din("ssd_a_log", [4, 8])
    C.lbT = din("lbT", [128, 4, 4])
    C.ssd_cw = din("ssd_cw", [4, 128, 8, 4])
    C.ssd_cb = din("ssd_cb", [4, 128, 8])
    C.s5_small = din("s5_small", [4, 128, 48])
    for nm in ("s5_bre", "s5_bim", "s5_cre", "s5_cim"):
        setattr(C, nm, din(nm, [4, 128, 16, 128]))
    C.s5_dT = din("s5_dT", [4, 128, 4])
    C.s5_gbT = din("s5_gbT", [4, 128, 8])
    C.out = nc.dram_tensor("out", [2048, D], F32, kind="ExternalOutput").ap()
    sk = "ExternalOutput" if debug else "Internal"
    C.hbuf = nc.dram_tensor("hbuf", [TP, D], F32, kind=sk).ap()
    C.YT = nc.dram_tensor("YT", [16, 128, TP], BF16, kind=sk).ap()
    if debug:
        C.dbg5 = nc.dram_tensor("dbg5", [128, 2784 + 137 * 48], F32, kind="ExternalOutput").ap()
    C.hb = [Buf("hb%d" % i) for i in range(NT)]
    C.YTb = [[Buf("yt%d_%d" % (m, i)) for i in range(NT)] for m in range(4)]
    C.outb = Buf("out")
    with ExitStack() as es:
        fw = FW(nc, es)
        C.ps = [TL(es.enter_context(nc.psum_tensor("ps%d" % j, [128, 512], F32)), "ps%d" % j) for j in range(6)]
        C.pb = [TL(es.enter_context(nc.psum_tensor("pb%d" % j, [128, 1024], BF16)), "pb%d" % j) for j in range(2)]
        C.consts = fw.tile(es, "consts", [128, CO_TOTAL[0]], F32)
        fw.dma("sp", C.consts.t[:, :], C.consts_d[:, :], writes=[C.consts.b])
        C.identb = fw.tile(es, "identb", [128, 128], BF16)
        C.identf = fw.tile(es, "identf", [128, 128], F32)
        fw.cp("dve", C.identb.t[:, :], cslice(C, "ident"), [C.consts.b], [C.identb.b])
        fw.cp("dve", C.identf.t[:, :], cslice(C, "ident"), [C.consts.b], [C.identf.b])
        C.uT = fw.tile(es, "uT", [128, 8, TP], BF16)
        C.uTb = [Buf("uT%d" % i) for i in range(NT)]
        C.lb_all = fw.tile(es, "lb_all", [128, 4, 4], F32)
        C.oml_all = fw.tile(es, "oml_all", [128, 4, 4], F32)
        lbe = fw.tile(es, "lbe", [128, 4, 4], F32)
        lsum = fw.tile(es, "lsum", [128, 4], F32)
        R = [lbe.b, lsum.b, C.lb_all.b, C.oml_all.b]
        fw.dma("sp", lbe.t[:, :, :], C.lbT[:, :, :], writes=[lbe.b])
        fw.act(lbe.t[:, :, :], lbe.t[:, :, :], AF.Exp, R, R)
        fw.tt("dve", lsum.t[:, :], lbe.t[:, 0, :], lbe.t[:, 1, :], ALU.add, R, R)
        fw.tt("dve", lsum.t[:, :], lsum.t[:, :], lbe.t[:, 2, :], ALU.add, R, R)
        fw.tt("dve", lsum.t[:, :], lsum.t[:, :], lbe.t[:, 3, :], ALU.add, R, R)
        fw.op("dve", lambda g: g.reciprocal(lsum.t[:, :], lsum.t[:, :]), R, R)
        fw.memset("dve", C.lb_all.t[:, 0, :], 0.0, R)
        for ll in range(1, 4):
            fw.tt("dve", lbe.t[:, ll, :], lbe.t[:, ll, :], lsum.t[:, :], ALU.mult, R, R)
            fw.tt("dve", C.lb_all.t[:, ll, :], C.lb_all.t[:, ll - 1, :], lbe.t[:, ll, :], ALU.add, R, R)
        fw.ts("dve", C.oml_all.t[:, :, :], C.lb_all.t[:, :, :], -1.0, 1.0, ALU.mult, ALU.add, R, R)
        fw.barrier()
        allp = ("norm", "ret", "s5", "ssd", "hg", "merge", "mlp")
        for l in range(n_layers):
            for pn in allp:
                if phases is not None and pn not in phases:
                    continue
                if pn == "norm":
                    phase_norm1(fw, C, l)
                elif pn == "ret":
                    phase_ret(fw, C, l)
                elif pn == "s5":
                    phase_s5(fw, C, l)
                elif pn == "ssd":
                    phase_ssd(fw, C, l)
                elif pn == "hg":
                    phase_hg(fw, C, l)
                elif pn == "merge":
                    phase_merge(fw, C, l)
                elif pn == "mlp":
                    phase_mlp(fw, C, l, last=(l == n_layers - 1))
        fw.barrier()
        C.n_inst, C.n_wait = fw.n_inst, fw.n_wait
    return nc, C


CO_TOTAL = [0]
_CONSTS = None


def get_consts():
    global _CONSTS
    if _CONSTS is None:
        _CONSTS = host_consts()
        CO_TOTAL[0] = _CONSTS.shape[1]
    return _CONSTS


def make_in_maps(inp):
    consts = get_consts()
    P = host_params(inp)
    f = lambda a: np.ascontiguousarray(np.asarray(a, np.float32))
    shared = {"consts": consts}
    for nm in ("w_in", "w_branch", "w_out", "w_up", "w_down", "s5_glu_w", "norm_pre_mix", "norm_post_mix", "norm_pre_mlp",
               "norm_post_mlp", "ret_gn_w", "ssd_norm_w", "hgrn_norm_w", "ssd_dt_bias", "ssd_a_log"):
        shared[nm] = f(inp[nm])
    for nm in ("lbT", "ssd_cw", "ssd_cb", "ssd_dsk", "s5_small", "s5_bre", "s5_bim", "s5_cre", "s5_cim", "s5_dT", "s5_gbT"):
        shared[nm] = P[nm]
    x = np.asarray(inp["x"], np.float32)
    meta = np.asarray(inp["meta_tokens"], np.float32)
    maps = []
    for b in range(x.shape[0]):
        h0 = np.zeros((TP, D), np.float32)
        h0[0:16] = meta
        h0[16:T] = x[b]
        m = dict(shared)
        m["h0"] = h0
        maps.append(m)
    return maps


_NC = None


def kernel(**inputs):
    global _NC
    maps = make_in_maps(inputs)
    if _NC is None:
        _NC = build()[0]
    res = run_bass_kernel_spmd(_NC, maps, core_ids=list(range(len(maps))))
    return np.stack([np.asarray(r["out"], np.float32) for r in res.results], axis=0)
```

```python
import math
import os
import numpy as np
from contextlib import ExitStack
import concourse.bass as bass
import concourse.mybir as mybir
from concourse.bass_utils import run_bass_kernel_spmd

F32 = mybir.dt.float32
BF16 = mybir.dt.bfloat16
ALU = mybir.AluOpType
AF = mybir.ActivationFunctionType
AX = mybir.AxisListType

DEPTH = 4
D = 1024
T = 2064
NT = 17
TP = NT * 128
EPS = 1e-6
N_IN = 9736
C_RET, C_S5, C_SSD, C_HG, C_GATE = 0, 1536, 2048, 3592, 5640
GAMMA = [1.0 - 2.0 ** (-5.0 - h) for h in range(4)]


class Buf:
    __slots__ = ("name", "w", "r")

    def __init__(self, name=""):
        self.name = name
        self.w = None
        self.r = []


class TL:
    def __init__(self, t, name):
        self.t = t
        self.b = Buf(name)


LAZY_PE_SIGNAL = os.environ.get("LAZY_PE", "0") == "1"


class VW:
    def __init__(self, t, b):
        self.t = t
        self.b = b


class FW:
    N_DMA_SEMS = 16

    def __init__(self, nc, es):
        self.nc = nc
        self.es = es
        self.eng = {"pe": nc.tensor, "dve": nc.vector, "act": nc.scalar, "pool": nc.gpsimd, "sp": nc.sync}
        self.sems = {}
        self.cnt = {}
        for e in ("pe", "dve", "act", "pool"):
            self.sems[e] = es.enter_context(nc.semaphore("s_" + e))
            self.cnt[e] = 0
        self.dma_keys = {}
        self.dma_rr = {}
        for q in ("sp", "pool"):
            ks = []
            for i in range(self.N_DMA_SEMS):
                k = "d_%s_%d" % (q, i)
                self.sems[k] = es.enter_context(nc.semaphore(k))
                self.cnt[k] = 0
                ks.append(k)
            self.dma_keys[q] = ks
            self.dma_rr[q] = 0
        self.known = {e: {} for e in self.eng}
        self.n_inst = 0
        self.n_wait = 0
        self.uid = 0
        self.pe_last = None
        self.pe_unsig = False
        self.n_sig = 0

    def tile(self, stack, name, shape, dt):
        self.uid += 1
        nm = "%s_%d" % (name, self.uid)
        return TL(stack.enter_context(self.nc.sbuf_tensor(nm, list(shape), dt)), nm)

    def _wait(self, e, ev):
        if ev is None:
            return
        k, v = ev
        if e == "pe" and k == "pe":
            return
        if k == "pe" and v > self.cnt["pe"]:
            self._flush_pe()
        kn = self.known[e]
        if kn.get(k, 0) >= v:
            return
        self.eng[e].wait_ge(self.sems[k], v)
        kn[k] = v
        self.n_wait += 1

    def _deps(self, e, reads, writes):
        for b in reads:
            self._wait(e, b.w)
        for b in writes:
            self._wait(e, b.w)
            for ev in b.r:
                self._wait(e, ev)

    def _mark(self, ev, reads, writes):
        for b in reads:
            b.r.append(ev)
        for b in writes:
            b.w = ev
            b.r = []

    def _flush_pe(self):
        if self.pe_unsig:
            self.cnt["pe"] += 1
            self.pe_last.then_inc(self.sems["pe"], 1)
            self.pe_unsig = False
            self.n_sig += 1

    def op(self, e, fn, reads=(), writes=()):
        self._deps(e, reads, writes)
        ins = fn(self.eng[e])
        if e == "pe" and LAZY_PE_SIGNAL:
            self.pe_last = ins
            self.pe_unsig = True
            ev = (e, self.cnt[e] + 1)
        else:
            self.cnt[e] += 1
            ins.then_inc(self.sems[e], 1)
            ev = (e, self.cnt[e])
        self._mark(ev, reads, writes)
        self.n_inst += 1
        return ev

    def dma(self, q, out, in_, reads=(), writes=(), **kw):
        self._deps(q, reads, writes)
        ks = self.dma_keys[q]
        k = ks[self.dma_rr[q] % len(ks)]
        self.dma_rr[q] += 1
        if self.cnt[k] > 0:
            self._wait(q, (k, self.cnt[k]))
        ins = self.eng[q].dma_start(out=out, in_=in_, **kw)
        self.cnt[k] += 16
        ins.then_inc(self.sems[k], 16)
        ev = (k, self.cnt[k])
        self._mark(ev, reads, writes)
        self.n_inst += 1
        return ev

    def barrier(self, engines=("pe", "dve", "act", "pool", "sp")):
        self._flush_pe()
        for e in engines:
            for k, v in self.cnt.items():
                if v > 0:
                    self._wait(e, (k, v))

    def tt(self, e, out, a, b, op, R, W):
        return self.op(e, lambda g: g.tensor_tensor(out, a, b, op), R, W)

    def ts(self, e, out, a, s1, s2, op0, op1, R, W):
        if s2 is None:
            return self.op(e, lambda g: g.tensor_scalar(out, a, s1, None, op0=op0), R, W)
        return self.op(e, lambda g: g.tensor_scalar(out, a, s1, s2, op0=op0, op1=op1), R, W)

    def stt(self, e, out, in0, sc, in1, op0, op1, R, W):
        e = "dve"
        return self.op(e, lambda g: g.scalar_tensor_tensor(out, in0, sc, in1, op0=op0, op1=op1), R, W)

    def cp(self, e, out, in_, R, W):
        if e == "act":
            return self.op(e, lambda g: g.copy(out, in_), R, W)
        return self.op(e, lambda g: g.tensor_copy(out, in_), R, W)

    def act(self, out, in_, func, R, W, bias=None, scale=None, accum=None):
        kw = {}
        if bias is not None:
            kw["bias"] = bias
        if scale is not None:
            kw["scale"] = scale
        if accum is not None:
            kw["accum_out"] = accum
        return self.op("act", lambda g: g.activation(out, in_, func, **kw), R, W)

    def mm(self, out, lhsT, rhs, start, stop, R, W):
        return self.op("pe", lambda g: g.matmul(out, lhsT, rhs, start=start, stop=stop), R, W)

    def tr(self, out, in_, ident, R, W):
        return self.op("pe", lambda g: g.transpose(out, in_, ident), R, W)

    def memset(self, e, ap, val, W):
        return self.op(e, lambda g: g.memset(ap, val), (), W)


CO = {}


def _pack(items):
    off = 0
    cols = []
    for name, arr in items:
        arr = np.asarray(arr, np.float32).reshape(128, -1)
        CO[name] = (off, arr.shape[1])
        off += arr.shape[1]
        cols.append(arr)
    return np.ascontiguousarray(np.concatenate(cols, axis=1))


def host_consts():
    s = np.arange(128)[:, None]
    t = np.arange(128)[None, :]
    ident = (s == t).astype(np.float32)
    triu = (s <= t).astype(np.float32)
    negm = np.where(s <= t, 0.0, -30000.0).astype(np.float32)
    ones = np.ones((128, 128), np.float32)
    retdt = np.zeros((128, 4, 128), np.float64)
    qdec = np.zeros((128, 4, 64), np.float64)
    kdec = np.zeros((128, 4, 64), np.float64)
    for h in range(4):
        g = GAMMA[h]
        retdt[:, h, :] = np.where(s <= t, 0.125 * g ** np.maximum(t - s, 0), 0.0)
        qdec[:, h, :] = (g ** (np.arange(128) + 1.0))[:, None]
        kdec[:, h, :] = (0.125 * g ** (127.0 - np.arange(128)))[:, None]
    half = 32
    inv_freq = (10000.0 ** (-np.arange(half, dtype=np.float32) / half)).astype(np.float32)
    pos = (np.arange(NT)[None, :] * 128 + np.arange(128)[:, None]).astype(np.float32)
    ang = pos[:, :, None] * inv_freq[None, None, :]
    cos = np.cos(ang).astype(np.float32)
    sin = np.sin(ang).astype(np.float32)
    halfpi = np.full((128, 1), math.pi / 2, np.float32)
    mvals = np.array(list(range(17)) + [0.5], np.float32)
    mtab = np.broadcast_to(mvals[None, :, None], (128, 18, 16))
    return _pack([("mtab", mtab), ("ident", ident), ("triu", triu), ("negm", negm), ("ones", ones), ("retdt", retdt),
                  ("qdec", qdec), ("kdec", kdec), ("cos", cos), ("sin", sin), ("halfpi", halfpi)])


def host_params(inp):
    P = {}
    f = lambda a: np.ascontiguousarray(np.asarray(a, np.float32))
    P["lbT"] = f(np.asarray(inp["hgrn_lb"]).reshape(4, 4, 128).transpose(2, 0, 1))
    P["ssd_cw"] = f(np.asarray(inp["ssd_conv_w"]).reshape(4, 4, 8, 128).transpose(0, 3, 2, 1))
    P["ssd_cb"] = f(np.asarray(inp["ssd_conv_b"]).reshape(4, 8, 128).transpose(0, 2, 1))
    P["ssd_dsk"] = f(np.repeat(np.asarray(inp["ssd_d"]), 64, axis=1))
    def pl_small(a):
        a = np.asarray(a).reshape(4, 16, 2, 64)
        return a.transpose(0, 2, 3, 1).reshape(4, 128, 16)
    ls = np.broadcast_to(np.asarray(inp["s5_log_step"])[:, :, None], (4, 32, 64))
    P["s5_small"] = f(np.concatenate([pl_small(inp["s5_lam_re"]), pl_small(inp["s5_lam_im"]), pl_small(ls)], axis=2))
    def pl_b(b):
        b = np.asarray(b)
        out = np.zeros((4, 2, 64, 16, 8, 16), np.float32)
        for g in range(32):
            out[:, g % 2, :, g // 2, g % 8, :] = b[:, g]
        return out.reshape(4, 128, 16, 128)
    def pl_c(c):
        c = np.asarray(c)
        out = np.zeros((4, 2, 64, 16, 8, 16), np.float32)
        for g in range(32):
            out[:, g % 2, :, g // 2, g % 8, :] = c[:, g].transpose(0, 2, 1)
        return out.reshape(4, 128, 16, 128)
    P["s5_bre"] = pl_b(inp["s5_b_re"])
    P["s5_bim"] = pl_b(inp["s5_b_im"])
    P["s5_cre"] = pl_c(inp["s5_c_re"])
    P["s5_cim"] = pl_c(inp["s5_c_im"])
    P["s5_dT"] = f(np.asarray(inp["s5_d"]).reshape(4, 4, 128).transpose(0, 2, 1))
    P["s5_gbT"] = f(np.asarray(inp["s5_glu_b"]).reshape(4, 8, 128).transpose(0, 2, 1))
    return P


class Ctx:
    pass


def cslice(C, name, a=None, b=None):
    off, n = CO[name]
    if a is None:
        return C.consts.t[:, off:off + n]
    return C.consts.t[:, off + a:off + b]


def load_w(fw, src2d, dst, c0, c1, kcs, R=()):
    v = src2d.rearrange("(kc p) n -> p kc n", p=128)
    step = int(os.environ.get("LW_CSTEP", "4096"))
    kstep = min(kcs, int(os.environ.get("LW_KSTEP", "16")))
    for k0 in range(0, kcs, kstep):
        for a in range(c0, c1, step):
            b = min(a + step, c1)
            fw.dma("pool", dst.t[:, k0:k0 + kstep, a - c0:b - c0], v[:, k0:k0 + kstep, a:b], reads=R, writes=[dst.b])


def rstd_from_ss(fw, C, ss_ap, out_ap, n, R, W):
    fw.ts("dve", out_ap, ss_ap, 1.0 / n, EPS, ALU.mult, ALU.add, R, W)
    fw.act(out_ap, out_ap, AF.Sqrt, W, W)
    fw.op("dve", lambda g: g.reciprocal(out_ap, out_ap), W, W)


def norm_rows_to_T(fw, C, ph, x_ap, xb, nw, dstT, dst_bufs, i, wk):
    junk, ss, ub = wk
    fw.act(junk.t[:, :], x_ap, AF.Square, [xb], [junk.b, ss.b], accum=ss.t[:, 0:1])
    rstd_from_ss(fw, C, ss.t[:, 0:1], ss.t[:, 1:2], 1024.0, [ss.b], [ss.b])
    fw.stt("dve", ub.t[:, :], x_ap, ss.t[:, 1:2], nw.t[:, :], ALU.mult, ALU.mult, [xb, ss.b, nw.b], [ub.b])
    pb = C.pb[i % 2]
    for kc in range(8):
        fw.tr(pb.t[:, kc * 128:(kc + 1) * 128], ub.t[:, kc * 128:(kc + 1) * 128], C.identb.t[:, :], [ub.b, C.identb.b], [pb.b])
    fw.cp("act" if i % 2 else "dve", dstT.t[:, :, i * 128:(i + 1) * 128],
          pb.t[:, :].rearrange("p (k t) -> p k t", k=8), [pb.b], [dst_bufs[i]])


def phase_norm1(fw, C, l):
    with ExitStack() as ph:
        nw = fw.tile(ph, "nw", [128, D], F32)
        fw.dma("sp", nw.t[:, :], C.norm_pre_mix[l].partition_broadcast(128), writes=[nw.b])
        xts = [fw.tile(ph, "xt", [128, D], F32) for _ in range(2)]
        wk = (fw.tile(ph, "junk", [128, D], BF16), fw.tile(ph, "ss", [128, 2], F32), fw.tile(ph, "ub", [128, D], BF16))
        for i in range(NT):
            xt = xts[i % 2]
            src = C.h0 if l == 0 else C.hbuf
            fw.dma("sp", xt.t[:, :], src[i * 128:(i + 1) * 128, :], reads=([] if l == 0 else [C.hb[i]]), writes=[xt.b])
            norm_rows_to_T(fw, C, ph, xt.t[:, :], xt.b, nw, C.uT, C.uTb, i, wk)
        fw.barrier()


ILV_OFF = set(os.environ.get("NOILV", "ret,ssd").split(","))
CUR_PHASE = [""]


def interleave(gb, ga):
    if CUR_PHASE[0] in ILV_OFF:
        for g in (ga, gb):
            if g is not None:
                for _ in g:
                    pass
        return
    done_a = ga is None
    done_b = False
    while not (done_a and done_b):
        if not done_b:
            try:
                next(gb)
            except StopIteration:
                done_b = True
        if not done_a:
            try:
                next(ga)
            except StopIteration:
                done_a = True


def interleave_n(gens):
    gens = list(gens)
    if CUR_PHASE[0] in ILV_OFF:
        for g in reversed(gens):
            for _ in g:
                pass
        return
    while gens:
        for g in list(gens):
            try:
                next(g)
            except StopIteration:
                gens.remove(g)


def chain_gens(*gens):
    for g in gens:
        if g is not None:
            yield from g


def emit_yT(fw, C, yo, yT, mix, i):
    pb = C.pb[1]
    for cb in range(4):
        fw.tr(pb.t[:, cb * 128:(cb + 1) * 128], yo.t[:, cb * 128:(cb + 1) * 128], C.identb.t[:, :], [yo.b, C.identb.b], [pb.b])
    fw.cp("act", yT.t[:, :, :], pb.t[:, 0:512].rearrange("p (c t) -> p c t", c=4), [pb.b], [yT.b])
    fw.dma("sp", C.YT[mix * 4:(mix + 1) * 4, :, i * 128:(i + 1) * 128].rearrange("c p t -> p c t"), yT.t[:, :, :],
           reads=[yT.b], writes=[C.YTb[mix][i]])


def tok_mm(fw, C, ps, Wt, c0, n, i, ncols=None):
    for kc in range(8):
        fw.mm(ps.t[:, 0:n], C.uT.t[:, kc, i * 128:(i + 1) * 128], Wt.t[:, kc, c0:c0 + n], kc == 0, kc == 7,
              [C.uTb[i], Wt.b], [ps.b])


def phase_ret(fw, C, l):
    CUR_PHASE[0] = "ret"
    with ExitStack() as ph:
        WR = fw.tile(ph, "WR", [128, 8, 1536], BF16)
        load_w(fw, C.w_in[l], WR, C_RET, C_RET + 1536, 8)
        gnw = fw.tile(ph, "gnw", [128, 512], F32)
        fw.dma("sp", gnw.t[:, :], C.ret_gn_w[l].partition_broadcast(128), writes=[gnw.b])
        S = fw.tile(ph, "rS", [64, 4, 128], F32)
        Sb = fw.tile(ph, "rSb", [64, 4, 128], BF16)
        fw.memset("dve", S.t[:, :, :], 0.0, [S.b])
        fw.memset("dve", Sb.t[:, :, :], 0.0, [Sb.b])
        qkr_2 = [fw.tile(ph, "qkr", [128, 8, 64], F32) for _ in range(2)]
        t1_2 = [fw.tile(ph, "t1", [128, 8, 32], F32) for _ in range(2)]
        t2_2 = [fw.tile(ph, "t2", [128, 8, 32], F32) for _ in range(2)]
        qb_2 = [fw.tile(ph, "qb", [128, 256], BF16) for _ in range(2)]
        qdb_2 = [fw.tile(ph, "qdb", [128, 256], BF16) for _ in range(2)]
        kb_2 = [fw.tile(ph, "kb", [128, 256], BF16) for _ in range(2)]
        khb_2 = [fw.tile(ph, "khb", [128, 256], BF16) for _ in range(2)]
        qkT_2 = [fw.tile(ph, "qkT", [64, 12, 128], BF16) for _ in range(2)]
        vb_2 = [fw.tile(ph, "vb", [128, 512], BF16) for _ in range(2)]
        sg_2 = [fw.tile(ph, "sg", [128, 512], F32) for _ in range(2)]
        PT_2 = [fw.tile(ph, "PT", [128, 512], BF16) for _ in range(2)]
        ysq_2 = [fw.tile(ph, "ysq", [128, 512], F32) for _ in range(2)]
        st_2 = [fw.tile(ph, "st", [128, 16], F32) for _ in range(2)]
        yn_2 = [fw.tile(ph, "yn", [128, 512], F32) for _ in range(2)]
        yo_2 = [fw.tile(ph, "yo", [128, 512], BF16) for _ in range(2)]
        yT_2 = [fw.tile(ph, "yT", [128, 4, 128], BF16) for _ in range(2)]
        ps_qk, ps_v, ps_g, ps_s, ps_y, ps_st = C.ps[0:6]
        cb_ = C.consts.b
        def stA(i):
                qkr, t1, t2, qb, qdb, kb, khb, qkT, vb, sg, PT, ysq, st, yn, yo, yT = qkr_2[i % 2], t1_2[i % 2], t2_2[i % 2], qb_2[i % 2], qdb_2[i % 2], kb_2[i % 2], khb_2[i % 2], qkT_2[i % 2], vb_2[i % 2], sg_2[i % 2], PT_2[i % 2], ysq_2[i % 2], st_2[i % 2], yn_2[i % 2], yo_2[i % 2], yT_2[i % 2]
                tok_mm(fw, C, ps_qk, WR, 0, 512, i)
                yield
                tok_mm(fw, C, ps_v, WR, 512, 512, i)
                yield
                tok_mm(fw, C, ps_g, WR, 1024, 512, i)
                yield
                qv = ps_qk.t[:, :].rearrange("p (h d) -> p h d", d=64)
                x1, x2 = qv[:, :, 0:32], qv[:, :, 32:64]
                co, cn = CO["cos"][0], CO["sin"][0]
                cosb = C.consts.t[:, co + i * 32:co + (i + 1) * 32].unsqueeze(1).to_broadcast([128, 8, 32])
                sinb = C.consts.t[:, cn + i * 32:cn + (i + 1) * 32].unsqueeze(1).to_broadcast([128, 8, 32])
                fw.tt("dve", t1.t[:, :, :], x1, cosb, ALU.mult, [ps_qk.b, cb_], [t1.b])
                fw.tt("dve", t2.t[:, :, :], x2, sinb, ALU.mult, [ps_qk.b, cb_], [t2.b])
                fw.tt("pool", qkr.t[:, :, 0:32], t1.t[:, :, :], t2.t[:, :, :], ALU.subtract, [t1.b, t2.b], [qkr.b])
                fw.tt("dve", t1.t[:, :, :], x1, sinb, ALU.mult, [ps_qk.b, cb_], [t1.b])
                fw.tt("dve", t2.t[:, :, :], x2, cosb, ALU.mult, [ps_qk.b, cb_], [t2.b])
                fw.tt("pool", qkr.t[:, :, 32:64], t1.t[:, :, :], t2.t[:, :, :], ALU.add, [t1.b, t2.b], [qkr.b])
                qf = qkr.t[:, 0:4, :].rearrange("p h d -> p (h d)")
                kf = qkr.t[:, 4:8, :].rearrange("p h d -> p (h d)")
                fw.cp("act", qb.t[:, :], qf, [qkr.b], [qb.b])
                fw.tt("pool", qdb.t[:, :], qf, cslice(C, "qdec"), ALU.mult, [qkr.b, cb_], [qdb.b])
                fw.cp("act", kb.t[:, :], kf, [qkr.b], [kb.b])
                fw.tt("pool", khb.t[:, :], kf, cslice(C, "kdec"), ALU.mult, [qkr.b, cb_], [khb.b])
                fw.cp("act", vb.t[:, :], ps_v.t[:, :], [ps_v.b], [vb.b])
                fw.act(sg.t[:, :], ps_g.t[:, :], AF.Silu, [ps_g.b], [sg.b])
        def stB(i):
                qkr, t1, t2, qb, qdb, kb, khb, qkT, vb, sg, PT, ysq, st, yn, yo, yT = qkr_2[i % 2], t1_2[i % 2], t2_2[i % 2], qb_2[i % 2], qdb_2[i % 2], kb_2[i % 2], khb_2[i % 2], qkT_2[i % 2], vb_2[i % 2], sg_2[i % 2], PT_2[i % 2], ysq_2[i % 2], st_2[i % 2], yn_2[i % 2], yo_2[i % 2], yT_2[i % 2]
                pb0, pb1 = C.pb
                for h in range(4):
                    fw.tr(pb0.t[0:64, h * 128:(h + 1) * 128], qb.t[:, h * 64:(h + 1) * 64], C.identb.t[:, :], [qb.b, C.identb.b], [pb0.b])
                    fw.tr(pb0.t[0:64, (4 + h) * 128:(5 + h) * 128], qdb.t[:, h * 64:(h + 1) * 64], C.identb.t[:, :], [qdb.b, C.identb.b], [pb0.b])
                    fw.tr(pb1.t[0:64, h * 128:(h + 1) * 128], kb.t[:, h * 64:(h + 1) * 64], C.identb.t[:, :], [kb.b, C.identb.b], [pb1.b])
                yield
                fw.cp("dve", qkT.t[:, 0:8, :], pb0.t[0:64, :].rearrange("p (j t) -> p j t", j=8), [pb0.b], [qkT.b])
                fw.cp("act", qkT.t[:, 8:12, :], pb1.t[0:64, 0:512].rearrange("p (j t) -> p j t", j=4), [pb1.b], [qkT.b])
                for h in range(4):
                    fw.mm(ps_s.t[:, h * 128:(h + 1) * 128], qkT.t[:, 8 + h, :], qkT.t[:, h, :], True, True, [qkT.b], [ps_s.b])
                yield
                fw.tt("dve", PT.t[:, :], ps_s.t[:, :], cslice(C, "retdt"), ALU.mult, [ps_s.b, cb_], [PT.b])
                for h in range(4):
                    hs = slice(h * 128, (h + 1) * 128)
                    fw.mm(ps_y.t[:, hs], PT.t[:, hs], vb.t[:, hs], True, False, [PT.b, vb.b], [ps_y.b])
                    fw.mm(ps_y.t[:, hs], qkT.t[:, 4 + h, :], Sb.t[:, h, :], False, True, [qkT.b, Sb.b], [ps_y.b])
                yield
                for h in range(4):
                    hs = slice(h * 128, (h + 1) * 128)
                    fw.mm(ps_st.t[0:64, hs], khb.t[:, h * 64:(h + 1) * 64], vb.t[:, hs], True, True, [khb.b, vb.b], [ps_st.b])
                yield
                for h in range(4):
                    hs = slice(h * 128, (h + 1) * 128)
                    fw.stt("dve", S.t[:, h, :], S.t[:, h, :], float(GAMMA[h] ** 128), ps_st.t[0:64, hs], ALU.mult, ALU.add,
                           [S.b, ps_st.b], [S.b])
                fw.cp("pool", Sb.t[:, :, :], S.t[:, :, :], [S.b], [Sb.b])
                yv = ps_y.t[:, :].rearrange("p (h e) -> p h e", h=4)
                fw.op("dve", lambda g: g.reduce_sum(st.t[:, 0:4], yv, axis=AX.X), [ps_y.b], [st.b])
                fw.act(ysq.t[:, :], ps_y.t[:, :], AF.Square, [ps_y.b], [ysq.b])
                fw.op("dve", lambda g: g.reduce_sum(st.t[:, 4:8], ysq.t[:, :].rearrange("p (h e) -> p h e", h=4), axis=AX.X), [ysq.b], [st.b])
                fw.ts("dve", st.t[:, 8:12], st.t[:, 0:4], 1.0 / 128, None, ALU.mult, None, [st.b], [st.b])
                fw.tt("dve", st.t[:, 0:4], st.t[:, 8:12], st.t[:, 8:12], ALU.mult, [st.b], [st.b])
                fw.stt("dve", st.t[:, 12:16], st.t[:, 4:8], 1.0 / 128, st.t[:, 0:4], ALU.mult, ALU.subtract, [st.b], [st.b])
                fw.ts("dve", st.t[:, 12:16], st.t[:, 12:16], EPS, None, ALU.add, None, [st.b], [st.b])
                fw.act(st.t[:, 12:16], st.t[:, 12:16], AF.Sqrt, [st.b], [st.b])
                fw.op("dve", lambda g: g.reciprocal(st.t[:, 12:16], st.t[:, 12:16]), [st.b], [st.b])
                for h in range(4):
                    hs = slice(h * 128, (h + 1) * 128)
                    fw.ts("dve", yn.t[:, hs], ps_y.t[:, hs], st.t[:, 8 + h:9 + h], st.t[:, 12 + h:13 + h], ALU.subtract, ALU.mult,
                          [ps_y.b, st.b], [yn.b])
                fw.tt("pool", yn.t[:, :], yn.t[:, :], gnw.t[:, :], ALU.mult, [yn.b, gnw.b], [yn.b])
                fw.tt("pool", yo.t[:, :], yn.t[:, :], sg.t[:, :], ALU.mult, [yn.b, sg.b], [yo.b])
                emit_yT(fw, C, yo, yT, 0, i)
                yield
        for _ in stA(0):
            pass
        for i in range(NT):
            interleave(stB(i), stA(i + 1) if i + 1 < NT else None)
        fw.barrier()


def phase_ssd(fw, C, l):
    CUR_PHASE[0] = "ssd"
    with ExitStack() as ph:
        WS = fw.tile(ph, "WS", [128, 8, 1544], BF16)
        load_w(fw, C.w_in[l], WS, C_SSD, C_SSD + 1544, 8)
        cw = fw.tile(ph, "cw", [128, 8, 4], F32)
        cbi = fw.tile(ph, "cbi", [128, 8], F32)
        dtb = fw.tile(ph, "dtb", [128, 8], F32)
        arow = fw.tile(ph, "arow", [128, 8], F32)
        dsk = fw.tile(ph, "dsk", [128, 512], F32)
        nw = fw.tile(ph, "snw", [128, 512], F32)
        fw.dma("sp", cw.t[:, :, :], C.ssd_cw[l], writes=[cw.b])
        fw.dma("sp", cbi.t[:, :], C.ssd_cb[l], writes=[cbi.b])
        fw.dma("sp", dtb.t[:, :], C.ssd_dt_bias[l].partition_broadcast(128), writes=[dtb.b])
        fw.dma("sp", arow.t[:, :], C.ssd_a_log[l].partition_broadcast(128), writes=[arow.b])
        fw.dma("sp", dsk.t[:, :], C.ssd_dsk[l].partition_broadcast(128), writes=[dsk.b])
        fw.dma("sp", nw.t[:, :], C.ssd_norm_w[l].partition_broadcast(128), writes=[nw.b])
        fw.act(arow.t[:, :], arow.t[:, :], AF.Exp, [arow.b], [arow.b])
        fw.ts("dve", arow.t[:, :], arow.t[:, :], -1.0, None, ALU.mult, None, [arow.b], [arow.b])
        S = fw.tile(ph, "sS", [128, 512], F32)
        Sb = fw.tile(ph, "sSb", [128, 512], BF16)
        fw.memset("dve", S.t[:, :], 0.0, [S.b])
        fw.memset("dve", Sb.t[:, :], 0.0, [Sb.b])
        xrawg = fw.tile(ph, "xrawg", [128, 8, 515], F32)
        fw.memset("pool", xrawg.t[:, :, :], 0.0, [xrawg.b])
        xcg = fw.tile(ph, "xcg", [128, 8, 512], F32)
        xcbg_2 = [fw.tile(ph, "xcbg", [128, 8, 512], BF16) for _ in range(2)]
        xs_2 = [fw.tile(ph, "xs", [128, 512], F32) for _ in range(2)]
        Btok_2 = [fw.tile(ph, "Btok", [128, 2, 128], BF16) for _ in range(2)]
        sz_2 = [fw.tile(ph, "sz", [128, 512], F32) for _ in range(2)]
        sm_2 = [fw.tile(ph, "sm", [128, 104], F32) for _ in range(2)]
        rhs8_2 = [fw.tile(ph, "rhs8", [128, 8, 128], F32) for _ in range(2)]
        decT_2 = [fw.tile(ph, "decT", [128, 8, 128], F32) for _ in range(2)]
        PT_2 = [fw.tile(ph, "sPT", [128, 8, 128], BF16) for _ in range(2)]
        xdt_2 = [fw.tile(ph, "xdt", [128, 512], BF16) for _ in range(2)]
        xdtw_2 = [fw.tile(ph, "xdtw", [128, 512], BF16) for _ in range(2)]
        ya_2 = [fw.tile(ph, "ya", [128, 512], F32) for _ in range(2)]
        tmp_2 = [fw.tile(ph, "stmp", [128, 512], F32) for _ in range(2)]
        yo_2 = [fw.tile(ph, "syo", [128, 512], BF16) for _ in range(2)]
        yT_2 = [fw.tile(ph, "syT", [128, 4, 128], BF16) for _ in range(2)]
        ps = C.ps
        pb0, pb1 = C.pb
        cb_ = C.consts.b
        XD, AXc, EX, LN, DT, LA, CUM, NCUM, ECUM, WW, EDEC, DW = [slice(8 * j, 8 * j + 8) for j in range(12)]
        triu = cslice(C, "triu")
        ones = cslice(C, "ones")
        def stG(g):
                t0 = g * 512
                n = min(512, TP - t0)
                tiles = list(range(t0 // 128, (t0 + n) // 128))
                xcbg = xcbg_2[g % 2]
                if g > 0:
                    fw.cp("pool", xrawg.t[:, :, 0:3], xrawg.t[:, :, 512:515], [xrawg.b], [xrawg.b])
                for cb in range(8):
                    pst = ps[cb % 2]
                    for kc in range(8):
                        fw.mm(pst.t[:, 0:n], WS.t[:, kc, 512 + cb * 128:512 + (cb + 1) * 128], C.uT.t[:, kc, t0:t0 + n], kc == 0, kc == 7,
                              [WS.b] + [C.uTb[j] for j in tiles], [pst.b])
                    fw.cp("act" if cb % 2 else "dve", xrawg.t[:, cb, 3:3 + n], pst.t[:, 0:n], [pst.b], [xrawg.b])
                for cb in range(8):
                    fw.ts("dve" if cb % 2 == 0 else "pool", xcg.t[:, cb, 0:n], xrawg.t[:, cb, 3:3 + n], cw.t[:, cb, 3:4], cbi.t[:, cb:cb + 1],
                          ALU.mult, ALU.add, [xrawg.b, cw.b, cbi.b], [xcg.b])
                    for j in (2, 1, 0):
                        fw.stt("dve", xcg.t[:, cb, 0:n], xrawg.t[:, cb, j:j + n], cw.t[:, cb, j:j + 1], xcg.t[:, cb, 0:n], ALU.mult, ALU.add,
                               [xrawg.b, cw.b, xcg.b], [xcg.b])
                fw.act(xcbg.t[:, :, 0:n], xcg.t[:, :, 0:n], AF.Silu, [xcg.b], [xcbg.b])
                yield
        def stA(i):
                xs, Btok, sz, sm, decT, PT, xdt, xdtw, ya, tmp, yo, yT = xs_2[i % 2], Btok_2[i % 2], sz_2[i % 2], sm_2[i % 2], decT_2[i % 2], PT_2[i % 2], xdt_2[i % 2], xdtw_2[i % 2], ya_2[i % 2], tmp_2[i % 2], yo_2[i % 2], yT_2[i % 2]
                xcb = VW(xcbg_2[(i // 4) % 2].t[:, :, (i % 4) * 128:(i % 4 + 1) * 128], xcbg_2[(i // 4) % 2].b)
                tok = slice(i * 128, (i + 1) * 128)
                tok_mm(fw, C, ps[2], WS, 0, 512, i)
                yield
                fw.act(sz.t[:, :], ps[2].t[:, :], AF.Silu, [ps[2].b], [sz.b])
                for kc in range(8):
                    fw.mm(ps[3].t[:, 0:8], C.uT.t[:, kc, tok], WS.t[:, kc, 1536:1544], kc == 0, kc == 7, [C.uTb[i], WS.b], [ps[3].b])
                yield
                smb = [sm.b]
                fw.tt("dve", sm.t[:, XD], ps[3].t[:, 0:8], dtb.t[:, :], ALU.add, [ps[3].b, dtb.b], smb)
                fw.ts("dve", sm.t[:, AXc], sm.t[:, XD], -1.0, None, ALU.mult, None, smb, smb)
                fw.tt("dve", sm.t[:, AXc], sm.t[:, AXc], sm.t[:, XD], ALU.max, smb, smb)
                fw.act(sm.t[:, EX], sm.t[:, AXc], AF.Exp, smb, smb, scale=-1.0)
                fw.ts("dve", sm.t[:, EX], sm.t[:, EX], 1.0, None, ALU.add, None, smb, smb)
                fw.act(sm.t[:, LN], sm.t[:, EX], AF.Ln, smb, smb)
                fw.stt("dve", sm.t[:, DT], sm.t[:, XD], 0.0, sm.t[:, LN], ALU.max, ALU.add, smb, smb)
                fw.tt("dve", sm.t[:, LA], sm.t[:, DT], arow.t[:, :], ALU.mult, smb + [arow.b], smb)
                fw.mm(ps[3].t[:, 16:24], triu, sm.t[:, LA], True, True, [cb_, sm.b], [ps[3].b])
                yield
                fw.mm(ps[3].t[:, 32:40], ones, sm.t[:, LA], True, True, [cb_, sm.b], [ps[3].b])
                yield
                fw.cp("dve", sm.t[:, CUM], ps[3].t[:, 16:24], [ps[3].b], smb)
                fw.ts("dve", sm.t[:, NCUM], sm.t[:, CUM], -1.0, None, ALU.mult, None, smb, smb)
                fw.act(sm.t[:, ECUM], sm.t[:, CUM], AF.Exp, smb, smb)
                fw.tt("dve", sm.t[:, WW], ps[3].t[:, 32:40], sm.t[:, CUM], ALU.subtract, [ps[3].b] + smb, smb)
                fw.act(sm.t[:, WW], sm.t[:, WW], AF.Exp, smb, smb)
                fw.act(sm.t[:, EDEC], ps[3].t[:, 32:40], AF.Exp, [ps[3].b], smb)
                fw.tt("dve", sm.t[:, DW], sm.t[:, DT], sm.t[:, WW], ALU.mult, smb, smb)
        def stB(i):
                xs, Btok, sz, sm, decT, PT, xdt, xdtw, ya, tmp, yo, yT = xs_2[i % 2], Btok_2[i % 2], sz_2[i % 2], sm_2[i % 2], decT_2[i % 2], PT_2[i % 2], xdt_2[i % 2], xdtw_2[i % 2], ya_2[i % 2], tmp_2[i % 2], yo_2[i % 2], yT_2[i % 2]
                xcb = VW(xcbg_2[(i // 4) % 2].t[:, :, (i % 4) * 128:(i % 4 + 1) * 128], xcbg_2[(i // 4) % 2].b)
                smb = [sm.b]
                pb0, pb1 = C.pb
                rhs8 = rhs8_2[i % 2]
                for cb in range(4):
                    fw.tr(pb0.t[:, cb * 128:(cb + 1) * 128], xcb.t[:, cb, :], C.identb.t[:, :], [xcb.b, C.identb.b], [pb0.b])
                yield
                fw.cp("dve", xs.t[:, :], pb0.t[:, 0:512], [pb0.b], [xs.b])
                for g in range(2):
                    fw.tr(pb1.t[:, g * 128:(g + 1) * 128], xcb.t[:, 4 + g, :], C.identb.t[:, :], [xcb.b, C.identb.b], [pb1.b])
                yield
                fw.cp("act", Btok.t[:, :, :], pb1.t[:, 0:256].rearrange("p (g n) -> p g n", g=2), [pb1.b], [Btok.b])
                fw.tt("dve", rhs8.t[:, :, :], triu.unsqueeze(1).to_broadcast([128, 8, 128]),
                      sm.t[:, LA].unsqueeze(2).to_broadcast([128, 8, 128]), ALU.mult, [cb_, sm.b], [rhs8.b])
                fw.mm(ps[4].t[:, :], ones, rhs8.t[:, 0:4, :].rearrange("p h t -> p (h t)"), True, True, [cb_, rhs8.b], [ps[4].b])
                fw.mm(ps[5].t[:, :], ones, rhs8.t[:, 4:8, :].rearrange("p h t -> p (h t)"), True, True, [cb_, rhs8.b], [ps[5].b])
                for hb in range(2):
                    fw.tt("dve", decT.t[:, 4 * hb:4 * hb + 4, :], ps[4 + hb].t[:, :].rearrange("p (h t) -> p h t", h=4),
                          sm.t[:, 56 + 4 * hb:60 + 4 * hb].unsqueeze(2).to_broadcast([128, 4, 128]), ALU.add, [ps[4 + hb].b, sm.b], [decT.b])
                fw.tt("pool", decT.t[:, :, :], decT.t[:, :, :], cslice(C, "negm").unsqueeze(1).to_broadcast([128, 8, 128]), ALU.add,
                      [decT.b, cb_], [decT.b])
                fw.act(decT.t[:, :, :], decT.t[:, :, :], AF.Exp, [decT.b], [decT.b])
                for g in range(2):
                    fw.mm(ps[4].t[:, g * 128:(g + 1) * 128], xcb.t[:, 4 + g, :], xcb.t[:, 6 + g, :], True, True, [xcb.b], [ps[4].b])
                yield
                for g in range(2):
                    fw.tt("dve", PT.t[:, 4 * g:4 * g + 4, :], decT.t[:, 4 * g:4 * g + 4, :],
                          ps[4].t[:, g * 128:(g + 1) * 128].unsqueeze(1).to_broadcast([128, 4, 128]), ALU.mult, [decT.b, ps[4].b], [PT.b])
                xsv = xs.t[:, :].rearrange("p (h e) -> p h e", h=8)
                fw.tt("pool", xdt.t[:, :].rearrange("p (h e) -> p h e", h=8), xsv, sm.t[:, DT].unsqueeze(2).to_broadcast([128, 8, 64]),
                      ALU.mult, [xs.b, sm.b], [xdt.b])
                fw.tt("pool", xdtw.t[:, :].rearrange("p (h e) -> p h e", h=8), xsv, sm.t[:, DW].unsqueeze(2).to_broadcast([128, 8, 64]),
                      ALU.mult, [xs.b, sm.b], [xdtw.b])
                for h in range(8):
                    fw.mm(ps[5].t[:, h * 64:(h + 1) * 64], PT.t[:, h, :], xdt.t[:, h * 64:(h + 1) * 64], True, True, [PT.b, xdt.b], [ps[5].b])
                yield
                for g in range(2):
                    fw.mm(ps[4].t[:, g * 256:(g + 1) * 256], xcb.t[:, 6 + g, :], Sb.t[:, g * 256:(g + 1) * 256], True, True, [xcb.b, Sb.b], [ps[4].b])
                yield
                fw.tt("dve", ya.t[:, :].rearrange("p (h e) -> p h e", h=8), ps[4].t[:, :].rearrange("p (h e) -> p h e", h=8),
                      sm.t[:, ECUM].unsqueeze(2).to_broadcast([128, 8, 64]), ALU.mult, [ps[4].b, sm.b], [ya.b])
                fw.tt("dve", ya.t[:, :], ya.t[:, :], ps[5].t[:, :], ALU.add, [ya.b, ps[5].b], [ya.b])
                fw.tt("pool", tmp.t[:, :], xs.t[:, :], dsk.t[:, :], ALU.mult, [xs.b, dsk.b], [tmp.b])
                fw.tt("pool", ya.t[:, :], ya.t[:, :], tmp.t[:, :], ALU.add, [ya.b, tmp.b], [ya.b])
                fw.tt("pool", ya.t[:, :], ya.t[:, :], sz.t[:, :], ALU.mult, [ya.b, sz.b], [ya.b])
                for g in range(2):
                    fw.mm(ps[5].t[:, g * 256:(g + 1) * 256], Btok.t[:, g, :], xdtw.t[:, g * 256:(g + 1) * 256], True, True,
                          [Btok.b, xdtw.b], [ps[5].b])
                yield
                Sv = S.t[:, :].rearrange("p (h e) -> p h e", h=8)
                fw.tt("dve", Sv, Sv, sm.t[:, EDEC].unsqueeze(2).to_broadcast([128, 8, 64]), ALU.mult, [S.b, sm.b], [S.b])
                fw.tt("dve", S.t[:, :], S.t[:, :], ps[5].t[:, :], ALU.add, [S.b, ps[5].b], [S.b])
                fw.cp("pool", Sb.t[:, :], S.t[:, :], [S.b], [Sb.b])
                fw.act(tmp.t[:, :], ya.t[:, :], AF.Square, [ya.b], [tmp.b])
                fw.op("dve", lambda g_: g_.reduce_sum(sm.t[:, 96:98], tmp.t[:, :].rearrange("p (g e) -> p g e", g=2), axis=AX.X), [tmp.b], smb)
                rstd_from_ss(fw, C, sm.t[:, 96:98], sm.t[:, 98:100], 256.0, smb, smb)
                for g in range(2):
                    gs = slice(g * 256, (g + 1) * 256)
                    fw.ts("dve", tmp.t[:, gs], ya.t[:, gs], sm.t[:, 98 + g:99 + g], None, ALU.mult, None, [ya.b, sm.b], [tmp.b])
                fw.tt("pool", yo.t[:, :], tmp.t[:, :], nw.t[:, :], ALU.mult, [tmp.b, nw.b], [yo.b])
                emit_yT(fw, C, yo, yT, 2, i)
                yield
        for st0 in (stG(0), stA(0)):
            for _ in st0:
                pass
        for i in range(NT):
            if i + 1 < NT and (i + 1) % 4 == 0:
                for _ in stG((i + 1) // 4):
                    pass
            interleave(stB(i), stA(i + 1) if i + 1 < NT else None)
        fw.barrier()


def phase_hg(fw, C, l):
    CUR_PHASE[0] = "hg"
    with ExitStack() as ph:
        WH = fw.tile(ph, "WH", [128, 8, 2048], BF16)
        load_w(fw, C.w_in[l], WH, C_HG, C_HG + 2048, 8)
        nw = fw.tile(ph, "hnw", [128, 512], F32)
        fw.dma("sp", nw.t[:, :], C.hgrn_norm_w[l].partition_broadcast(128), writes=[nw.b])
        S = fw.tile(ph, "hS", [128, 4, 128], F32)
        Sb = fw.tile(ph, "hSb", [128, 4, 128], BF16)
        PT_2 = [fw.tile(ph, "hPT", [128, 4, 128], BF16) for _ in range(2)]
        fw.memset("dve", S.t[:, :, :], 0.0, [S.b])
        fw.memset("dve", Sb.t[:, :, :], 0.0, [Sb.b])
        for PT in PT_2:
            fw.memset("pool", PT.t[:, :, :], 0.0, [PT.b])

        qg_2 = [fw.tile(ph, "hqg", [128, 4, 512], F32) for _ in range(2)]
        fg_2 = [fw.tile(ph, "hfg", [128, 4, 512], F32) for _ in range(2)]
        la_2 = [fw.tile(ph, "hla", [128, 4, 128], F32) for _ in range(2)]
        kT_2 = [fw.tile(ph, "hk", [128, 4, 128], F32) for _ in range(2)]
        cum_2 = [fw.tile(ph, "hcum", [128, 4, 128], F32) for _ in range(2)]
        ncb_2 = [fw.tile(ph, "hncb", [128, 4, 4], F32) for _ in range(2)]
        for nb_ in ncb_2:
            fw.memset("pool", nb_.t[:, :, :], 0.0, [nb_.b])
        eq_2 = [fw.tile(ph, "heq", [128, 4, 128], F32) for _ in range(2)]
        qd_2 = [fw.tile(ph, "hqd", [128, 4, 128], BF16) for _ in range(2)]
        qst_2 = [fw.tile(ph, "hqst", [128, 4, 128], BF16) for _ in range(2)]
        ek_2 = [fw.tile(ph, "hek", [128, 4, 128], F32) for _ in range(2)]
        Kt_2 = [fw.tile(ph, "hKt", [128, 4, 4, 128], BF16) for _ in range(2)]
        khT_2 = [fw.tile(ph, "hkhT", [128, 4, 128], BF16) for _ in range(2)]
        khat_2 = [fw.tile(ph, "hkhat", [128, 4, 128], BF16) for _ in range(2)]
        dec_2 = [fw.tile(ph, "hdec", [128, 4], F32) for _ in range(2)]
        vb_3 = [fw.tile(ph, "hvb", [128, 512], BF16) for _ in range(3)]
        sgate_3 = [fw.tile(ph, "hsg", [128, 512], F32) for _ in range(3)]
        ysq_2 = [fw.tile(ph, "hysq", [128, 512], F32) for _ in range(2)]
        st_2 = [fw.tile(ph, "hst", [128, 8], F32) for _ in range(2)]
        yn_2 = [fw.tile(ph, "hyn", [128, 512], F32) for _ in range(2)]
        yo_2 = [fw.tile(ph, "hyo", [128, 512], BF16) for _ in range(2)]
        yT_2 = [fw.tile(ph, "hyT", [128, 4, 128], BF16) for _ in range(2)]
        ps = C.ps
        pb0, pb1 = C.pb
        cb_ = C.consts.b
        lbc = C.lb_all.t[:, l, :]
        omc = C.oml_all.t[:, l, :]
        ones = cslice(C, "ones")
        triu = cslice(C, "triu")
        def stG(g):
                t0 = g * 512
                n = min(512, TP - t0)
                tiles = list(range(t0 // 128, (t0 + n) // 128))
                qg, fg = qg_2[g % 2], fg_2[g % 2]
                k = 0
                for h in range(4):
                    for (c0, dst, func) in ((0, qg, AF.Silu), (512, fg, AF.Sigmoid)):
                        pst = ps[0]
                        for kc in range(8):
                            fw.mm(pst.t[:, 0:n], WH.t[:, kc, c0 + h * 128:c0 + (h + 1) * 128], C.uT.t[:, kc, t0:t0 + n], kc == 0, kc == 7,
                                  [WH.b] + [C.uTb[j] for j in tiles], [pst.b])
                        fw.act(dst.t[:, h, 0:n], pst.t[:, 0:n], func, [pst.b], [dst.b])
                        yield
        def stA(i):
                PT = PT_2[i % 2]
                la, kT, cum, ncb, eq, qd, qst, ek, Kt, khT, khat, dec, vb, sgate, ysq, st, yn, yo, yT = la_2[i % 2], kT_2[i % 2], cum_2[i % 2], ncb_2[i % 2], eq_2[i % 2], qd_2[i % 2], qst_2[i % 2], ek_2[i % 2], Kt_2[i % 2], khT_2[i % 2], khat_2[i % 2], dec_2[i % 2], vb_3[i % 3], sgate_3[i % 3], ysq_2[i % 2], st_2[i % 2], yn_2[i % 2], yo_2[i % 2], yT_2[i % 2]
                tok_mm(fw, C, ps[2], WH, 1024, 512, i)
                yield
                tok_mm(fw, C, ps[3], WH, 1536, 512, i)
                yield
                fw.cp("act", vb.t[:, :], ps[2].t[:, :], [ps[2].b], [vb.b])
                yield
                fw.act(sgate.t[:, :], ps[3].t[:, :], AF.Silu, [ps[3].b], [sgate.b])
                yield
        def stB1(i):
                PT = PT_2[i % 2]
                la, kT, cum, ncb, eq, qd, qst, ek, Kt, khT, khat, dec, vb, sgate, ysq, st, yn, yo, yT = la_2[i % 2], kT_2[i % 2], cum_2[i % 2], ncb_2[i % 2], eq_2[i % 2], qd_2[i % 2], qst_2[i % 2], ek_2[i % 2], Kt_2[i % 2], khT_2[i % 2], khat_2[i % 2], dec_2[i % 2], vb_3[i % 3], sgate_3[i % 3], ysq_2[i % 2], st_2[i % 2], yn_2[i % 2], yo_2[i % 2], yT_2[i % 2]
                qT = VW(qg_2[(i // 4) % 2].t[:, :, (i % 4) * 128:(i % 4 + 1) * 128], qg_2[(i // 4) % 2].b)
                fT = VW(fg_2[(i // 4) % 2].t[:, :, (i % 4) * 128:(i % 4 + 1) * 128], fg_2[(i // 4) % 2].b)
                for h in range(4):
                    fw.ts("dve", fT.t[:, h, :], fT.t[:, h, :], omc[:, h:h + 1], lbc[:, h:h + 1], ALU.mult, ALU.add,
                          [fT.b, C.lb_all.b, C.oml_all.b], [fT.b])
                yield
                fw.act(la.t[:, :, :], fT.t[:, :, :], AF.Ln, [fT.b], [la.b])
                yield
                fw.ts("pool", kT.t[:, :, :], fT.t[:, :, :], -1.0, 1.0, ALU.mult, ALU.add, [fT.b], [kT.b])
                yield
                for h in range(4):
                    fw.op("dve", lambda g: g.tensor_tensor_scan(cum.t[:, h, :], ones, la.t[:, h, :], 0.0, ALU.mult, ALU.add),
                          [la.b, cb_], [cum.b])
                yield
                cv = cum.t[:, :, :].rearrange("p h (b c) -> p h b c", c=32)
                fw.ts("dve", ncb.t[:, :, 1:4], cv[:, :, 0:3, 31], -1.0, None, ALU.mult, None, [cum.b], [ncb.b])
                yield
                fw.tt("dve", eq.t[:, :, :].rearrange("p h (b c) -> p h b c", c=32), cv,
                      ncb.t[:, :, :].unsqueeze(3).to_broadcast([128, 4, 4, 32]), ALU.add, [cum.b, ncb.b], [eq.b])
                yield
                fw.act(eq.t[:, :, :], eq.t[:, :, :], AF.Exp, [eq.b], [eq.b])
                yield
                fw.tt("pool", qd.t[:, :, :], qT.t[:, :, :], eq.t[:, :, :], ALU.mult, [qT.b, eq.b], [qd.b])
                yield
                fw.act(eq.t[:, :, :], cum.t[:, :, :], AF.Exp, [cum.b], [eq.b])
                yield
                fw.tt("pool", qst.t[:, :, :], qT.t[:, :, :], eq.t[:, :, :], ALU.mult, [qT.b, eq.b], [qst.b])
                yield
                for b in range(4):
                    W_ = 32 * (b + 1)
                    eb = ek if b % 2 == 0 else eq
                    fw.tt("dve", eb.t[:, :, 0:W_], cum.t[:, :, 0:W_], ncb.t[:, :, b:b + 1].to_broadcast([128, 4, W_]), ALU.add,
                          [cum.b, ncb.b], [eb.b])
                    fw.act(eb.t[:, :, 0:W_], eb.t[:, :, 0:W_], AF.Exp, [eb.b], [eb.b], scale=-1.0)
                    fw.tt("pool" if b % 2 == 0 else "dve", Kt.t[:, :, b, 0:W_], eb.t[:, :, 0:W_], kT.t[:, :, 0:W_], ALU.mult,
                          [eb.b, kT.b], [Kt.b])
                    yield
                fw.tt("dve", ek.t[:, :, :], cum.t[:, :, 127:128].to_broadcast([128, 4, 128]), cum.t[:, :, :], ALU.subtract, [cum.b], [ek.b])
                yield
                fw.act(ek.t[:, :, :], ek.t[:, :, :], AF.Exp, [ek.b], [ek.b])
                yield
                fw.tt("pool", khT.t[:, :, :], ek.t[:, :, :], kT.t[:, :, :], ALU.mult, [ek.b, kT.b], [khT.b])
                yield
                for h in range(4):
                    fw.tr(pb0.t[:, h * 128:(h + 1) * 128], khT.t[:, h, :], C.identb.t[:, :], [khT.b, C.identb.b], [pb0.b])
                yield
                fw.cp("act", khat.t[:, :, :], pb0.t[:, 0:512].rearrange("p (h d) -> p h d", h=4), [pb0.b], [khat.b])
                yield
                fw.act(dec.t[:, :], cum.t[:, :, 127], AF.Exp, [cum.b], [dec.b])
                yield
                for h in range(4):
                    for b in range(4):
                        W_ = 32 * (b + 1)
                        fw.mm(ps[4].t[0:W_, h * 128 + 32 * b:h * 128 + 32 * b + 32], Kt.t[:, h, b, 0:W_], qd.t[:, h, 32 * b:32 * b + 32],
                              True, True, [Kt.b, qd.b], [ps[4].b])
                yield
                psv = ps[4].t[:, :].rearrange("p (h t) -> p h t", h=4)
                for b in range(4):
                    W_ = 32 * (b + 1)
                    bs = slice(32 * b, 32 * b + 32)
                    to, tn = CO["triu"]
                    mk = C.consts.t[0:W_, to + 32 * b:to + 32 * b + 32].unsqueeze(1).to_broadcast([W_, 4, 32])
                    fw.tt("dve", PT.t[0:W_, :, bs], psv[0:W_, :, bs], mk, ALU.mult, [ps[4].b, cb_], [PT.b])
                yield
        def stB2(i):
                PT = PT_2[i % 2]
                la, kT, cum, ncb, eq, qd, qst, ek, Kt, khT, khat, dec, vb, sgate, ysq, st, yn, yo, yT = la_2[i % 2], kT_2[i % 2], cum_2[i % 2], ncb_2[i % 2], eq_2[i % 2], qd_2[i % 2], qst_2[i % 2], ek_2[i % 2], Kt_2[i % 2], khT_2[i % 2], khat_2[i % 2], dec_2[i % 2], vb_3[i % 3], sgate_3[i % 3], ysq_2[i % 2], st_2[i % 2], yn_2[i % 2], yo_2[i % 2], yT_2[i % 2]
                qT = VW(qg_2[(i // 4) % 2].t[:, :, (i % 4) * 128:(i % 4 + 1) * 128], qg_2[(i // 4) % 2].b)
                fT = VW(fg_2[(i // 4) % 2].t[:, :, (i % 4) * 128:(i % 4 + 1) * 128], fg_2[(i // 4) % 2].b)
                for h in range(4):
                    hs = slice(h * 128, (h + 1) * 128)
                    fw.mm(ps[5].t[:, hs], PT.t[:, h, :], vb.t[:, hs], True, False, [PT.b, vb.b], [ps[5].b])
                    fw.mm(ps[5].t[:, hs], qst.t[:, h, :], Sb.t[:, h, :], False, True, [qst.b, Sb.b], [ps[5].b])
                yield
                for h in range(4):
                    hs = slice(h * 128, (h + 1) * 128)
                    fw.mm(ps[1].t[:, hs], khat.t[:, h, :], vb.t[:, hs], True, True, [khat.b, vb.b], [ps[1].b])
                yield
                for h in range(4):
                    hs = slice(h * 128, (h + 1) * 128)
                    fw.stt("dve", S.t[:, h, :], S.t[:, h, :], dec.t[:, h:h + 1], ps[1].t[:, hs], ALU.mult, ALU.add, [S.b, dec.b, ps[1].b], [S.b])
                yield
                fw.cp("pool", Sb.t[:, :, :], S.t[:, :, :], [S.b], [Sb.b])
                yield
                fw.act(ysq.t[:, :], ps[5].t[:, :], AF.Square, [ps[5].b], [ysq.b])
                yield
                fw.op("dve", lambda g: g.reduce_sum(st.t[:, 0:4], ysq.t[:, :].rearrange("p (h e) -> p h e", h=4), axis=AX.X), [ysq.b], [st.b])
                yield
                rstd_from_ss(fw, C, st.t[:, 0:4], st.t[:, 4:8], 128.0, [st.b], [st.b])
                yield
                for h in range(4):
                    hs = slice(h * 128, (h + 1) * 128)
                    fw.ts("dve", yn.t[:, hs], ps[5].t[:, hs], st.t[:, 4 + h:5 + h], None, ALU.mult, None, [ps[5].b, st.b], [yn.b])
                yield
                fw.tt("pool", yn.t[:, :], yn.t[:, :], nw.t[:, :], ALU.mult, [yn.b, nw.b], [yn.b])
                yield
                fw.tt("pool", yo.t[:, :], yn.t[:, :], sgate.t[:, :], ALU.mult, [yn.b, sgate.b], [yo.b])
                yield
                emit_yT(fw, C, yo, yT, 3, i)
                yield
        for st0 in (stG(0), stA(0), stB1(0), stA(1) if NT > 1 else None):
            if st0 is not None:
                for _ in st0:
                    pass
        for i in range(NT):
            gens = [stB2(i)]
            if i + 1 < NT:
                gens.append(stB1(i + 1))
            if i + 2 < NT:
                gens.append(chain_gens(stG((i + 2) // 4) if (i + 2) % 4 == 0 else None, stA(i + 2)))
            interleave_n(gens)
        fw.barrier()


TGS = [(0, 512), (512, 512), (1024, 512), (1536, 512), (2048, 128)]


def phase_s5(fw, C, l):
    ps = C.ps
    pb0, pb1 = C.pb
    cb_ = C.consts.b
    with ExitStack() as ph:
        GW = fw.tile(ph, "GW", [128, 4, 1024], BF16)
        load_w(fw, C.s5_glu_w[l], GW, 0, 1024, 4)
        Cre = fw.tile(ph, "Cre", [128, 16, 128], BF16)
        nCim = fw.tile(ph, "nCim", [128, 16, 128], BF16)
        fw.dma("pool", Cre.t[:, :, :], C.s5_cre[l], writes=[Cre.b])
        fw.dma("pool", nCim.t[:, :, :], C.s5_cim[l], writes=[nCim.b])
        fw.ts("pool", nCim.t[:, :, :], nCim.t[:, :, :], -1.0, None, ALU.mult, None, [nCim.b], [nCim.b])
        d5 = fw.tile(ph, "d5", [128, 4], F32)
        gb = fw.tile(ph, "gb5", [128, 8], F32)
        fw.dma("sp", d5.t[:, :], C.s5_dT[l], writes=[d5.b])
        fw.dma("sp", gb.t[:, :], C.s5_gbT[l], writes=[gb.b])
        sp_ = fw.tile(ph, "s5sm", [128, 48], F32)
        fw.dma("sp", sp_.t[:, :], C.s5_small[l], writes=[sp_.b])
        u5T = fw.tile(ph, "u5T", [128, 4, TP], BF16)
        Sall = fw.tile(ph, "Sall", [128, 137, 3, 16], F32)
        KT = fw.tile(ph, "KT", [128, 4, 16, 128], BF16)
        PW = fw.tile(ph, "PW", [128, 2, 17, 16], F32)
        wk = fw.tile(ph, "s5wk", [128, 12, 16], F32)
        with ExitStack() as pa:
            W5 = fw.tile(pa, "W5", [128, 8, 512], BF16)
            load_w(fw, C.w_in[l], W5, C_S5, C_S5 + 512, 8)
            k = 0
            for ct in range(4):
                for (t0, n) in TGS:
                    pst = ps[k % 2]
                    for kc in range(8):
                        fw.mm(pst.t[:, 0:n], W5.t[:, kc, ct * 128:(ct + 1) * 128], C.uT.t[:, kc, t0:t0 + n], kc == 0, kc == 7,
                              [W5.b] + C.uTb[t0 // 128:(t0 + n) // 128], [pst.b])
                    fw.cp("act" if k % 2 else "dve", u5T.t[:, ct, t0:t0 + n], pst.t[:, 0:n], [pst.b], [u5T.b])
                    k += 1
            fw.barrier()
        if os.environ.get("S5_STOP") == "A":
            fw.barrier(); return
        lr, li, lst = sp_.t[:, 0:16], sp_.t[:, 16:32], sp_.t[:, 32:48]
        W_ = [wk.t[:, j, :] for j in range(12)]
        R = [sp_.b, wk.b, PW.b]
        step, lrs, ang, em1, re_, im_, t_a, t_b, inv, co_re, co_im, rr = W_
        big = [fw.tile(ph, "s5big", [128, 18, 16], F32) for _ in range(5)]
        bigi = fw.tile(ph, "s5bigi", [128, 18, 16], mybir.dt.int32)
        RB = R + [b_.b for b_ in big] + [bigi.b, cb_]
        FACT = [1.0, 1.0, 2.0, 6.0, 24.0, 120.0, 720.0, 5040.0, 40320.0, 362880.0, 3628800.0]

        def horner_exp(out, r, deg, minus1=False):
            fw.ts("dve", out, r, 1.0 / FACT[deg], None, ALU.mult, None, RB, RB)
            for j in range(deg - 1, 0, -1):
                fw.stt("dve", out, out, 1.0 / FACT[j], r, ALU.add, ALU.mult, RB, RB)
            if not minus1:
                fw.ts("dve", out, out, 1.0, None, ALU.add, None, RB, RB)

        fw.ts("dve", rr, lst, 0.125, None, ALU.mult, None, RB, RB)
        horner_exp(step, rr, 10)
        for _ in range(3):
            fw.tt("dve", step, step, step, ALU.mult, RB, RB)
        fw.tt("dve", lrs, lr, step, ALU.mult, RB, RB)
        fw.tt("dve", ang, li, step, ALU.mult, RB, RB)
        mo = CO["mtab"][0]
        mtab = C.consts.t[:, mo:mo + 288].rearrange("p (m q) -> p m q", m=18)
        TH, XM, MAG, SN, CS = [b_.t[:, :, :] for b_ in big]
        fw.tt("dve", TH, mtab, ang.unsqueeze(1).to_broadcast([128, 18, 16]), ALU.mult, RB, RB)
        fw.tt("dve", XM, mtab, lrs.unsqueeze(1).to_broadcast([128, 18, 16]), ALU.mult, RB, RB)
        horner_exp(MAG, XM, 10)
        C1, C2 = 6.28125, 2.0 * math.pi - 6.28125

        def sin_reduced(out, th):
            fw.ts("dve", out, th, 1.0 / (2.0 * math.pi), None, ALU.mult, None, RB, RB)
            fw.cp("dve", bigi.t[:, :, :], out, RB, RB)
            fw.cp("dve", XM, bigi.t[:, :, :], RB, RB)
            fw.stt("dve", out, XM, -C1, th, ALU.mult, ALU.add, RB, RB)
            fw.stt("dve", out, XM, -C2, out, ALU.mult, ALU.add, RB, RB)
            fw.act(out, out, AF.Sin, RB, RB)

        sin_reduced(SN, TH)
        fw.ts("dve", TH, TH, math.pi / 2, None, ALU.add, None, RB, RB)
        sin_reduced(CS, TH)
        fw.tt("dve", PW.t[:, 0, :, :], MAG[:, 0:17, :], CS[:, 0:17, :], ALU.mult, RB, RB)
        fw.tt("dve", PW.t[:, 1, :, :], MAG[:, 0:17, :], SN[:, 0:17, :], ALU.mult, RB, RB)
        horner_exp(em1, lrs, 7, minus1=True)
        fw.tt("dve", re_, em1, CS[:, 1, :], ALU.mult, RB, RB)
        fw.tt("dve", t_a, SN[:, 17, :], SN[:, 17, :], ALU.mult, RB, RB)
        fw.stt("dve", re_, t_a, -2.0, re_, ALU.mult, ALU.add, RB, RB)
        fw.ts("dve", t_b, em1, 1.0, None, ALU.add, None, RB, RB)
        fw.tt("dve", im_, t_b, SN[:, 1, :], ALU.mult, RB, RB)
        fw.tt("dve", t_a, lr, lr, ALU.mult, RB, RB)
        fw.tt("dve", t_b, li, li, ALU.mult, RB, RB)
        fw.tt("dve", inv, t_a, t_b, ALU.add, RB, RB)
        fw.op("dve", lambda g: g.reciprocal(inv, inv), RB, RB)
        fw.tt("dve", t_a, re_, lr, ALU.mult, RB, RB)
        fw.tt("dve", t_b, im_, li, ALU.mult, RB, RB)
        fw.tt("dve", t_a, t_a, t_b, ALU.add, RB, RB)
        fw.tt("dve", co_re, t_a, inv, ALU.mult, RB, RB)
        fw.tt("dve", t_a, im_, lr, ALU.mult, RB, RB)
        fw.tt("dve", t_b, re_, li, ALU.mult, RB, RB)
        fw.tt("dve", t_a, t_a, t_b, ALU.subtract, RB, RB)
        fw.tt("dve", co_im, t_a, inv, ALU.mult, RB, RB)
        if C.debug and l == 0:
            fw.dma("sp", C.dbg5[:, 0:192], wk.t[:, :, :].rearrange("p a b -> p (a b)"), reads=[wk.b], writes=[Buf()])
            fw.dma("sp", C.dbg5[:, 192:736], PW.t[:, :, :, :].rearrange("p a m q -> p (a m q)"), reads=[PW.b], writes=[Buf()])
        if os.environ.get("S5_STOP") == "B":
            fw.barrier(); return
        fw.memset("pool", Sall.t[:, 0, :, :], 0.0, [Sall.b])
        with ExitStack() as pd:
            Bst = fw.tile(pd, "Bst", [128, 2, 4, 128], F32)
            Bb = fw.tile(pd, "Bb", [128, 2, 4, 128], F32)
            t0_ = fw.tile(pd, "tB", [128, 128], F32)
            tA = [fw.tile(pd, "tA", [128, 16, 128], F32)] * 2
            tB = [fw.tile(pd, "tBB", [128, 16, 128], F32)] * 2
            Xs = [fw.tile(pd, "X", [128, 2, 16, 128], BF16) for _ in range(2)]
            XTs = [fw.tile(pd, "XT", [128, 4, 2, 128], BF16) for _ in range(2)]
            psK = ps[2:6]
            zt = fw.tile(pd, "zt", [128, 512], BF16)
            fw.memset("pool", zt.t[:, :], 0.0, [zt.b])
            u5D = fw.tile(pd, "u5D", [128, 4, 16, 136], BF16)
            for ct_ in range(4):
                fw.cp("pool" if ct_ % 2 else "dve", u5D.t[:, ct_, :, :], u5T.t[:, ct_, :].rearrange("p (c b) -> p b c", b=16),
                      [u5T.b], [u5D.b])
            it = 0
            for ct in range(4):
                for j in range(4):
                    fw.mm(psK[j].t[:, :], zt.t[:, 0:128], zt.t[:, :], True, False, [zt.b], [psK[j].b])
                fw.dma("sp", Bst.t[:, 0, :, :], C.s5_bre[l][:, 4 * ct:4 * ct + 4, :], writes=[Bst.b])
                fw.dma("sp", Bst.t[:, 1, :, :], C.s5_bim[l][:, 4 * ct:4 * ct + 4, :], writes=[Bst.b])
                for pl in range(4):
                    pair = 4 * ct + pl
                    cr, ci = co_re[:, pair:pair + 1], co_im[:, pair:pair + 1]
                    fw.ts("dve", t0_.t[:, :], Bst.t[:, 1, pl, :], ci, None, ALU.mult, None, [Bst.b, wk.b], [t0_.b])
                    fw.stt("dve", Bb.t[:, 0, pl, :], Bst.t[:, 0, pl, :], cr, t0_.t[:, :], ALU.mult, ALU.subtract, [Bst.b, wk.b, t0_.b], [Bb.b])
                    fw.ts("dve", t0_.t[:, :], Bst.t[:, 1, pl, :], cr, None, ALU.mult, None, [Bst.b, wk.b], [t0_.b])
                    fw.stt("dve", Bb.t[:, 1, pl, :], Bst.t[:, 0, pl, :], ci, t0_.t[:, :], ALU.mult, ALU.add, [Bst.b, wk.b, t0_.b], [Bb.b])
                for pl in range(4):
                    pair = 4 * ct + pl
                    psG = ps[pair % 2]
                    X = Xs[pair % 2]
                    ta, tb = tA[pair % 2], tB[pair % 2]
                    bre = Bb.t[:, 0, pl, :].unsqueeze(1).to_broadcast([128, 16, 128])
                    bim = Bb.t[:, 1, pl, :].unsqueeze(1).to_broadcast([128, 16, 128])
                    prb = PW.t[:, 0, 0:16, pair].unsqueeze(2).to_broadcast([128, 16, 128])
                    pib = PW.t[:, 1, 0:16, pair].unsqueeze(2).to_broadcast([128, 16, 128])
                    RB_ = [Bb.b, PW.b]
                    fw.tt("dve", ta.t[:, :, :], bre, prb, ALU.mult, RB_, [ta.b])
                    fw.tt("pool", tb.t[:, :, :], bim, pib, ALU.mult, RB_, [tb.b])
                    fw.tt("dve", X.t[:, 0, :, :], ta.t[:, :, :], tb.t[:, :, :], ALU.subtract, [ta.b, tb.b], [X.b])
                    fw.tt("pool", tb.t[:, :, :], bim, prb, ALU.mult, RB_, [tb.b])
                    fw.tt("dve", ta.t[:, :, :], bre, pib, ALU.mult, RB_, [ta.b])
                    fw.tt("dve", X.t[:, 1, :, :], ta.t[:, :, :], tb.t[:, :, :], ALU.add, [ta.b, tb.b], [X.b])
                    fw.mm(psG.t[:, 0:272], zt.t[:, 0:128], zt.t[:, 0:272], True, False, [zt.b], [psG.b])
                    for m in range(16):
                        pk = psK[m // 4]
                        ks = slice((m % 4) * 128, (m % 4 + 1) * 128)
                        fw.mm(pk.t[:, ks], X.t[:, 0, m, :], Cre.t[:, pair, :], False, False, [X.b, Cre.b], [pk.b])
                        fw.mm(pk.t[:, ks], X.t[:, 1, m, :], nCim.t[:, pair, :], False, pl == 3, [X.b, nCim.b], [pk.b])
                    for mg in range(4):
                        XT = XTs[it % 2]
                        pbt = C.pb[it % 2]
                        for mm_ in range(4):
                            m = 4 * mg + mm_
                            for part in range(2):
                                fw.tr(pbt.t[:, (2 * mm_ + part) * 128:(2 * mm_ + part + 1) * 128], X.t[:, part, m, :], C.identb.t[:, :],
                                      [X.b, C.identb.b], [pbt.b])
                        fw.cp("act" if it % 2 else "dve", XT.t[:, :, :, :], pbt.t[:, :].rearrange("p (m a q) -> p m a q", m=4, a=2),
                              [pbt.b], [XT.b])
                        for mm_ in range(4):
                            m = 4 * mg + mm_
                            tau = 15 - m
                            rhs = u5D.t[:, ct, tau, :]
                            fw.mm(psG.t[:, 0:136], XT.t[:, mm_, 0, :], rhs, False, m == 15, [XT.b, u5D.b], [psG.b])
                            fw.mm(psG.t[:, 136:272], XT.t[:, mm_, 1, :], rhs, False, m == 15, [XT.b, u5D.b], [psG.b])
                        it += 1
                    fw.cp("act", Sall.t[:, 1:137, 0, pair], psG.t[:, 0:136], [psG.b], [Sall.b])
                    fw.cp("act", Sall.t[:, 1:137, 1, pair], psG.t[:, 136:272], [psG.b], [Sall.b])
                for j in range(4):
                    fw.cp("act" if j % 2 else "dve", KT.t[:, ct, 4 * j:4 * j + 4, :], psK[j].t[:, :].rearrange("p (m c) -> p m c", m=4),
                          [psK[j].b], [KT.b])
            fw.barrier()
        if C.debug and l == 0:
            fw.dma("pool", C.dbg5[:, 736:736 + 2048], KT.t[:, 0, :, :].rearrange("p m c -> p (m c)"), reads=[KT.b], writes=[Buf()])
            fw.dma("sp", C.dbg5[:, 2784:2784 + 137 * 48], Sall.t[:, :, :, :].rearrange("p c a q -> p (c a q)"), reads=[Sall.b], writes=[Buf()])
        if os.environ.get("S5_STOP") == "D":
            fw.barrier(); return
        with ExitStack() as pe_:
            A1 = fw.tile(pe_, "A1", [128, 2, 16], F32)
            A2 = fw.tile(pe_, "A2", [128, 2, 16], F32)
            p1 = fw.tile(pe_, "p1", [128, 2, 16], F32)
            p2 = fw.tile(pe_, "p2", [128, 2, 16], F32)
            fw.cp("dve", A1.t[:, 0, :], PW.t[:, 0, 16, :], [PW.b], [A1.b])
            fw.cp("dve", A1.t[:, 1, :], PW.t[:, 0, 16, :], [PW.b], [A1.b])
            fw.ts("dve", A2.t[:, 0, :], PW.t[:, 1, 16, :], -1.0, None, ALU.mult, None, [PW.b], [A2.b])
            fw.cp("dve", A2.t[:, 1, :], PW.t[:, 1, 16, :], [PW.b], [A2.b])
            for c in range(136):
                fw.tt("dve", p1.t[:, :, :], A1.t[:, :, :], Sall.t[:, c, 0:2, :], ALU.mult, [A1.b, Sall.b], [p1.b])
                fw.tt("dve", p2.t[:, :, :], A2.t[:, :, :], Sall.t[:, c, 1:3, :], ALU.mult, [A2.b, Sall.b], [p2.b])
                fw.tt("dve", p1.t[:, :, :], p1.t[:, :, :], p2.t[:, :, :], ALU.add, [p1.b, p2.b], [p1.b])
                fw.tt("dve", Sall.t[:, c + 1, 0:2, :], Sall.t[:, c + 1, 0:2, :], p1.t[:, :, :], ALU.add, [Sall.b, p1.b], [Sall.b])
                fw.cp("dve", Sall.t[:, c + 1, 2, :], Sall.t[:, c + 1, 0, :], [Sall.b], [Sall.b])
            fw.barrier()
        if os.environ.get("S5_STOP") == "E":
            fw.barrier(); return
        pfg = ExitStack()
        gT = fw.tile(pfg, "g5T", [128, 4, TP], BF16)
        with ExitStack() as pf:
            SP = fw.tile(pf, "SP", [128, 4, 2, 136, 16], BF16)
            u1 = fw.tile(pf, "u1", [128, 136, 16], F32)
            u2 = fw.tile(pf, "u2", [128, 136, 16], F32)
            z = fw.tile(pf, "z5", [128, 512], F32)
            z2 = fw.tile(pf, "z52", [128, 512], F32)
            k = 0
            for ct in range(4):
                for pl in range(4):
                    pair = 4 * ct + pl
                    srb = Sall.t[:, 0:136, 0, pair].unsqueeze(2).to_broadcast([128, 136, 16])
                    sib = Sall.t[:, 0:136, 1, pair].unsqueeze(2).to_broadcast([128, 136, 16])
                    prb = PW.t[:, 0, 1:17, pair].unsqueeze(1).to_broadcast([128, 136, 16])
                    pib = PW.t[:, 1, 1:17, pair].unsqueeze(1).to_broadcast([128, 136, 16])
                    RS = [Sall.b, PW.b]
                    fw.tt("dve", u1.t[:, :, :], srb, prb, ALU.mult, RS, [u1.b])
                    fw.tt("pool", u2.t[:, :, :], sib, pib, ALU.mult, RS, [u2.b])
                    fw.tt("dve", SP.t[:, pl, 0, :, :], u1.t[:, :, :], u2.t[:, :, :], ALU.subtract, [u1.b, u2.b], [SP.b])
                    fw.tt("pool", u2.t[:, :, :], sib, prb, ALU.mult, RS, [u2.b])
                    fw.tt("dve", u1.t[:, :, :], srb, pib, ALU.mult, RS, [u1.b])
                    fw.tt("dve", SP.t[:, pl, 1, :, :], u1.t[:, :, :], u2.t[:, :, :], ALU.add, [u1.b, u2.b], [SP.b])
                for (t0, n) in TGS:
                    c0, nch = t0 // 16, n // 16
                    pst = ps[k % 2]
                    k += 1
                    pv = pst.t[:, 0:n].rearrange("p (c b) -> p c b", b=16)
                    uv = u5T.t[:, ct, t0:t0 + n].rearrange("p (c b) -> p c b", b=16)
                    for tau in range(16):
                        fw.mm(pv[:, :, tau:16], KT.t[:, ct, tau, :], uv[:, :, 0:16 - tau], tau == 0, False, [KT.b, u5T.b], [pst.b])
                    for pl in range(4):
                        pair = 4 * ct + pl
                        fw.mm(pst.t[:, 0:n], Cre.t[:, pair, :], SP.t[:, pl, 0, c0:c0 + nch, :].rearrange("p c b -> p (c b)"), False, False,
                              [Cre.b, SP.b], [pst.b])
                        fw.mm(pst.t[:, 0:n], nCim.t[:, pair, :], SP.t[:, pl, 1, c0:c0 + nch, :].rearrange("p c b -> p (c b)"), False, pl == 3,
                              [nCim.b, SP.b], [pst.b])
                    fw.stt("dve", z.t[:, 0:n], u5T.t[:, ct, t0:t0 + n], d5.t[:, ct:ct + 1], pst.t[:, 0:n], ALU.mult, ALU.add,
                           [u5T.b, d5.b, pst.b], [z.b])
                    fw.tt("pool", z2.t[:, 0:n], z.t[:, 0:n], z.t[:, 0:n], ALU.mult, [z.b], [z2.b])
                    fw.ts("pool", z2.t[:, 0:n], z2.t[:, 0:n], 0.044715, 1.0, ALU.mult, ALU.add, [z2.b], [z2.b])
                    fw.tt("pool", z2.t[:, 0:n], z2.t[:, 0:n], z.t[:, 0:n], ALU.mult, [z2.b, z.b], [z2.b])
                    fw.act(z2.t[:, 0:n], z2.t[:, 0:n], AF.Sigmoid, [z2.b], [z2.b], scale=2.0 * math.sqrt(2.0 / math.pi))
                    fw.tt("pool", gT.t[:, ct, t0:t0 + n], z.t[:, 0:n], z2.t[:, 0:n], ALU.mult, [z.b, z2.b], [gT.b])
            fw.barrier()
        if os.environ.get("S5_STOP") == "F":
            pfg.close(); fw.barrier(); return
        with ExitStack() as pg:
            sgs = [fw.tile(pg, "sg5", [128, 512], F32) for _ in range(2)]
            yos = [fw.tile(pg, "yo5", [128, 512], BF16) for _ in range(2)]
            k = 0
            for nb in range(4):
                for (t0, n) in TGS:
                    pa_, pg_ = ps[2 + 2 * (k % 2)], ps[3 + 2 * (k % 2)]
                    sg, yo = sgs[k % 2], yos[k % 2]
                    k += 1
                    for kc in range(4):
                        fw.mm(pa_.t[:, 0:n], GW.t[:, kc, nb * 128:(nb + 1) * 128], gT.t[:, kc, t0:t0 + n], kc == 0, kc == 3, [GW.b, gT.b], [pa_.b])
                    for kc in range(4):
                        fw.mm(pg_.t[:, 0:n], GW.t[:, kc, 512 + nb * 128:512 + (nb + 1) * 128], gT.t[:, kc, t0:t0 + n], kc == 0, kc == 3,
                              [GW.b, gT.b], [pg_.b])
                    fw.act(sg.t[:, 0:n], pg_.t[:, 0:n], AF.Sigmoid, [pg_.b, gb.b], [sg.b], bias=gb.t[:, 4 + nb:5 + nb])
                    fw.stt("dve", yo.t[:, 0:n], pa_.t[:, 0:n], gb.t[:, nb:nb + 1], sg.t[:, 0:n], ALU.add, ALU.mult, [pa_.b, gb.b, sg.b], [yo.b])
                    fw.dma("sp", C.YT[4 + nb, :, t0:t0 + n], yo.t[:, 0:n], reads=[yo.b], writes=C.YTb[1][t0 // 128:(t0 + n) // 128])
            fw.barrier()
        pfg.close()
        fw.barrier()


MGS = [(g * 256, min(256, TP - g * 256)) for g in range((TP + 255) // 256)]


def rms_epilogue(fw, C, psA, psB, nw, xt, wk2):
    junk, ss, tmp = wk2
    fw.act(junk.t[:, 0:512], psA.t[:, :], AF.Square, [psA.b], [junk.b, ss.b], accum=ss.t[:, 2:3])
    fw.act(junk.t[:, 512:1024], psB.t[:, :], AF.Square, [psB.b], [junk.b, ss.b], accum=ss.t[:, 3:4])
    fw.tt("dve", ss.t[:, 2:3], ss.t[:, 2:3], ss.t[:, 3:4], ALU.add, [ss.b], [ss.b])
    rstd_from_ss(fw, C, ss.t[:, 2:3], ss.t[:, 3:4], 1024.0, [ss.b], [ss.b])
    fw.stt("dve", tmp.t[:, 0:512], psA.t[:, :], ss.t[:, 3:4], nw.t[:, 0:512], ALU.mult, ALU.mult, [psA.b, ss.b, nw.b], [tmp.b])
    fw.stt("dve", tmp.t[:, 512:1024], psB.t[:, :], ss.t[:, 3:4], nw.t[:, 512:1024], ALU.mult, ALU.mult, [psB.b, ss.b, nw.b], [tmp.b])
    fw.tt("pool", xt.t[:, :], xt.t[:, :], tmp.t[:, :], ALU.add, [xt.b, tmp.b], [xt.b])


def phase_merge(fw, C, l):
    ps = C.ps
    with ExitStack() as ph:
        WG = fw.tile(ph, "WG", [128, 8, 4096], BF16)
        load_w(fw, C.w_in[l], WG, C_GATE, C_GATE + 4096, 8)
        WB = fw.tile(ph, "WB", [128, 16, 1024], BF16)
        for n in range(4):
            v = C.w_branch[l][n].rearrange("(cb p) d -> p cb d", p=128)
            fw.dma("pool", WB.t[:, 4 * n:4 * n + 4, :], v, writes=[WB.b])
        WO = fw.tile(ph, "WO", [128, 8, 1024], BF16)
        load_w(fw, C.w_out[l], WO, 0, 1024, 8)
        nw1 = fw.tile(ph, "nw1", [128, D], F32)
        nw2 = fw.tile(ph, "nw2", [128, D], F32)
        fw.dma("sp", nw1.t[:, :], C.norm_post_mix[l].partition_broadcast(128), writes=[nw1.b])
        fw.dma("sp", nw2.t[:, :], C.norm_pre_mlp[l].partition_broadcast(128), writes=[nw2.b])
        YTs = fw.tile(ph, "YTs", [128, 16, 256], BF16)
        mixT = fw.tile(ph, "mixT", [128, 8, 256], BF16)
        acc = fw.tile(ph, "macc", [128, 256], F32)
        sgs = [fw.tile(ph, "msg", [128, 256], F32) for _ in range(2)]
        tmpm = fw.tile(ph, "mtmp", [128, 256], F32)
        xts = [fw.tile(ph, "mxt", [128, D], F32) for _ in range(2)]
        wk = (fw.tile(ph, "junk", [128, D], BF16), fw.tile(ph, "ss", [128, 4], F32), fw.tile(ph, "ub", [128, D], BF16))
        wk2 = (wk[0], wk[1], fw.tile(ph, "mtmp2", [128, D], F32))
        k = 0
        for (t0, n) in MGS:
            tiles = list(range(t0 // 128, (t0 + n) // 128))
            fw.dma("sp", YTs.t[:, :, 0:n], C.YT[:, :, t0:t0 + n].rearrange("c p t -> p c t"),
                   reads=[C.YTb[m][i] for m in range(4) for i in tiles], writes=[YTs.b])
            for db in range(8):
                for nn in range(4):
                    pg_, pb_ = ps[2 * (k % 2)], ps[2 * (k % 2) + 1]
                    sg = sgs[k % 2]
                    k += 1
                    c0 = nn * 1024 + db * 128
                    for kc in range(8):
                        fw.mm(pg_.t[:, 0:n], WG.t[:, kc, c0:c0 + 128], C.uT.t[:, kc, t0:t0 + n], kc == 0, kc == 7,
                              [WG.b] + [C.uTb[i] for i in tiles], [pg_.b])
                    for cb in range(4):
                        fw.mm(pb_.t[:, 0:n], WB.t[:, 4 * nn + cb, db * 128:(db + 1) * 128], YTs.t[:, 4 * nn + cb, 0:n], cb == 0, cb == 3,
                              [WB.b, YTs.b], [pb_.b])
                    fw.act(sg.t[:, 0:n], pg_.t[:, 0:n], AF.Sigmoid, [pg_.b], [sg.b])
                    if nn == 0:
                        fw.tt("dve", acc.t[:, 0:n], sg.t[:, 0:n], pb_.t[:, 0:n], ALU.mult, [sg.b, pb_.b], [acc.b])
                    else:
                        fw.tt("dve", tmpm.t[:, 0:n], sg.t[:, 0:n], pb_.t[:, 0:n], ALU.mult, [sg.b, pb_.b], [tmpm.b])
                        if nn < 3:
                            fw.tt("pool", acc.t[:, 0:n], acc.t[:, 0:n], tmpm.t[:, 0:n], ALU.add, [acc.b, tmpm.b], [acc.b])
                        else:
                            fw.tt("pool", mixT.t[:, db, 0:n], acc.t[:, 0:n], tmpm.t[:, 0:n], ALU.add, [acc.b, tmpm.b], [mixT.b])
            for i in tiles:
                xt = xts[i % 2]
                src = C.h0 if l == 0 else C.hbuf
                fw.dma("sp", xt.t[:, :], src[i * 128:(i + 1) * 128, :], reads=([] if l == 0 else [C.hb[i]]), writes=[xt.b])
                sub = slice(i * 128 - t0, i * 128 - t0 + 128)
                for dh in range(2):
                    pst = ps[4 + dh]
                    for db in range(8):
                        fw.mm(pst.t[:, :], mixT.t[:, db, sub], WO.t[:, db, dh * 512:(dh + 1) * 512], db == 0, db == 7, [mixT.b, WO.b], [pst.b])
                rms_epilogue(fw, C, ps[4], ps[5], nw1, xt, wk2)
                fw.dma("sp", C.hbuf[i * 128:(i + 1) * 128, :], xt.t[:, :], reads=[xt.b], writes=[C.hb[i]])
                norm_rows_to_T(fw, C, ph, xt.t[:, :], xt.b, nw2, C.uT, C.uTb, i, wk)
        fw.barrier()


def phase_mlp(fw, C, l, last):
    ps = C.ps
    with ExitStack() as ph:
        WU = fw.tile(ph, "WU", [128, 8, 4096], BF16)
        load_w(fw, C.w_up[l], WU, 0, 4096, 8)
        WD = fw.tile(ph, "WD", [128, 32, 1024], BF16)
        load_w(fw, C.w_down[l], WD, 0, 1024, 32)
        nw = fw.tile(ph, "nw3", [128, D], F32)
        fw.dma("sp", nw.t[:, :], C.norm_post_mlp[l].partition_broadcast(128), writes=[nw.b])
        hT = fw.tile(ph, "hT", [128, 32, 256], BF16)
        rl = [fw.tile(ph, "rl", [128, 512], BF16) for _ in range(2)]
        xts = [fw.tile(ph, "pxt", [128, D], F32)] * 2
        ptmp = fw.tile(ph, "ptmp", [128, D], F32)
        wk2 = (ptmp, fw.tile(ph, "ss", [128, 4], F32), ptmp)
        k = 0
        for (t0, n) in MGS:
            tiles = list(range(t0 // 128, (t0 + n) // 128))
            for fp in range(16):
                pst = ps[k % 4]
                r = rl[k % 2]
                k += 1
                for j in range(2):
                    ffc = 2 * fp + j
                    for kc in range(8):
                        fw.mm(pst.t[:, j * 256:j * 256 + n], WU.t[:, kc, ffc * 128:(ffc + 1) * 128], C.uT.t[:, kc, t0:t0 + n], kc == 0, kc == 7,
                              [WU.b] + [C.uTb[i] for i in tiles], [pst.b])
                pv = pst.t[:, :].rearrange("p (j t) -> p j t", j=2)[:, :, 0:n]
                rv = r.t[:, :].rearrange("p (j t) -> p j t", j=2)[:, :, 0:n]
                fw.act(rv, pv, AF.Relu, [pst.b], [r.b])
                fw.tt("pool" if fp % 2 else "dve", hT.t[:, 2 * fp:2 * fp + 2, 0:n], rv, rv, ALU.mult, [r.b], [hT.b])
            for i in tiles:
                xt = xts[i % 2]
                fw.dma("sp", xt.t[:, :], C.hbuf[i * 128:(i + 1) * 128, :], reads=[C.hb[i]], writes=[xt.b])
                sub = slice(i * 128 - t0, i * 128 - t0 + 128)
                for dh in range(2):
                    pst = ps[4 + dh]
                    for ffc in range(32):
                        fw.mm(pst.t[:, :], hT.t[:, ffc, sub], WD.t[:, ffc, dh * 512:(dh + 1) * 512], ffc == 0, ffc == 31, [hT.b, WD.b], [pst.b])
                rms_epilogue(fw, C, ps[4], ps[5], nw, xt, wk2)
                if not last:
                    fw.dma("sp", C.hbuf[i * 128:(i + 1) * 128, :], xt.t[:, :], reads=[xt.b], writes=[C.hb[i]])
                else:
                    if C.debug:
                        fw.dma("sp", C.hbuf[i * 128:(i + 1) * 128, :], xt.t[:, :], reads=[xt.b], writes=[C.hb[i]])
                    lo = max(i * 128, 16)
                    hi = min((i + 1) * 128, T)
                    if hi > lo:
                        fw.dma("sp", C.out[lo - 16:hi - 16, :], xt.t[lo - i * 128:hi - i * 128, :], reads=[xt.b], writes=[C.outb])
        fw.barrier()


def build(debug=False, n_layers=DEPTH, phases=None):
    nc = bass.Bass("TRN2", target_bir_lowering=False)
    C = Ctx()
    C.debug = debug

    def din(name, shape):
        return nc.dram_tensor(name, list(shape), F32, kind="ExternalInput").ap()

    C.h0 = din("h0", [TP, D])
    C.consts_d = din("consts", [128, CO_TOTAL[0]])
    C.w_in = din("w_in", [4, D, N_IN])
    C.w_branch = din("w_branch", [4, 4, 512, D])
    C.w_out = din("w_out", [4, D, D])
    C.w_up = din("w_up", [4, D, 4 * D])
    C.w_down = din("w_down", [4, 4 * D, D])
    C.s5_glu_w = din("s5_glu_w", [4, 512, 1024])
    for nm in ("norm_pre_mix", "norm_post_mix", "norm_pre_mlp", "norm_post_mlp"):
        setattr(C, nm, din(nm, [4, D]))
    for nm in ("ret_gn_w", "ssd_norm_w", "hgrn_norm_w", "ssd_dsk"):
        setattr(C, nm, din(nm, [4, 512]))
    C.ssd_dt_bias = din("ssd_dt_bias", [4, 8])
    C.ssd_a_log = din("ssd_a_log", [4, 8])
    C.lbT = din("lbT", [128, 4, 4])
    C.ssd_cw = din("ssd_cw", [4, 128, 8, 4])
    C.ssd_cb = din("ssd_cb", [4, 128, 8])
    C.s5_small = din("s5_small", [4, 128, 48])
    for nm in ("s5_bre", "s5_bim", "s5_cre", "s5_cim"):
        setattr(C, nm, din(nm, [4, 128, 16, 128]))
    C.s5_dT = din("s5_dT", [4, 128, 4])
    C.s5_gbT = din("s5_gbT", [4, 128, 8])
    C.out = nc.dram_tensor("out", [2048, D], F32, kind="ExternalOutput").ap()
    sk = "ExternalOutput" if debug else "Internal"
    C.hbuf = nc.dram_tensor("hbuf", [TP, D], F32, kind=sk).ap()
    C.YT = nc.dram_tensor("YT", [16, 128, TP], BF16, kind=sk).ap()
    if debug:
        C.dbg5 = nc.dram_tensor("dbg5", [128, 2784 + 137 * 48], F32, kind="ExternalOutput").ap()
    C.hb = [Buf("hb%d" % i) for i in range(NT)]
    C.YTb = [[Buf("yt%d_%d" % (m, i)) for i in range(NT)] for m in range(4)]
    C.outb = Buf("out")
    with ExitStack() as es:
        fw = FW(nc, es)
        C.ps = [TL(es.enter_context(nc.psum_tensor("ps%d" % j, [128, 512], F32)), "ps%d" % j) for j in range(6)]
        C.pb = [TL(es.enter_context(nc.psum_tensor("pb%d" % j, [128, 1024], BF16)), "pb%d" % j) for j in range(2)]
        C.consts = fw.tile(es, "consts", [128, CO_TOTAL[0]], F32)
        fw.dma("sp", C.consts.t[:, :], C.consts_d[:, :], writes=[C.consts.b])
        C.identb = fw.tile(es, "identb", [128, 128], BF16)
        C.identf = fw.tile(es, "identf", [128, 128], F32)
        fw.cp("dve", C.identb.t[:, :], cslice(C, "ident"), [C.consts.b], [C.identb.b])
        fw.cp("dve", C.identf.t[:, :], cslice(C, "ident"), [C.consts.b], [C.identf.b])
        C.uT = fw.tile(es, "uT", [128, 8, TP], BF16)
        C.uTb = [Buf("uT%d" % i) for i in range(NT)]
        C.lb_all = fw.tile(es, "lb_all", [128, 4, 4], F32)
        C.oml_all = fw.tile(es, "oml_all", [128, 4, 4], F32)
        lbe = fw.tile(es, "lbe", [128, 4, 4], F32)
        lsum = fw.tile(es, "lsum", [128, 4], F32)
        R = [lbe.b, lsum.b, C.lb_all.b, C.oml_all.b]
        fw.dma("sp", lbe.t[:, :, :], C.lbT[:, :, :], writes=[lbe.b])
        fw.act(lbe.t[:, :, :], lbe.t[:, :, :], AF.Exp, R, R)
        fw.tt("dve", lsum.t[:, :], lbe.t[:, 0, :], lbe.t[:, 1, :], ALU.add, R, R)
        fw.tt("dve", lsum.t[:, :], lsum.t[:, :], lbe.t[:, 2, :], ALU.add, R, R)
        fw.tt("dve", lsum.t[:, :], lsum.t[:, :], lbe.t[:, 3, :], ALU.add, R, R)
        fw.op("dve", lambda g: g.reciprocal(lsum.t[:, :], lsum.t[:, :]), R, R)
        fw.memset("dve", C.lb_all.t[:, 0, :], 0.0, R)
        for ll in range(1, 4):
            fw.tt("dve", lbe.t[:, ll, :], lbe.t[:, ll, :], lsum.t[:, :], ALU.mult, R, R)
            fw.tt("dve", C.lb_all.t[:, ll, :], C.lb_all.t[:, ll - 1, :], lbe.t[:, ll, :], ALU.add, R, R)
        fw.ts("dve", C.oml_all.t[:, :, :], C.lb_all.t[:, :, :], -1.0, 1.0, ALU.mult, ALU.add, R, R)
        fw.barrier()
        allp = ("norm", "ret", "s5", "ssd", "hg", "merge", "mlp")
        for l in range(n_layers):
            for pn in allp:
                if phases is not None and pn not in phases:
                    continue
                if pn == "norm":
                    phase_norm1(fw, C, l)
                elif pn == "ret":
                    phase_ret(fw, C, l)
                elif pn == "s5":
                    phase_s5(fw, C, l)
                elif pn == "ssd":
                    phase_ssd(fw, C, l)
                elif pn == "hg":
                    phase_hg(fw, C, l)
                elif pn == "merge":
                    phase_merge(fw, C, l)
                elif pn == "mlp":
                    phase_mlp(fw, C, l, last=(l == n_layers - 1))
        fw.barrier()
        C.n_inst, C.n_wait = fw.n_inst, fw.n_wait
    return nc, C


CO_TOTAL = [0]
_CONSTS = None


def get_consts():
    global _CONSTS
    if _CONSTS is None:
        _CONSTS = host_consts()
        CO_TOTAL[0] = _CONSTS.shape[1]
    return _CONSTS


def make_in_maps(inp):
    consts = get_consts()
    P = host_params(inp)
    f = lambda a: np.ascontiguousarray(np.asarray(a, np.float32))
    shared = {"consts": consts}
    for nm in ("w_in", "w_branch", "w_out", "w_up", "w_down", "s5_glu_w", "norm_pre_mix", "norm_post_mix", "norm_pre_mlp",
               "norm_post_mlp", "ret_gn_w", "ssd_norm_w", "hgrn_norm_w", "ssd_dt_bias", "ssd_a_log"):
        shared[nm] = f(inp[nm])
    for nm in ("lbT", "ssd_cw", "ssd_cb", "ssd_dsk", "s5_small", "s5_bre", "s5_bim", "s5_cre", "s5_cim", "s5_dT", "s5_gbT"):
        shared[nm] = P[nm]
    x = np.asarray(inp["x"], np.float32)
    meta = np.asarray(inp["meta_tokens"], np.float32)
    maps = []
    for b in range(x.shape[0]):
        h0 = np.zeros((TP, D), np.float32)
        h0[0:16] = meta
        h0[16:T] = x[b]
        m = dict(shared)
        m["h0"] = h0
        maps.append(m)
    return maps


_NC = None


def kernel(**inputs):
    global _NC
    maps = make_in_maps(inputs)
    if _NC is None:
        _NC = build()[0]
    res = run_bass_kernel_spmd(_NC, maps, core_ids=list(range(len(maps))))
    return np.stack([np.asarray(r["out"], np.float32) for r in res.results], axis=0)
```
